# Optimizing a Trainium2 kernel written in Bass

```python
import math
import jax
import jax.numpy as jnp
from jax import lax
import numpy as np

D_MODEL = 1024
BATCH = 8
SEQ = 4096
DEPTH = 4

GRID_W = 64
CTX_LEN = 256
N_EVEN = (DEPTH + 1) // 2
N_ODD = DEPTH // 2
EPS = 1e-6
NEG_INF = -1e30
N_MOD = 6
ROPE_THETA = 10000.0
ROT_AXIS = 32

A_HEADS = 4
A_HD = 64
A_VD = 2 * A_HD
A_QK = A_HEADS * 2 * A_HD
A_V = A_HEADS * A_VD
Q_BLOCK = 128

B_HEADS = 4
B_DK = 128
B_DV = 128
B_W = B_HEADS * B_DV
B_CONV = 4
B_CHUNK = 64

C_WIDTH = 512
C_BLOCKS = 8
C_BD = C_WIDTH // C_BLOCKS
C_CONV = 4
C_POW = 8.0

D_HEADS = 8
D_KV = 2
D_GROUP = D_HEADS // D_KV
D_HD = 64
WINDOW = 128
D_BLOCK = 128

FFN = 2816
FFN_CONV = 3

EVEN_SPLIT = (A_QK, A_QK, A_V, 3 * B_W, B_W, 2 * B_HEADS, 2 * B_HEADS)
EVEN_IN = sum(EVEN_SPLIT)
EVEN_OUT = A_V + B_W
ODD_SPLIT = (C_WIDTH, C_WIDTH, D_HEADS * D_HD, D_KV * D_HD, D_KV * D_HD)
ODD_IN = sum(ODD_SPLIT)
ODD_OUT = C_WIDTH + D_HEADS * D_HD

kernel_name = 'hybrid_diffusion_prefix_backbone'


def _split(z, widths):
    idx = np.cumsum(np.array(widths))[:-1].tolist()
    return jnp.split(z, idx, axis=-1)


def _rmsnorm(x, g):
    xf = x.astype(jnp.float32)
    y = xf * lax.rsqrt(jnp.mean(xf * xf, axis=-1, keepdims=True) + EPS)
    return (y * g.astype(jnp.float32)).astype(x.dtype)


def _l2norm(x):
    xf = x.astype(jnp.float32)
    return xf * lax.rsqrt(jnp.sum(xf * xf, axis=-1, keepdims=True) + EPS)


def _modulate(x, g, shift, scale):
    return _rmsnorm(x, g) * (1.0 + scale) + shift


def _dwconv(x, w):
    k, ch = w.shape
    pl = k // 2
    return lax.conv_general_dilated(x, w.reshape(k, 1, ch).astype(x.dtype), window_strides=(1,),
                                    padding=[(pl, k - 1 - pl)], dimension_numbers=('NWC', 'WIO', 'NWC'),
                                    feature_group_count=ch)


def _rope_tables(rows):
    row = jnp.repeat(jnp.arange(rows, dtype=jnp.float32), GRID_W)
    col = jnp.tile(jnp.arange(GRID_W, dtype=jnp.float32), rows)
    inv = ROPE_THETA ** (-jnp.arange(0, ROT_AXIS, 2, dtype=jnp.float32) / ROT_AXIS)
    ar = row[:, None] * inv
    ac = col[:, None] * inv
    return (jnp.cos(ar), jnp.sin(ar), jnp.cos(ac), jnp.sin(ac))


def _rope_1d(x, cos, sin):
    f = cos.shape[-1]
    x1, x2 = x[..., :f], x[..., f:]
    c = cos[:, None, :].astype(x.dtype)
    s = sin[:, None, :].astype(x.dtype)
    return jnp.concatenate([x1 * c - x2 * s, x2 * c + x1 * s], axis=-1)


def _rope_2d(x, tabs):
    cr, sr, cc, sc = tabs
    return jnp.concatenate([_rope_1d(x[..., :ROT_AXIS], cr, sr), _rope_1d(x[..., ROT_AXIS:], cc, sc)], axis=-1)


def _diff_attend(q, k, v, lam):
    s = jnp.einsum('bqhmd,bkhmd->bhmqk', q, k).astype(jnp.float32) * (A_HD ** -0.5)
    p = jax.nn.softmax(s, axis=-1)
    w = p[:, :, 0] - lam * p[:, :, 1]
    return jnp.einsum('bhqk,bkhe->bqhe', w.astype(v.dtype), v)


def _diff_attention(q_l, k_l, v_l, q_c, k_c, v_c, lam, with_ctx_out):
    bsz, n = q_l.shape[:2]
    nb = n // Q_BLOCK
    k_all = jnp.concatenate([k_c, k_l], axis=1)
    v_all = jnp.concatenate([v_c, v_l], axis=1)
    qb = jnp.moveaxis(q_l.reshape(bsz, nb, Q_BLOCK, A_HEADS, 2, A_HD), 1, 0)
    o = lax.map(lambda qi: _diff_attend(qi, k_all, v_all, lam), qb)
    o_l = jnp.moveaxis(o, 0, 1).reshape(bsz, n, A_HEADS, A_VD)
    o_c = _diff_attend(q_c, k_c, v_c, lam) if with_ctx_out else None
    return o_l, o_c


def _chunk(t):
    bsz, L, h = t.shape[:3]
    t = t.reshape((bsz, L // B_CHUNK, B_CHUNK, h) + t.shape[3:])
    return jnp.moveaxis(jnp.moveaxis(t, 3, 2), 1, 0)


def _unchunk(o):
    nc, bsz, h, cl, dv = o.shape
    return jnp.moveaxis(jnp.moveaxis(o, 0, 1), 2, 3).reshape(bsz, nc * cl, h, dv)


def _gdn_prepare(q, k, v, beta, g):
    q = _chunk(q) * (B_DK ** -0.5)
    k = _chunk(k)
    v = _chunk(v)
    beta = _chunk(beta)
    G = jnp.cumsum(_chunk(g), axis=-1)
    idx = jnp.arange(B_CHUNK)
    lower = idx[:, None] >= idx[None, :]
    strict = idx[:, None] > idx[None, :]
    dg = G[..., :, None] - G[..., None, :]
    decay = jnp.where(lower, jnp.exp(jnp.where(lower, dg, 0.0)), 0.0)
    kk = jnp.einsum('nbhid,nbhjd->nbhij', k, k)
    t_mat = jnp.where(strict, beta[..., :, None] * kk * decay, 0.0) + jnp.eye(B_CHUNK, dtype=jnp.float32)
    u = lax.linalg.triangular_solve(t_mat, beta[..., None] * v, left_side=True, lower=True)
    w = lax.linalg.triangular_solve(t_mat, (beta * jnp.exp(G))[..., None] * k, left_side=True, lower=True)
    qk = jnp.einsum('nbhid,nbhjd->nbhij', q, k) * decay
    q_dec = q * jnp.exp(G)[..., None]
    k_dec = k * jnp.exp(G[..., -1:] - G)[..., None]
    g_last = jnp.exp(G[..., -1])
    return (u, w, qk, q_dec, k_dec, g_last)


def _gdn_scan(chunks, s0):
    def step(s, xs):
        u, w, qk, q_dec, k_dec, g_last = xs
        v_new = u - jnp.einsum('bhcd,bhde->bhce', w, s)
        o = jnp.einsum('bhcd,bhde->bhce', q_dec, s) + jnp.einsum('bhij,bhje->bhie', qk, v_new)
        s = s * g_last[..., None, None] + jnp.einsum('bhcd,bhce->bhde', k_dec, v_new)
        return s, o
    s, o = lax.scan(step, s0, chunks)
    return o, s


def _gdn_direction(q_c, k_c, v_c, b_c, g_c, q_l, k_l, v_l, b_l, g_l, reverse):
    def flip(t):
        return jnp.flip(t, axis=1) if reverse else t
    s0 = jnp.zeros((q_c.shape[0], B_HEADS, B_DK, B_DV), jnp.float32)
    o_c, s_c = _gdn_scan(_gdn_prepare(flip(q_c), flip(k_c), flip(v_c), flip(b_c), flip(g_c)), s0)
    o_l, _ = _gdn_scan(_gdn_prepare(flip(q_l), flip(k_l), flip(v_l), flip(b_l), flip(g_l)), s_c)
    return flip(_unchunk(o_c)), flip(_unchunk(o_l))


def _gdn_bidirectional(ctx_in, lat_in):
    q_c, k_c, v_c, beta_c, g_c = ctx_in
    q_l, k_l, v_l, beta_l, g_l = lat_in
    o_c, o_l = 0.0, 0.0
    for d in range(2):
        oc, ol = _gdn_direction(q_c, k_c, v_c, beta_c[:, :, d], g_c[:, :, d],
                                q_l, k_l, v_l, beta_l[:, :, d], g_l[:, :, d], d == 1)
        o_c = o_c + oc
        o_l = o_l + ol
    return o_c, o_l


def _even_prep(h, w_in, conv_w, a_log, dt_bias, tabs):
    bsz, n, _ = h.shape
    qa, ka, va, qkv, gate, b_raw, a_raw = _split(h @ w_in, EVEN_SPLIT)
    qa = qa.reshape(bsz, n, 2 * A_HEADS, A_HD)
    ka = ka.reshape(bsz, n, 2 * A_HEADS, A_HD)
    if tabs is not None:
        qa = _rope_2d(qa, tabs)
        ka = _rope_2d(ka, tabs)
    qa = qa.reshape(bsz, n, A_HEADS, 2, A_HD)
    ka = ka.reshape(bsz, n, A_HEADS, 2, A_HD)
    va = va.reshape(bsz, n, A_HEADS, A_VD)
    qkv = jax.nn.silu(_dwconv(qkv, conv_w))
    qb, kb, vb = _split(qkv, (B_W, B_W, B_W))
    qb = _l2norm(qb.reshape(bsz, n, B_HEADS, B_DK))
    kb = _l2norm(kb.reshape(bsz, n, B_HEADS, B_DK))
    vb = vb.reshape(bsz, n, B_HEADS, B_DV).astype(jnp.float32)
    beta = jax.nn.sigmoid(b_raw.astype(jnp.float32)).reshape(bsz, n, 2, B_HEADS)
    g = -jnp.exp(a_log.astype(jnp.float32)) * jax.nn.softplus(
        a_raw.astype(jnp.float32).reshape(bsz, n, 2, B_HEADS) + dt_bias.astype(jnp.float32))
    gate = gate.reshape(bsz, n, B_HEADS, B_DV)
    return (qa, ka, va), (qb, kb, vb, beta, g, gate)


def _even_mixer(u_lat, u_ctx, tabs, w_in, w_out, lam_vec, subln_g, conv_w, a_log, dt_bias, out_g,
                lam_init, with_ctx_out):
    (qa_l, ka_l, va_l), gdn_l = _even_prep(u_lat, w_in, conv_w, a_log, dt_bias, tabs)
    (qa_c, ka_c, va_c), gdn_c = _even_prep(u_ctx, w_in, conv_w, a_log, dt_bias, None)
    lv = lam_vec.astype(jnp.float32)
    lam = jnp.exp(jnp.sum(lv[0] * lv[1])) - jnp.exp(jnp.sum(lv[2] * lv[3])) + lam_init
    oa_l, oa_c = _diff_attention(qa_l, ka_l, va_l, qa_c, ka_c, va_c, lam, with_ctx_out)
    ob_c, ob_l = _gdn_bidirectional(gdn_c[:5], gdn_l[:5])

    def merge(oa, ob, gate, ref):
        bsz, n = ref.shape[:2]
        ya = (_rmsnorm(oa, subln_g) * (1.0 - lam_init)).reshape(bsz, n, A_V)
        yb = (_rmsnorm(ob, out_g) * jax.nn.silu(gate)).reshape(bsz, n, B_W)
        return jnp.concatenate([ya.astype(ref.dtype), yb.astype(ref.dtype)], axis=-1) @ w_out

    y_lat = merge(oa_l, ob_l, gdn_l[5], u_lat)
    y_ctx = merge(oa_c, ob_c, gdn_c[5], u_ctx) if with_ctx_out else None
    return y_lat, y_ctx


def _rglru_coeffs(xc, wa, ba, wx, bx, lam):
    bsz, L, _ = xc.shape
    xb = xc.reshape(bsz, L, C_BLOCKS, C_BD)
    r = jax.nn.sigmoid(jnp.einsum('blhi,hij->blhj', xb, wa.astype(jnp.float32)).reshape(bsz, L, C_WIDTH)
                       + ba.astype(jnp.float32))
    i = jax.nn.sigmoid(jnp.einsum('blhi,hij->blhj', xb, wx.astype(jnp.float32)).reshape(bsz, L, C_WIDTH)
                       + bx.astype(jnp.float32))
    log_a = -C_POW * r * jax.nn.softplus(-lam.astype(jnp.float32))
    a = jnp.exp(log_a)
    b = jnp.sqrt(-jnp.expm1(2.0 * log_a)) * (i * xc)
    return a, b


def _combine(left, right):
    a_l, b_l = left
    a_r, b_r = right
    return a_l * a_r, a_r * b_l + b_r


def _linear_scan(a, b, h0):
    b = b.at[:, 0].add(a[:, 0] * h0)
    _, h = lax.associative_scan(_combine, (a, b), axis=1)
    return h


def _rglru_direction(x_c, x_l, wa, ba, wx, bx, lam, reverse):
    def flip(t):
        return jnp.flip(t, axis=1) if reverse else t
    a_c, b_c = _rglru_coeffs(flip(x_c), wa, ba, wx, bx, lam)
    h_c = _linear_scan(a_c, b_c, jnp.zeros_like(a_c[:, 0]))
    a_l, b_l = _rglru_coeffs(flip(x_l), wa, ba, wx, bx, lam)
    h_l = _linear_scan(a_l, b_l, h_c[:, -1])
    return flip(h_c), flip(h_l)


def _sink_softmax(s, sink):
    sk = sink[None, :, :, None, None]
    m = jnp.maximum(jnp.max(s, axis=-1, keepdims=True), sk)
    e = jnp.exp(s - m)
    return e / (jnp.sum(e, axis=-1, keepdims=True) + jnp.exp(sk - m))


def _window_attention(q_l, k_l, v_l, q_c, k_c, v_c, sink, with_ctx_out):
    bsz, n = q_l.shape[:2]
    lc = k_c.shape[1]
    sink = sink.astype(jnp.float32).reshape(D_KV, D_GROUP)

    def attend(q, k, v, mask):
        s = jnp.einsum('bqhgd,bkhd->bhgqk', q, k).astype(jnp.float32) * (D_HD ** -0.5)
        if mask is not None:
            s = jnp.where(mask, s, NEG_INF)
        p = _sink_softmax(s, sink)
        return jnp.einsum('bhgqk,bkhd->bqhgd', p.astype(v.dtype), v)

    q_l = q_l.reshape(bsz, n, D_KV, D_GROUP, D_HD)
    pad = ((0, 0), (D_BLOCK, D_BLOCK), (0, 0), (0, 0))
    kp = jnp.pad(k_l, pad)
    vp = jnp.pad(v_l, pad)

    def block(i):
        qi = lax.dynamic_slice_in_dim(q_l, i * D_BLOCK, D_BLOCK, axis=1)
        ki = lax.dynamic_slice_in_dim(kp, i * D_BLOCK, 3 * D_BLOCK, axis=1)
        vi = lax.dynamic_slice_in_dim(vp, i * D_BLOCK, 3 * D_BLOCK, axis=1)
        qpos = i * D_BLOCK + jnp.arange(D_BLOCK)
        kpos = (i - 1) * D_BLOCK + jnp.arange(3 * D_BLOCK)
        rel = kpos[None, :] - qpos[:, None]
        local = (jnp.abs(rel) <= WINDOW) & (kpos[None, :] >= 0) & (kpos[None, :] < n)
        mask = jnp.concatenate([jnp.ones((D_BLOCK, lc), bool), local], axis=-1)
        return attend(qi, jnp.concatenate([k_c, ki], axis=1), jnp.concatenate([v_c, vi], axis=1), mask)

    o = lax.map(block, jnp.arange(n // D_BLOCK))
    o_l = jnp.moveaxis(o, 0, 1).reshape(bsz, n, D_HEADS * D_HD)
    o_c = None
    if with_ctx_out:
        o_c = attend(q_c.reshape(bsz, lc, D_KV, D_GROUP, D_HD), k_c, v_c, None).reshape(bsz, lc, D_HEADS * D_HD)
    return o_l, o_c


def _odd_prep(h, w_in, conv_w, conv_b, tabs):
    bsz, n, _ = h.shape
    xr, gate, qd, kd, vd = _split(h @ w_in, ODD_SPLIT)
    xr = (_dwconv(xr, conv_w) + conv_b).astype(jnp.float32)
    qd = qd.reshape(bsz, n, D_HEADS, D_HD)
    kd = kd.reshape(bsz, n, D_KV, D_HD)
    vd = vd.reshape(bsz, n, D_KV, D_HD)
    if tabs is not None:
        qd = _rope_2d(qd, tabs)
        kd = _rope_2d(kd, tabs)
    return xr, gate, qd, kd, vd


def _odd_mixer(u_lat, u_ctx, tabs, w_in, w_out, conv_w, conv_b, wa, ba, wx, bx, lam, sink, with_ctx_out):
    xr_l, gate_l, qd_l, kd_l, vd_l = _odd_prep(u_lat, w_in, conv_w, conv_b, tabs)
    xr_c, gate_c, qd_c, kd_c, vd_c = _odd_prep(u_ctx, w_in, conv_w, conv_b, None)
    h_c, h_l = 0.0, 0.0
    for d in range(2):
        hc, hl = _rglru_direction(xr_c, xr_l, wa[d], ba[d], wx[d], bx[d], lam[d], d == 1)
        h_c = h_c + hc
        h_l = h_l + hl
    od_l, od_c = _window_attention(qd_l, kd_l, vd_l, qd_c, kd_c, vd_c, sink, with_ctx_out)

    def merge(h, gate, od, ref):
        yc = (h * jax.nn.gelu(gate.astype(jnp.float32))).astype(ref.dtype)
        return jnp.concatenate([yc, od.astype(ref.dtype)], axis=-1) @ w_out

    y_lat = merge(h_l, gate_l, od_l, u_lat)
    y_ctx = merge(h_c, gate_c, od_c, u_ctx) if with_ctx_out else None
    return y_lat, y_ctx


def _conv_ffn(h, w_up, conv_w, w_down):
    u = _dwconv(h @ w_up, conv_w)
    gate, val = jnp.split(u, 2, axis=-1)
    return (jax.nn.silu(gate) * val) @ w_down


def setup_inputs(seed: int = 0) -> dict:
    key = jax.random.key(seed)
    ks = iter(jax.random.split(key, 40))

    def nrm(shape, scale):
        return jax.random.normal(next(ks), shape, jnp.float32) * scale

    def unif(shape, lo, hi):
        return jax.random.uniform(next(ks), shape, jnp.float32, lo, hi)

    d = D_MODEL
    dt = jnp.exp(unif((N_EVEN, 2, B_HEADS), math.log(1e-3), math.log(1e-1)))
    a_lru = unif((N_ODD, 2, C_WIDTH), 0.9, 0.999)
    return {
        'x': nrm((BATCH, SEQ, d), 1.0),
        'c': nrm((BATCH, d), 1.0),
        'ctx': nrm((BATCH, CTX_LEN, d), 1.0),
        'c_ctx': nrm((d,), 1.0),
        'w_ada': nrm((DEPTH, d, N_MOD * d), d ** -0.5),
        'b_ada': nrm((DEPTH, N_MOD * d), 0.02),
        'norm_mix': 1.0 + nrm((DEPTH, d), 0.1),
        'norm_ffn': 1.0 + nrm((DEPTH, d), 0.1),
        'ffn_w_up': nrm((DEPTH, d, 2 * FFN), d ** -0.5),
        'ffn_conv': nrm((DEPTH, FFN_CONV, 2 * FFN), FFN_CONV ** -0.5),
        'ffn_w_down': nrm((DEPTH, FFN, d), FFN ** -0.5),
        'final_norm': 1.0 + nrm((d,), 0.1),
        'ev_w_in': nrm((N_EVEN, d, EVEN_IN), d ** -0.5),
        'ev_w_out': nrm((N_EVEN, EVEN_OUT, d), EVEN_OUT ** -0.5),
        'diff_lambda': nrm((N_EVEN, 4, A_HD), 0.1),
        'diff_subln': 1.0 + nrm((N_EVEN, A_VD), 0.1),
        'gdn_conv': nrm((N_EVEN, B_CONV, 3 * B_W), B_CONV ** -0.5),
        'gdn_a_log': jnp.log(unif((N_EVEN, 2, B_HEADS), 1.0, 16.0)),
        'gdn_dt_bias': dt + jnp.log(-jnp.expm1(-dt)),
        'gdn_norm': 1.0 + nrm((N_EVEN, B_DV), 0.1),
        'od_w_in': nrm((N_ODD, d, ODD_IN), d ** -0.5),
        'od_w_out': nrm((N_ODD, ODD_OUT, d), ODD_OUT ** -0.5),
        'lru_conv': nrm((N_ODD, C_CONV, C_WIDTH), C_CONV ** -0.5),
        'lru_conv_b': nrm((N_ODD, C_WIDTH), 0.02),
        'lru_wa': nrm((N_ODD, 2, C_BLOCKS, C_BD, C_BD), C_BD ** -0.5),
        'lru_ba': nrm((N_ODD, 2, C_WIDTH), 0.1),
        'lru_wx': nrm((N_ODD, 2, C_BLOCKS, C_BD, C_BD), C_BD ** -0.5),
        'lru_bx': nrm((N_ODD, 2, C_WIDTH), 0.1),
        'lru_lambda': jnp.log(a_lru) - jnp.log1p(-a_lru),
        'swa_sink': nrm((N_ODD, D_HEADS), 0.5),
    }


def reference(x, c, ctx, c_ctx, w_ada, b_ada, norm_mix, norm_ffn, ffn_w_up, ffn_conv, ffn_w_down, final_norm,
              ev_w_in, ev_w_out, diff_lambda, diff_subln, gdn_conv, gdn_a_log, gdn_dt_bias, gdn_norm,
              od_w_in, od_w_out, lru_conv, lru_conv_b, lru_wa, lru_ba, lru_wx, lru_bx, lru_lambda, swa_sink):
    rows = x.shape[1] // GRID_W
    tabs = _rope_tables(rows)
    x_lat, x_ctx = x, ctx
    s_lat = jax.nn.silu(c)
    s_ctx = jax.nn.silu(c_ctx)[None]
    for layer in range(DEPTH):
        ctx_out = layer < DEPTH - 1
        j = layer // 2
        m_lat = jnp.split((s_lat @ w_ada[layer] + b_ada[layer])[:, None, :], N_MOD, axis=-1)
        m_ctx = jnp.split((s_ctx @ w_ada[layer] + b_ada[layer])[:, None, :], N_MOD, axis=-1)
        u_lat = _modulate(x_lat, norm_mix[layer], m_lat[0], m_lat[1])
        u_ctx = _modulate(x_ctx, norm_mix[layer], m_ctx[0], m_ctx[1])
        if layer % 2 == 0:
            lam_init = 0.8 - 0.6 * math.exp(-0.3 * layer)
            y_lat, y_ctx = _even_mixer(u_lat, u_ctx, tabs, ev_w_in[j], ev_w_out[j], diff_lambda[j], diff_subln[j],
                                       gdn_conv[j], gdn_a_log[j], gdn_dt_bias[j], gdn_norm[j], lam_init, ctx_out)
        else:
            y_lat, y_ctx = _odd_mixer(u_lat, u_ctx, tabs, od_w_in[j], od_w_out[j], lru_conv[j], lru_conv_b[j],
                                      lru_wa[j], lru_ba[j], lru_wx[j], lru_bx[j], lru_lambda[j], swa_sink[j], ctx_out)
        x_lat = x_lat + m_lat[2] * y_lat
        x_lat = x_lat + m_lat[5] * _conv_ffn(_modulate(x_lat, norm_ffn[layer], m_lat[3], m_lat[4]),
                                             ffn_w_up[layer], ffn_conv[layer], ffn_w_down[layer])
        if ctx_out:
            x_ctx = x_ctx + m_ctx[2] * y_ctx
            x_ctx = x_ctx + m_ctx[5] * _conv_ffn(_modulate(x_ctx, norm_ffn[layer], m_ctx[3], m_ctx[4]),
                                                 ffn_w_up[layer], ffn_conv[layer], ffn_w_down[layer])
    return _rmsnorm(x_lat, final_norm)
```

```python
import contextlib
import math
import numpy as np
import concourse.bass as bass
import concourse.mybir as mybir
from concourse.bass_utils import run_bass_kernel_spmd

F32 = mybir.dt.float32
BF16 = mybir.dt.bfloat16
AF = mybir.ActivationFunctionType
ALU = mybir.AluOpType
AX = mybir.AxisListType

D = 1024
NT = 4352
CT = 256
LT = 4096
DEPTH = 4
FFN = 2816
EPS = 1e-6
TILES = [(0, 256)] + [(256 + 512 * i, 512) for i in range(8)]


class Buf:
    __slots__ = ("w", "r", "name", "excl")

    def __init__(self, name="", excl=False):
        self.w = None
        self.r = []
        self.name = name
        self.excl = excl


class KB:
    EPOCH = 24000
    NDMA = 40

    def __init__(self, nc, same_eng_sync=True):
        self.nc = nc
        self.es = contextlib.ExitStack()
        self.eng = {"pe": nc.tensor, "act": nc.scalar, "dve": nc.vector, "pool": nc.gpsimd, "sp": nc.sync}
        self.same = same_eng_sync
        self.cnt = {e: 0 for e in self.eng}
        self.epoch = {e: 0 for e in self.eng}
        self.sems = {}
        self.known = {e: {} for e in self.eng}
        self.last_tok = {e: None for e in self.eng}
        self.dsem = [self.es.enter_context(nc.semaphore(f"dq{i}")) for i in range(self.NDMA)]
        self.dtot = [0] * self.NDMA
        self.dnext = 0
        self.nwait = 0
        self.nins = 0
        for e in self.eng:
            self._newsem(e)

    def _newsem(self, e):
        key = (e, self.epoch[e])
        self.sems[key] = self.es.enter_context(self.nc.semaphore(f"s_{e}_{self.epoch[e]}"))
        self.cnt[e] = 0

    def _semh(self, key):
        if isinstance(key, int):
            return self.dsem[key]
        return self.sems[key]

    def _wait(self, e, tok):
        key, val = tok
        if self.known[e].get(key, 0) >= val:
            return
        self.eng[e].wait_ge(self._semh(key), val)
        self.known[e][key] = val
        self.nwait += 1

    def _deps(self, e, r, w):
        toks = []
        for b in r:
            if b.w is not None:
                toks.append(b.w)
        for b in w:
            if b.w is not None:
                toks.append(b.w)
            toks.extend(b.r)
        for tok in toks:
            key = tok[0]
            if (not isinstance(key, int)) and key[0] == e and (e == "pe" or not self.same):
                continue
            self._wait(e, tok)

    def _mark(self, tok, r, w):
        for b in w:
            b.w = tok
            b.r = []
        for b in r:
            if b not in w:
                b.r.append(tok)
                if len(b.r) > 24:
                    d = {}
                    for k, v in b.r:
                        if d.get(k, 0) < v:
                            d[k] = v
                    b.r = list(d.items())

    def op(self, e, fn, r=(), w=()):
        w = list(w) + [b for b in r if b.excl and b not in w]
        r = [b for b in r if not b.excl]
        self._deps(e, r, w)
        if self.cnt[e] >= self.EPOCH:
            self.epoch[e] += 1
            self._newsem(e)
        ins = fn(self.eng[e])
        key = (e, self.epoch[e])
        self.cnt[e] += 1
        ins.then_inc(self.sems[key], 1)
        tok = (key, self.cnt[e])
        self.last_tok[e] = tok
        self._mark(tok, r, w)
        self.nins += 1
        return tok

    def dma(self, e, out, in_, r=(), w=(), **kw):
        r = list(r)
        w = list(w)
        self._deps(e, r, w)
        i = self.dnext
        self.dnext = (self.dnext + 1) % self.NDMA
        if self.dtot[i]:
            self._wait(e, (i, self.dtot[i]))
        if e == "pool":
            self.swq = getattr(self, "swq", [])
            if len(self.swq) >= 3:
                self._wait(e, self.swq[-3])
        ins = self.eng[e].dma_start(out=out, in_=in_, **kw)
        self.dtot[i] += 16
        ins.then_inc(self.dsem[i], 16)
        tok = (i, self.dtot[i])
        if e == "pool":
            self.swq.append(tok)
        self._mark(tok, r, w)
        self.nins += 1
        return tok

    def barrier(self):
        toks = [t for t in self.last_tok.values() if t is not None]
        toks += [(i, self.dtot[i]) for i in range(self.NDMA) if self.dtot[i]]
        for e in self.eng:
            for tok in toks:
                key = tok[0]
                if (not isinstance(key, int)) and key[0] == e:
                    continue
                self._wait(e, tok)

    def finish(self):
        self.barrier()
        self.es.close()


class Prog:
    def __init__(self, cfg=None):
        self.cfg = cfg or {}
        self.nc = bass.Bass("TRN2", target_bir_lowering=False)
        self.kb = KB(self.nc)
        self.root = contextlib.ExitStack()
        self.din = {}
        self.dbuf = {}
        self.psn = 0

    def inp(self, name, shape, dt=F32):
        t = self.nc.dram_tensor(name, list(shape), dt, kind="ExternalInput").ap()
        self.din[name] = t
        return t

    def outp(self, name, shape, dt=F32):
        return self.nc.dram_tensor(name, list(shape), dt, kind="ExternalOutput").ap()

    def scratch(self, name, shape, dt):
        return self.nc.dram_tensor(name, list(shape), dt, kind="Internal").ap()

    def sb(self, es, name, shape, dt):
        self.nsb = getattr(self, "nsb", 0) + 1
        return es.enter_context(self.nc.sbuf_tensor(f"sb{self.nsb}_{name}", list(shape), dt))

    def scope(self, name):
        return self.nc.named_scope(name)

    def setup_psum(self):
        self.ps = [self.root.enter_context(self.nc.psum_tensor(f"ps{i}", [128, 512], F32)) for i in range(8)]
        self.psb = [Buf(f"ps{i}", excl=True) for i in range(8)]

    def psum(self):
        lo, hi = getattr(self, "psr", (0, 8))
        if not (lo <= self.psn < hi):
            self.psn = lo
        i = self.psn
        self.psn = lo + (self.psn + 1 - lo) % (hi - lo)
        return self.ps[i], self.psb[i]


def build(cfg=None):
    cfg = cfg or {}
    P = Prog(cfg)
    nc, kb = P.nc, P.kb
    op, dma = kb.op, kb.dma
    P.setup_psum()
    root = P.root
    layers = cfg.get("layers", list(range(DEPTH)))

    x_in = P.inp("x", [LT, D])
    ctx_in = P.inp("ctx", [CT, D])
    c2_in = P.inp("c2", [128, 8, 2])
    w_ada = P.inp("w_ada", [DEPTH, D, 6 * D])
    b_ada = P.inp("b_ada", [DEPTH, 128, 48])
    norm_mix = P.inp("norm_mix", [DEPTH, 128, 8])
    norm_ffn = P.inp("norm_ffn", [DEPTH, 128, 8])
    final_norm = P.inp("final_norm", [128, 8])
    ffn_w_up = P.inp("ffn_w_up", [DEPTH, D, 2 * FFN])
    ffn_conv = P.inp("ffn_conv", [DEPTH, 128, 44, 3])
    ffn_w_down = P.inp("ffn_w_down", [DEPTH, FFN, D])
    ident_in = P.inp("ident", [128, 128])
    od_w_in = P.inp("od_w_in", [2, D, 1792])
    od_w_out = P.inp("od_w_out", [2, D, D])
    lru_conv = P.inp("lru_conv", [2, 128, 4, 4])
    lru_conv_b = P.inp("lru_conv_b", [2, 128, 4])
    lru_wa = P.inp("lru_wa", [2, 2, 4, 128, 128])
    lru_wx = P.inp("lru_wx", [2, 2, 4, 128, 128])
    lru_ba = P.inp("lru_ba", [2, 128, 2, 4])
    lru_bx = P.inp("lru_bx", [2, 128, 2, 4])
    lru_lam = P.inp("lru_lam", [2, 128, 2, 4])
    swa_sink = P.inp("swa_sink", [2, 128, 8])
    cos_in = P.inp("cosT", [128, LT])
    sin_in = P.inp("sinT", [128, LT])
    perm_in = P.inp("permT", [128, 128])
    mprev_in = P.inp("mprev", [128, 512])
    mnext_in = P.inp("mnext", [128, 512])
    ev_w_in = P.inp("ev_w_in", [2, D, 3600])
    ev_w_out = P.inp("ev_w_out", [2, D, D])
    diff_lambda = P.inp("diff_lambda", [2, 128, 256])
    diff_subln = P.inp("diff_subln", [2, 128, 1])
    gdn_conv = P.inp("gdn_conv", [2, 128, 12, 4])
    gdn_a_log = P.inp("gdn_a_log", [2, 128, 8])
    gdn_dt_bias = P.inp("gdn_dt_bias", [2, 128, 8])
    gdn_norm = P.inp("gdn_norm", [2, 128, 1])
    gmask_in = P.inp("gmask", [64, 2, 3, 64])
    out = P.outp("out", [LT, D])

    XT = P.scratch("XT", [D, NT], F32)
    xb = [Buf(f"X{t}") for t in range(9)]
    GS = P.scratch("GS", [FFN, NT], BF16)
    gsb = [Buf(f"G{j}") for j in range(22)]
    wup_b = P.scratch("wup_b", [DEPTH, D, 2 * FFN], BF16)
    wdn_b = P.scratch("wdn_b", [DEPTH, FFN, D], BF16)
    wupb = [Buf() for _ in range(DEPTH)]
    wdnb = [Buf() for _ in range(DEPTH)]

    odin_b = P.scratch("odin_b", [2, D, 1792], BF16)
    odout_b = P.scratch("odout_b", [2, D, D], BF16)
    odinb = [Buf() for _ in range(2)]
    odoutb = [Buf() for _ in range(2)]
    YS = P.scratch("YS", [D, NT], BF16)
    ysb = [Buf("Y")]
    QD = P.scratch("QD", [8, 64, NT], BF16)
    KD = P.scratch("KD", [2, 64, NT], BF16)
    qdb = [Buf("QD")]
    kdb = [Buf("KD")]

    evin_b = P.scratch("evin_b", [2, D, 3600], BF16)
    evout_b = P.scratch("evout_b", [2, D, D], BF16)
    evinb = [Buf() for _ in range(2)]
    evoutb = [Buf() for _ in range(2)]
    QA = P.scratch("QA", [8, 64, NT], BF16)
    KA = P.scratch("KA", [8, 64, NT], BF16)
    VA = P.scratch("VA", [NT, 512], BF16)
    GQ = P.scratch("GQ", [3, 4, 128, NT], F32)
    GG = P.scratch("GG", [NT, 512], F32)
    BG = P.scratch("BG", [NT, 16], F32)
    qab = [Buf("QA")]
    kab = [Buf("KA")]
    vab = [Buf("VA")]
    gqb = [Buf("GQ")]
    ggb = [Buf("GG")]
    bgb = [Buf("BG")]

    ident = P.sb(root, "ident", [128, 128], F32)
    ones = P.sb(root, "ones", [128, 128], F32)
    epsb = P.sb(root, "epsb", [128, 1], F32)
    MOD = P.sb(root, "MOD", [128, DEPTH, 48, 2], F32)
    AMX = P.sb(root, "AMX", [128, DEPTH, 8, 2], F32)
    AFF = P.sb(root, "AFF", [128, DEPTH, 8, 2], F32)
    fconv = P.sb(root, "fconv", [128, DEPTH, 44, 3], F32)
    fng = P.sb(root, "fng", [128, 8], F32)
    cb = Buf("consts")
    modb = Buf("mod")
    dma("sp", ident[:], ident_in[:, :], w=[cb])
    op("dve", lambda e: e.memset(ones[:], 1.0), w=[cb])
    op("dve", lambda e: e.memset(epsb[:], EPS), w=[cb])
    dma("sp", fconv[:], ffn_conv.rearrange("l p j k -> p l j k"), w=[cb])
    dma("sp", fng[:], final_norm[:, :], w=[cb])

    def cast_rows(dst, src, R, C, wbuf, cchunk):
        dv = dst.rearrange("r (a c) -> (r a) c", c=cchunk)
        sv = src.rearrange("r (a c) -> (r a) c", c=cchunk)
        rows = R * (C // cchunk)
        r0 = 0
        while r0 < rows:
            n = min(2048, rows - r0)
            dma("pool", dv[r0:r0 + n, :], sv[r0:r0 + n, :], w=[wbuf])
            r0 += n

    for l in layers:
        if l % 2 == 0:
            cast_rows(evin_b[l // 2], ev_w_in[l // 2], D, 3600, evinb[l // 2], 1800)
            cast_rows(evout_b[l // 2], ev_w_out[l // 2], D, D, evoutb[l // 2], 1024)
        if l % 2 == 1:
            cast_rows(odin_b[l // 2], od_w_in[l // 2], D, 1792, odinb[l // 2], 1792)
            cast_rows(odout_b[l // 2], od_w_out[l // 2], D, D, odoutb[l // 2], 1024)
        cast_rows(wup_b[l], ffn_w_up[l], D, 2 * FFN, wupb[l], 1408)
        cast_rows(wdn_b[l], ffn_w_down[l], FFN, D, wdnb[l], 1024)

    with contextlib.ExitStack() as es, P.scope("mod"):
        c2 = P.sb(es, "c2", [128, 8, 2], F32)
        s2 = P.sb(es, "s2", [128, 8, 2], F32)
        bad = P.sb(es, "bad", [128, DEPTH, 48], F32)
        nmx = P.sb(es, "nmx", [128, DEPTH, 8], F32)
        nff = P.sb(es, "nff", [128, DEPTH, 8], F32)
        wa = [P.sb(es, f"wa{i}", [128, 8, 512], F32) for i in range(2)]
        wab = [Buf() for _ in range(2)]
        b0 = Buf()
        dma("sp", c2[:], c2_in[:, :, :], w=[b0])
        dma("sp", bad[:], b_ada.rearrange("l p j -> p l j"), w=[b0])
        dma("sp", nmx[:], norm_mix.rearrange("l p c -> p l c"), w=[b0])
        dma("sp", nff[:], norm_ffn.rearrange("l p c -> p l c"), w=[b0])
        op("act", lambda e: e.activation(out=s2[:], in_=c2[:], func=AF.Silu), r=[b0], w=[b0])
        it = 0
        for l in layers:
            for blk in range(12):
                wt, wb = wa[it % 2], wab[it % 2]
                it += 1
                dma("sp", wt[:], w_ada[l][:, blk * 512:(blk + 1) * 512].rearrange("(k p) n -> p k n", p=128), w=[wb])
                pt, pb = P.psum()
                for jj in range(4):
                    for k in range(8):
                        op("pe", lambda e: e.matmul(pt[:, jj * 2:jj * 2 + 2], wt[:, k, jj * 128:(jj + 1) * 128], s2[:, k, :],
                                                    start=(k == 0), stop=(k == 7)), r=[wb, b0], w=[pb])
                for jj in range(4):
                    j = blk * 4 + jj
                    op("dve", lambda e: e.tensor_scalar(out=MOD[:, l, j, :], in0=pt[:, jj * 2:jj * 2 + 2], scalar1=bad[:, l, j:j + 1],
                                                        scalar2=None, op0=ALU.add), r=[pb, b0], w=[modb])
            for s in range(2):
                op("dve", lambda e: e.scalar_tensor_tensor(out=AMX[:, l, :, s], in0=MOD[:, l, 8:16, s], scalar=1.0, in1=nmx[:, l, :],
                                                           op0=ALU.add, op1=ALU.mult), r=[modb, b0], w=[modb])
                op("dve", lambda e: e.scalar_tensor_tensor(out=AFF[:, l, :, s], in0=MOD[:, l, 32:40, s], scalar=1.0, in1=nff[:, l, :],
                                                           op0=ALU.add, op1=ALU.mult), r=[modb, b0], w=[modb])
        kb.barrier()

    dbg_x = cfg.get("x_override")
    if dbg_x:
        xo = P.inp("xo", [D, NT])
        with contextlib.ExitStack() as es:
            tb = [P.sb(es, f"tb{i}", [128, 8, 512], F32) for i in range(2)]
            tbb = [Buf() for _ in range(2)]
            for ti, (t0, T) in enumerate(TILES):
                dma("sp", tb[ti % 2][:, :, :T], xo[:, t0:t0 + T].rearrange("(c p) t -> p c t", p=128), w=[tbb[ti % 2]])
                dma("sp", XT[:, t0:t0 + T].rearrange("(c p) t -> p c t", p=128), tb[ti % 2][:, :, :T], r=[tbb[ti % 2]], w=[xb[ti]])
            kb.barrier()
    else:
        with contextlib.ExitStack() as es:
            tin = [P.sb(es, f"tin{i}", [128, D], F32) for i in range(2)]
            tinb = [Buf() for _ in range(2)]
            tout = [P.sb(es, f"tout{i}", [128, 8, 512], F32) for i in range(2)]
            toutb = [Buf() for _ in range(2)]
            it = 0
            for ti, (t0, T) in enumerate(TILES):
                to, tob = tout[ti % 2], toutb[ti % 2]
                for bi in range(T // 128):
                    tk0 = t0 + bi * 128
                    src = ctx_in[tk0:tk0 + 128, :] if tk0 < CT else x_in[tk0 - CT:tk0 - CT + 128, :]
                    tt, ttb = tin[it % 2], tinb[it % 2]
                    it += 1
                    dma("sp", tt[:], src, w=[ttb])
                    for half in range(2):
                        pt, pb = P.psum()
                        for q in range(4):
                            c = half * 4 + q
                            op("pe", lambda e: e.transpose(pt[:, q * 128:(q + 1) * 128], tt[:, c * 128:(c + 1) * 128], ident[:]),
                               r=[ttb, cb], w=[pb])
                        op("act" if half else "dve",
                           (lambda e: e.activation(out=to[:, 4:8, bi * 128:(bi + 1) * 128], in_=pt[:].rearrange("p (q t) -> p q t", q=4), func=AF.Copy))
                           if half else
                           (lambda e: e.tensor_copy(out=to[:, 0:4, bi * 128:(bi + 1) * 128], in_=pt[:].rearrange("p (q t) -> p q t", q=4))),
                           r=[pb], w=[tob])
                dma("sp", XT[:, t0:t0 + T].rearrange("(c p) t -> p c t", p=128), to[:, :, :T], r=[tob], w=[xb[ti]])
            kb.barrier()

    def norm_stage(es, l, A, shift_idx, U, ub):
        xt = [P.sb(es, f"nx{i}", [128, 8, 512], F32) for i in range(2)]
        xtb = [Buf() for _ in range(2)]
        sq = [P.sb(es, f"nsq{i}", [128, 8, 512], F32) for i in range(2)]
        sqb = [Buf() for _ in range(2)]
        rs = [P.sb(es, f"nrs{i}", [128, 512], F32) for i in range(2)]
        rsb = [Buf() for _ in range(2)]
        tm = [P.sb(es, f"ntm{i}", [128, 512], F32) for i in range(3)]
        tmb = [Buf() for _ in range(3)]
        k3 = 0
        for ti, (t0, T) in enumerate(TILES):
            s = 1 if ti == 0 else 0
            x_, xb_ = xt[ti % 2], xtb[ti % 2]
            q_, qb_ = sq[ti % 2], sqb[ti % 2]
            r_, rb_ = rs[ti % 2], rsb[ti % 2]
            dma("sp", x_[:, :, :T], XT[:, t0:t0 + T].rearrange("(c p) t -> p c t", p=128), r=[xb[ti]], w=[xb_])
            for c in range(8):
                op("act", lambda e: e.activation(out=q_[:, c, :T], in_=x_[:, c, :T], func=AF.Square), r=[xb_], w=[qb_])
            pt, pb = P.psum()
            for c in range(8):
                op("pe", lambda e: e.matmul(pt[:, :T], ones[:], q_[:, c, :T], start=(c == 0), stop=(c == 7)), r=[qb_, cb], w=[pb])
            op("act", lambda e: e.activation(out=r_[:, :T], in_=pt[:, :T], func=AF.Sqrt, bias=epsb[:], scale=1.0 / D), r=[pb, cb], w=[rb_])
            op("dve", lambda e: e.reciprocal(r_[:, :T], r_[:, :T]), r=[rb_], w=[rb_])
            for c in range(8):
                t_, tb_ = tm[k3 % 3], tmb[k3 % 3]
                k3 += 1
                op("dve", lambda e: e.tensor_tensor(out=t_[:, :T], in0=x_[:, c, :T], in1=r_[:, :T], op=ALU.mult), r=[xb_, rb_], w=[tb_])
                op("act", lambda e: e.activation(out=U[:, c, t0:t0 + T], in_=t_[:, :T], func=AF.Identity, scale=A[:, l, c, s:s + 1],
                                                 bias=MOD[:, l, shift_idx * 8 + c, s:s + 1]),
                   r=[tb_, modb], w=[ub[ti]])

    def proj_residual(es, l, gate_idx, W, wbuf, KC, SRC, srcb, skip_ctx):
        wsb = P.sb(es, "pr_w", [128, KC, D], BF16)
        wsbb = Buf()
        dma("sp", wsb[:], W.rearrange("(k p) n -> p k n", p=128), r=[wbuf], w=[wsbb])
        gt = [P.sb(es, f"pr_g{i}", [128, KC, 512], BF16) for i in range(2)]
        gtb = [Buf() for _ in range(2)]
        xt = [P.sb(es, f"pr_x{i}", [128, 8, 512], F32) for i in range(2)]
        xtb = [Buf() for _ in range(2)]
        for ti, (t0, T) in enumerate(TILES):
            if skip_ctx and ti == 0:
                continue
            s = 1 if ti == 0 else 0
            g_, gb_ = gt[ti % 2], gtb[ti % 2]
            x_, xb_ = xt[ti % 2], xtb[ti % 2]
            dma("sp", g_[:, :, :T], SRC[:, t0:t0 + T].rearrange("(k p) t -> p k t", p=128), r=srcb, w=[gb_])
            dma("sp", x_[:, :, :T], XT[:, t0:t0 + T].rearrange("(c p) t -> p c t", p=128), r=[xb[ti]], w=[xb_])
            for m in range(8):
                pt, pb = P.psum()
                for k in range(KC):
                    op("pe", lambda e: e.matmul(pt[:, :T], wsb[:, k, m * 128:(m + 1) * 128], g_[:, k, :T], start=(k == 0), stop=(k == KC - 1)),
                       r=[wsbb, gb_], w=[pb])
                op("dve", lambda e: e.scalar_tensor_tensor(out=x_[:, m, :T], in0=pt[:, :T], scalar=MOD[:, l, gate_idx * 8 + m, s:s + 1],
                                                           in1=x_[:, m, :T], op0=ALU.mult, op1=ALU.add), r=[pb, modb, xb_], w=[xb_])
            dma("sp", XT[:, t0:t0 + T].rearrange("(c p) t -> p c t", p=128), x_[:, :, :T], r=[xb_], w=[xb[ti]])

    def ffn_layer(l, skip_ctx):
        with contextlib.ExitStack() as es:
            U = P.sb(es, "U", [128, 8, NT], BF16)
            ub = [Buf(f"U{t}") for t in range(9)]
            with contextlib.ExitStack() as es2, P.scope("ffn_norm"):
                norm_stage(es2, l, AFF, 3, U, ub)
                kb.barrier()
            with contextlib.ExitStack() as es2, P.scope("ffn_up"):
                wt = [P.sb(es2, f"fw{i}", [128, 8, 256], BF16) for i in range(2)]
                wtb = [Buf() for _ in range(2)]
                H = [P.sb(es2, f"fH{i}", [128, 2, NT], F32) for i in range(2)]
                Hb = [Buf() for _ in range(2)]
                Cg = P.sb(es2, "fC", [128, 2, NT], F32)
                Cb = Buf()
                Go = P.sb(es2, "fG", [128, NT], BF16)
                Gob = Buf()
                t_start = 1 if skip_ctx else 0
                segs = ([] if skip_ctx else [(0, CT)]) + [(CT, NT)]
                lo = CT if skip_ctx else 0
                for j in range(22):
                    w_, wb_ = wt[j % 2], wtb[j % 2]
                    h_, hb_ = H[j % 2], Hb[j % 2]
                    dma("sp", w_[:, :, 0:128], wup_b[l][:, j * 128:(j + 1) * 128].rearrange("(k p) n -> p k n", p=128), r=[wupb[l]], w=[wb_])
                    dma("sp", w_[:, :, 128:256], wup_b[l][:, FFN + j * 128:FFN + (j + 1) * 128].rearrange("(k p) n -> p k n", p=128),
                        r=[wupb[l]], w=[wb_])
                    for ti, (t0, T) in enumerate(TILES):
                        if ti < t_start:
                            continue
                        for gv in range(2):
                            pt, pb = P.psum()
                            for k in range(8):
                                op("pe", lambda e: e.matmul(pt[:, :T], w_[:, k, gv * 128:(gv + 1) * 128], U[:, k, t0:t0 + T],
                                                            start=(k == 0), stop=(k == 7)), r=[wb_, ub[ti]], w=[pb])
                            op("dve", lambda e: e.tensor_copy(out=h_[:, gv, t0:t0 + T], in_=pt[:, :T]), r=[pb], w=[hb_])
                            op("act", lambda e: e.activation(out=Cg[:, gv, t0:t0 + T], in_=pt[:, :T], func=AF.Copy, scale=fconv[:, l, gv * 22 + j, 1:2]),
                               r=[pb, cb], w=[Cb])
                    for gv in range(2):
                        ch = gv * 22 + j
                        for (a, b) in segs:
                            op("dve", lambda e: e.scalar_tensor_tensor(out=Cg[:, gv, a + 1:b], in0=h_[:, gv, a:b - 1], scalar=fconv[:, l, ch, 0:1],
                                                                       in1=Cg[:, gv, a + 1:b], op0=ALU.mult, op1=ALU.add), r=[hb_, cb, Cb], w=[Cb])
                            op("dve", lambda e: e.scalar_tensor_tensor(out=Cg[:, gv, a:b - 1], in0=h_[:, gv, a + 1:b], scalar=fconv[:, l, ch, 2:3],
                                                                       in1=Cg[:, gv, a:b - 1], op0=ALU.mult, op1=ALU.add), r=[hb_, cb, Cb], w=[Cb])
                    op("act", lambda e: e.activation(out=Cg[:, 0, lo:NT], in_=Cg[:, 0, lo:NT], func=AF.Silu), r=[Cb], w=[Cb])
                    op("dve", lambda e: e.tensor_tensor(out=Go[:, lo:NT], in0=Cg[:, 0, lo:NT], in1=Cg[:, 1, lo:NT], op=ALU.mult), r=[Cb], w=[Gob])
                    dma("sp", GS[j * 128:(j + 1) * 128, lo:NT], Go[:, lo:NT], r=[Gob], w=[gsb[j]])
                kb.barrier()
        with contextlib.ExitStack() as es, P.scope("ffn_down"):
            proj_residual(es, l, 5, wdn_b[l], wdnb[l], 22, GS, gsb, skip_ctx)
            kb.barrier()

    def proj_fm(W, wbuf, c0, U, ub, dst, dstb, wt, wtb, ev="act"):
        dma("sp", wt[:], W[:, c0:c0 + 128].rearrange("(k p) n -> p k n", p=128), r=[wbuf], w=[wtb])
        for ti, (t0, T) in enumerate(TILES):
            pt, pb = P.psum()
            for k in range(8):
                op("pe", lambda e: e.matmul(pt[:, :T], wt[:, k, :], U[:, k, t0:t0 + T], start=(k == 0), stop=(k == 7)), r=[wtb, ub[ti]], w=[pb])
            if ev == "act":
                op("act", lambda e: e.activation(out=dst[:, t0:t0 + T], in_=pt[:, :T], func=AF.Copy), r=[pb], w=[dstb])
            else:
                op("dve", lambda e: e.tensor_copy(out=dst[:, t0:t0 + T], in_=pt[:, :T]), r=[pb], w=[dstb])

    def rope_store(es_, raw, rawb, cosT, sinT, permT, tb, dstrows, dstbuf, tmp, tmpb, ob, obb):
        op("dve", lambda e: e.tensor_copy(out=ob[:, 0:CT], in_=raw[:, 0:CT]), r=[rawb], w=[obb])
        for ti, (t0, T) in enumerate(TILES):
            if ti == 0:
                continue
            l0 = t0 - CT
            pt, pb = P.psum()
            op("pe", lambda e: e.matmul(pt[:, :T], permT[:], raw[:, t0:t0 + T], start=True, stop=True), r=[rawb, tb], w=[pb])
            op("dve", lambda e: e.tensor_tensor(out=tmp[:, :T], in0=pt[:, :T], in1=sinT[:, l0:l0 + T], op=ALU.mult), r=[pb, tb], w=[tmpb])
            op("dve", lambda e: e.tensor_tensor(out=raw[:, t0:t0 + T], in0=raw[:, t0:t0 + T], in1=cosT[:, l0:l0 + T], op=ALU.mult), r=[rawb, tb, pb], w=[rawb])
            op("dve", lambda e: e.tensor_tensor(out=ob[:, t0:t0 + T], in0=raw[:, t0:t0 + T], in1=tmp[:, :T], op=ALU.add), r=[rawb, tmpb], w=[obb])
        for hh in range(2):
            dma("sp", dstrows[hh], ob[hh * 64:(hh + 1) * 64, :], r=[obb], w=[dstbuf])

    def odd_mixer(l, skip_ctx):
        j = l // 2
        W = odin_b[j]
        wbuf = odinb[j]
        with contextlib.ExitStack() as es:
            U = P.sb(es, "U", [128, 8, NT], BF16)
            ub = [Buf(f"U{t}") for t in range(9)]
            with contextlib.ExitStack() as es2, P.scope("mix_norm"):
                norm_stage(es2, l, AMX, 0, U, ub)
                kb.barrier()
            with contextlib.ExitStack() as es2, P.scope("odd_lru"):
                wt = [P.sb(es2, f"ow{i}", [128, 8, 128], BF16) for i in range(2)]
                wtb = [Buf() for _ in range(2)]
                B1 = P.sb(es2, "B1", [128, NT], F32); b1 = Buf()
                B2 = P.sb(es2, "B2", [128, NT], F32); b2 = Buf()
                B2h = P.sb(es2, "B2h", [128, NT], BF16); b2h = Buf()
                B4 = P.sb(es2, "B4", [128, NT], F32); b4 = Buf()
                B5 = P.sb(es2, "B5", [128, NT], F32); b5 = Buf()
                B6 = P.sb(es2, "B6", [128, NT], F32); b6 = Buf()
                B7 = P.sb(es2, "B7", [128, NT], F32); b7 = Buf()
                lc = P.sb(es2, "lc", [128, 4, 4], F32)
                lcb_ = P.sb(es2, "lcb", [128, 4], F32)
                lba = P.sb(es2, "lba", [128, 2, 4], F32)
                lbx = P.sb(es2, "lbx", [128, 2, 4], F32)
                llam = P.sb(es2, "llam", [128, 2, 4], F32)
                c8 = P.sb(es2, "c8", [128, 2, 4], F32)
                c16 = P.sb(es2, "c16", [128, 2, 4], F32)
                bdf = P.sb(es2, "bdf", [128, 2, 128], F32)
                bda = [P.sb(es2, f"bda{i}", [128, 2, 128], BF16) for i in range(2)]
                bdab = [Buf() for _ in range(2)]
                bdfb = Buf()
                sb0 = Buf()
                dma("sp", lc[:], lru_conv[j], w=[sb0])
                dma("sp", lcb_[:], lru_conv_b[j], w=[sb0])
                dma("sp", lba[:], lru_ba[j], w=[sb0])
                dma("sp", lbx[:], lru_bx[j], w=[sb0])
                dma("sp", llam[:], lru_lam[j], w=[sb0])
                op("act", lambda e: e.activation(out=c8[:], in_=llam[:], func=AF.Exp, scale=-1.0), r=[sb0], w=[sb0])
                op("act", lambda e: e.activation(out=c8[:], in_=c8[:], func=AF.Ln, bias=ones[:, 0:1], scale=1.0), r=[sb0, cb], w=[sb0])
                op("dve", lambda e: e.tensor_scalar(out=c16[:], in0=c8[:], scalar1=-16.0, scalar2=None, op0=ALU.mult), r=[sb0], w=[sb0])
                op("dve", lambda e: e.tensor_scalar(out=c8[:], in0=c8[:], scalar1=-8.0, scalar2=None, op0=ALU.mult), r=[sb0], w=[sb0])
                segs = [(0, CT), (CT, NT)]
                ib = 0
                for c in range(4):
                    proj_fm(W, wbuf, c * 128, U, ub, B1, b1, wt[c % 2], wtb[c % 2])
                    op("act", lambda e: e.activation(out=B2[:], in_=B1[:], func=AF.Identity, scale=lc[:, c, 2:3], bias=lcb_[:, c:c + 1]),
                       r=[b1, sb0], w=[b2])
                    for (a, b) in segs:
                        for tap, off in ((0, -2), (1, -1), (3, 1)):
                            if off < 0:
                                oa, ob_, ia, ib_ = a - off, b, a, b + off
                            else:
                                oa, ob_, ia, ib_ = a, b - off, a + off, b
                            op("dve", lambda e: e.scalar_tensor_tensor(out=B2[:, oa:ob_], in0=B1[:, ia:ib_], scalar=lc[:, c, tap:tap + 1], in1=B2[:, oa:ob_],
                                                                       op0=ALU.mult, op1=ALU.add), r=[b1, sb0, b2], w=[b2])
                    op("act", lambda e: e.activation(out=B2h[:], in_=B2[:], func=AF.Copy), r=[b2], w=[b2h])
                    for d in range(2):
                        bd_, bdb_ = bda[ib % 2], bdab[ib % 2]
                        ib += 1
                        dma("sp", bdf[:, 0, :], lru_wa[j, d, c], w=[bdfb])
                        dma("sp", bdf[:, 1, :], lru_wx[j, d, c], w=[bdfb])
                        op("dve", lambda e: e.tensor_copy(out=bd_[:], in_=bdf[:]), r=[bdfb], w=[bdb_])
                        for ti, (t0, T) in enumerate(TILES):
                            pr, prb = P.psum()
                            op("pe", lambda e: e.matmul(pr[:, :T], bd_[:, 0, :], B2h[:, t0:t0 + T], start=True, stop=True), r=[bdb_, b2h], w=[prb])
                            op("act", lambda e: e.activation(out=B4[:, t0:t0 + T], in_=pr[:, :T], func=AF.Sigmoid, bias=lba[:, d, c:c + 1], scale=1.0),
                               r=[prb, sb0], w=[b4])
                            pi, pib = P.psum()
                            op("pe", lambda e: e.matmul(pi[:, :T], bd_[:, 1, :], B2h[:, t0:t0 + T], start=True, stop=True), r=[bdb_, b2h], w=[pib])
                            op("act", lambda e: e.activation(out=B5[:, t0:t0 + T], in_=pi[:, :T], func=AF.Sigmoid, bias=lbx[:, d, c:c + 1], scale=1.0),
                               r=[pib, sb0], w=[b5])
                        op("act", lambda e: e.activation(out=B1[:], in_=B4[:], func=AF.Exp, scale=c16[:, d, c:c + 1]), r=[b4, sb0], w=[b1])
                        op("act", lambda e: e.activation(out=B4[:], in_=B4[:], func=AF.Exp, scale=c8[:, d, c:c + 1]), r=[b4, sb0], w=[b4])
                        op("act", lambda e: e.activation(out=B1[:], in_=B1[:], func=AF.Sqrt, scale=-1.0, bias=ones[:, 0:1]), r=[b1, cb], w=[b1])
                        op("dve", lambda e: e.tensor_tensor(out=B5[:], in0=B5[:], in1=B2[:], op=ALU.mult), r=[b5, b2], w=[b5])
                        op("dve", lambda e: e.tensor_tensor(out=B5[:], in0=B5[:], in1=B1[:], op=ALU.mult), r=[b5, b1], w=[b5])
                        if d == 0:
                            op("dve", lambda e: e.tensor_tensor_scan(out=B6[:], data0=B4[:], data1=B5[:], initial=0.0, op0=ALU.mult, op1=ALU.add),
                               r=[b4, b5], w=[b6])
                        else:
                            op("dve", lambda e: e.tensor_tensor_scan(out=B7[:, 0:CT][:, ::-1], data0=B4[:, 0:CT][:, ::-1], data1=B5[:, 0:CT][:, ::-1],
                                                                     initial=0.0, op0=ALU.mult, op1=ALU.add), r=[b4, b5], w=[b7])
                            op("dve", lambda e: e.tensor_tensor_scan(out=B7[:, CT:NT][:, ::-1], data0=B4[:, CT:NT][:, ::-1], data1=B5[:, CT:NT][:, ::-1],
                                                                     initial=B7[:, 0:1], op0=ALU.mult, op1=ALU.add), r=[b4, b5, b7], w=[b7])
                            op("dve", lambda e: e.tensor_tensor(out=B6[:], in0=B6[:], in1=B7[:], op=ALU.add), r=[b6, b7], w=[b6])
                    proj_fm(W, wbuf, 512 + c * 128, U, ub, B1, b1, wt[c % 2], wtb[c % 2])
                    op("act", lambda e: e.activation(out=B1[:], in_=B1[:], func=AF.Gelu_apprx_tanh), r=[b1], w=[b1])
                    op("dve", lambda e: e.tensor_tensor(out=B2h[:], in0=B6[:], in1=B1[:], op=ALU.mult), r=[b6, b1, b2h], w=[b2h])
                    dma("sp", YS[c * 128:(c + 1) * 128, :], B2h[:], r=[b2h], w=ysb)
                kb.barrier()
            with contextlib.ExitStack() as es2, P.scope("odd_qk"):
                wt = [P.sb(es2, f"aw{i}", [128, 8, 128], BF16) for i in range(2)]
                wtb = [Buf() for _ in range(2)]
                cosT = P.sb(es2, "cosT", [128, LT], F32)
                sinT = P.sb(es2, "sinT", [128, LT], F32)
                permT = P.sb(es2, "permT", [128, 128], F32)
                tb = Buf()
                dma("sp", cosT[:], cos_in[:, :], w=[tb])
                dma("sp", sinT[:], sin_in[:, :], w=[tb])
                dma("sp", permT[:], perm_in[:, :], w=[tb])
                raw = [P.sb(es2, f"raw{i}", [128, NT], F32) for i in range(2)]
                rawb = [Buf() for _ in range(2)]
                tmp = P.sb(es2, "rtmp", [128, 512], F32); tmpb = Buf()
                ob = [P.sb(es2, f"rob{i}", [128, NT], BF16) for i in range(2)]
                obb = [Buf() for _ in range(2)]
                for c in range(5):
                    proj_fm(W, wbuf, 1024 + c * 128, U, ub, raw[c % 2], rawb[c % 2], wt[c % 2], wtb[c % 2])
                    if c < 4:
                        rows = [QD[2 * c + hh] for hh in range(2)]
                        dbuf_ = qdb[0]
                    else:
                        rows = [KD[hh] for hh in range(2)]
                        dbuf_ = kdb[0]
                    rope_store(es2, raw[c % 2], rawb[c % 2], cosT, sinT, permT, tb, rows, dbuf_, tmp, tmpb, ob[c % 2], obb[c % 2])
                kb.barrier()
            with contextlib.ExitStack() as es2, P.scope("odd_attn"):
                wv = P.sb(es2, "wv", [128, 8, 128], BF16); wvb = Buf()
                dma("sp", wv[:], W[:, 1664:1792].rearrange("(k p) n -> p k n", p=128), r=[wbuf], w=[wvb])
                V = P.sb(es2, "V", [128, 34, 2, 65], BF16); vb = Buf()
                op("pool", lambda e: e.memset(V[:], 1.0), w=[vb])
                for blk in range(34):
                    ti = 0 if blk < 2 else 1 + (blk - 2) // 4
                    pt, pb = P.psum()
                    for k in range(8):
                        op("pe", lambda e: e.matmul(pt[:, 0:128], U[:, k, blk * 128:(blk + 1) * 128], wv[:, k, :], start=(k == 0), stop=(k == 7)),
                           r=[wvb, ub[ti]], w=[pb])
                    op("act", lambda e: e.activation(out=V[:, blk, :, 0:64], in_=pt[:, 0:128].rearrange("p (a b) -> p a b", a=2), func=AF.Copy),
                       r=[pb], w=[vb])
                kT = P.sb(es2, "kT", [64, 2, NT], BF16); ktb = Buf()
                dma("sp", kT[:], KD.rearrange("h d t -> d h t"), r=kdb, w=[ktb])
                qT = [P.sb(es2, f"qT{i}", [64, 4, NT], BF16) for i in range(2)]
                qtb = [Buf() for _ in range(2)]
                idb16 = P.sb(es2, "idb16", [128, 128], BF16)
                mk = P.sb(es2, "mk", [128, 2, 512], BF16)
                mkf = P.sb(es2, "mkf", [128, 2, 512], F32)
                esk = P.sb(es2, "esk", [128, 8], F32)
                mb = Buf()
                dma("sp", mkf[:, 0, :], mprev_in[:, :], w=[mb])
                dma("sp", mkf[:, 1, :], mnext_in[:, :], w=[mb])
                dma("sp", esk[:], swa_sink[j], w=[mb])
                op("dve", lambda e: e.tensor_copy(out=mk[:], in_=mkf[:]), r=[mb], w=[mb])
                op("dve", lambda e: e.tensor_copy(out=idb16[:], in_=ident[:]), r=[cb], w=[mb])
                op("act", lambda e: e.activation(out=esk[:], in_=esk[:], func=AF.Exp), r=[mb], w=[mb])
                PT = [P.sb(es2, f"PT{i}", [128, 512], BF16) for i in range(10)]
                ptb = [Buf() for _ in range(10)]
                od = [P.sb(es2, f"od{i}", [128, 512], F32) for i in range(2)]
                odb = [Buf() for _ in range(2)]
                rd = [P.sb(es2, f"rd{i}", [128, 8], F32) for i in range(2)]
                rdb = [Buf() for _ in range(2)]
                yt = [P.sb(es2, f"yt{i}", [128, 4, 128], BF16) for i in range(2)]
                ytb = [Buf() for _ in range(2)]
                for kv in range(2):
                    dma("sp", qT[kv][:], QD[kv * 4:(kv + 1) * 4].rearrange("h d t -> d h t"), r=qdb, w=[qtb[kv]])
                ipt = 0
                qblocks = list(range(2, 34)) if skip_ctx else list(range(34))
                for qi, qb_ in enumerate(qblocks):
                    o_, ob_ = od[qi % 2], odb[qi % 2]
                    r_, rb_ = rd[qi % 2], rdb[qi % 2]
                    if qb_ < 2:
                        keys = [(0, None), (1, None)]
                    else:
                        keys = [(0, None), (1, None)]
                        if qb_ - 1 >= 2:
                            keys.append((qb_ - 1, 0))
                        keys.append((qb_, None))
                        if qb_ + 1 < 34:
                            keys.append((qb_ + 1, 1))
                    for kv in range(2):
                        pts = []
                        for (kblk, mtype) in keys:
                            ps_, psb_ = P.psum()
                            if mtype is not None:
                                op("pe", lambda e: e.matmul(ps_[:], idb16[:], mk[:, mtype, :], start=True, stop=False), r=[mb], w=[psb_])
                            op("pe", lambda e: e.matmul(ps_[:].rearrange("p (g q) -> p g q", g=4), kT[:, kv, kblk * 128:(kblk + 1) * 128],
                                                        qT[kv][:, :, qb_ * 128:(qb_ + 1) * 128], start=(mtype is None), stop=True),
                               r=[ktb, qtb[kv]], w=[psb_])
                            p_, pb_ = PT[ipt % 10], ptb[ipt % 10]
                            ipt += 1
                            op("act", lambda e: e.activation(out=p_[:], in_=ps_[:], func=AF.Exp, scale=0.125), r=[psb_], w=[pb_])
                            pts.append((p_, pb_, kblk))
                        po, pob = P.psum()
                        for g in range(4):
                            for ki, (p_, pb_, kblk) in enumerate(pts):
                                op("pe", lambda e: e.matmul(po[:, g * 65:(g + 1) * 65], p_[:, g * 128:(g + 1) * 128], V[:, kblk, kv, :],
                                                            start=(ki == 0), stop=(ki == len(pts) - 1)), r=[pb_, vb], w=[pob])
                        pov = po[:, 0:260].rearrange("p (g e) -> p g e", g=4)
                        op("dve", lambda e: e.tensor_tensor(out=r_[:, kv * 4:(kv + 1) * 4], in0=pov[:, :, 64], in1=esk[:, kv * 4:(kv + 1) * 4], op=ALU.add),
                           r=[pob, mb], w=[rb_])
                        op("dve", lambda e: e.reciprocal(r_[:, kv * 4:(kv + 1) * 4], r_[:, kv * 4:(kv + 1) * 4]), r=[rb_], w=[rb_])
                        for g in range(4):
                            h = kv * 4 + g
                            op("act" if g % 2 else "dve",
                               (lambda e: e.activation(out=o_[:, h * 64:(h + 1) * 64], in_=po[:, g * 65:g * 65 + 64], func=AF.Copy, scale=r_[:, h:h + 1]))
                               if g % 2 else
                               (lambda e: e.tensor_scalar(out=o_[:, h * 64:(h + 1) * 64], in0=po[:, g * 65:g * 65 + 64], scalar1=r_[:, h:h + 1], scalar2=None,
                                                          op0=ALU.mult)),
                               r=[pob, rb_], w=[ob_])
                    y_, yb_ = yt[qi % 2], ytb[qi % 2]
                    pt, pb = P.psum()
                    for c in range(4):
                        op("pe", lambda e: e.transpose(pt[:, c * 128:(c + 1) * 128], o_[:, c * 128:(c + 1) * 128], ident[:]), r=[ob_, cb], w=[pb])
                    op("act", lambda e: e.activation(out=y_[:], in_=pt[:].rearrange("p (c t) -> p c t", c=4), func=AF.Copy), r=[pb], w=[yb_])
                    dma("sp", YS[512:1024, qb_ * 128:(qb_ + 1) * 128].rearrange("(c p) t -> p c t", p=128), y_[:], r=[yb_], w=ysb)
                kb.barrier()
        with contextlib.ExitStack() as es, P.scope("odd_out"):
            proj_residual(es, l, 2, odout_b[j], odoutb[j], 8, YS, ysb, skip_ctx)
            kb.barrier()

    def proj_tm(W, wbuf, c0, ncol, U, ub, wt, wtb, consume):
        dma("sp", wt[:, :, 0:ncol], W[:, c0:c0 + ncol].rearrange("(k p) n -> p k n", p=128), r=[wbuf], w=[wtb])
        for blk in range(34):
            ti = 0 if blk < 2 else 1 + (blk - 2) // 4
            pt, pb = P.psum()
            for k in range(8):
                op("pe", lambda e: e.matmul(pt[:, 0:ncol], U[:, k, blk * 128:(blk + 1) * 128], wt[:, k, 0:ncol], start=(k == 0), stop=(k == 7)),
                   r=[wtb, ub[ti]], w=[pb])
            consume(blk, pt, pb)

    def even_mixer(l, skip_ctx):
        j = l // 2
        lam_init = 0.8 - 0.6 * math.exp(-0.3 * l)
        W = evin_b[j]
        wbuf = evinb[j]
        with contextlib.ExitStack() as es:
            U = P.sb(es, "U", [128, 8, NT], BF16)
            ub = [Buf(f"U{t}") for t in range(9)]
            with contextlib.ExitStack() as es2, P.scope("mix_norm"):
                norm_stage(es2, l, AMX, 0, U, ub)
                kb.barrier()
            with contextlib.ExitStack() as es2, P.scope("ev_qk"):
                wt = [P.sb(es2, f"aw{i}", [128, 8, 128], BF16) for i in range(2)]
                wtb = [Buf() for _ in range(2)]
                cosT = P.sb(es2, "cosT", [128, LT], F32)
                sinT = P.sb(es2, "sinT", [128, LT], F32)
                permT = P.sb(es2, "permT", [128, 128], F32)
                tb = Buf()
                dma("sp", cosT[:], cos_in[:, :], w=[tb])
                dma("sp", sinT[:], sin_in[:, :], w=[tb])
                dma("sp", permT[:], perm_in[:, :], w=[tb])
                raw = [P.sb(es2, f"raw{i}", [128, NT], F32) for i in range(2)]
                rawb = [Buf() for _ in range(2)]
                tmp = P.sb(es2, "rtmp", [128, 512], F32); tmpb = Buf()
                ob = [P.sb(es2, f"rob{i}", [128, NT], BF16) for i in range(2)]
                obb = [Buf() for _ in range(2)]
                for c in range(8):
                    proj_fm(W, wbuf, c * 128, U, ub, raw[c % 2], rawb[c % 2], wt[c % 2], wtb[c % 2])
                    if c < 4:
                        rows = [QA[2 * c + hh] for hh in range(2)]
                        dbuf_ = qab[0]
                    else:
                        rows = [KA[2 * (c - 4) + hh] for hh in range(2)]
                        dbuf_ = kab[0]
                    rope_store(es2, raw[c % 2], rawb[c % 2], cosT, sinT, permT, tb, rows, dbuf_, tmp, tmpb, ob[c % 2], obb[c % 2])
                kb.barrier()
            with contextlib.ExitStack() as es2, P.scope("ev_tm"):
                wt = [P.sb(es2, f"tw{i}", [128, 8, 512], BF16) for i in range(2)]
                wtb = [Buf() for _ in range(2)]
                vo = [P.sb(es2, f"vo{i}", [128, 512], BF16) for i in range(2)]
                vob = [Buf() for _ in range(2)]
                go = [P.sb(es2, f"go{i}", [128, 512], F32) for i in range(2)]
                gob = [Buf() for _ in range(2)]
                bgo = [P.sb(es2, f"bgo{i}", [128, 16], F32) for i in range(2)]
                bgob = [Buf() for _ in range(2)]
                tq = [P.sb(es2, f"tq{i}", [128, 8], F32) for i in range(4)]
                tqb = Buf()
                alog = P.sb(es2, "alog", [128, 8], F32)
                dtb = P.sb(es2, "dtb", [128, 8], F32)
                cb2 = Buf()
                dma("sp", alog[:], gdn_a_log[j], w=[cb2])
                dma("sp", dtb[:], gdn_dt_bias[j], w=[cb2])
                op("act", lambda e: e.activation(out=alog[:], in_=alog[:], func=AF.Exp), r=[cb2], w=[cb2])
                op("dve", lambda e: e.tensor_scalar(out=alog[:], in0=alog[:], scalar1=-1.0, scalar2=None, op0=ALU.mult), r=[cb2], w=[cb2])

                def c_va(blk, pt, pb):
                    o_, ob_ = vo[blk % 2], vob[blk % 2]
                    op("act", lambda e: e.activation(out=o_[:], in_=pt[:], func=AF.Copy), r=[pb], w=[ob_])
                    dma("sp", VA[blk * 128:(blk + 1) * 128, :], o_[:], r=[ob_], w=vab)

                def c_gate(blk, pt, pb):
                    o_, ob_ = go[blk % 2], gob[blk % 2]
                    op("act", lambda e: e.activation(out=o_[:], in_=pt[:], func=AF.Silu), r=[pb], w=[ob_])
                    dma("sp", GG[blk * 128:(blk + 1) * 128, :], o_[:], r=[ob_], w=ggb)

                def c_bg(blk, pt, pb):
                    o_, ob_ = bgo[blk % 2], bgob[blk % 2]
                    x_, ax_, l_, mx_ = tq
                    op("act", lambda e: e.activation(out=o_[:, 0:8], in_=pt[:, 0:8], func=AF.Sigmoid), r=[pb], w=[ob_])
                    op("dve", lambda e: e.tensor_tensor(out=x_[:], in0=pt[:, 8:16], in1=dtb[:], op=ALU.add), r=[pb, cb2, tqb], w=[tqb])
                    op("act", lambda e: e.activation(out=ax_[:], in_=x_[:], func=AF.Abs), r=[tqb], w=[tqb])
                    op("act", lambda e: e.activation(out=l_[:], in_=ax_[:], func=AF.Exp, scale=-1.0), r=[tqb], w=[tqb])
                    op("act", lambda e: e.activation(out=l_[:], in_=l_[:], func=AF.Ln, bias=ones[:, 0:1], scale=1.0), r=[tqb, cb], w=[tqb])
                    op("dve", lambda e: e.tensor_scalar(out=mx_[:], in0=x_[:], scalar1=0.0, scalar2=None, op0=ALU.max), r=[tqb], w=[tqb])
                    op("dve", lambda e: e.tensor_tensor(out=mx_[:], in0=mx_[:], in1=l_[:], op=ALU.add), r=[tqb], w=[tqb])
                    op("dve", lambda e: e.tensor_tensor(out=o_[:, 8:16], in0=mx_[:], in1=alog[:], op=ALU.mult), r=[tqb, cb2, ob_], w=[ob_])
                    dma("sp", BG[blk * 128:(blk + 1) * 128, :], o_[:], r=[ob_], w=bgb)

                proj_tm(W, wbuf, 1024, 512, U, ub, wt[0], wtb[0], c_va)
                proj_tm(W, wbuf, 3072, 512, U, ub, wt[1], wtb[1], c_gate)
                proj_tm(W, wbuf, 3584, 16, U, ub, wt[0], wtb[0], c_bg)
                kb.barrier()
            with contextlib.ExitStack() as es2, P.scope("ev_gqkv"):
                wt = [P.sb(es2, f"gw{i}", [128, 8, 128], BF16) for i in range(2)]
                wtb = [Buf() for _ in range(2)]
                R1 = [P.sb(es2, f"gR{i}", [128, NT], F32) for i in range(2)]
                r1b = [Buf() for _ in range(2)]
                C1 = [P.sb(es2, f"gC{i}", [128, NT], F32) for i in range(2)]
                c1b = [Buf() for _ in range(2)]
                S1 = P.sb(es2, "gS", [128, NT], F32); s1b = Buf()
                rsq = [P.sb(es2, f"grs{i}", [128, 512], F32) for i in range(2)]
                rsqb = [Buf() for _ in range(2)]
                gc = P.sb(es2, "gc", [128, 12, 4], F32); gcb = Buf()
                dma("sp", gc[:], gdn_conv[j], w=[gcb])
                segs = [(0, CT), (CT, NT)]
                it = 0
                for kind in range(3):
                    for h in range(4):
                        cidx = kind * 4 + h
                        r_, rb_ = R1[it % 2], r1b[it % 2]
                        c_, cb_ = C1[it % 2], c1b[it % 2]
                        proj_fm(W, wbuf, 1536 + cidx * 128, U, ub, r_, rb_, wt[it % 2], wtb[it % 2])
                        it += 1
                        op("act", lambda e: e.activation(out=c_[:], in_=r_[:], func=AF.Copy, scale=gc[:, cidx, 2:3]), r=[rb_, gcb], w=[cb_])
                        for (a, b) in segs:
                            for tap, off in ((0, -2), (1, -1), (3, 1)):
                                if off < 0:
                                    oa, ob_, ia, ib_ = a - off, b, a, b + off
                                else:
                                    oa, ob_, ia, ib_ = a, b - off, a + off, b
                                op("dve", lambda e: e.scalar_tensor_tensor(out=c_[:, oa:ob_], in0=r_[:, ia:ib_], scalar=gc[:, cidx, tap:tap + 1], in1=c_[:, oa:ob_],
                                                                           op0=ALU.mult, op1=ALU.add), r=[rb_, gcb, cb_], w=[cb_])
                        op("act", lambda e: e.activation(out=c_[:], in_=c_[:], func=AF.Silu), r=[cb_], w=[cb_])
                        if kind < 2:
                            op("act", lambda e: e.activation(out=S1[:], in_=c_[:], func=AF.Square), r=[cb_], w=[s1b])
                            for ti, (t0, T) in enumerate(TILES):
                                pt, pb = P.psum()
                                op("pe", lambda e: e.matmul(pt[:, :T], ones[:], S1[:, t0:t0 + T], start=True, stop=True), r=[s1b, cb], w=[pb])
                                q_, qb_ = rsq[ti % 2], rsqb[ti % 2]
                                op("act", lambda e: e.activation(out=q_[:, :T], in_=pt[:, :T], func=AF.Sqrt, bias=epsb[:], scale=1.0), r=[pb, cb], w=[qb_])
                                op("dve", lambda e: e.reciprocal(q_[:, :T], q_[:, :T]), r=[qb_], w=[qb_])
                                if kind == 0:
                                    op("dve", lambda e: e.scalar_tensor_tensor(out=c_[:, t0:t0 + T], in0=c_[:, t0:t0 + T], scalar=128.0 ** -0.5, in1=q_[:, :T],
                                                                               op0=ALU.mult, op1=ALU.mult), r=[cb_, qb_], w=[cb_])
                                else:
                                    op("dve", lambda e: e.tensor_tensor(out=c_[:, t0:t0 + T], in0=c_[:, t0:t0 + T], in1=q_[:, :T], op=ALU.mult), r=[cb_, qb_], w=[cb_])
                        dma("sp", GQ[kind, h], c_[:], r=[cb_], w=gqb)
                kb.barrier()
        with contextlib.ExitStack() as es, P.scope("ev_attn"):
            P.psr = (4, 8)
            acc = [(P.ps[i], P.psb[i]) for i in range(4)]
            kT = P.sb(es, "kT", [64, 2, NT], BF16); ktb = Buf()
            qT = P.sb(es, "qT", [64, 2, NT], BF16); qtb = Buf()
            V = P.sb(es, "V", [128, 34, 129], BF16); vb = Buf()
            PT = [P.sb(es, f"PT{i}", [128, 512], BF16) for i in range(6)]
            ptb = [Buf() for _ in range(6)]
            t0b_ = P.sb(es, "t0b", [128, 4, 128], F32); t0bb = Buf()
            oa = P.sb(es, "oa", [128, 4, 128], F32); oab = Buf()
            sqj = P.sb(es, "sqj", [128, 128], F32); sqjb = Buf()
            rdn = P.sb(es, "rdn", [128, 8], F32); rdnb = Buf()
            ssq = P.sb(es, "ssq", [128, 4], F32); ssqb = Buf()
            yT = [P.sb(es, f"yT{i}", [128, 512], BF16) for i in range(2)]
            ytb = [Buf() for _ in range(2)]
            lv = P.sb(es, "lv", [128, 256], F32)
            lp = P.sb(es, "lp", [128, 2, 64], F32)
            lsum = P.sb(es, "lsum", [128, 2], F32)
            neglam = P.sb(es, "neglam", [128, 1], F32)
            sg = P.sb(es, "sg", [128, 1], F32)
            eps128 = P.sb(es, "eps128", [128, 1], F32)
            lb = Buf()
            dma("sp", lv[:], diff_lambda[j], w=[lb])
            dma("sp", sg[:], diff_subln[j], w=[lb])
            op("dve", lambda e: e.tensor_tensor(out=lp[:, 0, :], in0=lv[:, 0:64], in1=lv[:, 64:128], op=ALU.mult), r=[lb], w=[lb])
            op("dve", lambda e: e.tensor_tensor(out=lp[:, 1, :], in0=lv[:, 128:192], in1=lv[:, 192:256], op=ALU.mult), r=[lb], w=[lb])
            op("dve", lambda e: e.tensor_reduce(out=lsum[:], in_=lp[:], axis=AX.X, op=ALU.add), r=[lb], w=[lb])
            op("act", lambda e: e.activation(out=lsum[:], in_=lsum[:], func=AF.Exp), r=[lb], w=[lb])
            op("dve", lambda e: e.tensor_tensor(out=neglam[:], in0=lsum[:, 1:2], in1=lsum[:, 0:1], op=ALU.subtract), r=[lb], w=[lb])
            op("dve", lambda e: e.tensor_scalar(out=neglam[:], in0=neglam[:], scalar1=-lam_init, scalar2=None, op0=ALU.add), r=[lb], w=[lb])
            op("dve", lambda e: e.tensor_scalar(out=sg[:], in0=sg[:], scalar1=(1.0 - lam_init), scalar2=None, op0=ALU.mult), r=[lb], w=[lb])
            op("dve", lambda e: e.memset(eps128[:], EPS), w=[lb])
            ipt = 0
            iy = 0
            for h in range(4):
                dma("sp", kT[:], KA[2 * h:2 * h + 2].rearrange("m d t -> d m t"), r=kab, w=[ktb])
                dma("sp", qT[:], QA[2 * h:2 * h + 2].rearrange("m d t -> d m t"), r=qab, w=[qtb])
                dma("sp", V[:, :, 0:128], VA[:, h * 128:(h + 1) * 128].rearrange("(b p) e -> p b e", p=128), r=vab, w=[vb])
                op("pool", lambda e: e.memset(V[:, :, 128:129], 1.0), w=[vb])
                for ti, (t0, T) in enumerate(TILES):
                    if ti == 0 and skip_ctx:
                        continue
                    nq = T // 128
                    keys = [0, 1] if ti == 0 else list(range(34))
                    for m in range(2):
                        for ki, kblk in enumerate(keys):
                            s_, sb_ = P.psum()
                            op("pe", lambda e: e.matmul(s_[:, :T], kT[:, m, kblk * 128:(kblk + 1) * 128], qT[:, m, t0:t0 + T], start=True, stop=True),
                               r=[ktb, qtb], w=[sb_])
                            p_, pb_ = PT[ipt % 6], ptb[ipt % 6]
                            ipt += 1
                            op("act", lambda e: e.activation(out=p_[:, :T], in_=s_[:, :T], func=AF.Exp, scale=0.125), r=[sb_], w=[pb_])
                            for qq in range(nq):
                                op("pe", lambda e: e.matmul(acc[qq][0][:, 0:129], p_[:, qq * 128:(qq + 1) * 128], V[:, kblk, :],
                                                            start=(ki == 0), stop=(ki == len(keys) - 1)), r=[pb_, vb], w=[acc[qq][1]])
                        for qq in range(nq):
                            a_, ab_ = acc[qq]
                            col = m * 4 + qq
                            op("dve", lambda e: e.reciprocal(rdn[:, col:col + 1], a_[:, 128:129]), r=[ab_], w=[rdnb])
                            if m == 0:
                                op("act", lambda e: e.activation(out=t0b_[:, qq, :], in_=a_[:, 0:128], func=AF.Copy, scale=rdn[:, col:col + 1]), r=[ab_, rdnb], w=[t0bb])
                            else:
                                op("dve", lambda e: e.tensor_tensor(out=rdn[:, col:col + 1], in0=rdn[:, col:col + 1], in1=neglam[:], op=ALU.mult), r=[rdnb, lb], w=[rdnb])
                                op("dve", lambda e: e.scalar_tensor_tensor(out=oa[:, qq, :], in0=a_[:, 0:128], scalar=rdn[:, col:col + 1], in1=t0b_[:, qq, :],
                                                                           op0=ALU.mult, op1=ALU.add), r=[ab_, rdnb, t0bb], w=[oab])
                    y_, yb_ = yT[iy % 2], ytb[iy % 2]
                    iy += 1
                    pt, pb = P.psum()
                    for qq in range(nq):
                        op("act", lambda e: e.activation(out=sqj[:], in_=oa[:, qq, :], func=AF.Square, accum_out=ssq[:, qq:qq + 1]), r=[oab], w=[sqjb, ssqb])
                        op("act", lambda e: e.activation(out=ssq[:, qq:qq + 1], in_=ssq[:, qq:qq + 1], func=AF.Sqrt, bias=eps128[:], scale=1.0 / 128), r=[ssqb, lb], w=[ssqb])
                        op("dve", lambda e: e.reciprocal(ssq[:, qq:qq + 1], ssq[:, qq:qq + 1]), r=[ssqb], w=[ssqb])
                        op("dve", lambda e: e.tensor_scalar(out=oa[:, qq, :], in0=oa[:, qq, :], scalar1=ssq[:, qq:qq + 1], scalar2=None, op0=ALU.mult), r=[oab, ssqb], w=[oab])
                        op("pe", lambda e: e.transpose(pt[:, qq * 128:(qq + 1) * 128], oa[:, qq, :], ident[:]), r=[oab, cb], w=[pb])
                    op("act", lambda e: e.activation(out=y_[:, :T], in_=pt[:, :T], func=AF.Copy, scale=sg[:, 0:1]), r=[pb, lb], w=[yb_])
                    dma("sp", YS[h * 128:(h + 1) * 128, t0:t0 + T], y_[:, :T], r=[yb_], w=ysb)
            P.psr = (0, 8)
            kb.barrier()
        with contextlib.ExitStack() as es, P.scope("ev_gdn"):
            gdn_core(es, l, j, skip_ctx)
            kb.barrier()
        with contextlib.ExitStack() as es, P.scope("ev_out"):
            proj_residual(es, l, 2, evout_b[j], evoutb[j], 8, YS, ysb, skip_ctx)
            kb.barrier()

    def gdn_core(es, l, j, skip_ctx):
        NCH = NT // 64
        gm = P.sb(es, "gmask", [64, 2, 3, 64], F32); gmb = Buf()
        dma("sp", gm[:], gmask_in[:, :, :, :], w=[gmb])
        gn = P.sb(es, "gnorm", [128, 1], F32)
        dma("sp", gn[:], gdn_norm[j], w=[gmb])
        qF = P.sb(es, "qF", [128, NT], F32); qfb = Buf()
        kF = P.sb(es, "kF", [128, NT], F32); kfb = Buf()
        vF = P.sb(es, "vF", [128, NT], F32); vfb = Buf()
        qB = P.sb(es, "qB", [128, NT], BF16); qbb = Buf()
        BGt = P.sb(es, "BGt", [64, NCH, 16], F32); bgtb = Buf()
        OB = [P.sb(es, f"OB{d}", [64, NCH, 128], F32) for d in range(2)]
        obb = [Buf() for _ in range(2)]
        eps64 = P.sb(es, "eps64", [64, 1], F32)
        op("dve", lambda e: e.memset(eps64[:], EPS), w=[gmb])
        dma("sp", BGt[:], BG.rearrange("(n c) f -> c n f", c=64), r=bgb, w=[bgtb])
        NR = 3
        ktok = [P.sb(es, f"ktok{i}", [64, 128], F32) for i in range(NR)]
        vtok = [P.sb(es, f"vtok{i}", [64, 128], BF16) for i in range(NR)]
        kkS = [P.sb(es, f"kkS{i}", [64, 64], F32) for i in range(NR)]
        qkS = [P.sb(es, f"qkS{i}", [64, 64], F32) for i in range(NR)]
        chb = [Buf() for _ in range(NR)]
        NP = 3
        def ring(name, shape, dt):
            return [[P.sb(es, f"{name}{d}_{i}", shape, dt) for i in range(NP)] for d in range(2)]
        sc = ring("sc", [128, 4], F32)
        gbm = ring("gbm", [64, 2, 64], F32)
        dTi = ring("dTi", [64, 64], F32)
        dTs = ring("dTs", [64, 64], F32)
        Nm = ring("Nm", [64, 2, 64], F32)
        Xm = ring("Xm", [64, 64], F32)
        Mm = ring("Mm", [64, 2, 2, 64], F32)
        Xb = ring("Xb", [64, 64], BF16)
        AT = ring("AT", [64, 64], BF16)
        ubf = ring("ubf", [64, 128], F32)
        kg = ring("kg", [64, 128], BF16)
        wT = ring("wT", [128, 64], BF16)
        kd = ring("kd", [64, 128], BF16)
        prb = [[Buf() for _ in range(NP)] for _ in range(2)]
        S = [P.sb(es, f"S{d}", [128, 128], F32) for d in range(2)]
        Sb = [P.sb(es, f"Sb{d}", [128, 128], BF16) for d in range(2)]
        Sbuf = [Buf() for _ in range(2)]
        vn = [[P.sb(es, f"vn{d}_{i}", [64, 128], BF16) for i in range(2)] for d in range(2)]
        vnb = [[Buf() for _ in range(2)] for _ in range(2)]
        t1 = [[P.sb(es, f"t1{d}_{i}", [64, 128], F32) for i in range(2)] for d in range(2)]
        t1b = [[Buf() for _ in range(2)] for _ in range(2)]
        idf = ident[0:64, 0:64]
        order = [list(range(NCH)), [3, 2, 1, 0] + list(range(NCH - 1, 3, -1))]
        yb_t = [P.sb(es, f"gy{i}", [64, 8, 128], F32) for i in range(2)]
        ybb = [Buf() for _ in range(2)]
        yo = [P.sb(es, f"gyo{i}", [128, 512], BF16) for i in range(2)]
        yob = [Buf() for _ in range(2)]
        gt_ = [P.sb(es, f"ggt{i}", [64, 8, 128], F32) for i in range(2)]
        gtb_ = [Buf() for _ in range(2)]
        rst = P.sb(es, "grst", [64, NCH], F32); rstb = Buf()
        sqt = P.sb(es, "gsqt", [64, 128], F32); sqtb = Buf()

        def chunk_shared(h, n, slot):
            c0 = n * 64
            b_ = chb[slot]
            pt, pb = P.psum()
            op("pe", lambda e: e.transpose(pt[0:64, 0:128], kF[:, c0:c0 + 64], ident[:]), r=[kfb, cb], w=[pb])
            op("pe", lambda e: e.transpose(pt[0:64, 128:256], vF[:, c0:c0 + 64], ident[:]), r=[vfb, cb], w=[pb])
            op("pe", lambda e: e.matmul(pt[0:64, 256:320], kF[:, c0:c0 + 64], kF[:, c0:c0 + 64], start=True, stop=True), r=[kfb], w=[pb])
            op("pe", lambda e: e.matmul(pt[0:64, 320:384], kF[:, c0:c0 + 64], qF[:, c0:c0 + 64], start=True, stop=True), r=[kfb, qfb], w=[pb])
            op("act", lambda e: e.activation(out=ktok[slot][:], in_=pt[0:64, 0:128], func=AF.Copy), r=[pb], w=[b_])
            op("dve", lambda e: e.tensor_copy(out=vtok[slot][:], in_=pt[0:64, 128:256]), r=[pb, b_], w=[b_])
            op("act", lambda e: e.activation(out=kkS[slot][:], in_=pt[0:64, 256:320], func=AF.Copy), r=[pb, b_], w=[b_])
            op("dve", lambda e: e.tensor_copy(out=qkS[slot][:], in_=pt[0:64, 320:384]), r=[pb, b_], w=[b_])

        def prepare(h, n, d, slot, ps_):
            b_ = prb[d][ps_]
            cs = chb[slot]
            gcol = BGt[:, n, 8 + d * 4 + h:8 + d * 4 + h + 1]
            bcol = BGt[:, n, d * 4 + h:d * 4 + h + 1]
            mC = gm[:, d, 0, :]
            mS = gm[:, d, 1, :]
            mN = gm[:, d, 2, :]
            sc_ = sc[d][ps_]
            pt, pb = P.psum()
            op("pe", lambda e: e.matmul(pt[0:64, 0:1], mC, gcol, start=True, stop=True), r=[gmb, bgtb], w=[pb])
            op("pe", lambda e: e.matmul(pt[0:64, 1:2], mS, gcol, start=True, stop=True), r=[gmb, bgtb], w=[pb])
            op("pe", lambda e: e.matmul(pt[:, 2:3], ones[0:64, :], gcol, start=True, stop=True), r=[cb, bgtb], w=[pb])
            op("act", lambda e: e.activation(out=sc_[0:64, 0:2], in_=pt[0:64, 0:2], func=AF.Exp), r=[pb], w=[b_])
            op("act", lambda e: e.activation(out=sc_[:, 2:3], in_=pt[:, 2:3], func=AF.Exp), r=[pb, b_], w=[b_])
            op("dve", lambda e: e.tensor_scalar(out=sc_[0:64, 3:4], in0=bcol, scalar1=-1.0, scalar2=None, op0=ALU.mult), r=[bgtb, b_], w=[b_])
            g2 = gbm[d][ps_]
            op("dve", lambda e: e.tensor_scalar(out=g2[:, 0, :], in0=ones[0:64, 0:64], scalar1=gcol, scalar2=None, op0=ALU.mult), r=[bgtb, cb, b_], w=[b_])
            op("dve", lambda e: e.tensor_scalar(out=g2[:, 1, :], in0=mC, scalar1=gcol, scalar2=-1.0, op0=ALU.mult, op1=ALU.mult), r=[bgtb, gmb, b_], w=[b_])
            op("pe", lambda e: e.matmul(pt[0:64, 64:128], g2[:, 0, :], mC, start=True, stop=False), r=[b_, gmb], w=[pb])
            op("pe", lambda e: e.matmul(pt[0:64, 64:128], g2[:, 1, :], ones[0:64, 0:64], start=False, stop=False), r=[b_, cb], w=[pb])
            op("pe", lambda e: e.matmul(pt[0:64, 64:128], idf, mN, start=False, stop=True), r=[gmb, cb], w=[pb])
            op("act", lambda e: e.activation(out=dTi[d][ps_][:], in_=pt[0:64, 64:128], func=AF.Exp), r=[pb, b_], w=[b_])
            op("dve", lambda e: e.tensor_tensor(out=dTs[d][ps_][:], in0=dTi[d][ps_][:], in1=idf, op=ALU.subtract), r=[b_, cb], w=[b_])
            op("pool", lambda e: e.tensor_tensor(out=AT[d][ps_][:], in0=qkS[slot][:], in1=dTi[d][ps_][:], op=ALU.mult), r=[cs, b_], w=[b_])
            N_ = Nm[d][ps_]
            op("dve", lambda e: e.scalar_tensor_tensor(out=N_[:, 0, :], in0=kkS[slot][:], scalar=bcol, in1=dTs[d][ps_][:], op0=ALU.mult, op1=ALU.mult),
               r=[cs, bgtb, b_], w=[b_])
            pt2, pb2 = P.psum()
            op("pe", lambda e: e.transpose(pt2[0:64, 0:64], N_[:, 0, :], idf), r=[b_, cb], w=[pb2])
            op("act", lambda e: e.activation(out=N_[:, 1, :], in_=pt2[0:64, 0:64], func=AF.Copy), r=[pb2, b_], w=[b_])
            X_ = Xm[d][ps_]
            op("dve", lambda e: e.tensor_tensor(out=X_[:], in0=idf, in1=N_[:, 0, :], op=ALU.subtract), r=[b_, cb], w=[b_])
            M_ = Mm[d][ps_]
            curM, curMT = N_[:, 0, :], N_[:, 1, :]
            for k in range(1, 6):
                pq, pqb = P.psum()
                op("pe", lambda e: e.matmul(pq[0:64, 0:64], curMT, curM, start=True, stop=True), r=[b_], w=[pqb])
                op("pe", lambda e: e.matmul(pq[0:64, 64:128], curM, curMT, start=True, stop=True), r=[b_], w=[pqb])
                pp = k % 2
                op("act", lambda e: e.activation(out=M_[:, pp, :, :], in_=pq[0:64, 0:128].rearrange("p (a b) -> p a b", a=2), func=AF.Copy), r=[pqb, b_], w=[b_])
                curM, curMT = M_[:, pp, 0, :], M_[:, pp, 1, :]
                px, pxb = P.psum()
                op("pe", lambda e: e.matmul(px[0:64, 0:64], curMT, X_[:], start=True, stop=True), r=[b_], w=[pxb])
                op("dve", lambda e: e.tensor_tensor(out=X_[:], in0=X_[:], in1=px[0:64, 0:64], op=ALU.add), r=[pxb, b_], w=[b_])
            op("act", lambda e: e.activation(out=Xb[d][ps_][:], in_=X_[:], func=AF.Copy), r=[b_], w=[b_])
            op("pool", lambda e: e.tensor_scalar(out=kg[d][ps_][:], in0=ktok[slot][:], scalar1=sc_[0:64, 0:1], scalar2=None, op0=ALU.mult), r=[cs, b_], w=[b_])
            op("pool", lambda e: e.tensor_scalar(out=kd[d][ps_][:], in0=ktok[slot][:], scalar1=sc_[0:64, 1:2], scalar2=None, op0=ALU.mult), r=[cs, b_], w=[b_])
            pu, pub = P.psum()
            op("pe", lambda e: e.matmul(pu[0:64, 0:128], Xb[d][ps_][:], vtok[slot][:], start=True, stop=True), r=[b_, cs], w=[pub])
            op("pe", lambda e: e.matmul(pu[:, 128:192], kg[d][ps_][:], Xb[d][ps_][:], start=True, stop=True), r=[b_], w=[pub])
            op("dve", lambda e: e.tensor_scalar(out=ubf[d][ps_][:], in0=pu[0:64, 0:128], scalar1=bcol, scalar2=None, op0=ALU.mult), r=[pub, bgtb, b_], w=[b_])
            op("act", lambda e: e.activation(out=wT[d][ps_][:], in_=pu[:, 128:192], func=AF.Copy), r=[pub, b_], w=[b_])

        def step(h, n, d, ps_, si):
            b_ = prb[d][ps_]
            sc_ = sc[d][ps_]
            c0 = n * 64
            v_, vb_ = vn[d][si % 2], vnb[d][si % 2]
            t_, tb_ = t1[d][si % 2], t1b[d][si % 2]
            pw, pwb = P.psum()
            op("pe", lambda e: e.matmul(pw[0:64, 0:128], wT[d][ps_][:], Sb[d][:], start=True, stop=True), r=[b_, Sbuf[d]], w=[pwb])
            op("dve", lambda e: e.scalar_tensor_tensor(out=v_[:], in0=pw[0:64, 0:128], scalar=sc_[0:64, 3:4], in1=ubf[d][ps_][:], op0=ALU.mult, op1=ALU.add),
               r=[pwb, b_], w=[vb_])
            po, pob = P.psum()
            op("pe", lambda e: e.matmul(po[0:64, 0:128], qB[:, c0:c0 + 64], Sb[d][:], start=True, stop=True), r=[qbb, Sbuf[d]], w=[pob])
            op("pe", lambda e: e.matmul(po[0:64, 128:256], AT[d][ps_][:], v_[:], start=True, stop=True), r=[b_, vb_], w=[pob])
            op("pe", lambda e: e.matmul(po[:, 256:384], kd[d][ps_][:], v_[:], start=True, stop=True), r=[b_, vb_], w=[pob])
            op("dve", lambda e: e.scalar_tensor_tensor(out=S[d][:], in0=S[d][:], scalar=sc_[:, 2:3], in1=po[:, 256:384], op0=ALU.mult, op1=ALU.add),
               r=[pob, b_, Sbuf[d]], w=[Sbuf[d]])
            op("act", lambda e: e.activation(out=Sb[d][:], in_=S[d][:], func=AF.Copy), r=[Sbuf[d]], w=[Sbuf[d]])
            op("act", lambda e: e.activation(out=t_[:], in_=po[0:64, 0:128], func=AF.Copy, scale=sc_[0:64, 0:1]), r=[pob, b_], w=[tb_])
            op("pool" if False else "dve", lambda e: e.tensor_tensor(out=OB[d][:, n, :], in0=t_[:], in1=po[0:64, 128:256], op=ALU.add), r=[tb_, pob], w=[obb[d]])

        ig = 0
        for h in range(4):
            dma("sp", qF[:], GQ[0, h], r=gqb, w=[qfb])
            dma("sp", kF[:], GQ[1, h], r=gqb, w=[kfb])
            dma("sp", vF[:], GQ[2, h], r=gqb, w=[vfb])
            op("act", lambda e: e.activation(out=qB[:], in_=qF[:], func=AF.Copy), r=[qfb], w=[qbb])
            for d in range(2):
                op("dve", lambda e: e.memset(S[d][:], 0.0), w=[Sbuf[d]])
                op("pool", lambda e: e.memset(Sb[d][:], 0.0), w=[Sbuf[d]])
            islot = 0
            for i in range(NCH):
                for d in range(2):
                    n = order[d][i]
                    slot = islot % NR
                    islot += 1
                    chunk_shared(h, n, slot)
                    prepare(h, n, d, slot, i % NP)
                    step(h, n, d, i % NP, i)
            op("pool", lambda e: e.tensor_tensor(out=OB[0][:], in0=OB[0][:], in1=OB[1][:], op=ALU.add), r=[obb[0], obb[1]], w=[obb[0]])
            for n in range(NCH):
                op("act", lambda e: e.activation(out=sqt[:], in_=OB[0][:, n, :], func=AF.Square, accum_out=rst[:, n:n + 1]), r=[obb[0]], w=[sqtb, rstb])
            op("act", lambda e: e.activation(out=rst[:], in_=rst[:], func=AF.Sqrt, bias=eps64[:], scale=1.0 / 128), r=[rstb, gmb], w=[rstb])
            op("dve", lambda e: e.reciprocal(rst[:], rst[:]), r=[rstb], w=[rstb])
            for g8 in range(NCH // 8 + (1 if NCH % 8 else 0)):
                n0 = g8 * 8
                nn = min(8, NCH - n0)
                if skip_ctx and n0 + nn <= 4:
                    continue
                y_, yb2 = yb_t[ig % 2], ybb[ig % 2]
                g_, gb2 = gt_[ig % 2], gtb_[ig % 2]
                o_, ob2 = yo[ig % 2], yob[ig % 2]
                ig += 1
                dma("sp", g_[:, 0:nn, :], GG[n0 * 64:(n0 + nn) * 64, h * 128:(h + 1) * 128].rearrange("(n c) e -> c n e", c=64), r=ggb, w=[gb2])
                pt, pb = P.psum()
                for q in range(nn):
                    n = n0 + q
                    op("dve", lambda e: e.scalar_tensor_tensor(out=y_[:, q, :], in0=OB[0][:, n, :], scalar=rst[:, n:n + 1], in1=g_[:, q, :], op0=ALU.mult, op1=ALU.mult),
                       r=[obb[0], rstb, gb2], w=[yb2])
                    op("pe", lambda e: e.transpose(pt[:, q * 64:(q + 1) * 64], y_[:, q, :], idf), r=[yb2, cb], w=[pb])
                op("act", lambda e: e.activation(out=o_[:, 0:nn * 64], in_=pt[:, 0:nn * 64], func=AF.Copy, scale=gn[:, 0:1]), r=[pb, gmb], w=[ob2])
                dma("sp", YS[512 + h * 128:512 + (h + 1) * 128, n0 * 64:(n0 + nn) * 64], o_[:, 0:nn * 64], r=[ob2], w=ysb)

    for l in layers:
        last = (l == DEPTH - 1)
        if cfg.get("do_mixer", True):
            if l % 2 == 1:
                odd_mixer(l, skip_ctx=last)
            else:
                even_mixer(l, skip_ctx=last)
        if cfg.get("do_ffn", True):
            ffn_layer(l, skip_ctx=last)

    if cfg.get("dump_xt"):
        xd = P.outp("xt_dump", [D, NT])
        with contextlib.ExitStack() as es:
            tb = [P.sb(es, f"db{i}", [128, 8, 512], F32) for i in range(2)]
            tbb = [Buf() for _ in range(2)]
            for ti, (t0, T) in enumerate(TILES):
                dma("sp", tb[ti % 2][:, :, :T], XT[:, t0:t0 + T].rearrange("(c p) t -> p c t", p=128), r=[xb[ti]], w=[tbb[ti % 2]])
                dma("sp", xd[:, t0:t0 + T].rearrange("(c p) t -> p c t", p=128), tb[ti % 2][:, :, :T], r=[tbb[ti % 2]])
            kb.barrier()

    with contextlib.ExitStack() as es:
        xt = [P.sb(es, f"fx{i}", [128, 8, 512], F32) for i in range(2)]
        xtb = [Buf() for _ in range(2)]
        sq = [P.sb(es, f"fsq{i}", [128, 8, 512], F32) for i in range(2)]
        sqb = [Buf() for _ in range(2)]
        rs = [P.sb(es, f"frs{i}", [128, 512], F32) for i in range(2)]
        rsb = [Buf() for _ in range(2)]
        ot = [P.sb(es, f"fo{i}", [128, D], F32) for i in range(2)]
        otb = [Buf() for _ in range(2)]
        io = 0
        for ti, (t0, T) in enumerate(TILES):
            if ti == 0:
                continue
            x_, xb_ = xt[ti % 2], xtb[ti % 2]
            q_, qb_ = sq[ti % 2], sqb[ti % 2]
            r_, rb_ = rs[ti % 2], rsb[ti % 2]
            dma("sp", x_[:], XT[:, t0:t0 + T].rearrange("(c p) t -> p c t", p=128), r=[xb[ti]], w=[xb_])
            for c in range(8):
                op("act", lambda e: e.activation(out=q_[:, c, :], in_=x_[:, c, :], func=AF.Square), r=[xb_], w=[qb_])
            pt, pb = P.psum()
            for c in range(8):
                op("pe", lambda e: e.matmul(pt[:], ones[:], q_[:, c, :], start=(c == 0), stop=(c == 7)), r=[qb_, cb], w=[pb])
            op("act", lambda e: e.activation(out=r_[:], in_=pt[:], func=AF.Sqrt, bias=epsb[:], scale=1.0 / D), r=[pb, cb], w=[rb_])
            op("dve", lambda e: e.reciprocal(r_[:], r_[:]), r=[rb_], w=[rb_])
            for c in range(8):
                op("dve", lambda e: e.scalar_tensor_tensor(out=q_[:, c, :], in0=x_[:, c, :], scalar=fng[:, c:c + 1], in1=r_[:],
                                                           op0=ALU.mult, op1=ALU.mult), r=[xb_, rb_, cb, qb_], w=[qb_])
            for bi in range(4):
                o_, ob_ = ot[io % 2], otb[io % 2]
                io += 1
                for half in range(2):
                    pt, pb = P.psum()
                    for q in range(4):
                        c = half * 4 + q
                        op("pe", lambda e: e.transpose(pt[:, q * 128:(q + 1) * 128], q_[:, c, bi * 128:(bi + 1) * 128], ident[:]), r=[qb_, cb], w=[pb])
                    if half:
                        op("act", lambda e: e.activation(out=o_[:, 512:1024], in_=pt[:], func=AF.Copy), r=[pb], w=[ob_])
                    else:
                        op("dve", lambda e: e.tensor_copy(out=o_[:, 0:512], in_=pt[:]), r=[pb], w=[ob_])
                tk0 = t0 - CT + bi * 128
                dma("sp", out[tk0:tk0 + 128, :], o_[:], r=[ob_])
        kb.barrier()

    kb.finish()
    root.close()
    return P


def host_inputs(inputs, b):
    f = lambda a: np.ascontiguousarray(a, dtype=np.float32)
    c2 = np.stack([inputs["c"][b], inputs["c_ctx"]], -1).reshape(8, 128, 2).transpose(1, 0, 2)
    m = {
        "x": f(inputs["x"][b]),
        "ctx": f(inputs["ctx"][b]),
        "c2": f(c2),
        "w_ada": f(inputs["w_ada"]),
        "b_ada": f(inputs["b_ada"].reshape(DEPTH, 48, 128).transpose(0, 2, 1)),
        "norm_mix": f(inputs["norm_mix"].reshape(DEPTH, 8, 128).transpose(0, 2, 1)),
        "norm_ffn": f(inputs["norm_ffn"].reshape(DEPTH, 8, 128).transpose(0, 2, 1)),
        "final_norm": f(inputs["final_norm"].reshape(8, 128).T),
        "ffn_w_up": f(inputs["ffn_w_up"]),
        "ffn_conv": f(inputs["ffn_conv"].reshape(DEPTH, 3, 44, 128).transpose(0, 3, 2, 1)),
        "ffn_w_down": f(inputs["ffn_w_down"]),
        "ident": np.eye(128, dtype=np.float32),
    }
    m["od_w_in"] = f(inputs["od_w_in"])
    m["od_w_out"] = f(inputs["od_w_out"])
    m["lru_conv"] = f(inputs["lru_conv"].reshape(2, 4, 4, 128).transpose(0, 3, 2, 1))
    m["lru_conv_b"] = f(inputs["lru_conv_b"].reshape(2, 4, 128).transpose(0, 2, 1))
    def bd(w):
        o = np.zeros((2, 2, 4, 128, 128), np.float32)
        for c in range(4):
            o[:, :, c, 0:64, 0:64] = w[:, :, 2 * c]
            o[:, :, c, 64:128, 64:128] = w[:, :, 2 * c + 1]
        return o
    m["lru_wa"] = bd(inputs["lru_wa"])
    m["lru_wx"] = bd(inputs["lru_wx"])
    for nm, key in (("lru_ba", "lru_ba"), ("lru_bx", "lru_bx"), ("lru_lam", "lru_lambda")):
        m[nm] = f(inputs[key].reshape(2, 2, 4, 128).transpose(0, 3, 1, 2))
    m["swa_sink"] = f(np.broadcast_to(inputs["swa_sink"][:, None, :], (2, 128, 8)))
    m["ev_w_in"] = f(inputs["ev_w_in"])
    m["ev_w_out"] = f(inputs["ev_w_out"])
    m["diff_lambda"] = f(np.broadcast_to(inputs["diff_lambda"].reshape(2, 1, 256), (2, 128, 256)))
    m["diff_subln"] = f(inputs["diff_subln"].reshape(2, 128, 1))
    m["gdn_conv"] = f(inputs["gdn_conv"].reshape(2, 4, 12, 128).transpose(0, 3, 2, 1))
    m["gdn_a_log"] = f(np.broadcast_to(inputs["gdn_a_log"].reshape(2, 1, 8), (2, 128, 8)))
    m["gdn_dt_bias"] = f(np.broadcast_to(inputs["gdn_dt_bias"].reshape(2, 1, 8), (2, 128, 8)))
    m["gdn_norm"] = f(inputs["gdn_norm"].reshape(2, 128, 1))
    m.update(CONSTS)
    return m


def _make_consts():
    t = np.arange(LT)
    row = (t // 64).astype(np.float64)
    col = (t % 64).astype(np.float64)
    inv = (10000.0 ** (-np.arange(0, 32, 2, dtype=np.float32) / 32)).astype(np.float32)
    ar = (row[:, None].astype(np.float32) * inv).astype(np.float32)
    ac = (col[:, None].astype(np.float32) * inv).astype(np.float32)
    cr, sr, cc, sc = np.cos(ar), np.sin(ar), np.cos(ac), np.sin(ac)
    cosT = np.zeros((128, LT), np.float32)
    sinT = np.zeros((128, LT), np.float32)
    perm = np.zeros((128, 128), np.float32)
    for p in range(128):
        dd = p % 64
        i = dd % 16
        q = dd // 16
        if q == 0:
            cosT[p], sinT[p], partner = cr[:, i], -sr[:, i], p + 16
        elif q == 1:
            cosT[p], sinT[p], partner = cr[:, i], sr[:, i], p - 16
        elif q == 2:
            cosT[p], sinT[p], partner = cc[:, i], -sc[:, i], p + 16
        else:
            cosT[p], sinT[p], partner = cc[:, i], sc[:, i], p - 16
        perm[partner, p] = 1.0
    a = np.arange(128)[:, None]
    bq = np.arange(128)[None, :]
    NEG = -30000.0
    mprev = np.where(a >= bq, 0.0, NEG).astype(np.float32)
    mnext = np.where(a <= bq, 0.0, NEG).astype(np.float32)
    tt = np.arange(64)[:, None]
    ii = np.arange(64)[None, :]
    gmask = np.zeros((64, 2, 3, 64), np.float32)
    gmask[:, 0, 0] = (tt <= ii)
    gmask[:, 0, 1] = (tt > ii)
    gmask[:, 0, 2] = np.where(tt <= ii, 0.0, NEG)
    gmask[:, 1, 0] = (tt >= ii)
    gmask[:, 1, 1] = (tt < ii)
    gmask[:, 1, 2] = np.where(tt >= ii, 0.0, NEG)
    return {"cosT": cosT, "sinT": sinT, "permT": perm, "mprev": np.tile(mprev, (1, 4)), "mnext": np.tile(mnext, (1, 4)), "gmask": gmask}


CONSTS = _make_consts()


def kernel(**inputs):
    inputs = {k: np.asarray(v) for k, v in inputs.items()}
    P = build()
    n = 8
    in_maps = []
    for b in range(n):
        m = host_inputs(inputs, b)
        in_maps.append({k: m[k] for k in P.din})
    res = run_bass_kernel_spmd(P.nc, in_maps, core_ids=list(range(n)))
    return np.stack([np.asarray(r["out"], dtype=np.float32) for r in res.results], 0)
```

```python
import contextlib
import math
import numpy as np
import concourse.bass as bass
import concourse.mybir as mybir
from concourse.bass_utils import run_bass_kernel_spmd

F32 = mybir.dt.float32
BF16 = mybir.dt.bfloat16
AF = mybir.ActivationFunctionType
ALU = mybir.AluOpType
AX = mybir.AxisListType

D = 1024
NT = 4352
CT = 256
LT = 4096
DEPTH = 4
FFN = 2816
EPS = 1e-6
TILES = [(0, 256)] + [(256 + 512 * i, 512) for i in range(8)]


class Buf:
    __slots__ = ("w", "r", "name", "excl")

    def __init__(self, name="", excl=False):
        self.w = None
        self.r = []
        self.name = name
        self.excl = excl


class KB:
    EPOCH = 24000
    NDMA = 40

    def __init__(self, nc, same_eng_sync=True):
        self.nc = nc
        self.es = contextlib.ExitStack()
        self.eng = {"pe": nc.tensor, "act": nc.scalar, "dve": nc.vector, "pool": nc.gpsimd, "sp": nc.sync}
        self.same = same_eng_sync
        self.cnt = {e: 0 for e in self.eng}
        self.epoch = {e: 0 for e in self.eng}
        self.sems = {}
        self.known = {e: {} for e in self.eng}
        self.last_tok = {e: None for e in self.eng}
        self.dsem = [self.es.enter_context(nc.semaphore(f"dq{i}")) for i in range(self.NDMA)]
        self.dtot = [0] * self.NDMA
        self.dnext = 0
        self.nwait = 0
        self.nins = 0
        for e in self.eng:
            self._newsem(e)

    def _newsem(self, e):
        key = (e, self.epoch[e])
        self.sems[key] = self.es.enter_context(self.nc.semaphore(f"s_{e}_{self.epoch[e]}"))
        self.cnt[e] = 0

    def _semh(self, key):
        if isinstance(key, int):
            return self.dsem[key]
        return self.sems[key]

    def _wait(self, e, tok):
        key, val = tok
        if self.known[e].get(key, 0) >= val:
            return
        self.eng[e].wait_ge(self._semh(key), val)
        self.known[e][key] = val
        self.nwait += 1

    def _deps(self, e, r, w):
        toks = []
        for b in r:
            if b.w is not None:
                toks.append(b.w)
        for b in w:
            if b.w is not None:
                toks.append(b.w)
            toks.extend(b.r)
        for tok in toks:
            key = tok[0]
            if (not isinstance(key, int)) and key[0] == e and (e == "pe" or not self.same):
                continue
            self._wait(e, tok)

    def _mark(self, tok, r, w):
        for b in w:
            b.w = tok
            b.r = []
        for b in r:
            if b not in w:
                b.r.append(tok)
                if len(b.r) > 24:
                    d = {}
                    for k, v in b.r:
                        if d.get(k, 0) < v:
                            d[k] = v
                    b.r = list(d.items())

    def op(self, e, fn, r=(), w=()):
        w = list(w) + [b for b in r if b.excl and b not in w]
        r = [b for b in r if not b.excl]
        self._deps(e, r, w)
        if self.cnt[e] >= self.EPOCH:
            self.epoch[e] += 1
            self._newsem(e)
        ins = fn(self.eng[e])
        key = (e, self.epoch[e])
        self.cnt[e] += 1
        ins.then_inc(self.sems[key], 1)
        tok = (key, self.cnt[e])
        self.last_tok[e] = tok
        self._mark(tok, r, w)
        self.nins += 1
        return tok

    def dma(self, e, out, in_, r=(), w=(), **kw):
        r = list(r)
        w = list(w)
        self._deps(e, r, w)
        i = self.dnext
        self.dnext = (self.dnext + 1) % self.NDMA
        if self.dtot[i]:
            self._wait(e, (i, self.dtot[i]))
        if e == "pool":
            self.swq = getattr(self, "swq", [])
            if len(self.swq) >= 3:
                self._wait(e, self.swq[-3])
        ins = self.eng[e].dma_start(out=out, in_=in_, **kw)
        self.dtot[i] += 16
        ins.then_inc(self.dsem[i], 16)
        tok = (i, self.dtot[i])
        if e == "pool":
            self.swq.append(tok)
        self._mark(tok, r, w)
        self.nins += 1
        return tok

    def barrier(self):
        toks = [t for t in self.last_tok.values() if t is not None]
        toks += [(i, self.dtot[i]) for i in range(self.NDMA) if self.dtot[i]]
        for e in self.eng:
            for tok in toks:
                key = tok[0]
                if (not isinstance(key, int)) and key[0] == e:
                    continue
                self._wait(e, tok)

    def finish(self):
        self.barrier()
        self.es.close()


class Prog:
    def __init__(self, cfg=None):
        self.cfg = cfg or {}
        self.nc = bass.Bass("TRN2", target_bir_lowering=False)
        self.kb = KB(self.nc)
        self.root = contextlib.ExitStack()
        self.din = {}
        self.dbuf = {}
        self.psn = 0

    def inp(self, name, shape, dt=F32):
        t = self.nc.dram_tensor(name, list(shape), dt, kind="ExternalInput").ap()
        self.din[name] = t
        return t

    def outp(self, name, shape, dt=F32):
        return self.nc.dram_tensor(name, list(shape), dt, kind="ExternalOutput").ap()

    def scratch(self, name, shape, dt):
        return self.nc.dram_tensor(name, list(shape), dt, kind="Internal").ap()

    def sb(self, es, name, shape, dt):
        self.nsb = getattr(self, "nsb", 0) + 1
        return es.enter_context(self.nc.sbuf_tensor(f"sb{self.nsb}_{name}", list(shape), dt))

    def scope(self, name):
        return self.nc.named_scope(name)

    def setup_psum(self):
        self.ps = [self.root.enter_context(self.nc.psum_tensor(f"ps{i}", [128, 512], F32)) for i in range(8)]
        self.psb = [Buf(f"ps{i}", excl=True) for i in range(8)]

    def psum(self):
        lo, hi = getattr(self, "psr", (0, 8))
        if not (lo <= self.psn < hi):
            self.psn = lo
        i = self.psn
        self.psn = lo + (self.psn + 1 - lo) % (hi - lo)
        return self.ps[i], self.psb[i]


def build(cfg=None):
    cfg = cfg or {}
    P = Prog(cfg)
    nc, kb = P.nc, P.kb
    op, dma = kb.op, kb.dma
    P.setup_psum()
    root = P.root
    layers = cfg.get("layers", list(range(DEPTH)))

    x_in = P.inp("x", [LT, D])
    ctx_in = P.inp("ctx", [CT, D])
    c2_in = P.inp("c2", [128, 8, 2])
    w_ada = P.inp("w_ada", [DEPTH, D, 6 * D])
    b_ada = P.inp("b_ada", [DEPTH, 128, 48])
    norm_mix = P.inp("norm_mix", [DEPTH, 128, 8])
    norm_ffn = P.inp("norm_ffn", [DEPTH, 128, 8])
    final_norm = P.inp("final_norm", [128, 8])
    ffn_w_up = P.inp("ffn_w_up", [DEPTH, D, 2 * FFN])
    ffn_conv = P.inp("ffn_conv", [DEPTH, 128, 44, 3])
    ffn_w_down = P.inp("ffn_w_down", [DEPTH, FFN, D])
    ident_in = P.inp("ident", [128, 128])
    od_w_in = P.inp("od_w_in", [2, D, 1792])
    od_w_out = P.inp("od_w_out", [2, D, D])
    lru_conv = P.inp("lru_conv", [2, 128, 4, 4])
    lru_conv_b = P.inp("lru_conv_b", [2, 128, 4])
    lru_wa = P.inp("lru_wa", [2, 2, 4, 128, 128])
    lru_wx = P.inp("lru_wx", [2, 2, 4, 128, 128])
    lru_ba = P.inp("lru_ba", [2, 128, 2, 4])
    lru_bx = P.inp("lru_bx", [2, 128, 2, 4])
    lru_lam = P.inp("lru_lam", [2, 128, 2, 4])
    swa_sink = P.inp("swa_sink", [2, 128, 8])
    cos_in = P.inp("cosT", [128, LT])
    sin_in = P.inp("sinT", [128, LT])
    perm_in = P.inp("permT", [128, 128])
    mprev_in = P.inp("mprev", [128, 512])
    mnext_in = P.inp("mnext", [128, 512])
    ev_w_in = P.inp("ev_w_in", [2, D, 3600])
    ev_w_out = P.inp("ev_w_out", [2, D, D])
    diff_lambda = P.inp("diff_lambda", [2, 128, 256])
    diff_subln = P.inp("diff_subln", [2, 128, 1])
    gdn_conv = P.inp("gdn_conv", [2, 128, 12, 4])
    gdn_a_log = P.inp("gdn_a_log", [2, 128, 8])
    gdn_dt_bias = P.inp("gdn_dt_bias", [2, 128, 8])
    gdn_norm = P.inp("gdn_norm", [2, 128, 1])
    gmask_in = P.inp("gmask", [128, 2, 3, 128])
    gaux_in = P.inp("gaux", [128, 258])
    out = P.outp("out", [LT, D])

    XT = P.scratch("XT", [D, NT], F32)
    xb = [Buf(f"X{t}") for t in range(9)]
    GS = P.scratch("GS", [FFN, NT], BF16)
    gsb = [Buf(f"G{j}") for j in range(22)]
    wup_b = P.scratch("wup_b", [DEPTH, D, 2 * FFN], BF16)
    wdn_b = P.scratch("wdn_b", [DEPTH, FFN, D], BF16)
    wupb = [Buf() for _ in range(DEPTH)]
    wdnb = [Buf() for _ in range(DEPTH)]

    odin_b = P.scratch("odin_b", [2, D, 1792], BF16)
    odout_b = P.scratch("odout_b", [2, D, D], BF16)
    odinb = [Buf() for _ in range(2)]
    odoutb = [Buf() for _ in range(2)]
    YS = P.scratch("YS", [D, NT], BF16)
    ysb = [Buf("Y")]
    QD = P.scratch("QD", [8, 64, NT], BF16)
    KD = P.scratch("KD", [2, 64, NT], BF16)
    qdb = [Buf("QD")]
    kdb = [Buf("KD")]

    evin_b = P.scratch("evin_b", [2, D, 3600], BF16)
    evout_b = P.scratch("evout_b", [2, D, D], BF16)
    evinb = [Buf() for _ in range(2)]
    evoutb = [Buf() for _ in range(2)]
    QA = P.scratch("QA", [8, 64, NT], BF16)
    KA = P.scratch("KA", [8, 64, NT], BF16)
    VA = P.scratch("VA", [NT, 512], BF16)
    GQ = P.scratch("GQ", [3, 4, 128, NT], BF16)
    GQK = P.scratch("GQK", [4, 128, NT], F32)
    GG = P.scratch("GG", [NT, 512], F32)
    BG = P.scratch("BG", [NT, 16], F32)
    OBD = P.scratch("OBD", [2, 4, NT, 128], F32)
    obdb = [Buf("OBD")]
    qab = [Buf("QA")]
    kab = [Buf("KA")]
    vab = [Buf("VA")]
    gqb = [Buf("GQ")]
    ggb = [Buf("GG")]
    bgb = [Buf("BG")]

    ident = P.sb(root, "ident", [128, 128], F32)
    ones = P.sb(root, "ones", [128, 128], F32)
    epsb = P.sb(root, "epsb", [128, 1], F32)
    MOD = P.sb(root, "MOD", [128, DEPTH, 48, 2], F32)
    AMX = P.sb(root, "AMX", [128, DEPTH, 8, 2], F32)
    AFF = P.sb(root, "AFF", [128, DEPTH, 8, 2], F32)
    fconv = P.sb(root, "fconv", [128, DEPTH, 44, 3], F32)
    fng = P.sb(root, "fng", [128, 8], F32)
    cb = Buf("consts")
    modb = Buf("mod")
    dma("sp", ident[:], ident_in[:, :], w=[cb])
    op("dve", lambda e: e.memset(ones[:], 1.0), w=[cb])
    op("dve", lambda e: e.memset(epsb[:], EPS), w=[cb])
    dma("sp", fconv[:], ffn_conv.rearrange("l p j k -> p l j k"), w=[cb])
    dma("sp", fng[:], final_norm[:, :], w=[cb])

    def cast_rows(dst, src, R, C, wbuf, cchunk):
        dv = dst.rearrange("r (a c) -> (r a) c", c=cchunk)
        sv = src.rearrange("r (a c) -> (r a) c", c=cchunk)
        rows = R * (C // cchunk)
        r0 = 0
        while r0 < rows:
            n = min(2048, rows - r0)
            dma("pool", dv[r0:r0 + n, :], sv[r0:r0 + n, :], w=[wbuf])
            r0 += n

    for l in layers:
        if l % 2 == 0:
            cast_rows(evin_b[l // 2], ev_w_in[l // 2], D, 3600, evinb[l // 2], 1800)
            cast_rows(evout_b[l // 2], ev_w_out[l // 2], D, D, evoutb[l // 2], 1024)
        if l % 2 == 1:
            cast_rows(odin_b[l // 2], od_w_in[l // 2], D, 1792, odinb[l // 2], 1792)
            cast_rows(odout_b[l // 2], od_w_out[l // 2], D, D, odoutb[l // 2], 1024)
        cast_rows(wup_b[l], ffn_w_up[l], D, 2 * FFN, wupb[l], 1408)
        cast_rows(wdn_b[l], ffn_w_down[l], FFN, D, wdnb[l], 1024)

    with contextlib.ExitStack() as es, P.scope("mod"):
        c2 = P.sb(es, "c2", [128, 8, 2], F32)
        s2 = P.sb(es, "s2", [128, 8, 2], F32)
        bad = P.sb(es, "bad", [128, DEPTH, 48], F32)
        nmx = P.sb(es, "nmx", [128, DEPTH, 8], F32)
        nff = P.sb(es, "nff", [128, DEPTH, 8], F32)
        wa = [P.sb(es, f"wa{i}", [128, 8, 512], F32) for i in range(2)]
        wab = [Buf() for _ in range(2)]
        b0 = Buf()
        dma("sp", c2[:], c2_in[:, :, :], w=[b0])
        dma("sp", bad[:], b_ada.rearrange("l p j -> p l j"), w=[b0])
        dma("sp", nmx[:], norm_mix.rearrange("l p c -> p l c"), w=[b0])
        dma("sp", nff[:], norm_ffn.rearrange("l p c -> p l c"), w=[b0])
        op("act", lambda e: e.activation(out=s2[:], in_=c2[:], func=AF.Silu), r=[b0], w=[b0])
        it = 0
        for l in layers:
            for blk in range(12):
                wt, wb = wa[it % 2], wab[it % 2]
                it += 1
                dma("sp", wt[:], w_ada[l][:, blk * 512:(blk + 1) * 512].rearrange("(k p) n -> p k n", p=128), w=[wb])
                pt, pb = P.psum()
                for jj in range(4):
                    for k in range(8):
                        op("pe", lambda e: e.matmul(pt[:, jj * 2:jj * 2 + 2], wt[:, k, jj * 128:(jj + 1) * 128], s2[:, k, :],
                                                    start=(k == 0), stop=(k == 7)), r=[wb, b0], w=[pb])
                for jj in range(4):
                    j = blk * 4 + jj
                    op("dve", lambda e: e.tensor_scalar(out=MOD[:, l, j, :], in0=pt[:, jj * 2:jj * 2 + 2], scalar1=bad[:, l, j:j + 1],
                                                        scalar2=None, op0=ALU.add), r=[pb, b0], w=[modb])
            for s in range(2):
                op("dve", lambda e: e.scalar_tensor_tensor(out=AMX[:, l, :, s], in0=MOD[:, l, 8:16, s], scalar=1.0, in1=nmx[:, l, :],
                                                           op0=ALU.add, op1=ALU.mult), r=[modb, b0], w=[modb])
                op("dve", lambda e: e.scalar_tensor_tensor(out=AFF[:, l, :, s], in0=MOD[:, l, 32:40, s], scalar=1.0, in1=nff[:, l, :],
                                                           op0=ALU.add, op1=ALU.mult), r=[modb, b0], w=[modb])
        kb.barrier()

    dbg_x = cfg.get("x_override")
    if dbg_x:
        xo = P.inp("xo", [D, NT])
        with contextlib.ExitStack() as es:
            tb = [P.sb(es, f"tb{i}", [128, 8, 512], F32) for i in range(2)]
            tbb = [Buf() for _ in range(2)]
            for ti, (t0, T) in enumerate(TILES):
                dma("sp", tb[ti % 2][:, :, :T], xo[:, t0:t0 + T].rearrange("(c p) t -> p c t", p=128), w=[tbb[ti % 2]])
                dma("sp", XT[:, t0:t0 + T].rearrange("(c p) t -> p c t", p=128), tb[ti % 2][:, :, :T], r=[tbb[ti % 2]], w=[xb[ti]])
            kb.barrier()
    else:
        with contextlib.ExitStack() as es:
            tin = [P.sb(es, f"tin{i}", [128, D], F32) for i in range(2)]
            tinb = [Buf() for _ in range(2)]
            tout = [P.sb(es, f"tout{i}", [128, 8, 512], F32) for i in range(2)]
            toutb = [Buf() for _ in range(2)]
            it = 0
            for ti, (t0, T) in enumerate(TILES):
                to, tob = tout[ti % 2], toutb[ti % 2]
                for bi in range(T // 128):
                    tk0 = t0 + bi * 128
                    src = ctx_in[tk0:tk0 + 128, :] if tk0 < CT else x_in[tk0 - CT:tk0 - CT + 128, :]
                    tt, ttb = tin[it % 2], tinb[it % 2]
                    it += 1
                    dma("sp", tt[:], src, w=[ttb])
                    for half in range(2):
                        pt, pb = P.psum()
                        for q in range(4):
                            c = half * 4 + q
                            op("pe", lambda e: e.transpose(pt[:, q * 128:(q + 1) * 128], tt[:, c * 128:(c + 1) * 128], ident[:]),
                               r=[ttb, cb], w=[pb])
                        op("act" if half else "dve",
                           (lambda e: e.activation(out=to[:, 4:8, bi * 128:(bi + 1) * 128], in_=pt[:].rearrange("p (q t) -> p q t", q=4), func=AF.Copy))
                           if half else
                           (lambda e: e.tensor_copy(out=to[:, 0:4, bi * 128:(bi + 1) * 128], in_=pt[:].rearrange("p (q t) -> p q t", q=4))),
                           r=[pb], w=[tob])
                dma("sp", XT[:, t0:t0 + T].rearrange("(c p) t -> p c t", p=128), to[:, :, :T], r=[tob], w=[xb[ti]])
            kb.barrier()

    def norm_stage(es, l, A, shift_idx, U, ub):
        xt = [P.sb(es, f"nx{i}", [128, 8, 512], F32) for i in range(2)]
        xtb = [Buf() for _ in range(2)]
        sq = [P.sb(es, f"nsq{i}", [128, 8, 512], F32) for i in range(2)]
        sqb = [Buf() for _ in range(2)]
        rs = [P.sb(es, f"nrs{i}", [128, 512], F32) for i in range(2)]
        rsb = [Buf() for _ in range(2)]
        tm = [P.sb(es, f"ntm{i}", [128, 512], F32) for i in range(3)]
        tmb = [Buf() for _ in range(3)]
        k3 = 0
        for ti, (t0, T) in enumerate(TILES):
            s = 1 if ti == 0 else 0
            x_, xb_ = xt[ti % 2], xtb[ti % 2]
            q_, qb_ = sq[ti % 2], sqb[ti % 2]
            r_, rb_ = rs[ti % 2], rsb[ti % 2]
            dma("sp", x_[:, :, :T], XT[:, t0:t0 + T].rearrange("(c p) t -> p c t", p=128), r=[xb[ti]], w=[xb_])
            for c in range(8):
                op("act", lambda e: e.activation(out=q_[:, c, :T], in_=x_[:, c, :T], func=AF.Square), r=[xb_], w=[qb_])
            pt, pb = P.psum()
            for c in range(8):
                op("pe", lambda e: e.matmul(pt[:, :T], ones[:], q_[:, c, :T], start=(c == 0), stop=(c == 7)), r=[qb_, cb], w=[pb])
            op("act", lambda e: e.activation(out=r_[:, :T], in_=pt[:, :T], func=AF.Sqrt, bias=epsb[:], scale=1.0 / D), r=[pb, cb], w=[rb_])
            op("dve", lambda e: e.reciprocal(r_[:, :T], r_[:, :T]), r=[rb_], w=[rb_])
            for c in range(8):
                t_, tb_ = tm[k3 % 3], tmb[k3 % 3]
                k3 += 1
                op("dve", lambda e: e.tensor_tensor(out=t_[:, :T], in0=x_[:, c, :T], in1=r_[:, :T], op=ALU.mult), r=[xb_, rb_], w=[tb_])
                op("act", lambda e: e.activation(out=U[:, c, t0:t0 + T], in_=t_[:, :T], func=AF.Identity, scale=A[:, l, c, s:s + 1],
                                                 bias=MOD[:, l, shift_idx * 8 + c, s:s + 1]),
                   r=[tb_, modb], w=[ub[ti]])

    def proj_residual(es, l, gate_idx, W, wbuf, KC, SRC, srcb, skip_ctx):
        wsb = P.sb(es, "pr_w", [128, KC, D], BF16)
        wsbb = Buf()
        dma("sp", wsb[:], W.rearrange("(k p) n -> p k n", p=128), r=[wbuf], w=[wsbb])
        gt = [P.sb(es, f"pr_g{i}", [128, KC, 512], BF16) for i in range(2)]
        gtb = [Buf() for _ in range(2)]
        xt = [P.sb(es, f"pr_x{i}", [128, 8, 512], F32) for i in range(2)]
        xtb = [Buf() for _ in range(2)]
        for ti, (t0, T) in enumerate(TILES):
            if skip_ctx and ti == 0:
                continue
            s = 1 if ti == 0 else 0
            g_, gb_ = gt[ti % 2], gtb[ti % 2]
            x_, xb_ = xt[ti % 2], xtb[ti % 2]
            dma("sp", g_[:, :, :T], SRC[:, t0:t0 + T].rearrange("(k p) t -> p k t", p=128), r=srcb, w=[gb_])
            dma("sp", x_[:, :, :T], XT[:, t0:t0 + T].rearrange("(c p) t -> p c t", p=128), r=[xb[ti]], w=[xb_])
            for m in range(8):
                pt, pb = P.psum()
                for k in range(KC):
                    op("pe", lambda e: e.matmul(pt[:, :T], wsb[:, k, m * 128:(m + 1) * 128], g_[:, k, :T], start=(k == 0), stop=(k == KC - 1)),
                       r=[wsbb, gb_], w=[pb])
                op("dve", lambda e: e.scalar_tensor_tensor(out=x_[:, m, :T], in0=pt[:, :T], scalar=MOD[:, l, gate_idx * 8 + m, s:s + 1],
                                                           in1=x_[:, m, :T], op0=ALU.mult, op1=ALU.add), r=[pb, modb, xb_], w=[xb_])
            dma("sp", XT[:, t0:t0 + T].rearrange("(c p) t -> p c t", p=128), x_[:, :, :T], r=[xb_], w=[xb[ti]])

    def ffn_layer(l, skip_ctx):
        with contextlib.ExitStack() as es:
            U = P.sb(es, "U", [128, 8, NT], BF16)
            ub = [Buf(f"U{t}") for t in range(9)]
            with contextlib.ExitStack() as es2, P.scope("ffn_norm"):
                norm_stage(es2, l, AFF, 3, U, ub)
                kb.barrier()
            with contextlib.ExitStack() as es2, P.scope("ffn_up"):
                wt = [P.sb(es2, f"fw{i}", [128, 8, 256], BF16) for i in range(2)]
                wtb = [Buf() for _ in range(2)]
                H = [P.sb(es2, f"fH{i}", [128, 2, NT], F32) for i in range(2)]
                Hb = [Buf() for _ in range(2)]
                Cg = P.sb(es2, "fC", [128, 2, NT], F32)
                Cb = Buf()
                Go = P.sb(es2, "fG", [128, NT], BF16)
                Gob = Buf()
                t_start = 1 if skip_ctx else 0
                segs = ([] if skip_ctx else [(0, CT)]) + [(CT, NT)]
                lo = CT if skip_ctx else 0
                for j in range(22):
                    w_, wb_ = wt[j % 2], wtb[j % 2]
                    h_, hb_ = H[j % 2], Hb[j % 2]
                    dma("sp", w_[:, :, 0:128], wup_b[l][:, j * 128:(j + 1) * 128].rearrange("(k p) n -> p k n", p=128), r=[wupb[l]], w=[wb_])
                    dma("sp", w_[:, :, 128:256], wup_b[l][:, FFN + j * 128:FFN + (j + 1) * 128].rearrange("(k p) n -> p k n", p=128),
                        r=[wupb[l]], w=[wb_])
                    for ti, (t0, T) in enumerate(TILES):
                        if ti < t_start:
                            continue
                        for gv in range(2):
                            pt, pb = P.psum()
                            for k in range(8):
                                op("pe", lambda e: e.matmul(pt[:, :T], w_[:, k, gv * 128:(gv + 1) * 128], U[:, k, t0:t0 + T],
                                                            start=(k == 0), stop=(k == 7)), r=[wb_, ub[ti]], w=[pb])
                            op("dve", lambda e: e.tensor_copy(out=h_[:, gv, t0:t0 + T], in_=pt[:, :T]), r=[pb], w=[hb_])
                            op("act", lambda e: e.activation(out=Cg[:, gv, t0:t0 + T], in_=pt[:, :T], func=AF.Copy, scale=fconv[:, l, gv * 22 + j, 1:2]),
                               r=[pb, cb], w=[Cb])
                    for gv in range(2):
                        ch = gv * 22 + j
                        for (a, b) in segs:
                            op("dve", lambda e: e.scalar_tensor_tensor(out=Cg[:, gv, a + 1:b], in0=h_[:, gv, a:b - 1], scalar=fconv[:, l, ch, 0:1],
                                                                       in1=Cg[:, gv, a + 1:b], op0=ALU.mult, op1=ALU.add), r=[hb_, cb, Cb], w=[Cb])
                            op("dve", lambda e: e.scalar_tensor_tensor(out=Cg[:, gv, a:b - 1], in0=h_[:, gv, a + 1:b], scalar=fconv[:, l, ch, 2:3],
                                                                       in1=Cg[:, gv, a:b - 1], op0=ALU.mult, op1=ALU.add), r=[hb_, cb, Cb], w=[Cb])
                    op("act", lambda e: e.activation(out=Cg[:, 0, lo:NT], in_=Cg[:, 0, lo:NT], func=AF.Silu), r=[Cb], w=[Cb])
                    op("dve", lambda e: e.tensor_tensor(out=Go[:, lo:NT], in0=Cg[:, 0, lo:NT], in1=Cg[:, 1, lo:NT], op=ALU.mult), r=[Cb], w=[Gob])
                    dma("sp", GS[j * 128:(j + 1) * 128, lo:NT], Go[:, lo:NT], r=[Gob], w=[gsb[j]])
                kb.barrier()
        with contextlib.ExitStack() as es, P.scope("ffn_down"):
            proj_residual(es, l, 5, wdn_b[l], wdnb[l], 22, GS, gsb, skip_ctx)
            kb.barrier()

    def proj_fm(W, wbuf, c0, U, ub, dst, dstb, wt, wtb, ev="act"):
        dma("sp", wt[:], W[:, c0:c0 + 128].rearrange("(k p) n -> p k n", p=128), r=[wbuf], w=[wtb])
        for ti, (t0, T) in enumerate(TILES):
            pt, pb = P.psum()
            for k in range(8):
                op("pe", lambda e: e.matmul(pt[:, :T], wt[:, k, :], U[:, k, t0:t0 + T], start=(k == 0), stop=(k == 7)), r=[wtb, ub[ti]], w=[pb])
            if ev == "act":
                op("act", lambda e: e.activation(out=dst[:, t0:t0 + T], in_=pt[:, :T], func=AF.Copy), r=[pb], w=[dstb])
            else:
                op("dve", lambda e: e.tensor_copy(out=dst[:, t0:t0 + T], in_=pt[:, :T]), r=[pb], w=[dstb])

    def rope_store(es_, raw, rawb, cosT, sinT, permT, tb, dstrows, dstbuf, tmp, tmpb, ob, obb):
        op("dve", lambda e: e.tensor_copy(out=ob[:, 0:CT], in_=raw[:, 0:CT]), r=[rawb], w=[obb])
        for ti, (t0, T) in enumerate(TILES):
            if ti == 0:
                continue
            l0 = t0 - CT
            pt, pb = P.psum()
            op("pe", lambda e: e.matmul(pt[:, :T], permT[:], raw[:, t0:t0 + T], start=True, stop=True), r=[rawb, tb], w=[pb])
            op("dve", lambda e: e.tensor_tensor(out=tmp[:, :T], in0=pt[:, :T], in1=sinT[:, l0:l0 + T], op=ALU.mult), r=[pb, tb], w=[tmpb])
            op("dve", lambda e: e.tensor_tensor(out=raw[:, t0:t0 + T], in0=raw[:, t0:t0 + T], in1=cosT[:, l0:l0 + T], op=ALU.mult), r=[rawb, tb, pb], w=[rawb])
            op("dve", lambda e: e.tensor_tensor(out=ob[:, t0:t0 + T], in0=raw[:, t0:t0 + T], in1=tmp[:, :T], op=ALU.add), r=[rawb, tmpb], w=[obb])
        for hh in range(2):
            dma("sp", dstrows[hh], ob[hh * 64:(hh + 1) * 64, :], r=[obb], w=[dstbuf])

    def odd_mixer(l, skip_ctx):
        j = l // 2
        W = odin_b[j]
        wbuf = odinb[j]
        with contextlib.ExitStack() as es:
            U = P.sb(es, "U", [128, 8, NT], BF16)
            ub = [Buf(f"U{t}") for t in range(9)]
            with contextlib.ExitStack() as es2, P.scope("mix_norm"):
                norm_stage(es2, l, AMX, 0, U, ub)
                kb.barrier()
            with contextlib.ExitStack() as es2, P.scope("odd_lru"):
                wt = [P.sb(es2, f"ow{i}", [128, 8, 128], BF16) for i in range(2)]
                wtb = [Buf() for _ in range(2)]
                B1 = P.sb(es2, "B1", [128, NT], F32); b1 = Buf()
                B2 = P.sb(es2, "B2", [128, NT], F32); b2 = Buf()
                B2h = P.sb(es2, "B2h", [128, NT], BF16); b2h = Buf()
                B4 = P.sb(es2, "B4", [128, NT], F32); b4 = Buf()
                B5 = P.sb(es2, "B5", [128, NT], F32); b5 = Buf()
                B6 = P.sb(es2, "B6", [128, NT], F32); b6 = Buf()
                B7 = P.sb(es2, "B7", [128, NT], F32); b7 = Buf()
                lc = P.sb(es2, "lc", [128, 4, 4], F32)
                lcb_ = P.sb(es2, "lcb", [128, 4], F32)
                lba = P.sb(es2, "lba", [128, 2, 4], F32)
                lbx = P.sb(es2, "lbx", [128, 2, 4], F32)
                llam = P.sb(es2, "llam", [128, 2, 4], F32)
                c8 = P.sb(es2, "c8", [128, 2, 4], F32)
                c16 = P.sb(es2, "c16", [128, 2, 4], F32)
                bdf = P.sb(es2, "bdf", [128, 2, 128], F32)
                bda = [P.sb(es2, f"bda{i}", [128, 2, 128], BF16) for i in range(2)]
                bdab = [Buf() for _ in range(2)]
                bdfb = Buf()
                sb0 = Buf()
                dma("sp", lc[:], lru_conv[j], w=[sb0])
                dma("sp", lcb_[:], lru_conv_b[j], w=[sb0])
                dma("sp", lba[:], lru_ba[j], w=[sb0])
                dma("sp", lbx[:], lru_bx[j], w=[sb0])
                dma("sp", llam[:], lru_lam[j], w=[sb0])
                op("act", lambda e: e.activation(out=c8[:], in_=llam[:], func=AF.Exp, scale=-1.0), r=[sb0], w=[sb0])
                op("act", lambda e: e.activation(out=c8[:], in_=c8[:], func=AF.Ln, bias=ones[:, 0:1], scale=1.0), r=[sb0, cb], w=[sb0])
                op("dve", lambda e: e.tensor_scalar(out=c16[:], in0=c8[:], scalar1=-16.0, scalar2=None, op0=ALU.mult), r=[sb0], w=[sb0])
                op("dve", lambda e: e.tensor_scalar(out=c8[:], in0=c8[:], scalar1=-8.0, scalar2=None, op0=ALU.mult), r=[sb0], w=[sb0])
                segs = [(0, CT), (CT, NT)]
                ib = 0
                for c in range(4):
                    proj_fm(W, wbuf, c * 128, U, ub, B1, b1, wt[c % 2], wtb[c % 2])
                    op("act", lambda e: e.activation(out=B2[:], in_=B1[:], func=AF.Identity, scale=lc[:, c, 2:3], bias=lcb_[:, c:c + 1]),
                       r=[b1, sb0], w=[b2])
                    for (a, b) in segs:
                        for tap, off in ((0, -2), (1, -1), (3, 1)):
                            if off < 0:
                                oa, ob_, ia, ib_ = a - off, b, a, b + off
                            else:
                                oa, ob_, ia, ib_ = a, b - off, a + off, b
                            op("dve", lambda e: e.scalar_tensor_tensor(out=B2[:, oa:ob_], in0=B1[:, ia:ib_], scalar=lc[:, c, tap:tap + 1], in1=B2[:, oa:ob_],
                                                                       op0=ALU.mult, op1=ALU.add), r=[b1, sb0, b2], w=[b2])
                    op("act", lambda e: e.activation(out=B2h[:], in_=B2[:], func=AF.Copy), r=[b2], w=[b2h])
                    for d in range(2):
                        bd_, bdb_ = bda[ib % 2], bdab[ib % 2]
                        ib += 1
                        dma("sp", bdf[:, 0, :], lru_wa[j, d, c], w=[bdfb])
                        dma("sp", bdf[:, 1, :], lru_wx[j, d, c], w=[bdfb])
                        op("dve", lambda e: e.tensor_copy(out=bd_[:], in_=bdf[:]), r=[bdfb], w=[bdb_])
                        for ti, (t0, T) in enumerate(TILES):
                            pr, prb = P.psum()
                            op("pe", lambda e: e.matmul(pr[:, :T], bd_[:, 0, :], B2h[:, t0:t0 + T], start=True, stop=True), r=[bdb_, b2h], w=[prb])
                            op("act", lambda e: e.activation(out=B4[:, t0:t0 + T], in_=pr[:, :T], func=AF.Sigmoid, bias=lba[:, d, c:c + 1], scale=1.0),
                               r=[prb, sb0], w=[b4])
                            pi, pib = P.psum()
                            op("pe", lambda e: e.matmul(pi[:, :T], bd_[:, 1, :], B2h[:, t0:t0 + T], start=True, stop=True), r=[bdb_, b2h], w=[pib])
                            op("act", lambda e: e.activation(out=B5[:, t0:t0 + T], in_=pi[:, :T], func=AF.Sigmoid, bias=lbx[:, d, c:c + 1], scale=1.0),
                               r=[pib, sb0], w=[b5])
                        op("act", lambda e: e.activation(out=B1[:], in_=B4[:], func=AF.Exp, scale=c16[:, d, c:c + 1]), r=[b4, sb0], w=[b1])
                        op("act", lambda e: e.activation(out=B4[:], in_=B4[:], func=AF.Exp, scale=c8[:, d, c:c + 1]), r=[b4, sb0], w=[b4])
                        op("act", lambda e: e.activation(out=B1[:], in_=B1[:], func=AF.Sqrt, scale=-1.0, bias=ones[:, 0:1]), r=[b1, cb], w=[b1])
                        op("dve", lambda e: e.tensor_tensor(out=B5[:], in0=B5[:], in1=B2[:], op=ALU.mult), r=[b5, b2], w=[b5])
                        op("dve", lambda e: e.tensor_tensor(out=B5[:], in0=B5[:], in1=B1[:], op=ALU.mult), r=[b5, b1], w=[b5])
                        if d == 0:
                            op("dve", lambda e: e.tensor_tensor_scan(out=B6[:], data0=B4[:], data1=B5[:], initial=0.0, op0=ALU.mult, op1=ALU.add),
                               r=[b4, b5], w=[b6])
                        else:
                            op("dve", lambda e: e.tensor_tensor_scan(out=B7[:, 0:CT][:, ::-1], data0=B4[:, 0:CT][:, ::-1], data1=B5[:, 0:CT][:, ::-1],
                                                                     initial=0.0, op0=ALU.mult, op1=ALU.add), r=[b4, b5], w=[b7])
                            op("dve", lambda e: e.tensor_tensor_scan(out=B7[:, CT:NT][:, ::-1], data0=B4[:, CT:NT][:, ::-1], data1=B5[:, CT:NT][:, ::-1],
                                                                     initial=B7[:, 0:1], op0=ALU.mult, op1=ALU.add), r=[b4, b5, b7], w=[b7])
                            op("dve", lambda e: e.tensor_tensor(out=B6[:], in0=B6[:], in1=B7[:], op=ALU.add), r=[b6, b7], w=[b6])
                    proj_fm(W, wbuf, 512 + c * 128, U, ub, B1, b1, wt[c % 2], wtb[c % 2])
                    op("act", lambda e: e.activation(out=B1[:], in_=B1[:], func=AF.Gelu_apprx_tanh), r=[b1], w=[b1])
                    op("dve", lambda e: e.tensor_tensor(out=B2h[:], in0=B6[:], in1=B1[:], op=ALU.mult), r=[b6, b1, b2h], w=[b2h])
                    dma("sp", YS[c * 128:(c + 1) * 128, :], B2h[:], r=[b2h], w=ysb)
                kb.barrier()
            with contextlib.ExitStack() as es2, P.scope("odd_qk"):
                wt = [P.sb(es2, f"aw{i}", [128, 8, 128], BF16) for i in range(2)]
                wtb = [Buf() for _ in range(2)]
                cosT = P.sb(es2, "cosT", [128, LT], F32)
                sinT = P.sb(es2, "sinT", [128, LT], F32)
                permT = P.sb(es2, "permT", [128, 128], F32)
                tb = Buf()
                dma("sp", cosT[:], cos_in[:, :], w=[tb])
                dma("sp", sinT[:], sin_in[:, :], w=[tb])
                dma("sp", permT[:], perm_in[:, :], w=[tb])
                raw = [P.sb(es2, f"raw{i}", [128, NT], F32) for i in range(2)]
                rawb = [Buf() for _ in range(2)]
                tmp = P.sb(es2, "rtmp", [128, 512], F32); tmpb = Buf()
                ob = [P.sb(es2, f"rob{i}", [128, NT], BF16) for i in range(2)]
                obb = [Buf() for _ in range(2)]
                for c in range(5):
                    proj_fm(W, wbuf, 1024 + c * 128, U, ub, raw[c % 2], rawb[c % 2], wt[c % 2], wtb[c % 2])
                    if c < 4:
                        rows = [QD[2 * c + hh] for hh in range(2)]
                        dbuf_ = qdb[0]
                    else:
                        rows = [KD[hh] for hh in range(2)]
                        dbuf_ = kdb[0]
                    rope_store(es2, raw[c % 2], rawb[c % 2], cosT, sinT, permT, tb, rows, dbuf_, tmp, tmpb, ob[c % 2], obb[c % 2])
                kb.barrier()
            with contextlib.ExitStack() as es2, P.scope("odd_attn"):
                wv = P.sb(es2, "wv", [128, 8, 128], BF16); wvb = Buf()
                dma("sp", wv[:], W[:, 1664:1792].rearrange("(k p) n -> p k n", p=128), r=[wbuf], w=[wvb])
                V = P.sb(es2, "V", [128, 34, 2, 65], BF16); vb = Buf()
                op("pool", lambda e: e.memset(V[:], 1.0), w=[vb])
                for blk in range(34):
                    ti = 0 if blk < 2 else 1 + (blk - 2) // 4
                    pt, pb = P.psum()
                    for k in range(8):
                        op("pe", lambda e: e.matmul(pt[:, 0:128], U[:, k, blk * 128:(blk + 1) * 128], wv[:, k, :], start=(k == 0), stop=(k == 7)),
                           r=[wvb, ub[ti]], w=[pb])
                    op("act", lambda e: e.activation(out=V[:, blk, :, 0:64], in_=pt[:, 0:128].rearrange("p (a b) -> p a b", a=2), func=AF.Copy),
                       r=[pb], w=[vb])
                kT = P.sb(es2, "kT", [64, 2, NT], BF16); ktb = Buf()
                dma("sp", kT[:], KD.rearrange("h d t -> d h t"), r=kdb, w=[ktb])
                qT = [P.sb(es2, f"qT{i}", [64, 4, NT], BF16) for i in range(2)]
                qtb = [Buf() for _ in range(2)]
                idb16 = P.sb(es2, "idb16", [128, 128], BF16)
                mk = P.sb(es2, "mk", [128, 2, 512], BF16)
                mkf = P.sb(es2, "mkf", [128, 2, 512], F32)
                esk = P.sb(es2, "esk", [128, 8], F32)
                mb = Buf()
                dma("sp", mkf[:, 0, :], mprev_in[:, :], w=[mb])
                dma("sp", mkf[:, 1, :], mnext_in[:, :], w=[mb])
                dma("sp", esk[:], swa_sink[j], w=[mb])
                op("dve", lambda e: e.tensor_copy(out=mk[:], in_=mkf[:]), r=[mb], w=[mb])
                op("dve", lambda e: e.tensor_copy(out=idb16[:], in_=ident[:]), r=[cb], w=[mb])
                op("act", lambda e: e.activation(out=esk[:], in_=esk[:], func=AF.Exp), r=[mb], w=[mb])
                PT = [P.sb(es2, f"PT{i}", [128, 512], BF16) for i in range(10)]
                ptb = [Buf() for _ in range(10)]
                od = [P.sb(es2, f"od{i}", [128, 512], F32) for i in range(2)]
                odb = [Buf() for _ in range(2)]
                rd = [P.sb(es2, f"rd{i}", [128, 8], F32) for i in range(2)]
                rdb = [Buf() for _ in range(2)]
                yt = [P.sb(es2, f"yt{i}", [128, 4, 128], BF16) for i in range(2)]
                ytb = [Buf() for _ in range(2)]
                for kv in range(2):
                    dma("sp", qT[kv][:], QD[kv * 4:(kv + 1) * 4].rearrange("h d t -> d h t"), r=qdb, w=[qtb[kv]])
                ipt = 0
                qblocks = list(range(2, 34)) if skip_ctx else list(range(34))
                for qi, qb_ in enumerate(qblocks):
                    o_, ob_ = od[qi % 2], odb[qi % 2]
                    r_, rb_ = rd[qi % 2], rdb[qi % 2]
                    if qb_ < 2:
                        keys = [(0, None), (1, None)]
                    else:
                        keys = [(0, None), (1, None)]
                        if qb_ - 1 >= 2:
                            keys.append((qb_ - 1, 0))
                        keys.append((qb_, None))
                        if qb_ + 1 < 34:
                            keys.append((qb_ + 1, 1))
                    for kv in range(2):
                        pts = []
                        for (kblk, mtype) in keys:
                            ps_, psb_ = P.psum()
                            if mtype is not None:
                                op("pe", lambda e: e.matmul(ps_[:], idb16[:], mk[:, mtype, :], start=True, stop=False), r=[mb], w=[psb_])
                            op("pe", lambda e: e.matmul(ps_[:].rearrange("p (g q) -> p g q", g=4), kT[:, kv, kblk * 128:(kblk + 1) * 128],
                                                        qT[kv][:, :, qb_ * 128:(qb_ + 1) * 128], start=(mtype is None), stop=True),
                               r=[ktb, qtb[kv]], w=[psb_])
                            p_, pb_ = PT[ipt % 10], ptb[ipt % 10]
                            ipt += 1
                            op("act", lambda e: e.activation(out=p_[:], in_=ps_[:], func=AF.Exp, scale=0.125), r=[psb_], w=[pb_])
                            pts.append((p_, pb_, kblk))
                        po, pob = P.psum()
                        for g in range(4):
                            for ki, (p_, pb_, kblk) in enumerate(pts):
                                op("pe", lambda e: e.matmul(po[:, g * 65:(g + 1) * 65], p_[:, g * 128:(g + 1) * 128], V[:, kblk, kv, :],
                                                            start=(ki == 0), stop=(ki == len(pts) - 1)), r=[pb_, vb], w=[pob])
                        pov = po[:, 0:260].rearrange("p (g e) -> p g e", g=4)
                        op("dve", lambda e: e.tensor_tensor(out=r_[:, kv * 4:(kv + 1) * 4], in0=pov[:, :, 64], in1=esk[:, kv * 4:(kv + 1) * 4], op=ALU.add),
                           r=[pob, mb], w=[rb_])
                        op("dve", lambda e: e.reciprocal(r_[:, kv * 4:(kv + 1) * 4], r_[:, kv * 4:(kv + 1) * 4]), r=[rb_], w=[rb_])
                        for g in range(4):
                            h = kv * 4 + g
                            op("act" if g % 2 else "dve",
                               (lambda e: e.activation(out=o_[:, h * 64:(h + 1) * 64], in_=po[:, g * 65:g * 65 + 64], func=AF.Copy, scale=r_[:, h:h + 1]))
                               if g % 2 else
                               (lambda e: e.tensor_scalar(out=o_[:, h * 64:(h + 1) * 64], in0=po[:, g * 65:g * 65 + 64], scalar1=r_[:, h:h + 1], scalar2=None,
                                                          op0=ALU.mult)),
                               r=[pob, rb_], w=[ob_])
                    y_, yb_ = yt[qi % 2], ytb[qi % 2]
                    pt, pb = P.psum()
                    for c in range(4):
                        op("pe", lambda e: e.transpose(pt[:, c * 128:(c + 1) * 128], o_[:, c * 128:(c + 1) * 128], ident[:]), r=[ob_, cb], w=[pb])
                    op("act", lambda e: e.activation(out=y_[:], in_=pt[:].rearrange("p (c t) -> p c t", c=4), func=AF.Copy), r=[pb], w=[yb_])
                    dma("sp", YS[512:1024, qb_ * 128:(qb_ + 1) * 128].rearrange("(c p) t -> p c t", p=128), y_[:], r=[yb_], w=ysb)
                kb.barrier()
        with contextlib.ExitStack() as es, P.scope("odd_out"):
            proj_residual(es, l, 2, odout_b[j], odoutb[j], 8, YS, ysb, skip_ctx)
            kb.barrier()

    def proj_tm(W, wbuf, c0, ncol, U, ub, wt, wtb, consume):
        dma("sp", wt[:, :, 0:ncol], W[:, c0:c0 + ncol].rearrange("(k p) n -> p k n", p=128), r=[wbuf], w=[wtb])
        for blk in range(34):
            ti = 0 if blk < 2 else 1 + (blk - 2) // 4
            pt, pb = P.psum()
            for k in range(8):
                op("pe", lambda e: e.matmul(pt[:, 0:ncol], U[:, k, blk * 128:(blk + 1) * 128], wt[:, k, 0:ncol], start=(k == 0), stop=(k == 7)),
                   r=[wtb, ub[ti]], w=[pb])
            consume(blk, pt, pb)

    def even_mixer(l, skip_ctx):
        j = l // 2
        lam_init = 0.8 - 0.6 * math.exp(-0.3 * l)
        W = evin_b[j]
        wbuf = evinb[j]
        with contextlib.ExitStack() as es:
            U = P.sb(es, "U", [128, 8, NT], BF16)
            ub = [Buf(f"U{t}") for t in range(9)]
            with contextlib.ExitStack() as es2, P.scope("mix_norm"):
                norm_stage(es2, l, AMX, 0, U, ub)
                kb.barrier()
            with contextlib.ExitStack() as es2, P.scope("ev_qk"):
                wt = [P.sb(es2, f"aw{i}", [128, 8, 128], BF16) for i in range(2)]
                wtb = [Buf() for _ in range(2)]
                cosT = P.sb(es2, "cosT", [128, LT], F32)
                sinT = P.sb(es2, "sinT", [128, LT], F32)
                permT = P.sb(es2, "permT", [128, 128], F32)
                tb = Buf()
                dma("sp", cosT[:], cos_in[:, :], w=[tb])
                dma("sp", sinT[:], sin_in[:, :], w=[tb])
                dma("sp", permT[:], perm_in[:, :], w=[tb])
                raw = [P.sb(es2, f"raw{i}", [128, NT], F32) for i in range(2)]
                rawb = [Buf() for _ in range(2)]
                tmp = P.sb(es2, "rtmp", [128, 512], F32); tmpb = Buf()
                ob = [P.sb(es2, f"rob{i}", [128, NT], BF16) for i in range(2)]
                obb = [Buf() for _ in range(2)]
                for c in range(8):
                    proj_fm(W, wbuf, c * 128, U, ub, raw[c % 2], rawb[c % 2], wt[c % 2], wtb[c % 2])
                    if c < 4:
                        rows = [QA[2 * c + hh] for hh in range(2)]
                        dbuf_ = qab[0]
                    else:
                        rows = [KA[2 * (c - 4) + hh] for hh in range(2)]
                        dbuf_ = kab[0]
                    rope_store(es2, raw[c % 2], rawb[c % 2], cosT, sinT, permT, tb, rows, dbuf_, tmp, tmpb, ob[c % 2], obb[c % 2])
                kb.barrier()
            with contextlib.ExitStack() as es2, P.scope("ev_tm"):
                wt = [P.sb(es2, f"tw{i}", [128, 8, 512], BF16) for i in range(2)]
                wtb = [Buf() for _ in range(2)]
                vo = [P.sb(es2, f"vo{i}", [128, 512], BF16) for i in range(2)]
                vob = [Buf() for _ in range(2)]
                go = [P.sb(es2, f"go{i}", [128, 512], F32) for i in range(2)]
                gob = [Buf() for _ in range(2)]
                bgo = [P.sb(es2, f"bgo{i}", [128, 16], F32) for i in range(2)]
                bgob = [Buf() for _ in range(2)]
                tq = [P.sb(es2, f"tq{i}", [128, 8], F32) for i in range(4)]
                tqb = Buf()
                alog = P.sb(es2, "alog", [128, 8], F32)
                dtb = P.sb(es2, "dtb", [128, 8], F32)
                cb2 = Buf()
                dma("sp", alog[:], gdn_a_log[j], w=[cb2])
                dma("sp", dtb[:], gdn_dt_bias[j], w=[cb2])
                op("act", lambda e: e.activation(out=alog[:], in_=alog[:], func=AF.Exp), r=[cb2], w=[cb2])
                op("dve", lambda e: e.tensor_scalar(out=alog[:], in0=alog[:], scalar1=-1.0, scalar2=None, op0=ALU.mult), r=[cb2], w=[cb2])

                def c_va(blk, pt, pb):
                    o_, ob_ = vo[blk % 2], vob[blk % 2]
                    op("act", lambda e: e.activation(out=o_[:], in_=pt[:], func=AF.Copy), r=[pb], w=[ob_])
                    dma("sp", VA[blk * 128:(blk + 1) * 128, :], o_[:], r=[ob_], w=vab)

                def c_gate(blk, pt, pb):
                    o_, ob_ = go[blk % 2], gob[blk % 2]
                    op("act", lambda e: e.activation(out=o_[:], in_=pt[:], func=AF.Silu), r=[pb], w=[ob_])
                    dma("sp", GG[blk * 128:(blk + 1) * 128, :], o_[:], r=[ob_], w=ggb)

                def c_bg(blk, pt, pb):
                    o_, ob_ = bgo[blk % 2], bgob[blk % 2]
                    x_, ax_, l_, mx_ = tq
                    op("act", lambda e: e.activation(out=o_[:, 0:8], in_=pt[:, 0:8], func=AF.Sigmoid), r=[pb], w=[ob_])
                    op("dve", lambda e: e.tensor_tensor(out=x_[:], in0=pt[:, 8:16], in1=dtb[:], op=ALU.add), r=[pb, cb2, tqb], w=[tqb])
                    op("act", lambda e: e.activation(out=ax_[:], in_=x_[:], func=AF.Abs), r=[tqb], w=[tqb])
                    op("act", lambda e: e.activation(out=l_[:], in_=ax_[:], func=AF.Exp, scale=-1.0), r=[tqb], w=[tqb])
                    op("act", lambda e: e.activation(out=l_[:], in_=l_[:], func=AF.Ln, bias=ones[:, 0:1], scale=1.0), r=[tqb, cb], w=[tqb])
                    op("dve", lambda e: e.tensor_scalar(out=mx_[:], in0=x_[:], scalar1=0.0, scalar2=None, op0=ALU.max), r=[tqb], w=[tqb])
                    op("dve", lambda e: e.tensor_tensor(out=mx_[:], in0=mx_[:], in1=l_[:], op=ALU.add), r=[tqb], w=[tqb])
                    op("dve", lambda e: e.tensor_tensor(out=o_[:, 8:16], in0=mx_[:], in1=alog[:], op=ALU.mult), r=[tqb, cb2, ob_], w=[ob_])
                    dma("sp", BG[blk * 128:(blk + 1) * 128, :], o_[:], r=[ob_], w=bgb)

                proj_tm(W, wbuf, 1024, 512, U, ub, wt[0], wtb[0], c_va)
                proj_tm(W, wbuf, 3072, 512, U, ub, wt[1], wtb[1], c_gate)
                proj_tm(W, wbuf, 3584, 16, U, ub, wt[0], wtb[0], c_bg)
                kb.barrier()
            with contextlib.ExitStack() as es2, P.scope("ev_gqkv"):
                wt = [P.sb(es2, f"gw{i}", [128, 8, 128], BF16) for i in range(2)]
                wtb = [Buf() for _ in range(2)]
                R1 = [P.sb(es2, f"gR{i}", [128, NT], F32) for i in range(2)]
                r1b = [Buf() for _ in range(2)]
                C1 = [P.sb(es2, f"gC{i}", [128, NT], F32) for i in range(2)]
                c1b = [Buf() for _ in range(2)]
                S1 = P.sb(es2, "gS", [128, NT], F32); s1b = Buf()
                CB = [P.sb(es2, f"gCB{i}", [128, NT], BF16) for i in range(2)]
                cbb = [Buf() for _ in range(2)]
                rsq = [P.sb(es2, f"grs{i}", [128, 512], F32) for i in range(2)]
                rsqb = [Buf() for _ in range(2)]
                gc = P.sb(es2, "gc", [128, 12, 4], F32); gcb = Buf()
                dma("sp", gc[:], gdn_conv[j], w=[gcb])
                segs = [(0, CT), (CT, NT)]
                it = 0
                for kind in range(3):
                    for h in range(4):
                        cidx = kind * 4 + h
                        r_, rb_ = R1[it % 2], r1b[it % 2]
                        c_, cb_ = C1[it % 2], c1b[it % 2]
                        o16, o16b = CB[it % 2], cbb[it % 2]
                        proj_fm(W, wbuf, 1536 + cidx * 128, U, ub, r_, rb_, wt[it % 2], wtb[it % 2])
                        it += 1
                        op("act", lambda e: e.activation(out=c_[:], in_=r_[:], func=AF.Copy, scale=gc[:, cidx, 2:3]), r=[rb_, gcb], w=[cb_])
                        for (a, b) in segs:
                            for tap, off in ((0, -2), (1, -1), (3, 1)):
                                if off < 0:
                                    oa, ob_, ia, ib_ = a - off, b, a, b + off
                                else:
                                    oa, ob_, ia, ib_ = a, b - off, a + off, b
                                op("dve", lambda e: e.scalar_tensor_tensor(out=c_[:, oa:ob_], in0=r_[:, ia:ib_], scalar=gc[:, cidx, tap:tap + 1], in1=c_[:, oa:ob_],
                                                                           op0=ALU.mult, op1=ALU.add), r=[rb_, gcb, cb_], w=[cb_])
                        op("act", lambda e: e.activation(out=c_[:], in_=c_[:], func=AF.Silu), r=[cb_], w=[cb_])
                        if kind < 2:
                            op("act", lambda e: e.activation(out=S1[:], in_=c_[:], func=AF.Square), r=[cb_], w=[s1b])
                            for ti, (t0, T) in enumerate(TILES):
                                pt, pb = P.psum()
                                op("pe", lambda e: e.matmul(pt[:, :T], ones[:], S1[:, t0:t0 + T], start=True, stop=True), r=[s1b, cb], w=[pb])
                                q_, qb_ = rsq[ti % 2], rsqb[ti % 2]
                                op("act", lambda e: e.activation(out=q_[:, :T], in_=pt[:, :T], func=AF.Sqrt, bias=epsb[:], scale=1.0), r=[pb, cb], w=[qb_])
                                op("dve", lambda e: e.reciprocal(q_[:, :T], q_[:, :T]), r=[qb_], w=[qb_])
                                if kind == 0:
                                    op("dve", lambda e: e.scalar_tensor_tensor(out=o16[:, t0:t0 + T], in0=c_[:, t0:t0 + T], scalar=128.0 ** -0.5, in1=q_[:, :T],
                                                                               op0=ALU.mult, op1=ALU.mult), r=[cb_, qb_], w=[o16b])
                                else:
                                    op("dve", lambda e: e.tensor_tensor(out=c_[:, t0:t0 + T], in0=c_[:, t0:t0 + T], in1=q_[:, :T], op=ALU.mult), r=[cb_, qb_], w=[cb_])
                                    op("act", lambda e: e.activation(out=o16[:, t0:t0 + T], in_=c_[:, t0:t0 + T], func=AF.Copy), r=[cb_], w=[o16b])
                        else:
                            op("dve", lambda e: e.tensor_copy(out=o16[:], in_=c_[:]), r=[cb_], w=[o16b])
                        dma("sp", GQ[kind, h], o16[:], r=[o16b], w=gqb)
                        if kind == 1:
                            dma("sp", GQK[h], c_[:], r=[cb_], w=gqb)
                kb.barrier()
        with contextlib.ExitStack() as es, P.scope("ev_attn"):
            P.psr = (4, 8)
            acc = [(P.ps[i], P.psb[i]) for i in range(4)]
            kT = P.sb(es, "kT", [64, 2, NT], BF16); ktb = Buf()
            qT = P.sb(es, "qT", [64, 2, NT], BF16); qtb = Buf()
            V = P.sb(es, "V", [128, 34, 128], BF16); vb = Buf()
            PT = [P.sb(es, f"PT{i}", [128, 512], BF16) for i in range(6)]
            ptb = [Buf() for _ in range(6)]
            onesb = P.sb(es, "onesb", [128, 128], BF16)
            rden = P.sb(es, "rden", [128, 512], F32); rdb = Buf()
            t0b_ = P.sb(es, "t0b", [128, 512], F32); t0bb = Buf()
            oa = P.sb(es, "oa", [128, 512], F32); oab = Buf()
            sqb_ = P.sb(es, "sqb", [128, 512], F32); sqbb = Buf()
            rsd = P.sb(es, "rsd", [128, 512], F32); rsdb = Buf()
            yT = [P.sb(es, f"yT{i}", [128, 512], BF16) for i in range(2)]
            ytb = [Buf() for _ in range(2)]
            lv = P.sb(es, "lv", [128, 256], F32)
            lp = P.sb(es, "lp", [128, 2, 64], F32)
            lsum = P.sb(es, "lsum", [128, 2], F32)
            neglam = P.sb(es, "neglam", [128, 1], F32)
            sg = P.sb(es, "sg", [128, 1], F32)
            eps128 = P.sb(es, "eps128", [128, 1], F32)
            lb = Buf()
            dma("sp", lv[:], diff_lambda[j], w=[lb])
            dma("sp", sg[:], diff_subln[j], w=[lb])
            op("dve", lambda e: e.tensor_copy(out=onesb[:], in_=ones[:]), r=[cb], w=[lb])
            op("dve", lambda e: e.tensor_tensor(out=lp[:, 0, :], in0=lv[:, 0:64], in1=lv[:, 64:128], op=ALU.mult), r=[lb], w=[lb])
            op("dve", lambda e: e.tensor_tensor(out=lp[:, 1, :], in0=lv[:, 128:192], in1=lv[:, 192:256], op=ALU.mult), r=[lb], w=[lb])
            op("dve", lambda e: e.tensor_reduce(out=lsum[:], in_=lp[:], axis=AX.X, op=ALU.add), r=[lb], w=[lb])
            op("act", lambda e: e.activation(out=lsum[:], in_=lsum[:], func=AF.Exp), r=[lb], w=[lb])
            op("dve", lambda e: e.tensor_tensor(out=neglam[:], in0=lsum[:, 1:2], in1=lsum[:, 0:1], op=ALU.subtract), r=[lb], w=[lb])
            op("dve", lambda e: e.tensor_scalar(out=neglam[:], in0=neglam[:], scalar1=-lam_init, scalar2=None, op0=ALU.add), r=[lb], w=[lb])
            op("dve", lambda e: e.tensor_scalar(out=sg[:], in0=sg[:], scalar1=(1.0 - lam_init), scalar2=None, op0=ALU.mult), r=[lb], w=[lb])
            op("dve", lambda e: e.memset(eps128[:], EPS), w=[lb])
            ipt = 0
            iy = 0
            for h in range(4):
                dma("sp", kT[:], KA[2 * h:2 * h + 2].rearrange("m d t -> d m t"), r=kab, w=[ktb])
                dma("sp", qT[:], QA[2 * h:2 * h + 2].rearrange("m d t -> d m t"), r=qab, w=[qtb])
                dma("sp", V[:], VA[:, h * 128:(h + 1) * 128].rearrange("(b p) e -> p b e", p=128), r=vab, w=[vb])
                for ti, (t0, T) in enumerate(TILES):
                    if ti == 0 and skip_ctx:
                        continue
                    keys = [0, 1] if ti == 0 else list(range(34))
                    for m in range(2):
                        A_, Ab_ = acc[2 * m]
                        B_, Bb_ = acc[2 * m + 1]
                        LA = 2
                        nk = len(keys)
                        pend = []
                        for ki in range(nk + LA):
                            if ki < nk:
                                kblk = keys[ki]
                                s_, sb_ = P.psum()
                                op("pe", lambda e: e.matmul(s_[:, :T], kT[:, m, kblk * 128:(kblk + 1) * 128], qT[:, m, t0:t0 + T], start=True, stop=True),
                                   r=[ktb, qtb], w=[sb_])
                                p_, pb_ = PT[ipt % 6], ptb[ipt % 6]
                                ipt += 1
                                op("act", lambda e: e.activation(out=p_[:, :T], in_=s_[:, :T], func=AF.Exp, scale=0.125), r=[sb_], w=[pb_])
                                pend.append((p_, pb_, kblk))
                            if ki >= LA:
                                kj = ki - LA
                                p2, pb2, kb2 = pend[kj]
                                op("pe", lambda e: e.matmul(A_[:, :T], V[:, kb2, :], p2[:, :T], start=(kj == 0), stop=(kj == nk - 1)), r=[pb2, vb], w=[Ab_])
                                op("pe", lambda e: e.matmul(B_[:, :T], onesb[:], p2[:, :T], start=(kj == 0), stop=(kj == nk - 1)), r=[pb2, lb], w=[Bb_])
                        op("dve", lambda e: e.reciprocal(rden[:, :T], B_[:, :T]), r=[Bb_], w=[rdb])
                        if m == 0:
                            op("dve", lambda e: e.tensor_tensor(out=t0b_[:, :T], in0=A_[:, :T], in1=rden[:, :T], op=ALU.mult), r=[Ab_, rdb], w=[t0bb])
                        else:
                            op("dve", lambda e: e.tensor_tensor(out=oa[:, :T], in0=A_[:, :T], in1=rden[:, :T], op=ALU.mult), r=[Ab_, rdb], w=[oab])
                            op("dve", lambda e: e.scalar_tensor_tensor(out=oa[:, :T], in0=oa[:, :T], scalar=neglam[:, 0:1], in1=t0b_[:, :T], op0=ALU.mult, op1=ALU.add),
                               r=[oab, lb, t0bb], w=[oab])
                    y_, yb_ = yT[iy % 2], ytb[iy % 2]
                    iy += 1
                    op("act", lambda e: e.activation(out=sqb_[:, :T], in_=oa[:, :T], func=AF.Square), r=[oab], w=[sqbb])
                    pt, pb = P.psum()
                    op("pe", lambda e: e.matmul(pt[:, :T], ones[:], sqb_[:, :T], start=True, stop=True), r=[sqbb, cb], w=[pb])
                    op("act", lambda e: e.activation(out=rsd[:, :T], in_=pt[:, :T], func=AF.Sqrt, bias=eps128[:], scale=1.0 / 128), r=[pb, lb], w=[rsdb])
                    op("dve", lambda e: e.reciprocal(rsd[:, :T], rsd[:, :T]), r=[rsdb], w=[rsdb])
                    op("dve", lambda e: e.scalar_tensor_tensor(out=y_[:, :T], in0=oa[:, :T], scalar=sg[:, 0:1], in1=rsd[:, :T], op0=ALU.mult, op1=ALU.mult),
                       r=[oab, lb, rsdb], w=[yb_])
                    dma("sp", YS[h * 128:(h + 1) * 128, t0:t0 + T], y_[:, :T], r=[yb_], w=ysb)
            P.psr = (0, 8)
            kb.barrier()
        with contextlib.ExitStack() as es, P.scope("ev_gdn"):
            gdn_core(es, l, j, skip_ctx)
            kb.barrier()
        with contextlib.ExitStack() as es, P.scope("ev_out"):
            proj_residual(es, l, 2, evout_b[j], evoutb[j], 8, YS, ysb, skip_ctx)
            kb.barrier()

    def gdn_core(es0, l, j, skip_ctx):
        NB = NT // 128
        for hg in range(2):
          chains = [(h, d) for h in (2 * hg, 2 * hg + 1) for d in range(2)]
          NCH = len(chains)
          with contextlib.ExitStack() as es:
              gm = P.sb(es, "gmask", [128, 2, 3, 128], F32); gmb = Buf()
              gaux = P.sb(es, "gaux", [128, 258], F32)
              dma("sp", gm[:], gmask_in[:, :, :, :], w=[gmb])
              dma("sp", gaux[:], gaux_in[:, :], w=[gmb])
              idb = P.sb(es, "idb", [128, 128], BF16)
              op("dve", lambda e: e.tensor_copy(out=idb[:], in_=ident[:]), r=[cb], w=[gmb])
              qkv = P.sb(es, "qkv", [128, 3, 2, NT], BF16); qkvb = Buf()
              kF = P.sb(es, "kF", [128, 2, NT], F32)
              for kind in range(3):
                  dma("sp", qkv[:, kind, :, :], GQ[kind, 2 * hg:2 * hg + 2].rearrange("h p t -> p h t"), r=gqb, w=[qkvb])
              dma("sp", kF[:], GQK[2 * hg:2 * hg + 2].rearrange("h p t -> p h t"), r=gqb, w=[qkvb])
              BGt = P.sb(es, "BGt", [128, NB, 16], F32); bgtb = Buf()
              dma("sp", BGt[:], BG.rearrange("(n p) f -> p n f", p=128), r=bgb, w=[bgtb])
              NBt = P.sb(es, "NBt", [128, NB, 8], F32)
              op("dve", lambda e: e.tensor_scalar(out=NBt[:], in0=BGt[:, :, 0:8], scalar1=-1.0, scalar2=None, op0=ALU.mult), r=[bgtb], w=[bgtb])

              def ch(name, shape, dt):
                  return [P.sb(es, f"{name}{c}", shape, dt) for c in range(NCH)]
              sc = ch("sc", [128, 8], F32)
              g2 = ch("g2", [128, 2, 128], F32)
              dT = ch("dT", [128, 128], F32)
              Nb = ch("Nb", [128, 2, 128], F32)
              Xf = ch("Xf", [128, 128], F32)
              Xb = ch("Xb", [128, 128], BF16)
              Mb = ch("Mb", [128, 2, 2, 128], F32)
              AT = ch("AT", [128, 128], BF16)
              ubf = ch("ubf", [128, 128], F32)
              kg = ch("kg", [128, 3, 128], BF16)
              vtok = ch("vtok", [128, 128], BF16)
              wT = ch("wT", [128, 128], BF16)
              vn = ch("vn", [128, 128], BF16)
              t1 = ch("t1", [128, 128], F32)
              ot = ch("ot", [128, 128], F32)
              S = ch("S", [128, 128], F32)
              Sb = ch("Sb", [128, 128], BF16)
              cbuf = [Buf(f"chain{c}") for c in range(NCH)]
              sbuf_ = [Buf(f"S{c}") for c in range(NCH)]
              vnb = [Buf(f"vn{c}") for c in range(NCH)]
              otb = [Buf(f"ot{c}") for c in range(NCH)]
              for c in range(NCH):
                  op("dve", lambda e: e.memset(S[c][:], 0.0), w=[sbuf_[c]])
                  op("dve", lambda e: e.memset(Sb[c][:], 0.0), w=[sbuf_[c]])
                  op("dve", lambda e: e.memset(vn[c][:], 0.0), w=[vnb[c]])
              order = [list(range(NB)), [1, 0] + list(range(NB - 1, 1, -1))]

              class Reg:
                  def __init__(self):
                      self.k = 4
                  def get(self):
                      if self.k == 4:
                          self.bank = P.psum()
                          self.k = 0
                      r = (self.bank[0][:, self.k * 128:(self.k + 1) * 128], self.bank[1], self.bank[0], self.k * 128)
                      self.k += 1
                      return r
              rg = Reg()

              for i in range(NB):
                  blk = [order[d][i] for (h, d) in chains]
                  pa, pab = P.psum()
                  for c, (h, d) in enumerate(chains):
                      n = blk[c]
                      gcol = BGt[:, n, 8 + d * 4 + h:8 + d * 4 + h + 1]
                      op("pe", lambda e: e.matmul(pa[:, c * 4:c * 4 + 1], gm[:, d, 0, :], gcol, start=True, stop=True), r=[gmb, bgtb], w=[pab])
                      op("pe", lambda e: e.matmul(pa[:, c * 4 + 1:c * 4 + 2], gm[:, d, 1, :], gcol, start=True, stop=True), r=[gmb, bgtb], w=[pab])
                      op("pe", lambda e: e.matmul(pa[:, c * 4 + 2:c * 4 + 3], gaux[:, 0:128], gcol, start=True, stop=True), r=[gmb, bgtb], w=[pab])
                      op("pe", lambda e: e.matmul(pa[:, c * 4 + 3:c * 4 + 4], gaux[:, 128:256], gcol, start=True, stop=True), r=[gmb, bgtb], w=[pab])
                  for c in range(NCH):
                      op("act", lambda e: e.activation(out=sc[c][:, 0:4], in_=pa[:, c * 4:c * 4 + 4], func=AF.Exp), r=[pab], w=[cbuf[c]])
                      op("dve", lambda e: e.tensor_scalar(out=sc[c][:, 4:6], in0=gaux[:, 256:258], scalar1=sc[c][:, 1:2], scalar2=None, op0=ALU.mult),
                         r=[gmb, cbuf[c]], w=[cbuf[c]])
                  for c, (h, d) in enumerate(chains):
                      n = blk[c]
                      gcol = BGt[:, n, 8 + d * 4 + h:8 + d * 4 + h + 1]
                      op("dve", lambda e: e.tensor_scalar(out=g2[c][:, 0, :], in0=ones[:], scalar1=gcol, scalar2=None, op0=ALU.mult), r=[bgtb, cb, cbuf[c]], w=[cbuf[c]])
                      op("dve", lambda e: e.tensor_scalar(out=g2[c][:, 1, :], in0=gm[:, d, 0, :], scalar1=gcol, scalar2=-1.0, op0=ALU.mult, op1=ALU.mult),
                         r=[bgtb, gmb, cbuf[c]], w=[cbuf[c]])
                  regD = []
                  for c, (h, d) in enumerate(chains):
                      rv, rb_, _, _ = rg.get()
                      regD.append((rv, rb_))
                      op("pe", lambda e: e.matmul(rv, g2[c][:, 0, :], gm[:, d, 0, :], start=True, stop=False), r=[cbuf[c], gmb], w=[rb_])
                      op("pe", lambda e: e.matmul(rv, g2[c][:, 1, :], ones[:], start=False, stop=False), r=[cbuf[c], cb], w=[rb_])
                      op("pe", lambda e: e.matmul(rv, ident[:], gm[:, d, 2, :], start=False, stop=True), r=[gmb, cb], w=[rb_])
                  for c in range(NCH):
                      rv, rb_ = regD[c]
                      op("act", lambda e: e.activation(out=dT[c][:], in_=rv, func=AF.Exp), r=[rb_, cbuf[c]], w=[cbuf[c]])
                  regF = []
                  for c, (h, d) in enumerate(chains):
                      n = blk[c]
                      kT_ = qkv[:, 1, h % 2, n * 128:(n + 1) * 128]
                      kF_ = kF[:, h % 2, n * 128:(n + 1) * 128]
                      qT_ = qkv[:, 0, h % 2, n * 128:(n + 1) * 128]
                      vT_ = qkv[:, 2, h % 2, n * 128:(n + 1) * 128]
                      bank = P.psum()
                      pt, pb = bank
                      regF.append(bank)
                      op("pe", lambda e: e.matmul(pt[:, 0:128], kF_, kF_, start=True, stop=True), r=[qkvb], w=[pb])
                      op("pe", lambda e: e.matmul(pt[:, 128:256], kT_, qT_, start=True, stop=True), r=[qkvb], w=[pb])
                      op("pe", lambda e: e.matmul(pt[:, 256:384], kT_, idb[:], start=True, stop=True), r=[qkvb, gmb], w=[pb])
                      op("pe", lambda e: e.matmul(pt[:, 384:512], vT_, idb[:], start=True, stop=True), r=[qkvb, gmb], w=[pb])
                      bcol = BGt[:, n, d * 4 + h:d * 4 + h + 1]
                      nbcol = NBt[:, n, d * 4 + h:d * 4 + h + 1]
                      op("dve", lambda e: e.tensor_tensor(out=AT[c][:], in0=pt[:, 128:256], in1=dT[c][:], op=ALU.mult), r=[pb, cbuf[c]], w=[cbuf[c]])
                      op("dve", lambda e: e.tensor_tensor(out=dT[c][:], in0=dT[c][:], in1=ident[:], op=ALU.subtract), r=[cbuf[c], cb], w=[cbuf[c]])
                      op("dve", lambda e: e.scalar_tensor_tensor(out=g2[c][:, 0, :], in0=pt[:, 0:128], scalar=nbcol, in1=dT[c][:], op0=ALU.mult, op1=ALU.mult),
                         r=[pb, bgtb, cbuf[c]], w=[cbuf[c]])
                      op("act", lambda e: e.activation(out=Nb[c][:, 0, :], in_=g2[c][:, 0, :], func=AF.Copy, scale=-1.0), r=[cbuf[c]], w=[cbuf[c]])
                      op("dve", lambda e: e.tensor_tensor(out=Xf[c][:], in0=g2[c][:, 0, :], in1=ident[:], op=ALU.add), r=[cbuf[c], cb], w=[cbuf[c]])
                      op("act", lambda e: e.activation(out=kg[c][:, 0, :], in_=pt[:, 256:384], func=AF.Copy, scale=sc[c][:, 0:1]), r=[pb, cbuf[c]], w=[cbuf[c]])
                      op("act", lambda e: e.activation(out=kg[c][:, 1, :], in_=pt[:, 256:384], func=AF.Copy, scale=sc[c][:, 4:5]), r=[pb, cbuf[c]], w=[cbuf[c]])
                      op("act", lambda e: e.activation(out=kg[c][:, 2, :], in_=pt[:, 256:384], func=AF.Copy, scale=sc[c][:, 5:6]), r=[pb, cbuf[c]], w=[cbuf[c]])
                      op("dve", lambda e: e.tensor_copy(out=vtok[c][:], in_=pt[:, 384:512]), r=[pb, cbuf[c]], w=[cbuf[c]])
                  regH = []
                  for c in range(NCH):
                      rv, rb_, _, _ = rg.get()
                      regH.append((rv, rb_))
                      op("pe", lambda e: e.transpose(rv, Nb[c][:, 0, :], ident[:]), r=[cbuf[c], cb], w=[rb_])
                  for c in range(NCH):
                      rv, rb_ = regH[c]
                      op("act" if c % 2 else "dve",
                         (lambda e: e.activation(out=Nb[c][:, 1, :], in_=rv, func=AF.Copy)) if c % 2 else (lambda e: e.tensor_copy(out=Nb[c][:, 1, :], in_=rv)),
                         r=[rb_, cbuf[c]], w=[cbuf[c]])
                  cur = [(Nb[c][:, 0, :], Nb[c][:, 1, :]) for c in range(NCH)]
                  for k in range(1, 6):
                      regS = []
                      for c in range(NCH):
                          bank = P.psum() if c % 2 == 0 else bank
                          pt, pb = bank
                          o0 = (c % 2) * 256
                          regS.append((pt, pb, o0))
                          M_, MT_ = cur[c]
                          op("pe", lambda e: e.matmul(pt[:, o0:o0 + 128], MT_, M_, start=True, stop=True), r=[cbuf[c]], w=[pb])
                          op("pe", lambda e: e.matmul(pt[:, o0 + 128:o0 + 256], M_, MT_, start=True, stop=True), r=[cbuf[c]], w=[pb])
                      for c in range(NCH):
                          pt, pb, o0 = regS[c]
                          pp = k % 2
                          op("act" if c % 2 else "dve",
                             (lambda e: e.activation(out=Mb[c][:, pp, :, :], in_=pt[:, o0:o0 + 256].rearrange("p (a b) -> p a b", a=2), func=AF.Copy)) if c % 2 else
                             (lambda e: e.tensor_copy(out=Mb[c][:, pp, :, :], in_=pt[:, o0:o0 + 256].rearrange("p (a b) -> p a b", a=2))),
                             r=[pb, cbuf[c]], w=[cbuf[c]])
                          cur[c] = (Mb[c][:, pp, 0, :], Mb[c][:, pp, 1, :])
                      regX = []
                      for c in range(NCH):
                          rv, rb_, _, _ = rg.get()
                          regX.append((rv, rb_))
                          op("pe", lambda e: e.matmul(rv, cur[c][1], Xf[c][:], start=True, stop=True), r=[cbuf[c]], w=[rb_])
                      for c in range(NCH):
                          rv, rb_ = regX[c]
                          op("dve", lambda e: e.tensor_tensor(out=Xf[c][:], in0=Xf[c][:], in1=rv, op=ALU.add), r=[rb_, cbuf[c]], w=[cbuf[c]])
                          if k == 5:
                              op("act", lambda e: e.activation(out=Xb[c][:], in_=Xf[c][:], func=AF.Copy), r=[cbuf[c]], w=[cbuf[c]])
                  regJ = []
                  for c in range(NCH):
                      bank = P.psum() if c % 2 == 0 else bank
                      pt, pb = bank
                      o0 = (c % 2) * 256
                      regJ.append((pt, pb, o0))
                      op("pe", lambda e: e.matmul(pt[:, o0:o0 + 128], Xb[c][:], vtok[c][:], start=True, stop=True), r=[cbuf[c]], w=[pb])
                      op("pe", lambda e: e.matmul(pt[:, o0 + 128:o0 + 256], kg[c][:, 0, :], Xb[c][:], start=True, stop=True), r=[cbuf[c]], w=[pb])
                  for c, (h, d) in enumerate(chains):
                      pt, pb, o0 = regJ[c]
                      n = blk[c]
                      bcol = BGt[:, n, d * 4 + h:d * 4 + h + 1]
                      op("dve", lambda e: e.tensor_scalar(out=ubf[c][:], in0=pt[:, o0:o0 + 128], scalar1=bcol, scalar2=None, op0=ALU.mult), r=[pb, bgtb, cbuf[c]], w=[cbuf[c]])
                      op("act", lambda e: e.activation(out=wT[c][:], in_=pt[:, o0 + 128:o0 + 256], func=AF.Copy), r=[pb, cbuf[c]], w=[cbuf[c]])
                  for step_i in range(2):
                      regW = []
                      for c, (h, d) in enumerate(chains):
                          a = step_i if d == 0 else 1 - step_i
                          lo, hi = a * 64, (a + 1) * 64
                          rv, rb_, bank_t, off = rg.get()
                          regW.append((bank_t, rb_, off))
                          op("pe", lambda e: e.matmul(bank_t[lo:hi, off:off + 128], wT[c][:, lo:hi], Sb[c][:], start=True, stop=True), r=[cbuf[c], sbuf_[c]], w=[rb_])
                      for c, (h, d) in enumerate(chains):
                          a = step_i if d == 0 else 1 - step_i
                          lo, hi = a * 64, (a + 1) * 64
                          n = blk[c]
                          bank_t, rb_, off = regW[c]
                          nbcol = NBt[lo:hi, n, d * 4 + h:d * 4 + h + 1]
                          op("dve", lambda e: e.scalar_tensor_tensor(out=vn[c][lo:hi, :], in0=bank_t[lo:hi, off:off + 128], scalar=nbcol, in1=ubf[c][lo:hi, :],
                                                                     op0=ALU.mult, op1=ALU.add), r=[rb_, bgtb, cbuf[c]], w=[vnb[c]])
                      regO = []
                      for c, (h, d) in enumerate(chains):
                          a = step_i if d == 0 else 1 - step_i
                          lo, hi = a * 64, (a + 1) * 64
                          n = blk[c]
                          bank = P.psum()
                          pt, pb = bank
                          regO.append(bank)
                          qT_ = qkv[:, 0, h % 2, n * 128 + lo:n * 128 + hi]
                          op("pe", lambda e: e.matmul(pt[lo:hi, 0:128], qT_, Sb[c][:], start=True, stop=True), r=[qkvb, sbuf_[c]], w=[pb])
                          op("pe", lambda e: e.matmul(pt[lo:hi, 128:256], AT[c][:, lo:hi], vn[c][:], start=True, stop=True), r=[cbuf[c], vnb[c]], w=[pb])
                          op("pe", lambda e: e.matmul(pt[:, 256:384], kg[c][:, 1 + a, :], vn[c][:], start=True, stop=True), r=[cbuf[c], vnb[c]], w=[pb])
                      for c, (h, d) in enumerate(chains):
                          a = step_i if d == 0 else 1 - step_i
                          lo, hi = a * 64, (a + 1) * 64
                          pt, pb = regO[c]
                          op("dve", lambda e: e.scalar_tensor_tensor(out=S[c][:], in0=S[c][:], scalar=sc[c][:, 2 + a:3 + a], in1=pt[:, 256:384], op0=ALU.mult, op1=ALU.add),
                             r=[pb, cbuf[c], sbuf_[c]], w=[sbuf_[c]])
                          op("act", lambda e: e.activation(out=Sb[c][:], in_=S[c][:], func=AF.Copy), r=[sbuf_[c]], w=[sbuf_[c]])
                          op("act", lambda e: e.activation(out=t1[c][lo:hi, :], in_=pt[lo:hi, 0:128], func=AF.Copy, scale=sc[c][lo:hi, 0:1]), r=[pb, cbuf[c]], w=[cbuf[c]])
                          op("dve", lambda e: e.tensor_tensor(out=ot[c][lo:hi, :], in0=t1[c][lo:hi, :], in1=pt[lo:hi, 128:256], op=ALU.add), r=[pb, cbuf[c]], w=[otb[c]])
                  for c, (h, d) in enumerate(chains):
                      n = blk[c]
                      dma("sp", OBD[d, h, n * 128:(n + 1) * 128, :], ot[c][:], r=[otb[c]], w=obdb)
              kb.barrier()
        with contextlib.ExitStack() as es:
            gn = P.sb(es, "gnorm", [128, 1], F32); gnb = Buf()
            dma("sp", gn[:], gdn_norm[j], w=[gnb])
            eps_ = P.sb(es, "geps", [128, 1], F32)
            op("dve", lambda e: e.memset(eps_[:], EPS), w=[gnb])
            OB = [[P.sb(es, f"OB{i}_{d}", [128, NB, 128], F32) for d in range(2)] for i in range(2)]
            obb = [Buf() for _ in range(2)]
            GT = [P.sb(es, f"GT{i}", [128, NB, 128], F32) for i in range(2)]
            gtb = [Buf() for _ in range(2)]
            rst = [P.sb(es, f"rst{i}", [128, NB], F32) for i in range(2)]
            rstb = [Buf() for _ in range(2)]
            sqt = P.sb(es, "gsq", [128, 128], F32); sqtb = Buf()
            yo = [P.sb(es, f"gyo{i}", [128, 512], BF16) for i in range(2)]
            yob = [Buf() for _ in range(2)]
            ig = 0
            for h in range(4):
                o0_, o1_ = OB[h % 2]
                ob_ = obb[h % 2]
                g_, gb_ = GT[h % 2], gtb[h % 2]
                r_, rb_ = rst[h % 2], rstb[h % 2]
                dma("sp", o0_[:], OBD[0, h].rearrange("(n p) e -> p n e", p=128), r=obdb, w=[ob_])
                dma("sp", o1_[:], OBD[1, h].rearrange("(n p) e -> p n e", p=128), r=obdb, w=[ob_])
                dma("sp", g_[:], GG[:, h * 128:(h + 1) * 128].rearrange("(n p) e -> p n e", p=128), r=ggb, w=[gb_])
                op("dve", lambda e: e.tensor_tensor(out=o0_[:], in0=o0_[:], in1=o1_[:], op=ALU.add), r=[ob_], w=[ob_])
                for n in range(NB):
                    op("act", lambda e: e.activation(out=sqt[:], in_=o0_[:, n, :], func=AF.Square, accum_out=r_[:, n:n + 1]), r=[ob_], w=[sqtb, rb_])
                op("act", lambda e: e.activation(out=r_[:], in_=r_[:], func=AF.Sqrt, bias=eps_[:], scale=1.0 / 128), r=[rb_, gnb], w=[rb_])
                op("dve", lambda e: e.reciprocal(r_[:], r_[:]), r=[rb_], w=[rb_])
                for n0 in range(0, NB, 4):
                    nn = min(4, NB - n0)
                    o_, ob2 = yo[ig % 2], yob[ig % 2]
                    ig += 1
                    pt, pb = P.psum()
                    for q in range(nn):
                        n = n0 + q
                        op("dve", lambda e: e.scalar_tensor_tensor(out=o0_[:, n, :], in0=o0_[:, n, :], scalar=r_[:, n:n + 1], in1=g_[:, n, :], op0=ALU.mult, op1=ALU.mult),
                           r=[ob_, rb_, gb_], w=[ob_])
                        op("pe", lambda e: e.transpose(pt[:, q * 128:(q + 1) * 128], o0_[:, n, :], ident[:]), r=[ob_, cb], w=[pb])
                    op("act", lambda e: e.activation(out=o_[:, 0:nn * 128], in_=pt[:, 0:nn * 128], func=AF.Copy, scale=gn[:, 0:1]), r=[pb, gnb], w=[ob2])
                    dma("sp", YS[512 + h * 128:512 + (h + 1) * 128, n0 * 128:(n0 + nn) * 128], o_[:, 0:nn * 128], r=[ob2], w=ysb)

    for l in layers:
        last = (l == DEPTH - 1)
        if cfg.get("do_mixer", True):
            if l % 2 == 1:
                odd_mixer(l, skip_ctx=last)
            else:
                even_mixer(l, skip_ctx=last)
        if cfg.get("do_ffn", True):
            ffn_layer(l, skip_ctx=last)

    if cfg.get("dump_ys"):
        yd = P.outp("ys_dump", [D, NT])
        with contextlib.ExitStack() as es:
            tb16 = [P.sb(es, f"dy{i}", [128, 8, 512], BF16) for i in range(2)]
            tb32 = [P.sb(es, f"dz{i}", [128, 8, 512], F32) for i in range(2)]
            tbb = [Buf() for _ in range(2)]
            for ti, (t0, T) in enumerate(TILES):
                dma("sp", tb16[ti % 2][:, :, :T], YS[:, t0:t0 + T].rearrange("(c p) t -> p c t", p=128), r=ysb, w=[tbb[ti % 2]])
                op("dve", lambda e: e.tensor_copy(out=tb32[ti % 2][:, :, :T], in_=tb16[ti % 2][:, :, :T]), r=[tbb[ti % 2]], w=[tbb[ti % 2]])
                dma("sp", yd[:, t0:t0 + T].rearrange("(c p) t -> p c t", p=128), tb32[ti % 2][:, :, :T], r=[tbb[ti % 2]])
            kb.barrier()
    if cfg.get("dump_xt"):
        xd = P.outp("xt_dump", [D, NT])
        with contextlib.ExitStack() as es:
            tb = [P.sb(es, f"db{i}", [128, 8, 512], F32) for i in range(2)]
            tbb = [Buf() for _ in range(2)]
            for ti, (t0, T) in enumerate(TILES):
                dma("sp", tb[ti % 2][:, :, :T], XT[:, t0:t0 + T].rearrange("(c p) t -> p c t", p=128), r=[xb[ti]], w=[tbb[ti % 2]])
                dma("sp", xd[:, t0:t0 + T].rearrange("(c p) t -> p c t", p=128), tb[ti % 2][:, :, :T], r=[tbb[ti % 2]])
            kb.barrier()

    with contextlib.ExitStack() as es:
        xt = [P.sb(es, f"fx{i}", [128, 8, 512], F32) for i in range(2)]
        xtb = [Buf() for _ in range(2)]
        sq = [P.sb(es, f"fsq{i}", [128, 8, 512], F32) for i in range(2)]
        sqb = [Buf() for _ in range(2)]
        rs = [P.sb(es, f"frs{i}", [128, 512], F32) for i in range(2)]
        rsb = [Buf() for _ in range(2)]
        ot = [P.sb(es, f"fo{i}", [128, D], F32) for i in range(2)]
        otb = [Buf() for _ in range(2)]
        io = 0
        for ti, (t0, T) in enumerate(TILES):
            if ti == 0:
                continue
            x_, xb_ = xt[ti % 2], xtb[ti % 2]
            q_, qb_ = sq[ti % 2], sqb[ti % 2]
            r_, rb_ = rs[ti % 2], rsb[ti % 2]
            dma("sp", x_[:], XT[:, t0:t0 + T].rearrange("(c p) t -> p c t", p=128), r=[xb[ti]], w=[xb_])
            for c in range(8):
                op("act", lambda e: e.activation(out=q_[:, c, :], in_=x_[:, c, :], func=AF.Square), r=[xb_], w=[qb_])
            pt, pb = P.psum()
            for c in range(8):
                op("pe", lambda e: e.matmul(pt[:], ones[:], q_[:, c, :], start=(c == 0), stop=(c == 7)), r=[qb_, cb], w=[pb])
            op("act", lambda e: e.activation(out=r_[:], in_=pt[:], func=AF.Sqrt, bias=epsb[:], scale=1.0 / D), r=[pb, cb], w=[rb_])
            op("dve", lambda e: e.reciprocal(r_[:], r_[:]), r=[rb_], w=[rb_])
            for c in range(8):
                op("dve", lambda e: e.scalar_tensor_tensor(out=q_[:, c, :], in0=x_[:, c, :], scalar=fng[:, c:c + 1], in1=r_[:],
                                                           op0=ALU.mult, op1=ALU.mult), r=[xb_, rb_, cb, qb_], w=[qb_])
            for bi in range(4):
                o_, ob_ = ot[io % 2], otb[io % 2]
                io += 1
                for half in range(2):
                    pt, pb = P.psum()
                    for q in range(4):
                        c = half * 4 + q
                        op("pe", lambda e: e.transpose(pt[:, q * 128:(q + 1) * 128], q_[:, c, bi * 128:(bi + 1) * 128], ident[:]), r=[qb_, cb], w=[pb])
                    if half:
                        op("act", lambda e: e.activation(out=o_[:, 512:1024], in_=pt[:], func=AF.Copy), r=[pb], w=[ob_])
                    else:
                        op("dve", lambda e: e.tensor_copy(out=o_[:, 0:512], in_=pt[:]), r=[pb], w=[ob_])
                tk0 = t0 - CT + bi * 128
                dma("sp", out[tk0:tk0 + 128, :], o_[:], r=[ob_])
        kb.barrier()

    kb.finish()
    root.close()
    return P


def host_inputs(inputs, b):
    f = lambda a: np.ascontiguousarray(a, dtype=np.float32)
    c2 = np.stack([inputs["c"][b], inputs["c_ctx"]], -1).reshape(8, 128, 2).transpose(1, 0, 2)
    m = {
        "x": f(inputs["x"][b]),
        "ctx": f(inputs["ctx"][b]),
        "c2": f(c2),
        "w_ada": f(inputs["w_ada"]),
        "b_ada": f(inputs["b_ada"].reshape(DEPTH, 48, 128).transpose(0, 2, 1)),
        "norm_mix": f(inputs["norm_mix"].reshape(DEPTH, 8, 128).transpose(0, 2, 1)),
        "norm_ffn": f(inputs["norm_ffn"].reshape(DEPTH, 8, 128).transpose(0, 2, 1)),
        "final_norm": f(inputs["final_norm"].reshape(8, 128).T),
        "ffn_w_up": f(inputs["ffn_w_up"]),
        "ffn_conv": f(inputs["ffn_conv"].reshape(DEPTH, 3, 44, 128).transpose(0, 3, 2, 1)),
        "ffn_w_down": f(inputs["ffn_w_down"]),
        "ident": np.eye(128, dtype=np.float32),
    }
    m["od_w_in"] = f(inputs["od_w_in"])
    m["od_w_out"] = f(inputs["od_w_out"])
    m["lru_conv"] = f(inputs["lru_conv"].reshape(2, 4, 4, 128).transpose(0, 3, 2, 1))
    m["lru_conv_b"] = f(inputs["lru_conv_b"].reshape(2, 4, 128).transpose(0, 2, 1))
    def bd(w):
        o = np.zeros((2, 2, 4, 128, 128), np.float32)
        for c in range(4):
            o[:, :, c, 0:64, 0:64] = w[:, :, 2 * c]
            o[:, :, c, 64:128, 64:128] = w[:, :, 2 * c + 1]
        return o
    m["lru_wa"] = bd(inputs["lru_wa"])
    m["lru_wx"] = bd(inputs["lru_wx"])
    for nm, key in (("lru_ba", "lru_ba"), ("lru_bx", "lru_bx"), ("lru_lam", "lru_lambda")):
        m[nm] = f(inputs[key].reshape(2, 2, 4, 128).transpose(0, 3, 1, 2))
    m["swa_sink"] = f(np.broadcast_to(inputs["swa_sink"][:, None, :], (2, 128, 8)))
    m["ev_w_in"] = f(inputs["ev_w_in"])
    m["ev_w_out"] = f(inputs["ev_w_out"])
    m["diff_lambda"] = f(np.broadcast_to(inputs["diff_lambda"].reshape(2, 1, 256), (2, 128, 256)))
    m["diff_subln"] = f(inputs["diff_subln"].reshape(2, 128, 1))
    m["gdn_conv"] = f(inputs["gdn_conv"].reshape(2, 4, 12, 128).transpose(0, 3, 2, 1))
    m["gdn_a_log"] = f(np.broadcast_to(inputs["gdn_a_log"].reshape(2, 1, 8), (2, 128, 8)))
    m["gdn_dt_bias"] = f(np.broadcast_to(inputs["gdn_dt_bias"].reshape(2, 1, 8), (2, 128, 8)))
    m["gdn_norm"] = f(inputs["gdn_norm"].reshape(2, 128, 1))
    m.update(CONSTS)
    return m


def _make_consts():
    t = np.arange(LT)
    row = (t // 64).astype(np.float64)
    col = (t % 64).astype(np.float64)
    inv = (10000.0 ** (-np.arange(0, 32, 2, dtype=np.float32) / 32)).astype(np.float32)
    ar = (row[:, None].astype(np.float32) * inv).astype(np.float32)
    ac = (col[:, None].astype(np.float32) * inv).astype(np.float32)
    cr, sr, cc, sc = np.cos(ar), np.sin(ar), np.cos(ac), np.sin(ac)
    cosT = np.zeros((128, LT), np.float32)
    sinT = np.zeros((128, LT), np.float32)
    perm = np.zeros((128, 128), np.float32)
    for p in range(128):
        dd = p % 64
        i = dd % 16
        q = dd // 16
        if q == 0:
            cosT[p], sinT[p], partner = cr[:, i], -sr[:, i], p + 16
        elif q == 1:
            cosT[p], sinT[p], partner = cr[:, i], sr[:, i], p - 16
        elif q == 2:
            cosT[p], sinT[p], partner = cc[:, i], -sc[:, i], p + 16
        else:
            cosT[p], sinT[p], partner = cc[:, i], sc[:, i], p - 16
        perm[partner, p] = 1.0
    a = np.arange(128)[:, None]
    bq = np.arange(128)[None, :]
    NEG = -30000.0
    mprev = np.where(a >= bq, 0.0, NEG).astype(np.float32)
    mnext = np.where(a <= bq, 0.0, NEG).astype(np.float32)
    tt = np.arange(128)[:, None]
    ii = np.arange(128)[None, :]
    same = (tt // 64) == (ii // 64)
    gmask = np.zeros((128, 2, 3, 128), np.float32)
    gmask[:, 0, 0] = (tt <= ii) & same
    gmask[:, 0, 1] = (tt > ii) & same
    gmask[:, 0, 2] = np.where((tt <= ii) & same, 0.0, NEG)
    gmask[:, 1, 0] = (tt >= ii) & same
    gmask[:, 1, 1] = (tt < ii) & same
    gmask[:, 1, 2] = np.where((tt >= ii) & same, 0.0, NEG)
    gaux = np.zeros((128, 258), np.float32)
    gaux[0:64, 0:128] = 1.0
    gaux[64:128, 128:256] = 1.0
    gaux[0:64, 256] = 1.0
    gaux[64:128, 257] = 1.0
    return {"cosT": cosT, "sinT": sinT, "permT": perm, "mprev": np.tile(mprev, (1, 4)), "mnext": np.tile(mnext, (1, 4)), "gmask": gmask, "gaux": gaux}


CONSTS = _make_consts()


def kernel(**inputs):
    inputs = {k: np.asarray(v) for k, v in inputs.items()}
    P = build()
    n = 8
    in_maps = []
    for b in range(n):
        m = host_inputs(inputs, b)
        in_maps.append({k: m[k] for k in P.din})
    res = run_bass_kernel_spmd(P.nc, in_maps, core_ids=list(range(n)))
    return np.stack([np.asarray(r["out"], dtype=np.float32) for r in res.results], 0)
```

```python
import contextlib
import math
import numpy as np
import concourse.bass as bass
import concourse.mybir as mybir
from concourse.bass_utils import run_bass_kernel_spmd

F32 = mybir.dt.float32
BF16 = mybir.dt.bfloat16
AF = mybir.ActivationFunctionType
ALU = mybir.AluOpType
AX = mybir.AxisListType

D = 1024
NT = 4352
CT = 256
LT = 4096
DEPTH = 4
FFN = 2816
EPS = 1e-6
TILES = [(0, 256)] + [(256 + 512 * i, 512) for i in range(8)]


class Buf:
    __slots__ = ("w", "r", "name", "excl")

    def __init__(self, name="", excl=False):
        self.w = None
        self.r = []
        self.name = name
        self.excl = excl


class KB:
    EPOCH = 24000
    NDMA = 40

    def __init__(self, nc, same_eng_sync=True):
        self.nc = nc
        self.es = contextlib.ExitStack()
        self.eng = {"pe": nc.tensor, "act": nc.scalar, "dve": nc.vector, "pool": nc.gpsimd, "sp": nc.sync}
        self.same = same_eng_sync
        self.cnt = {e: 0 for e in self.eng}
        self.epoch = {e: 0 for e in self.eng}
        self.sems = {}
        self.known = {e: {} for e in self.eng}
        self.last_tok = {e: None for e in self.eng}
        self.dsem = [self.es.enter_context(nc.semaphore(f"dq{i}")) for i in range(self.NDMA)]
        self.dtot = [0] * self.NDMA
        self.dnext = 0
        self.nwait = 0
        self.nins = 0
        for e in self.eng:
            self._newsem(e)

    def _newsem(self, e):
        key = (e, self.epoch[e])
        self.sems[key] = self.es.enter_context(self.nc.semaphore(f"s_{e}_{self.epoch[e]}"))
        self.cnt[e] = 0

    def _semh(self, key):
        if isinstance(key, int):
            return self.dsem[key]
        return self.sems[key]

    def _wait(self, e, tok):
        key, val = tok
        if self.known[e].get(key, 0) >= val:
            return
        self.eng[e].wait_ge(self._semh(key), val)
        self.known[e][key] = val
        self.nwait += 1

    def _deps(self, e, r, w):
        toks = []
        for b in r:
            if b.w is not None:
                toks.append(b.w)
        for b in w:
            if b.w is not None:
                toks.append(b.w)
            toks.extend(b.r)
        for tok in toks:
            key = tok[0]
            if (not isinstance(key, int)) and key[0] == e and (e == "pe" or not self.same):
                continue
            self._wait(e, tok)

    def _mark(self, tok, r, w):
        for b in w:
            b.w = tok
            b.r = []
        for b in r:
            if b not in w:
                b.r.append(tok)
                if len(b.r) > 24:
                    d = {}
                    for k, v in b.r:
                        if d.get(k, 0) < v:
                            d[k] = v
                    b.r = list(d.items())

    def op(self, e, fn, r=(), w=()):
        w = list(w) + [b for b in r if b.excl and b not in w]
        r = [b for b in r if not b.excl]
        self._deps(e, r, w)
        if self.cnt[e] >= self.EPOCH:
            self.epoch[e] += 1
            self._newsem(e)
        ins = fn(self.eng[e])
        key = (e, self.epoch[e])
        self.cnt[e] += 1
        ins.then_inc(self.sems[key], 1)
        tok = (key, self.cnt[e])
        self.last_tok[e] = tok
        self._mark(tok, r, w)
        self.nins += 1
        return tok

    def dma(self, e, out, in_, r=(), w=(), **kw):
        r = list(r)
        w = list(w)
        self._deps(e, r, w)
        i = self.dnext
        self.dnext = (self.dnext + 1) % self.NDMA
        if self.dtot[i]:
            self._wait(e, (i, self.dtot[i]))
        if e == "pool":
            self.swq = getattr(self, "swq", [])
            if len(self.swq) >= 3:
                self._wait(e, self.swq[-3])
        ins = self.eng[e].dma_start(out=out, in_=in_, **kw)
        self.dtot[i] += 16
        ins.then_inc(self.dsem[i], 16)
        tok = (i, self.dtot[i])
        if e == "pool":
            self.swq.append(tok)
        self._mark(tok, r, w)
        self.nins += 1
        return tok

    def barrier(self):
        toks = [t for t in self.last_tok.values() if t is not None]
        toks += [(i, self.dtot[i]) for i in range(self.NDMA) if self.dtot[i]]
        for e in self.eng:
            for tok in toks:
                key = tok[0]
                if (not isinstance(key, int)) and key[0] == e:
                    continue
                self._wait(e, tok)

    def finish(self):
        self.barrier()
        self.es.close()


class Prog:
    def __init__(self, cfg=None):
        self.cfg = cfg or {}
        self.nc = bass.Bass("TRN2", target_bir_lowering=False)
        self.kb = KB(self.nc)
        self.root = contextlib.ExitStack()
        self.din = {}
        self.dbuf = {}
        self.psn = 0

    def inp(self, name, shape, dt=F32):
        t = self.nc.dram_tensor(name, list(shape), dt, kind="ExternalInput").ap()
        self.din[name] = t
        return t

    def outp(self, name, shape, dt=F32):
        return self.nc.dram_tensor(name, list(shape), dt, kind="ExternalOutput").ap()

    def scratch(self, name, shape, dt):
        return self.nc.dram_tensor(name, list(shape), dt, kind="Internal").ap()

    def sb(self, es, name, shape, dt):
        self.nsb = getattr(self, "nsb", 0) + 1
        return es.enter_context(self.nc.sbuf_tensor(f"sb{self.nsb}_{name}", list(shape), dt))

    def scope(self, name):
        return self.nc.named_scope(name)

    def setup_psum(self):
        self.ps = [self.root.enter_context(self.nc.psum_tensor(f"ps{i}", [128, 512], F32)) for i in range(8)]
        self.psb = [Buf(f"ps{i}", excl=True) for i in range(8)]

    def psum(self):
        lo, hi = getattr(self, "psr", (0, 8))
        if not (lo <= self.psn < hi):
            self.psn = lo
        i = self.psn
        self.psn = lo + (self.psn + 1 - lo) % (hi - lo)
        return self.ps[i], self.psb[i]


def build(cfg=None):
    cfg = cfg or {}
    P = Prog(cfg)
    nc, kb = P.nc, P.kb
    op, dma = kb.op, kb.dma
    P.setup_psum()
    root = P.root
    layers = cfg.get("layers", list(range(DEPTH)))

    x_in = P.inp("x", [LT, D])
    ctx_in = P.inp("ctx", [CT, D])
    c2_in = P.inp("c2", [128, 8, 2])
    w_ada = P.inp("w_ada", [DEPTH, D, 6 * D])
    b_ada = P.inp("b_ada", [DEPTH, 128, 48])
    norm_mix = P.inp("norm_mix", [DEPTH, 128, 8])
    norm_ffn = P.inp("norm_ffn", [DEPTH, 128, 8])
    final_norm = P.inp("final_norm", [128, 8])
    ffn_w_up = P.inp("ffn_w_up", [DEPTH, D, 2 * FFN])
    ffn_conv = P.inp("ffn_conv", [DEPTH, 128, 44, 3])
    ffn_w_down = P.inp("ffn_w_down", [DEPTH, FFN, D])
    ident_in = P.inp("ident", [128, 128])
    od_w_in = P.inp("od_w_in", [2, D, 1792])
    od_w_out = P.inp("od_w_out", [2, D, D])
    lru_conv = P.inp("lru_conv", [2, 128, 4, 4])
    lru_conv_b = P.inp("lru_conv_b", [2, 128, 4])
    lru_wa = P.inp("lru_wa", [2, 2, 4, 128, 128])
    lru_wx = P.inp("lru_wx", [2, 2, 4, 128, 128])
    lru_ba = P.inp("lru_ba", [2, 128, 2, 4])
    lru_bx = P.inp("lru_bx", [2, 128, 2, 4])
    lru_lam = P.inp("lru_lam", [2, 128, 2, 4])
    swa_sink = P.inp("swa_sink", [2, 128, 8])
    cos_in = P.inp("cosT", [128, LT])
    sin_in = P.inp("sinT", [128, LT])
    perm_in = P.inp("permT", [128, 128])
    mprev_in = P.inp("mprev", [128, 512])
    mnext_in = P.inp("mnext", [128, 512])
    ev_w_in = P.inp("ev_w_in", [2, D, 3600])
    ev_w_out = P.inp("ev_w_out", [2, D, D])
    diff_lambda = P.inp("diff_lambda", [2, 128, 256])
    diff_subln = P.inp("diff_subln", [2, 128, 1])
    gdn_conv = P.inp("gdn_conv", [2, 128, 12, 4])
    gdn_a_log = P.inp("gdn_a_log", [2, 128, 8])
    gdn_dt_bias = P.inp("gdn_dt_bias", [2, 128, 8])
    gdn_norm = P.inp("gdn_norm", [2, 128, 1])
    gmask_in = P.inp("gmask", [128, 2, 3, 128])
    gaux_in = P.inp("gaux", [128, 258])
    out = P.outp("out", [LT, D])

    XT = P.scratch("XT", [D, NT], F32)
    xb = [Buf(f"X{t}") for t in range(9)]
    GS = P.scratch("GS", [FFN, NT], BF16)
    gsb = [Buf(f"G{j}") for j in range(22)]
    wup_b = P.scratch("wup_b", [DEPTH, D, 2 * FFN], BF16)
    wdn_b = P.scratch("wdn_b", [DEPTH, FFN, D], BF16)
    wupb = [Buf() for _ in range(DEPTH)]
    wdnb = [Buf() for _ in range(DEPTH)]

    odin_b = P.scratch("odin_b", [2, D, 1792], BF16)
    odout_b = P.scratch("odout_b", [2, D, D], BF16)
    odinb = [Buf() for _ in range(2)]
    odoutb = [Buf() for _ in range(2)]
    YS = P.scratch("YS", [D, NT], BF16)
    ysb = [Buf("Y")]
    QD = P.scratch("QD", [8, 64, NT], BF16)
    KD = P.scratch("KD", [2, 64, NT], BF16)
    qdb = [Buf("QD")]
    kdb = [Buf("KD")]

    evin_b = P.scratch("evin_b", [2, D, 3600], BF16)
    evout_b = P.scratch("evout_b", [2, D, D], BF16)
    evinb = [Buf() for _ in range(2)]
    evoutb = [Buf() for _ in range(2)]
    QA = P.scratch("QA", [8, 64, NT], BF16)
    KA = P.scratch("KA", [8, 64, NT], BF16)
    VA = P.scratch("VA", [NT, 512], BF16)
    GQ = P.scratch("GQ", [3, 4, 128, NT], BF16)
    GQK = P.scratch("GQK", [4, 128, NT], F32)
    GG = P.scratch("GG", [NT, 512], F32)
    BG = P.scratch("BG", [NT, 16], F32)
    OBD = P.scratch("OBD", [2, 4, NT, 128], F32)
    obdb = [Buf("OBD")]
    qab = [Buf("QA")]
    kab = [Buf("KA")]
    vab = [Buf("VA")]
    gqb = [Buf("GQ")]
    ggb = [Buf("GG")]
    bgb = [Buf("BG")]

    ident = P.sb(root, "ident", [128, 128], F32)
    ones = P.sb(root, "ones", [128, 128], F32)
    epsb = P.sb(root, "epsb", [128, 1], F32)
    ones16 = P.sb(root, "ones16", [128, 128], BF16)
    MOD = P.sb(root, "MOD", [128, DEPTH, 48, 2], F32)
    AMX = P.sb(root, "AMX", [128, DEPTH, 8, 2], F32)
    AFF = P.sb(root, "AFF", [128, DEPTH, 8, 2], F32)
    fconv = P.sb(root, "fconv", [128, DEPTH, 44, 3], F32)
    fng = P.sb(root, "fng", [128, 8], F32)
    cb = Buf("consts")
    modb = Buf("mod")
    dma("sp", ident[:], ident_in[:, :], w=[cb])
    op("dve", lambda e: e.memset(ones[:], 1.0), w=[cb])
    op("dve", lambda e: e.memset(epsb[:], EPS), w=[cb])
    op("dve", lambda e: e.memset(ones16[:], 1.0), w=[cb])
    dma("sp", fconv[:], ffn_conv.rearrange("l p j k -> p l j k"), w=[cb])
    dma("sp", fng[:], final_norm[:, :], w=[cb])

    def cast_rows(dst, src, R, C, wbuf, cchunk):
        dv = dst.rearrange("r (a c) -> (r a) c", c=cchunk)
        sv = src.rearrange("r (a c) -> (r a) c", c=cchunk)
        rows = R * (C // cchunk)
        r0 = 0
        while r0 < rows:
            n = min(2048, rows - r0)
            dma("pool", dv[r0:r0 + n, :], sv[r0:r0 + n, :], w=[wbuf])
            r0 += n

    for l in layers:
        if l % 2 == 0:
            cast_rows(evin_b[l // 2], ev_w_in[l // 2], D, 3600, evinb[l // 2], 1800)
            cast_rows(evout_b[l // 2], ev_w_out[l // 2], D, D, evoutb[l // 2], 1024)
        if l % 2 == 1:
            cast_rows(odin_b[l // 2], od_w_in[l // 2], D, 1792, odinb[l // 2], 1792)
            cast_rows(odout_b[l // 2], od_w_out[l // 2], D, D, odoutb[l // 2], 1024)
        cast_rows(wup_b[l], ffn_w_up[l], D, 2 * FFN, wupb[l], 1408)
        cast_rows(wdn_b[l], ffn_w_down[l], FFN, D, wdnb[l], 1024)

    with contextlib.ExitStack() as es, P.scope("mod"):
        c2 = P.sb(es, "c2", [128, 8, 2], F32)
        s2 = P.sb(es, "s2", [128, 8, 2], F32)
        bad = P.sb(es, "bad", [128, DEPTH, 48], F32)
        nmx = P.sb(es, "nmx", [128, DEPTH, 8], F32)
        nff = P.sb(es, "nff", [128, DEPTH, 8], F32)
        wa = [P.sb(es, f"wa{i}", [128, 8, 512], F32) for i in range(2)]
        wab = [Buf() for _ in range(2)]
        b0 = Buf()
        dma("sp", c2[:], c2_in[:, :, :], w=[b0])
        dma("sp", bad[:], b_ada.rearrange("l p j -> p l j"), w=[b0])
        dma("sp", nmx[:], norm_mix.rearrange("l p c -> p l c"), w=[b0])
        dma("sp", nff[:], norm_ffn.rearrange("l p c -> p l c"), w=[b0])
        op("act", lambda e: e.activation(out=s2[:], in_=c2[:], func=AF.Silu), r=[b0], w=[b0])
        it = 0
        for l in layers:
            for blk in range(12):
                wt, wb = wa[it % 2], wab[it % 2]
                it += 1
                dma("sp", wt[:], w_ada[l][:, blk * 512:(blk + 1) * 512].rearrange("(k p) n -> p k n", p=128), w=[wb])
                pt, pb = P.psum()
                for jj in range(4):
                    for k in range(8):
                        op("pe", lambda e: e.matmul(pt[:, jj * 2:jj * 2 + 2], wt[:, k, jj * 128:(jj + 1) * 128], s2[:, k, :],
                                                    start=(k == 0), stop=(k == 7)), r=[wb, b0], w=[pb])
                for jj in range(4):
                    j = blk * 4 + jj
                    op("dve", lambda e: e.tensor_scalar(out=MOD[:, l, j, :], in0=pt[:, jj * 2:jj * 2 + 2], scalar1=bad[:, l, j:j + 1],
                                                        scalar2=None, op0=ALU.add), r=[pb, b0], w=[modb])
            for s in range(2):
                op("dve", lambda e: e.scalar_tensor_tensor(out=AMX[:, l, :, s], in0=MOD[:, l, 8:16, s], scalar=1.0, in1=nmx[:, l, :],
                                                           op0=ALU.add, op1=ALU.mult), r=[modb, b0], w=[modb])
                op("dve", lambda e: e.scalar_tensor_tensor(out=AFF[:, l, :, s], in0=MOD[:, l, 32:40, s], scalar=1.0, in1=nff[:, l, :],
                                                           op0=ALU.add, op1=ALU.mult), r=[modb, b0], w=[modb])
        kb.barrier()

    dbg_x = cfg.get("x_override")
    if dbg_x:
        xo = P.inp("xo", [D, NT])
        with contextlib.ExitStack() as es:
            tb = [P.sb(es, f"tb{i}", [128, 8, 512], F32) for i in range(2)]
            tbb = [Buf() for _ in range(2)]
            for ti, (t0, T) in enumerate(TILES):
                dma("sp", tb[ti % 2][:, :, :T], xo[:, t0:t0 + T].rearrange("(c p) t -> p c t", p=128), w=[tbb[ti % 2]])
                dma("sp", XT[:, t0:t0 + T].rearrange("(c p) t -> p c t", p=128), tb[ti % 2][:, :, :T], r=[tbb[ti % 2]], w=[xb[ti]])
            kb.barrier()
    else:
        with contextlib.ExitStack() as es:
            tin = [P.sb(es, f"tin{i}", [128, D], F32) for i in range(2)]
            tinb = [Buf() for _ in range(2)]
            tout = [P.sb(es, f"tout{i}", [128, 8, 512], F32) for i in range(2)]
            toutb = [Buf() for _ in range(2)]
            it = 0
            for ti, (t0, T) in enumerate(TILES):
                to, tob = tout[ti % 2], toutb[ti % 2]
                for bi in range(T // 128):
                    tk0 = t0 + bi * 128
                    src = ctx_in[tk0:tk0 + 128, :] if tk0 < CT else x_in[tk0 - CT:tk0 - CT + 128, :]
                    tt, ttb = tin[it % 2], tinb[it % 2]
                    it += 1
                    dma("sp", tt[:], src, w=[ttb])
                    for half in range(2):
                        pt, pb = P.psum()
                        for q in range(4):
                            c = half * 4 + q
                            op("pe", lambda e: e.transpose(pt[:, q * 128:(q + 1) * 128], tt[:, c * 128:(c + 1) * 128], ident[:]),
                               r=[ttb, cb], w=[pb])
                        op("act" if half else "dve",
                           (lambda e: e.activation(out=to[:, 4:8, bi * 128:(bi + 1) * 128], in_=pt[:].rearrange("p (q t) -> p q t", q=4), func=AF.Copy))
                           if half else
                           (lambda e: e.tensor_copy(out=to[:, 0:4, bi * 128:(bi + 1) * 128], in_=pt[:].rearrange("p (q t) -> p q t", q=4))),
                           r=[pb], w=[tob])
                dma("sp", XT[:, t0:t0 + T].rearrange("(c p) t -> p c t", p=128), to[:, :, :T], r=[tob], w=[xb[ti]])
            kb.barrier()

    def norm_stage(es, l, A, shift_idx, U, ub):
        xt = [P.sb(es, f"nx{i}", [128, 8, 512], F32) for i in range(2)]
        xtb = [Buf() for _ in range(2)]
        sq = [P.sb(es, f"nsq{i}", [128, 8, 512], BF16) for i in range(2)]
        sqb = [Buf() for _ in range(2)]
        rs = [P.sb(es, f"nrs{i}", [128, 512], F32) for i in range(2)]
        rsb = [Buf() for _ in range(2)]
        tm = [P.sb(es, f"ntm{i}", [128, 512], F32) for i in range(3)]
        tmb = [Buf() for _ in range(3)]
        k3 = 0
        for ti, (t0, T) in enumerate(TILES):
            s = 1 if ti == 0 else 0
            x_, xb_ = xt[ti % 2], xtb[ti % 2]
            q_, qb_ = sq[ti % 2], sqb[ti % 2]
            r_, rb_ = rs[ti % 2], rsb[ti % 2]
            dma("sp", x_[:, :, :T], XT[:, t0:t0 + T].rearrange("(c p) t -> p c t", p=128), r=[xb[ti]], w=[xb_])
            for c in range(8):
                op("act", lambda e: e.activation(out=q_[:, c, :T], in_=x_[:, c, :T], func=AF.Square), r=[xb_], w=[qb_])
            pt, pb = P.psum()
            for c in range(8):
                op("pe", lambda e: e.matmul(pt[:, :T], ones16[:], q_[:, c, :T], start=(c == 0), stop=(c == 7)), r=[qb_, cb], w=[pb])
            op("act", lambda e: e.activation(out=r_[:, :T], in_=pt[:, :T], func=AF.Sqrt, bias=epsb[:], scale=1.0 / D), r=[pb, cb], w=[rb_])
            op("dve", lambda e: e.reciprocal(r_[:, :T], r_[:, :T]), r=[rb_], w=[rb_])
            for c in range(8):
                t_, tb_ = tm[k3 % 3], tmb[k3 % 3]
                k3 += 1
                op("dve", lambda e: e.tensor_tensor(out=t_[:, :T], in0=x_[:, c, :T], in1=r_[:, :T], op=ALU.mult), r=[xb_, rb_], w=[tb_])
                op("act", lambda e: e.activation(out=U[:, c, t0:t0 + T], in_=t_[:, :T], func=AF.Identity, scale=A[:, l, c, s:s + 1],
                                                 bias=MOD[:, l, shift_idx * 8 + c, s:s + 1]),
                   r=[tb_, modb], w=[ub[ti]])

    def proj_residual(es, l, gate_idx, W, wbuf, KC, SRC, srcb, skip_ctx):
        wsb = P.sb(es, "pr_w", [128, KC, D], BF16)
        wsbb = Buf()
        dma("sp", wsb[:], W.rearrange("(k p) n -> p k n", p=128), r=[wbuf], w=[wsbb])
        gt = [P.sb(es, f"pr_g{i}", [128, KC, 512], BF16) for i in range(2)]
        gtb = [Buf() for _ in range(2)]
        xt = [P.sb(es, f"pr_x{i}", [128, 8, 512], F32) for i in range(2)]
        xtb = [Buf() for _ in range(2)]
        for ti, (t0, T) in enumerate(TILES):
            if skip_ctx and ti == 0:
                continue
            s = 1 if ti == 0 else 0
            g_, gb_ = gt[ti % 2], gtb[ti % 2]
            x_, xb_ = xt[ti % 2], xtb[ti % 2]
            dma("sp", g_[:, :, :T], SRC[:, t0:t0 + T].rearrange("(k p) t -> p k t", p=128), r=srcb, w=[gb_])
            dma("sp", x_[:, :, :T], XT[:, t0:t0 + T].rearrange("(c p) t -> p c t", p=128), r=[xb[ti]], w=[xb_])
            for m in range(8):
                pt, pb = P.psum()
                for k in range(KC):
                    op("pe", lambda e: e.matmul(pt[:, :T], wsb[:, k, m * 128:(m + 1) * 128], g_[:, k, :T], start=(k == 0), stop=(k == KC - 1)),
                       r=[wsbb, gb_], w=[pb])
                op("dve", lambda e: e.scalar_tensor_tensor(out=x_[:, m, :T], in0=pt[:, :T], scalar=MOD[:, l, gate_idx * 8 + m, s:s + 1],
                                                           in1=x_[:, m, :T], op0=ALU.mult, op1=ALU.add), r=[pb, modb, xb_], w=[xb_])
            dma("sp", XT[:, t0:t0 + T].rearrange("(c p) t -> p c t", p=128), x_[:, :, :T], r=[xb_], w=[xb[ti]])

    def ffn_layer(l, skip_ctx):
        with contextlib.ExitStack() as es:
            U = P.sb(es, "U", [128, 8, NT], BF16)
            ub = [Buf(f"U{t}") for t in range(9)]
            with contextlib.ExitStack() as es2, P.scope("ffn_norm"):
                norm_stage(es2, l, AFF, 3, U, ub)
                kb.barrier()
            with contextlib.ExitStack() as es2, P.scope("ffn_up"):
                wt = [P.sb(es2, f"fw{i}", [128, 8, 256], BF16) for i in range(2)]
                wtb = [Buf() for _ in range(2)]
                H = P.sb(es2, "fH", [128, 2, NT], F32)
                Cg = P.sb(es2, "fC", [128, 2, NT], F32)
                hb = [[Buf() for _ in range(9)] for _ in range(2)]
                cgb = [[Buf() for _ in range(9)] for _ in range(2)]
                Go = [P.sb(es2, f"fG{i}", [128, NT], BF16) for i in range(2)]
                gob = [[Buf() for _ in range(9)] for _ in range(2)]
                t_start = 1 if skip_ctx else 0
                lo = CT if skip_ctx else 0
                seg_start = {0, CT}
                seg_end = {CT, NT}

                def conv_tile(j, ti):
                    t0, T = TILES[ti]
                    go_, gb_ = Go[j % 2], gob[j % 2][ti]
                    for gv in range(2):
                        ch = gv * 22 + j
                        oa = t0 + 1 if t0 in seg_start else t0
                        rds = [hb[gv][ti]] + ([hb[gv][ti - 1]] if (t0 not in seg_start) else [])
                        op("dve", lambda e: e.scalar_tensor_tensor(out=Cg[:, gv, oa:t0 + T], in0=H[:, gv, oa - 1:t0 + T - 1], scalar=fconv[:, l, ch, 0:1],
                                                                   in1=Cg[:, gv, oa:t0 + T], op0=ALU.mult, op1=ALU.add), r=rds + [cb], w=[cgb[gv][ti]])
                        ob_ = t0 + T - 1 if (t0 + T) in seg_end else t0 + T
                        rds = [hb[gv][ti]] + ([hb[gv][ti + 1]] if ((t0 + T) not in seg_end) else [])
                        op("dve", lambda e: e.scalar_tensor_tensor(out=Cg[:, gv, t0:ob_], in0=H[:, gv, t0 + 1:ob_ + 1], scalar=fconv[:, l, ch, 2:3],
                                                                   in1=Cg[:, gv, t0:ob_], op0=ALU.mult, op1=ALU.add), r=rds + [cb], w=[cgb[gv][ti]])
                    op("act", lambda e: e.activation(out=Cg[:, 0, t0:t0 + T], in_=Cg[:, 0, t0:t0 + T], func=AF.Silu), r=[cgb[0][ti]], w=[cgb[0][ti]])
                    op("dve", lambda e: e.tensor_tensor(out=go_[:, t0:t0 + T], in0=Cg[:, 0, t0:t0 + T], in1=Cg[:, 1, t0:t0 + T], op=ALU.mult),
                       r=[cgb[0][ti], cgb[1][ti]], w=[gb_])

                for j in range(22):
                    w_, wb_ = wt[j % 2], wtb[j % 2]
                    dma("sp", w_[:, :, 0:128], wup_b[l][:, j * 128:(j + 1) * 128].rearrange("(k p) n -> p k n", p=128), r=[wupb[l]], w=[wb_])
                    dma("sp", w_[:, :, 128:256], wup_b[l][:, FFN + j * 128:FFN + (j + 1) * 128].rearrange("(k p) n -> p k n", p=128),
                        r=[wupb[l]], w=[wb_])
                    for ti, (t0, T) in enumerate(TILES):
                        if ti < t_start:
                            continue
                        for gv in range(2):
                            pt, pb = P.psum()
                            for k in range(8):
                                op("pe", lambda e: e.matmul(pt[:, :T], w_[:, k, gv * 128:(gv + 1) * 128], U[:, k, t0:t0 + T],
                                                            start=(k == 0), stop=(k == 7)), r=[wb_, ub[ti]], w=[pb])
                            op("act", lambda e: e.activation(out=H[:, gv, t0:t0 + T], in_=pt[:, :T], func=AF.Copy), r=[pb], w=[hb[gv][ti]])
                            op("act", lambda e: e.activation(out=Cg[:, gv, t0:t0 + T], in_=pt[:, :T], func=AF.Copy, scale=fconv[:, l, gv * 22 + j, 1:2]),
                               r=[pb, cb], w=[cgb[gv][ti]])
                        if ti - 1 >= t_start:
                            conv_tile(j, ti - 1)
                    conv_tile(j, 8)
                    dma("sp", GS[j * 128:(j + 1) * 128, lo:NT], Go[j % 2][:, lo:NT], r=gob[j % 2][t_start:], w=[gsb[j]])
                kb.barrier()
        with contextlib.ExitStack() as es, P.scope("ffn_down"):
            proj_residual(es, l, 5, wdn_b[l], wdnb[l], 22, GS, gsb, skip_ctx)
            kb.barrier()

    def proj_fm(W, wbuf, c0, U, ub, dst, dstb, wt, wtb, ev="act"):
        dma("sp", wt[:], W[:, c0:c0 + 128].rearrange("(k p) n -> p k n", p=128), r=[wbuf], w=[wtb])
        for ti, (t0, T) in enumerate(TILES):
            pt, pb = P.psum()
            for k in range(8):
                op("pe", lambda e: e.matmul(pt[:, :T], wt[:, k, :], U[:, k, t0:t0 + T], start=(k == 0), stop=(k == 7)), r=[wtb, ub[ti]], w=[pb])
            if ev == "act":
                op("act", lambda e: e.activation(out=dst[:, t0:t0 + T], in_=pt[:, :T], func=AF.Copy), r=[pb], w=[dstb])
            else:
                op("dve", lambda e: e.tensor_copy(out=dst[:, t0:t0 + T], in_=pt[:, :T]), r=[pb], w=[dstb])

    def rope_store(es_, raw, rawb, cosT, sinT, permT, tb, dstrows, dstbuf, tmp, tmpb, ob, obb):
        op("dve", lambda e: e.tensor_copy(out=ob[:, 0:CT], in_=raw[:, 0:CT]), r=[rawb], w=[obb])
        for ti, (t0, T) in enumerate(TILES):
            if ti == 0:
                continue
            l0 = t0 - CT
            pt, pb = P.psum()
            op("pe", lambda e: e.matmul(pt[:, :T], permT[:], raw[:, t0:t0 + T], start=True, stop=True), r=[rawb, tb], w=[pb])
            op("dve", lambda e: e.tensor_tensor(out=tmp[:, :T], in0=pt[:, :T], in1=sinT[:, l0:l0 + T], op=ALU.mult), r=[pb, tb], w=[tmpb])
            op("dve", lambda e: e.tensor_tensor(out=raw[:, t0:t0 + T], in0=raw[:, t0:t0 + T], in1=cosT[:, l0:l0 + T], op=ALU.mult), r=[rawb, tb, pb], w=[rawb])
            op("dve", lambda e: e.tensor_tensor(out=ob[:, t0:t0 + T], in0=raw[:, t0:t0 + T], in1=tmp[:, :T], op=ALU.add), r=[rawb, tmpb], w=[obb])
        for hh in range(2):
            dma("sp", dstrows[hh], ob[hh * 64:(hh + 1) * 64, :], r=[obb], w=[dstbuf])

    def odd_mixer(l, skip_ctx):
        j = l // 2
        W = odin_b[j]
        wbuf = odinb[j]
        with contextlib.ExitStack() as es:
            U = P.sb(es, "U", [128, 8, NT], BF16)
            ub = [Buf(f"U{t}") for t in range(9)]
            with contextlib.ExitStack() as es2, P.scope("mix_norm"):
                norm_stage(es2, l, AMX, 0, U, ub)
                kb.barrier()
            with contextlib.ExitStack() as es2, P.scope("odd_lru"):
                wt = [P.sb(es2, f"ow{i}", [128, 8, 128], BF16) for i in range(2)]
                wtb = [Buf() for _ in range(2)]
                B1 = P.sb(es2, "B1", [128, NT], F32); b1 = Buf()
                B2 = P.sb(es2, "B2", [128, NT], F32); b2 = Buf()
                B2h = P.sb(es2, "B2h", [128, NT], BF16); b2h = Buf()
                B4 = P.sb(es2, "B4", [128, NT], F32); b4 = Buf()
                B5 = P.sb(es2, "B5", [128, NT], F32); b5 = Buf()
                B6 = P.sb(es2, "B6", [128, NT], F32); b6 = Buf()
                B7 = P.sb(es2, "B7", [128, NT], F32); b7 = Buf()
                lc = P.sb(es2, "lc", [128, 4, 4], F32)
                lcb_ = P.sb(es2, "lcb", [128, 4], F32)
                lba = P.sb(es2, "lba", [128, 2, 4], F32)
                lbx = P.sb(es2, "lbx", [128, 2, 4], F32)
                llam = P.sb(es2, "llam", [128, 2, 4], F32)
                c8 = P.sb(es2, "c8", [128, 2, 4], F32)
                c16 = P.sb(es2, "c16", [128, 2, 4], F32)
                bdf = P.sb(es2, "bdf", [128, 2, 128], F32)
                bda = [P.sb(es2, f"bda{i}", [128, 2, 128], BF16) for i in range(2)]
                bdab = [Buf() for _ in range(2)]
                bdfb = Buf()
                sb0 = Buf()
                dma("sp", lc[:], lru_conv[j], w=[sb0])
                dma("sp", lcb_[:], lru_conv_b[j], w=[sb0])
                dma("sp", lba[:], lru_ba[j], w=[sb0])
                dma("sp", lbx[:], lru_bx[j], w=[sb0])
                dma("sp", llam[:], lru_lam[j], w=[sb0])
                op("act", lambda e: e.activation(out=c8[:], in_=llam[:], func=AF.Exp, scale=-1.0), r=[sb0], w=[sb0])
                op("act", lambda e: e.activation(out=c8[:], in_=c8[:], func=AF.Ln, bias=ones[:, 0:1], scale=1.0), r=[sb0, cb], w=[sb0])
                op("dve", lambda e: e.tensor_scalar(out=c16[:], in0=c8[:], scalar1=-16.0, scalar2=None, op0=ALU.mult), r=[sb0], w=[sb0])
                op("dve", lambda e: e.tensor_scalar(out=c8[:], in0=c8[:], scalar1=-8.0, scalar2=None, op0=ALU.mult), r=[sb0], w=[sb0])
                segs = [(0, CT), (CT, NT)]
                ib = 0
                for c in range(4):
                    proj_fm(W, wbuf, c * 128, U, ub, B1, b1, wt[c % 2], wtb[c % 2])
                    op("act", lambda e: e.activation(out=B2[:], in_=B1[:], func=AF.Identity, scale=lc[:, c, 2:3], bias=lcb_[:, c:c + 1]),
                       r=[b1, sb0], w=[b2])
                    for (a, b) in segs:
                        for tap, off in ((0, -2), (1, -1), (3, 1)):
                            if off < 0:
                                oa, ob_, ia, ib_ = a - off, b, a, b + off
                            else:
                                oa, ob_, ia, ib_ = a, b - off, a + off, b
                            op("dve", lambda e: e.scalar_tensor_tensor(out=B2[:, oa:ob_], in0=B1[:, ia:ib_], scalar=lc[:, c, tap:tap + 1], in1=B2[:, oa:ob_],
                                                                       op0=ALU.mult, op1=ALU.add), r=[b1, sb0, b2], w=[b2])
                    op("act", lambda e: e.activation(out=B2h[:], in_=B2[:], func=AF.Copy), r=[b2], w=[b2h])
                    for d in range(2):
                        bd_, bdb_ = bda[ib % 2], bdab[ib % 2]
                        ib += 1
                        dma("sp", bdf[:, 0, :], lru_wa[j, d, c], w=[bdfb])
                        dma("sp", bdf[:, 1, :], lru_wx[j, d, c], w=[bdfb])
                        op("dve", lambda e: e.tensor_copy(out=bd_[:], in_=bdf[:]), r=[bdfb], w=[bdb_])
                        for ti, (t0, T) in enumerate(TILES):
                            pr, prb = P.psum()
                            op("pe", lambda e: e.matmul(pr[:, :T], bd_[:, 0, :], B2h[:, t0:t0 + T], start=True, stop=True), r=[bdb_, b2h], w=[prb])
                            op("act", lambda e: e.activation(out=B4[:, t0:t0 + T], in_=pr[:, :T], func=AF.Sigmoid, bias=lba[:, d, c:c + 1], scale=1.0),
                               r=[prb, sb0], w=[b4])
                            pi, pib = P.psum()
                            op("pe", lambda e: e.matmul(pi[:, :T], bd_[:, 1, :], B2h[:, t0:t0 + T], start=True, stop=True), r=[bdb_, b2h], w=[pib])
                            op("act", lambda e: e.activation(out=B5[:, t0:t0 + T], in_=pi[:, :T], func=AF.Sigmoid, bias=lbx[:, d, c:c + 1], scale=1.0),
                               r=[pib, sb0], w=[b5])
                        op("act", lambda e: e.activation(out=B1[:], in_=B4[:], func=AF.Exp, scale=c16[:, d, c:c + 1]), r=[b4, sb0], w=[b1])
                        op("act", lambda e: e.activation(out=B4[:], in_=B4[:], func=AF.Exp, scale=c8[:, d, c:c + 1]), r=[b4, sb0], w=[b4])
                        op("act", lambda e: e.activation(out=B1[:], in_=B1[:], func=AF.Sqrt, scale=-1.0, bias=ones[:, 0:1]), r=[b1, cb], w=[b1])
                        op("dve", lambda e: e.tensor_tensor(out=B5[:], in0=B5[:], in1=B2[:], op=ALU.mult), r=[b5, b2], w=[b5])
                        op("dve", lambda e: e.tensor_tensor(out=B5[:], in0=B5[:], in1=B1[:], op=ALU.mult), r=[b5, b1], w=[b5])
                        if d == 0:
                            op("dve", lambda e: e.tensor_tensor_scan(out=B6[:], data0=B4[:], data1=B5[:], initial=0.0, op0=ALU.mult, op1=ALU.add),
                               r=[b4, b5], w=[b6])
                        else:
                            op("dve", lambda e: e.tensor_tensor_scan(out=B7[:, 0:CT][:, ::-1], data0=B4[:, 0:CT][:, ::-1], data1=B5[:, 0:CT][:, ::-1],
                                                                     initial=0.0, op0=ALU.mult, op1=ALU.add), r=[b4, b5], w=[b7])
                            op("dve", lambda e: e.tensor_tensor_scan(out=B7[:, CT:NT][:, ::-1], data0=B4[:, CT:NT][:, ::-1], data1=B5[:, CT:NT][:, ::-1],
                                                                     initial=B7[:, 0:1], op0=ALU.mult, op1=ALU.add), r=[b4, b5, b7], w=[b7])
                            op("dve", lambda e: e.tensor_tensor(out=B6[:], in0=B6[:], in1=B7[:], op=ALU.add), r=[b6, b7], w=[b6])
                    proj_fm(W, wbuf, 512 + c * 128, U, ub, B1, b1, wt[c % 2], wtb[c % 2])
                    op("act", lambda e: e.activation(out=B1[:], in_=B1[:], func=AF.Gelu_apprx_tanh), r=[b1], w=[b1])
                    op("dve", lambda e: e.tensor_tensor(out=B2h[:], in0=B6[:], in1=B1[:], op=ALU.mult), r=[b6, b1, b2h], w=[b2h])
                    dma("sp", YS[c * 128:(c + 1) * 128, :], B2h[:], r=[b2h], w=ysb)
                kb.barrier()
            with contextlib.ExitStack() as es2, P.scope("odd_qk"):
                wt = [P.sb(es2, f"aw{i}", [128, 8, 128], BF16) for i in range(2)]
                wtb = [Buf() for _ in range(2)]
                cosT = P.sb(es2, "cosT", [128, LT], F32)
                sinT = P.sb(es2, "sinT", [128, LT], F32)
                permT = P.sb(es2, "permT", [128, 128], F32)
                tb = Buf()
                dma("sp", cosT[:], cos_in[:, :], w=[tb])
                dma("sp", sinT[:], sin_in[:, :], w=[tb])
                dma("sp", permT[:], perm_in[:, :], w=[tb])
                raw = [P.sb(es2, f"raw{i}", [128, NT], F32) for i in range(2)]
                rawb = [Buf() for _ in range(2)]
                tmp = P.sb(es2, "rtmp", [128, 512], F32); tmpb = Buf()
                ob = [P.sb(es2, f"rob{i}", [128, NT], BF16) for i in range(2)]
                obb = [Buf() for _ in range(2)]
                for c in range(5):
                    proj_fm(W, wbuf, 1024 + c * 128, U, ub, raw[c % 2], rawb[c % 2], wt[c % 2], wtb[c % 2])
                    if c < 4:
                        rows = [QD[2 * c + hh] for hh in range(2)]
                        dbuf_ = qdb[0]
                    else:
                        rows = [KD[hh] for hh in range(2)]
                        dbuf_ = kdb[0]
                    rope_store(es2, raw[c % 2], rawb[c % 2], cosT, sinT, permT, tb, rows, dbuf_, tmp, tmpb, ob[c % 2], obb[c % 2])
                kb.barrier()
            with contextlib.ExitStack() as es2, P.scope("odd_attn"):
                wv = P.sb(es2, "wv", [128, 8, 128], BF16); wvb = Buf()
                dma("sp", wv[:], W[:, 1664:1792].rearrange("(k p) n -> p k n", p=128), r=[wbuf], w=[wvb])
                V = P.sb(es2, "V", [128, 34, 2, 65], BF16); vb = Buf()
                op("pool", lambda e: e.memset(V[:], 1.0), w=[vb])
                for blk in range(34):
                    ti = 0 if blk < 2 else 1 + (blk - 2) // 4
                    pt, pb = P.psum()
                    for k in range(8):
                        op("pe", lambda e: e.matmul(pt[:, 0:128], U[:, k, blk * 128:(blk + 1) * 128], wv[:, k, :], start=(k == 0), stop=(k == 7)),
                           r=[wvb, ub[ti]], w=[pb])
                    op("act", lambda e: e.activation(out=V[:, blk, :, 0:64], in_=pt[:, 0:128].rearrange("p (a b) -> p a b", a=2), func=AF.Copy),
                       r=[pb], w=[vb])
                kT = P.sb(es2, "kT", [64, 2, NT], BF16); ktb = Buf()
                dma("sp", kT[:], KD.rearrange("h d t -> d h t"), r=kdb, w=[ktb])
                qT = [P.sb(es2, f"qT{i}", [64, 4, NT], BF16) for i in range(2)]
                qtb = [Buf() for _ in range(2)]
                idb16 = P.sb(es2, "idb16", [128, 128], BF16)
                mk = P.sb(es2, "mk", [128, 2, 512], BF16)
                mkf = P.sb(es2, "mkf", [128, 2, 512], F32)
                esk = P.sb(es2, "esk", [128, 8], F32)
                mb = Buf()
                dma("sp", mkf[:, 0, :], mprev_in[:, :], w=[mb])
                dma("sp", mkf[:, 1, :], mnext_in[:, :], w=[mb])
                dma("sp", esk[:], swa_sink[j], w=[mb])
                op("dve", lambda e: e.tensor_copy(out=mk[:], in_=mkf[:]), r=[mb], w=[mb])
                op("dve", lambda e: e.tensor_copy(out=idb16[:], in_=ident[:]), r=[cb], w=[mb])
                op("act", lambda e: e.activation(out=esk[:], in_=esk[:], func=AF.Exp), r=[mb], w=[mb])
                PT = [P.sb(es2, f"PT{i}", [128, 512], BF16) for i in range(10)]
                ptb = [Buf() for _ in range(10)]
                od = [P.sb(es2, f"od{i}", [128, 512], F32) for i in range(2)]
                odb = [Buf() for _ in range(2)]
                rd = [P.sb(es2, f"rd{i}", [128, 8], F32) for i in range(2)]
                rdb = [Buf() for _ in range(2)]
                yt = [P.sb(es2, f"yt{i}", [128, 4, 128], BF16) for i in range(2)]
                ytb = [Buf() for _ in range(2)]
                for kv in range(2):
                    dma("sp", qT[kv][:], QD[kv * 4:(kv + 1) * 4].rearrange("h d t -> d h t"), r=qdb, w=[qtb[kv]])
                ipt = 0
                qblocks = list(range(2, 34)) if skip_ctx else list(range(34))
                for qi, qb_ in enumerate(qblocks):
                    o_, ob_ = od[qi % 2], odb[qi % 2]
                    r_, rb_ = rd[qi % 2], rdb[qi % 2]
                    if qb_ < 2:
                        keys = [(0, None), (1, None)]
                    else:
                        keys = [(0, None), (1, None)]
                        if qb_ - 1 >= 2:
                            keys.append((qb_ - 1, 0))
                        keys.append((qb_, None))
                        if qb_ + 1 < 34:
                            keys.append((qb_ + 1, 1))
                    for kv in range(2):
                        pts = []
                        for (kblk, mtype) in keys:
                            ps_, psb_ = P.psum()
                            if mtype is not None:
                                op("pe", lambda e: e.matmul(ps_[:], idb16[:], mk[:, mtype, :], start=True, stop=False), r=[mb], w=[psb_])
                            op("pe", lambda e: e.matmul(ps_[:].rearrange("p (g q) -> p g q", g=4), kT[:, kv, kblk * 128:(kblk + 1) * 128],
                                                        qT[kv][:, :, qb_ * 128:(qb_ + 1) * 128], start=(mtype is None), stop=True),
                               r=[ktb, qtb[kv]], w=[psb_])
                            p_, pb_ = PT[ipt % 10], ptb[ipt % 10]
                            ipt += 1
                            op("act", lambda e: e.activation(out=p_[:], in_=ps_[:], func=AF.Exp, scale=0.125), r=[psb_], w=[pb_])
                            pts.append((p_, pb_, kblk))
                        po, pob = P.psum()
                        for g in range(4):
                            for ki, (p_, pb_, kblk) in enumerate(pts):
                                op("pe", lambda e: e.matmul(po[:, g * 65:(g + 1) * 65], p_[:, g * 128:(g + 1) * 128], V[:, kblk, kv, :],
                                                            start=(ki == 0), stop=(ki == len(pts) - 1)), r=[pb_, vb], w=[pob])
                        pov = po[:, 0:260].rearrange("p (g e) -> p g e", g=4)
                        op("dve", lambda e: e.tensor_tensor(out=r_[:, kv * 4:(kv + 1) * 4], in0=pov[:, :, 64], in1=esk[:, kv * 4:(kv + 1) * 4], op=ALU.add),
                           r=[pob, mb], w=[rb_])
                        op("dve", lambda e: e.reciprocal(r_[:, kv * 4:(kv + 1) * 4], r_[:, kv * 4:(kv + 1) * 4]), r=[rb_], w=[rb_])
                        for g in range(4):
                            h = kv * 4 + g
                            op("act" if g % 2 else "dve",
                               (lambda e: e.activation(out=o_[:, h * 64:(h + 1) * 64], in_=po[:, g * 65:g * 65 + 64], func=AF.Copy, scale=r_[:, h:h + 1]))
                               if g % 2 else
                               (lambda e: e.tensor_scalar(out=o_[:, h * 64:(h + 1) * 64], in0=po[:, g * 65:g * 65 + 64], scalar1=r_[:, h:h + 1], scalar2=None,
                                                          op0=ALU.mult)),
                               r=[pob, rb_], w=[ob_])
                    y_, yb_ = yt[qi % 2], ytb[qi % 2]
                    pt, pb = P.psum()
                    for c in range(4):
                        op("pe", lambda e: e.transpose(pt[:, c * 128:(c + 1) * 128], o_[:, c * 128:(c + 1) * 128], ident[:]), r=[ob_, cb], w=[pb])
                    op("act", lambda e: e.activation(out=y_[:], in_=pt[:].rearrange("p (c t) -> p c t", c=4), func=AF.Copy), r=[pb], w=[yb_])
                    dma("sp", YS[512:1024, qb_ * 128:(qb_ + 1) * 128].rearrange("(c p) t -> p c t", p=128), y_[:], r=[yb_], w=ysb)
                kb.barrier()
        with contextlib.ExitStack() as es, P.scope("odd_out"):
            proj_residual(es, l, 2, odout_b[j], odoutb[j], 8, YS, ysb, skip_ctx)
            kb.barrier()

    def proj_tm(W, wbuf, c0, ncol, U, ub, wt, wtb, consume):
        dma("sp", wt[:, :, 0:ncol], W[:, c0:c0 + ncol].rearrange("(k p) n -> p k n", p=128), r=[wbuf], w=[wtb])
        for blk in range(34):
            ti = 0 if blk < 2 else 1 + (blk - 2) // 4
            pt, pb = P.psum()
            for k in range(8):
                op("pe", lambda e: e.matmul(pt[:, 0:ncol], U[:, k, blk * 128:(blk + 1) * 128], wt[:, k, 0:ncol], start=(k == 0), stop=(k == 7)),
                   r=[wtb, ub[ti]], w=[pb])
            consume(blk, pt, pb)

    def even_mixer(l, skip_ctx):
        j = l // 2
        lam_init = 0.8 - 0.6 * math.exp(-0.3 * l)
        W = evin_b[j]
        wbuf = evinb[j]
        with contextlib.ExitStack() as es:
            U = P.sb(es, "U", [128, 8, NT], BF16)
            ub = [Buf(f"U{t}") for t in range(9)]
            with contextlib.ExitStack() as es2, P.scope("mix_norm"):
                norm_stage(es2, l, AMX, 0, U, ub)
                kb.barrier()
            with contextlib.ExitStack() as es2, P.scope("ev_qk"):
                wt = [P.sb(es2, f"aw{i}", [128, 8, 128], BF16) for i in range(2)]
                wtb = [Buf() for _ in range(2)]
                cosT = P.sb(es2, "cosT", [128, LT], F32)
                sinT = P.sb(es2, "sinT", [128, LT], F32)
                permT = P.sb(es2, "permT", [128, 128], F32)
                tb = Buf()
                dma("sp", cosT[:], cos_in[:, :], w=[tb])
                dma("sp", sinT[:], sin_in[:, :], w=[tb])
                dma("sp", permT[:], perm_in[:, :], w=[tb])
                raw = [P.sb(es2, f"raw{i}", [128, NT], F32) for i in range(2)]
                rawb = [Buf() for _ in range(2)]
                tmp = P.sb(es2, "rtmp", [128, 512], F32); tmpb = Buf()
                ob = [P.sb(es2, f"rob{i}", [128, NT], BF16) for i in range(2)]
                obb = [Buf() for _ in range(2)]
                for c in range(8):
                    proj_fm(W, wbuf, c * 128, U, ub, raw[c % 2], rawb[c % 2], wt[c % 2], wtb[c % 2])
                    if c < 4:
                        rows = [QA[2 * c + hh] for hh in range(2)]
                        dbuf_ = qab[0]
                    else:
                        rows = [KA[2 * (c - 4) + hh] for hh in range(2)]
                        dbuf_ = kab[0]
                    rope_store(es2, raw[c % 2], rawb[c % 2], cosT, sinT, permT, tb, rows, dbuf_, tmp, tmpb, ob[c % 2], obb[c % 2])
                kb.barrier()
            with contextlib.ExitStack() as es2, P.scope("ev_tm"):
                wt = [P.sb(es2, f"tw{i}", [128, 8, 512], BF16) for i in range(2)]
                wtb = [Buf() for _ in range(2)]
                vo = [P.sb(es2, f"vo{i}", [128, 512], BF16) for i in range(2)]
                vob = [Buf() for _ in range(2)]
                go = [P.sb(es2, f"go{i}", [128, 512], F32) for i in range(2)]
                gob = [Buf() for _ in range(2)]
                bgo = [P.sb(es2, f"bgo{i}", [128, 16], F32) for i in range(2)]
                bgob = [Buf() for _ in range(2)]
                tq = [P.sb(es2, f"tq{i}", [128, 8], F32) for i in range(4)]
                tqb = Buf()
                alog = P.sb(es2, "alog", [128, 8], F32)
                dtb = P.sb(es2, "dtb", [128, 8], F32)
                cb2 = Buf()
                dma("sp", alog[:], gdn_a_log[j], w=[cb2])
                dma("sp", dtb[:], gdn_dt_bias[j], w=[cb2])
                op("act", lambda e: e.activation(out=alog[:], in_=alog[:], func=AF.Exp), r=[cb2], w=[cb2])
                op("dve", lambda e: e.tensor_scalar(out=alog[:], in0=alog[:], scalar1=-1.0, scalar2=None, op0=ALU.mult), r=[cb2], w=[cb2])

                def c_va(blk, pt, pb):
                    o_, ob_ = vo[blk % 2], vob[blk % 2]
                    op("act", lambda e: e.activation(out=o_[:], in_=pt[:], func=AF.Copy), r=[pb], w=[ob_])
                    dma("sp", VA[blk * 128:(blk + 1) * 128, :], o_[:], r=[ob_], w=vab)

                def c_gate(blk, pt, pb):
                    o_, ob_ = go[blk % 2], gob[blk % 2]
                    op("act", lambda e: e.activation(out=o_[:], in_=pt[:], func=AF.Silu), r=[pb], w=[ob_])
                    dma("sp", GG[blk * 128:(blk + 1) * 128, :], o_[:], r=[ob_], w=ggb)

                def c_bg(blk, pt, pb):
                    o_, ob_ = bgo[blk % 2], bgob[blk % 2]
                    x_, ax_, l_, mx_ = tq
                    op("act", lambda e: e.activation(out=o_[:, 0:8], in_=pt[:, 0:8], func=AF.Sigmoid), r=[pb], w=[ob_])
                    op("dve", lambda e: e.tensor_tensor(out=x_[:], in0=pt[:, 8:16], in1=dtb[:], op=ALU.add), r=[pb, cb2, tqb], w=[tqb])
                    op("act", lambda e: e.activation(out=ax_[:], in_=x_[:], func=AF.Abs), r=[tqb], w=[tqb])
                    op("act", lambda e: e.activation(out=l_[:], in_=ax_[:], func=AF.Exp, scale=-1.0), r=[tqb], w=[tqb])
                    op("act", lambda e: e.activation(out=l_[:], in_=l_[:], func=AF.Ln, bias=ones[:, 0:1], scale=1.0), r=[tqb, cb], w=[tqb])
                    op("dve", lambda e: e.tensor_scalar(out=mx_[:], in0=x_[:], scalar1=0.0, scalar2=None, op0=ALU.max), r=[tqb], w=[tqb])
                    op("dve", lambda e: e.tensor_tensor(out=mx_[:], in0=mx_[:], in1=l_[:], op=ALU.add), r=[tqb], w=[tqb])
                    op("dve", lambda e: e.tensor_tensor(out=o_[:, 8:16], in0=mx_[:], in1=alog[:], op=ALU.mult), r=[tqb, cb2, ob_], w=[ob_])
                    dma("sp", BG[blk * 128:(blk + 1) * 128, :], o_[:], r=[ob_], w=bgb)

                proj_tm(W, wbuf, 1024, 512, U, ub, wt[0], wtb[0], c_va)
                proj_tm(W, wbuf, 3072, 512, U, ub, wt[1], wtb[1], c_gate)
                proj_tm(W, wbuf, 3584, 16, U, ub, wt[0], wtb[0], c_bg)
                kb.barrier()
            with contextlib.ExitStack() as es2, P.scope("ev_gqkv"):
                wt = [P.sb(es2, f"gw{i}", [128, 8, 128], BF16) for i in range(2)]
                wtb = [Buf() for _ in range(2)]
                R1 = [P.sb(es2, f"gR{i}", [128, NT], F32) for i in range(2)]
                r1b = [Buf() for _ in range(2)]
                C1 = [P.sb(es2, f"gC{i}", [128, NT], F32) for i in range(2)]
                c1b = [Buf() for _ in range(2)]
                S1 = P.sb(es2, "gS", [128, NT], F32); s1b = Buf()
                CB = [P.sb(es2, f"gCB{i}", [128, NT], BF16) for i in range(2)]
                cbb = [Buf() for _ in range(2)]
                rsq = [P.sb(es2, f"grs{i}", [128, 512], F32) for i in range(2)]
                rsqb = [Buf() for _ in range(2)]
                gc = P.sb(es2, "gc", [128, 12, 4], F32); gcb = Buf()
                dma("sp", gc[:], gdn_conv[j], w=[gcb])
                segs = [(0, CT), (CT, NT)]
                it = 0
                for kind in range(3):
                    for h in range(4):
                        cidx = kind * 4 + h
                        r_, rb_ = R1[it % 2], r1b[it % 2]
                        c_, cb_ = C1[it % 2], c1b[it % 2]
                        o16, o16b = CB[it % 2], cbb[it % 2]
                        proj_fm(W, wbuf, 1536 + cidx * 128, U, ub, r_, rb_, wt[it % 2], wtb[it % 2])
                        it += 1
                        op("act", lambda e: e.activation(out=c_[:], in_=r_[:], func=AF.Copy, scale=gc[:, cidx, 2:3]), r=[rb_, gcb], w=[cb_])
                        for (a, b) in segs:
                            for tap, off in ((0, -2), (1, -1), (3, 1)):
                                if off < 0:
                                    oa, ob_, ia, ib_ = a - off, b, a, b + off
                                else:
                                    oa, ob_, ia, ib_ = a, b - off, a + off, b
                                op("dve", lambda e: e.scalar_tensor_tensor(out=c_[:, oa:ob_], in0=r_[:, ia:ib_], scalar=gc[:, cidx, tap:tap + 1], in1=c_[:, oa:ob_],
                                                                           op0=ALU.mult, op1=ALU.add), r=[rb_, gcb, cb_], w=[cb_])
                        op("act", lambda e: e.activation(out=c_[:], in_=c_[:], func=AF.Silu), r=[cb_], w=[cb_])
                        if kind < 2:
                            op("act", lambda e: e.activation(out=S1[:], in_=c_[:], func=AF.Square), r=[cb_], w=[s1b])
                            for ti, (t0, T) in enumerate(TILES):
                                pt, pb = P.psum()
                                op("pe", lambda e: e.matmul(pt[:, :T], ones[:], S1[:, t0:t0 + T], start=True, stop=True), r=[s1b, cb], w=[pb])
                                q_, qb_ = rsq[ti % 2], rsqb[ti % 2]
                                op("act", lambda e: e.activation(out=q_[:, :T], in_=pt[:, :T], func=AF.Sqrt, bias=epsb[:], scale=1.0), r=[pb, cb], w=[qb_])
                                op("dve", lambda e: e.reciprocal(q_[:, :T], q_[:, :T]), r=[qb_], w=[qb_])
                                if kind == 0:
                                    op("dve", lambda e: e.scalar_tensor_tensor(out=o16[:, t0:t0 + T], in0=c_[:, t0:t0 + T], scalar=128.0 ** -0.5, in1=q_[:, :T],
                                                                               op0=ALU.mult, op1=ALU.mult), r=[cb_, qb_], w=[o16b])
                                else:
                                    op("dve", lambda e: e.tensor_tensor(out=c_[:, t0:t0 + T], in0=c_[:, t0:t0 + T], in1=q_[:, :T], op=ALU.mult), r=[cb_, qb_], w=[cb_])
                                    op("act", lambda e: e.activation(out=o16[:, t0:t0 + T], in_=c_[:, t0:t0 + T], func=AF.Copy), r=[cb_], w=[o16b])
                        else:
                            op("dve", lambda e: e.tensor_copy(out=o16[:], in_=c_[:]), r=[cb_], w=[o16b])
                        dma("sp", GQ[kind, h], o16[:], r=[o16b], w=gqb)
                        if kind == 1:
                            dma("sp", GQK[h], c_[:], r=[cb_], w=gqb)
                kb.barrier()
        with contextlib.ExitStack() as es, P.scope("ev_attn"):
            P.psr = (4, 8)
            acc = [(P.ps[i], P.psb[i]) for i in range(4)]
            kT = P.sb(es, "kT", [64, 2, NT], BF16); ktb = Buf()
            qT = P.sb(es, "qT", [64, 2, NT], BF16); qtb = Buf()
            V = P.sb(es, "V", [128, 34, 128], BF16); vb = Buf()
            PT = [P.sb(es, f"PT{i}", [128, 512], BF16) for i in range(6)]
            ptb = [Buf() for _ in range(6)]
            onesb = P.sb(es, "onesb", [128, 128], BF16)
            rden = P.sb(es, "rden", [128, 512], F32); rdb = Buf()
            t0b_ = P.sb(es, "t0b", [128, 512], F32); t0bb = Buf()
            oa = P.sb(es, "oa", [128, 512], F32); oab = Buf()
            sqb_ = P.sb(es, "sqb", [128, 512], F32); sqbb = Buf()
            rsd = P.sb(es, "rsd", [128, 512], F32); rsdb = Buf()
            yT = [P.sb(es, f"yT{i}", [128, 512], BF16) for i in range(2)]
            ytb = [Buf() for _ in range(2)]
            lv = P.sb(es, "lv", [128, 256], F32)
            lp = P.sb(es, "lp", [128, 2, 64], F32)
            lsum = P.sb(es, "lsum", [128, 2], F32)
            neglam = P.sb(es, "neglam", [128, 1], F32)
            sg = P.sb(es, "sg", [128, 1], F32)
            eps128 = P.sb(es, "eps128", [128, 1], F32)
            lb = Buf()
            dma("sp", lv[:], diff_lambda[j], w=[lb])
            dma("sp", sg[:], diff_subln[j], w=[lb])
            op("dve", lambda e: e.tensor_copy(out=onesb[:], in_=ones[:]), r=[cb], w=[lb])
            op("dve", lambda e: e.tensor_tensor(out=lp[:, 0, :], in0=lv[:, 0:64], in1=lv[:, 64:128], op=ALU.mult), r=[lb], w=[lb])
            op("dve", lambda e: e.tensor_tensor(out=lp[:, 1, :], in0=lv[:, 128:192], in1=lv[:, 192:256], op=ALU.mult), r=[lb], w=[lb])
            op("dve", lambda e: e.tensor_reduce(out=lsum[:], in_=lp[:], axis=AX.X, op=ALU.add), r=[lb], w=[lb])
            op("act", lambda e: e.activation(out=lsum[:], in_=lsum[:], func=AF.Exp), r=[lb], w=[lb])
            op("dve", lambda e: e.tensor_tensor(out=neglam[:], in0=lsum[:, 1:2], in1=lsum[:, 0:1], op=ALU.subtract), r=[lb], w=[lb])
            op("dve", lambda e: e.tensor_scalar(out=neglam[:], in0=neglam[:], scalar1=-lam_init, scalar2=None, op0=ALU.add), r=[lb], w=[lb])
            op("dve", lambda e: e.tensor_scalar(out=sg[:], in0=sg[:], scalar1=(1.0 - lam_init), scalar2=None, op0=ALU.mult), r=[lb], w=[lb])
            op("dve", lambda e: e.memset(eps128[:], EPS), w=[lb])
            ipt = 0
            iy = 0
            for h in range(4):
                dma("sp", kT[:], KA[2 * h:2 * h + 2].rearrange("m d t -> d m t"), r=kab, w=[ktb])
                dma("sp", qT[:], QA[2 * h:2 * h + 2].rearrange("m d t -> d m t"), r=qab, w=[qtb])
                dma("sp", V[:], VA[:, h * 128:(h + 1) * 128].rearrange("(b p) e -> p b e", p=128), r=vab, w=[vb])
                for ti, (t0, T) in enumerate(TILES):
                    if ti == 0 and skip_ctx:
                        continue
                    keys = [0, 1] if ti == 0 else list(range(34))
                    for m in range(2):
                        A_, Ab_ = acc[2 * m]
                        B_, Bb_ = acc[2 * m + 1]
                        LA = 2
                        nk = len(keys)
                        pend = []
                        for ki in range(nk + LA):
                            if ki < nk:
                                kblk = keys[ki]
                                s_, sb_ = P.psum()
                                op("pe", lambda e: e.matmul(s_[:, :T], kT[:, m, kblk * 128:(kblk + 1) * 128], qT[:, m, t0:t0 + T], start=True, stop=True),
                                   r=[ktb, qtb], w=[sb_])
                                p_, pb_ = PT[ipt % 6], ptb[ipt % 6]
                                ipt += 1
                                op("act", lambda e: e.activation(out=p_[:, :T], in_=s_[:, :T], func=AF.Exp, scale=0.125), r=[sb_], w=[pb_])
                                pend.append((p_, pb_, kblk))
                            if ki >= LA:
                                kj = ki - LA
                                p2, pb2, kb2 = pend[kj]
                                op("pe", lambda e: e.matmul(A_[:, :T], V[:, kb2, :], p2[:, :T], start=(kj == 0), stop=(kj == nk - 1)), r=[pb2, vb], w=[Ab_])
                                op("pe", lambda e: e.matmul(B_[:, :T], onesb[:], p2[:, :T], start=(kj == 0), stop=(kj == nk - 1)), r=[pb2, lb], w=[Bb_])
                        op("dve", lambda e: e.reciprocal(rden[:, :T], B_[:, :T]), r=[Bb_], w=[rdb])
                        if m == 0:
                            op("dve", lambda e: e.tensor_tensor(out=t0b_[:, :T], in0=A_[:, :T], in1=rden[:, :T], op=ALU.mult), r=[Ab_, rdb], w=[t0bb])
                        else:
                            op("dve", lambda e: e.tensor_tensor(out=oa[:, :T], in0=A_[:, :T], in1=rden[:, :T], op=ALU.mult), r=[Ab_, rdb], w=[oab])
                            op("dve", lambda e: e.scalar_tensor_tensor(out=oa[:, :T], in0=oa[:, :T], scalar=neglam[:, 0:1], in1=t0b_[:, :T], op0=ALU.mult, op1=ALU.add),
                               r=[oab, lb, t0bb], w=[oab])
                    y_, yb_ = yT[iy % 2], ytb[iy % 2]
                    iy += 1
                    op("act", lambda e: e.activation(out=sqb_[:, :T], in_=oa[:, :T], func=AF.Square), r=[oab], w=[sqbb])
                    pt, pb = P.psum()
                    op("pe", lambda e: e.matmul(pt[:, :T], ones[:], sqb_[:, :T], start=True, stop=True), r=[sqbb, cb], w=[pb])
                    op("act", lambda e: e.activation(out=rsd[:, :T], in_=pt[:, :T], func=AF.Sqrt, bias=eps128[:], scale=1.0 / 128), r=[pb, lb], w=[rsdb])
                    op("dve", lambda e: e.reciprocal(rsd[:, :T], rsd[:, :T]), r=[rsdb], w=[rsdb])
                    op("dve", lambda e: e.scalar_tensor_tensor(out=y_[:, :T], in0=oa[:, :T], scalar=sg[:, 0:1], in1=rsd[:, :T], op0=ALU.mult, op1=ALU.mult),
                       r=[oab, lb, rsdb], w=[yb_])
                    dma("sp", YS[h * 128:(h + 1) * 128, t0:t0 + T], y_[:, :T], r=[yb_], w=ysb)
            P.psr = (0, 8)
            kb.barrier()
        with contextlib.ExitStack() as es, P.scope("ev_gdn"):
            gdn_core(es, l, j, skip_ctx)
            kb.barrier()
        with contextlib.ExitStack() as es, P.scope("ev_out"):
            proj_residual(es, l, 2, evout_b[j], evoutb[j], 8, YS, ysb, skip_ctx)
            kb.barrier()

    def gdn_core(es0, l, j, skip_ctx):
        NB = NT // 128
        for hg in range(2):
          chains = [(h, d, sub) for h in (2 * hg, 2 * hg + 1) for d in range(2) for sub in range(2)]
          NCH = len(chains)
          with contextlib.ExitStack() as es:
              gm = P.sb(es, "gmask", [128, 2, 3, 128], F32); gmb = Buf()
              gaux = P.sb(es, "gaux", [128, 258], F32)
              dma("sp", gm[:], gmask_in[:, :, :, :], w=[gmb])
              dma("sp", gaux[:], gaux_in[:, :], w=[gmb])
              idb = P.sb(es, "idb", [128, 128], BF16)
              op("dve", lambda e: e.tensor_copy(out=idb[:], in_=ident[:]), r=[cb], w=[gmb])
              qkv = P.sb(es, "qkv", [128, 3, 2, NT], BF16); qkvb = Buf()
              kF = P.sb(es, "kF", [128, 2, NT], F32)
              for kind in range(3):
                  dma("sp", qkv[:, kind, :, :], GQ[kind, 2 * hg:2 * hg + 2].rearrange("h p t -> p h t"), r=gqb, w=[qkvb])
              dma("sp", kF[:], GQK[2 * hg:2 * hg + 2].rearrange("h p t -> p h t"), r=gqb, w=[qkvb])
              BGt = P.sb(es, "BGt", [128, NB, 16], F32); bgtb = Buf()
              dma("sp", BGt[:], BG.rearrange("(n p) f -> p n f", p=128), r=bgb, w=[bgtb])
              NBt = P.sb(es, "NBt", [128, NB, 8], F32)
              op("dve", lambda e: e.tensor_scalar(out=NBt[:], in0=BGt[:, :, 0:8], scalar1=-1.0, scalar2=None, op0=ALU.mult), r=[bgtb], w=[bgtb])

              def ch(name, shape, dt):
                  return [P.sb(es, f"{name}{c}", shape, dt) for c in range(NCH)]
              g2 = ch("g2", [128, 2, 128], F32)
              dT = ch("dT", [128, 128], F32)
              Nb = ch("Nb", [128, 2, 128], F32)
              Wk = ch("Wk", [128, 2, 2, 128], F32)
              MTk = ch("MTk", [128, 2, 128], F32)
              Xb = ch("Xb", [128, 128], BF16)
              AT = ch("AT", [128, 128], BF16)
              ubf = ch("ubf", [128, 128], F32)
              kg = ch("kg", [128, 3, 128], BF16)
              vtok = ch("vtok", [128, 128], BF16)
              wT = ch("wT", [128, 128], BF16)
              vn = ch("vn", [128, 128], BF16)
              t1 = ch("t1", [128, 128], F32)
              ot = ch("ot", [128, 128], F32)
              S = ch("S", [128, 128], F32)
              Sb = ch("Sb", [128, 128], BF16)
              cbuf = [Buf(f"chain{c}") for c in range(NCH)]
              sbuf_ = [Buf(f"S{c}") for c in range(NCH)]
              vnb = [Buf(f"vn{c}") for c in range(NCH)]
              otb = [Buf(f"ot{c}") for c in range(NCH)]
              for c in range(NCH):
                  op("dve", lambda e: e.memset(S[c - c % 2][:], 0.0), w=[sbuf_[c - c % 2]])
                  op("dve", lambda e: e.memset(Sb[c - c % 2][:], 0.0), w=[sbuf_[c - c % 2]])
                  op("dve", lambda e: e.memset(vn[c - c % 2][:], 0.0), w=[vnb[c - c % 2]])
              order = [list(range(NB)), [1, 0] + list(range(NB - 1, 1, -1))]
              SCA = P.sb(es, "SCA", [128, 2, 6, NB, 2], F32); scab = Buf()
              for d in range(2):
                  pa, pab = P.psum()
                  grhs = BGt[:, :, 8 + d * 4 + 2 * hg:8 + d * 4 + 2 * hg + 2]
                  for kind, lh in enumerate((gm[:, d, 0, :], gm[:, d, 1, :], gaux[:, 0:128], gaux[:, 128:256])):
                      op("pe", lambda e: e.matmul(pa[:, kind * 2 * NB:(kind + 1) * 2 * NB].rearrange("p (n f) -> p n f", f=2), lh, grhs, start=True, stop=True),
                         r=[gmb, bgtb], w=[pab])
                  op("act", lambda e: e.activation(out=SCA[:, d, 0:4, :, :], in_=pa[:, 0:8 * NB].rearrange("p (k n f) -> p k n f", k=4, f=2), func=AF.Exp), r=[pab], w=[scab])
                  for a_ in range(2):
                      op("dve", lambda e: e.tensor_scalar(out=SCA[:, d, 4 + a_, :, :], in0=SCA[:, d, 1, :, :], scalar1=gaux[:, 256 + a_:257 + a_], scalar2=None, op0=ALU.mult),
                         r=[gmb, scab], w=[scab])

              class Reg:
                  def __init__(self):
                      self.k = 4
                  def get(self):
                      if self.k == 4:
                          self.bank = P.psum()
                          self.k = 0
                      r = (self.bank[0][:, self.k * 128:(self.k + 1) * 128], self.bank[1], self.bank[0], self.k * 128)
                      self.k += 1
                      return r
              rg = Reg()

              for i in range(0, NB, 2):
                  blk = [order[d][i + sub] for (h, d, sub) in chains]
                  for c, (h, d, sub) in enumerate(chains):
                      n = blk[c]
                      gcol = BGt[:, n, 8 + d * 4 + h:8 + d * 4 + h + 1]
                      op("dve", lambda e: e.tensor_scalar(out=g2[c][:, 0, :], in0=ones[:], scalar1=gcol, scalar2=None, op0=ALU.mult), r=[bgtb, cb, cbuf[c]], w=[cbuf[c]])
                      op("dve", lambda e: e.tensor_scalar(out=g2[c][:, 1, :], in0=gm[:, d, 0, :], scalar1=gcol, scalar2=-1.0, op0=ALU.mult, op1=ALU.mult),
                         r=[bgtb, gmb, cbuf[c]], w=[cbuf[c]])
                  regD = []
                  for c, (h, d, sub) in enumerate(chains):
                      rv, rb_, _, _ = rg.get()
                      regD.append((rv, rb_))
                      op("pe", lambda e: e.matmul(rv, g2[c][:, 0, :], gm[:, d, 0, :], start=True, stop=False), r=[cbuf[c], gmb], w=[rb_])
                      op("pe", lambda e: e.matmul(rv, g2[c][:, 1, :], ones[:], start=False, stop=False), r=[cbuf[c], cb], w=[rb_])
                      op("pe", lambda e: e.matmul(rv, ident[:], gm[:, d, 2, :], start=False, stop=True), r=[gmb, cb], w=[rb_])
                  for c in range(NCH):
                      rv, rb_ = regD[c]
                      op("act", lambda e: e.activation(out=dT[c][:], in_=rv, func=AF.Exp), r=[rb_, cbuf[c]], w=[cbuf[c]])
                  regF = []
                  for c, (h, d, sub) in enumerate(chains):
                      n = blk[c]
                      kT_ = qkv[:, 1, h % 2, n * 128:(n + 1) * 128]
                      kF_ = kF[:, h % 2, n * 128:(n + 1) * 128]
                      qT_ = qkv[:, 0, h % 2, n * 128:(n + 1) * 128]
                      vT_ = qkv[:, 2, h % 2, n * 128:(n + 1) * 128]
                      bank = P.psum()
                      pt, pb = bank
                      regF.append(bank)
                      op("pe", lambda e: e.matmul(pt[:, 0:128], kF_, kF_, start=True, stop=True), r=[qkvb], w=[pb])
                      op("pe", lambda e: e.matmul(pt[:, 128:256], kT_, qT_, start=True, stop=True), r=[qkvb], w=[pb])
                      op("pe", lambda e: e.matmul(pt[:, 256:384], kT_, idb[:], start=True, stop=True), r=[qkvb, gmb], w=[pb])
                      op("pe", lambda e: e.matmul(pt[:, 384:512], vT_, idb[:], start=True, stop=True), r=[qkvb, gmb], w=[pb])
                      bcol = BGt[:, n, d * 4 + h:d * 4 + h + 1]
                      nbcol = NBt[:, n, d * 4 + h:d * 4 + h + 1]
                      op("dve", lambda e: e.tensor_tensor(out=AT[c][:], in0=pt[:, 128:256], in1=dT[c][:], op=ALU.mult), r=[pb, cbuf[c]], w=[cbuf[c]])
                      op("dve", lambda e: e.tensor_tensor(out=dT[c][:], in0=dT[c][:], in1=ident[:], op=ALU.subtract), r=[cbuf[c], cb], w=[cbuf[c]])
                      op("dve", lambda e: e.scalar_tensor_tensor(out=g2[c][:, 0, :], in0=pt[:, 0:128], scalar=nbcol, in1=dT[c][:], op0=ALU.mult, op1=ALU.mult),
                         r=[pb, bgtb, cbuf[c]], w=[cbuf[c]])
                      op("act", lambda e: e.activation(out=Nb[c][:, 0, :], in_=g2[c][:, 0, :], func=AF.Copy, scale=-1.0), r=[cbuf[c]], w=[cbuf[c]])
                      op("dve", lambda e: e.tensor_tensor(out=Wk[c][:, 1, 1, :], in0=g2[c][:, 0, :], in1=ident[:], op=ALU.add), r=[cbuf[c], cb], w=[cbuf[c]])
                      op("act", lambda e: e.activation(out=kg[c][:, 0, :], in_=pt[:, 256:384], func=AF.Copy, scale=SCA[:, d, 0, n, h % 2:h % 2 + 1]), r=[pb, cbuf[c], scab], w=[cbuf[c]])
                      op("act", lambda e: e.activation(out=kg[c][:, 1, :], in_=pt[:, 256:384], func=AF.Copy, scale=SCA[:, d, 4, n, h % 2:h % 2 + 1]), r=[pb, cbuf[c], scab], w=[cbuf[c]])
                      op("act", lambda e: e.activation(out=kg[c][:, 2, :], in_=pt[:, 256:384], func=AF.Copy, scale=SCA[:, d, 5, n, h % 2:h % 2 + 1]), r=[pb, cbuf[c], scab], w=[cbuf[c]])
                      op("dve", lambda e: e.tensor_copy(out=vtok[c][:], in_=pt[:, 384:512]), r=[pb, cbuf[c]], w=[cbuf[c]])
                  regH = []
                  for c in range(NCH):
                      rv, rb_, _, _ = rg.get()
                      regH.append((rv, rb_))
                      op("pe", lambda e: e.transpose(rv, Nb[c][:, 0, :], ident[:]), r=[cbuf[c], cb], w=[rb_])
                  for c in range(NCH):
                      rv, rb_ = regH[c]
                      op("act" if c % 2 else "dve",
                         (lambda e: e.activation(out=Nb[c][:, 1, :], in_=rv, func=AF.Copy)) if c % 2 else (lambda e: e.tensor_copy(out=Nb[c][:, 1, :], in_=rv)),
                         r=[rb_, cbuf[c]], w=[cbuf[c]])
                  for k in range(0, 6):
                      regS = []
                      for c in range(NCH):
                          pt, pb = P.psum()
                          regS.append((pt, pb))
                          if k == 0:
                              M_, MT_ = Nb[c][:, 0, :], Nb[c][:, 1, :]
                              op("pe", lambda e: e.matmul(pt[:, 0:128], MT_, M_, start=True, stop=True), r=[cbuf[c]], w=[pb])
                              op("pe", lambda e: e.matmul(pt[:, 256:384], M_, MT_, start=True, stop=True), r=[cbuf[c]], w=[pb])
                          elif k < 5:
                              pp = k % 2
                              op("pe", lambda e: e.matmul(pt[:, 0:256], MTk[c][:, pp, :], Wk[c][:, pp, :, :], start=True, stop=True), r=[cbuf[c]], w=[pb])
                              op("pe", lambda e: e.matmul(pt[:, 256:384], Wk[c][:, pp, 0, :], MTk[c][:, pp, :], start=True, stop=True), r=[cbuf[c]], w=[pb])
                          else:
                              op("pe", lambda e: e.matmul(pt[:, 128:256], MTk[c][:, 1, :], Wk[c][:, 1, 1, :], start=True, stop=True), r=[cbuf[c]], w=[pb])
                      for c in range(NCH):
                          pt, pb = regS[c]
                          np_ = (k + 1) % 2
                          if k < 5:
                              op("act", lambda e: e.activation(out=Wk[c][:, np_, 0, :], in_=pt[:, 0:128], func=AF.Copy), r=[pb, cbuf[c]], w=[cbuf[c]])
                              op("act" if c % 2 else "dve",
                                 (lambda e: e.activation(out=MTk[c][:, np_, :], in_=pt[:, 256:384], func=AF.Copy)) if c % 2 else
                                 (lambda e: e.tensor_copy(out=MTk[c][:, np_, :], in_=pt[:, 256:384])), r=[pb, cbuf[c]], w=[cbuf[c]])
                          if 1 <= k < 5:
                              op("dve", lambda e: e.tensor_tensor(out=Wk[c][:, np_, 1, :], in0=Wk[c][:, k % 2, 1, :], in1=pt[:, 128:256], op=ALU.add), r=[pb, cbuf[c]], w=[cbuf[c]])
                          if k == 5:
                              op("dve", lambda e: e.tensor_tensor(out=Xb[c][:], in0=Wk[c][:, 1, 1, :], in1=pt[:, 128:256], op=ALU.add), r=[pb, cbuf[c]], w=[cbuf[c]])
                  regJ = []
                  for c in range(NCH):
                      bank = P.psum() if c % 2 == 0 else bank
                      pt, pb = bank
                      o0 = (c % 2) * 256
                      regJ.append((pt, pb, o0))
                      op("pe", lambda e: e.matmul(pt[:, o0:o0 + 128], Xb[c][:], vtok[c][:], start=True, stop=True), r=[cbuf[c]], w=[pb])
                      op("pe", lambda e: e.matmul(pt[:, o0 + 128:o0 + 256], kg[c][:, 0, :], Xb[c][:], start=True, stop=True), r=[cbuf[c]], w=[pb])
                  for c, (h, d, sub) in enumerate(chains):
                      pt, pb, o0 = regJ[c]
                      n = blk[c]
                      bcol = BGt[:, n, d * 4 + h:d * 4 + h + 1]
                      op("dve", lambda e: e.tensor_scalar(out=ubf[c][:], in0=pt[:, o0:o0 + 128], scalar1=bcol, scalar2=None, op0=ALU.mult), r=[pb, bgtb, cbuf[c]], w=[cbuf[c]])
                      op("act", lambda e: e.activation(out=wT[c][:], in_=pt[:, o0 + 128:o0 + 256], func=AF.Copy), r=[pb, cbuf[c]], w=[cbuf[c]])
                  for sub_s, step_i in ((0, 0), (0, 1), (1, 0), (1, 1)):
                      regW = {}
                      for c, (h, d, sub) in enumerate(chains):
                          if sub != sub_s:
                              continue
                          a = step_i if d == 0 else 1 - step_i
                          lo, hi = a * 64, (a + 1) * 64
                          rv, rb_, bank_t, off = rg.get()
                          regW[c] = (bank_t, rb_, off)
                          op("pe", lambda e: e.matmul(bank_t[lo:hi, off:off + 128], wT[c][:, lo:hi], Sb[c - c % 2][:], start=True, stop=True), r=[cbuf[c], sbuf_[c - c % 2]], w=[rb_])
                      for c, (h, d, sub) in enumerate(chains):
                          if sub != sub_s:
                              continue
                          a = step_i if d == 0 else 1 - step_i
                          lo, hi = a * 64, (a + 1) * 64
                          n = blk[c]
                          bank_t, rb_, off = regW[c]
                          nbcol = NBt[lo:hi, n, d * 4 + h:d * 4 + h + 1]
                          op("dve", lambda e: e.scalar_tensor_tensor(out=vn[c - c % 2][lo:hi, :], in0=bank_t[lo:hi, off:off + 128], scalar=nbcol, in1=ubf[c][lo:hi, :],
                                                                     op0=ALU.mult, op1=ALU.add), r=[rb_, bgtb, cbuf[c]], w=[vnb[c - c % 2]])
                      regO = {}
                      for c, (h, d, sub) in enumerate(chains):
                          if sub != sub_s:
                              continue
                          a = step_i if d == 0 else 1 - step_i
                          lo, hi = a * 64, (a + 1) * 64
                          n = blk[c]
                          bank = P.psum()
                          pt, pb = bank
                          regO[c] = bank
                          qT_ = qkv[:, 0, h % 2, n * 128 + lo:n * 128 + hi]
                          op("pe", lambda e: e.matmul(pt[lo:hi, 0:128], qT_, Sb[c - c % 2][:], start=True, stop=True), r=[qkvb, sbuf_[c - c % 2]], w=[pb])
                          op("pe", lambda e: e.matmul(pt[lo:hi, 128:256], AT[c][:, lo:hi], vn[c - c % 2][:], start=True, stop=True), r=[cbuf[c], vnb[c - c % 2]], w=[pb])
                          op("pe", lambda e: e.matmul(pt[:, 256:384], kg[c][:, 1 + a, :], vn[c - c % 2][:], start=True, stop=True), r=[cbuf[c], vnb[c - c % 2]], w=[pb])
                      for c, (h, d, sub) in enumerate(chains):
                          if sub != sub_s:
                              continue
                          a = step_i if d == 0 else 1 - step_i
                          lo, hi = a * 64, (a + 1) * 64
                          pt, pb = regO[c]
                          op("dve", lambda e: e.scalar_tensor_tensor(out=S[c - c % 2][:], in0=S[c - c % 2][:], scalar=SCA[:, d, 2 + a, blk[c], h % 2:h % 2 + 1], in1=pt[:, 256:384], op0=ALU.mult, op1=ALU.add),
                             r=[pb, cbuf[c], sbuf_[c - c % 2]], w=[sbuf_[c - c % 2]])
                          op("act", lambda e: e.activation(out=Sb[c - c % 2][:], in_=S[c - c % 2][:], func=AF.Copy), r=[sbuf_[c - c % 2]], w=[sbuf_[c - c % 2]])
                          op("act", lambda e: e.activation(out=t1[c][lo:hi, :], in_=pt[lo:hi, 0:128], func=AF.Copy, scale=SCA[lo:hi, d, 0, blk[c], h % 2:h % 2 + 1]), r=[pb, cbuf[c], scab], w=[cbuf[c]])
                          op("dve", lambda e: e.tensor_tensor(out=ot[c][lo:hi, :], in0=t1[c][lo:hi, :], in1=pt[lo:hi, 128:256], op=ALU.add), r=[pb, cbuf[c]], w=[otb[c]])
                  for c, (h, d, sub) in enumerate(chains):
                      n = blk[c]
                      dma("sp", OBD[d, h, n * 128:(n + 1) * 128, :], ot[c][:], r=[otb[c]], w=obdb)
              kb.barrier()
        with contextlib.ExitStack() as es:
            gn = P.sb(es, "gnorm", [128, 1], F32); gnb = Buf()
            dma("sp", gn[:], gdn_norm[j], w=[gnb])
            eps_ = P.sb(es, "geps", [128, 1], F32)
            op("dve", lambda e: e.memset(eps_[:], EPS), w=[gnb])
            OB = [[P.sb(es, f"OB{i}_{d}", [128, NB, 128], F32) for d in range(2)] for i in range(2)]
            obb = [Buf() for _ in range(2)]
            GT = [P.sb(es, f"GT{i}", [128, NB, 128], F32) for i in range(2)]
            gtb = [Buf() for _ in range(2)]
            rst = [P.sb(es, f"rst{i}", [128, NB], F32) for i in range(2)]
            rstb = [Buf() for _ in range(2)]
            sqt = P.sb(es, "gsq", [128, 128], F32); sqtb = Buf()
            yo = [P.sb(es, f"gyo{i}", [128, 512], BF16) for i in range(2)]
            yob = [Buf() for _ in range(2)]
            ig = 0
            for h in range(4):
                o0_, o1_ = OB[h % 2]
                ob_ = obb[h % 2]
                g_, gb_ = GT[h % 2], gtb[h % 2]
                r_, rb_ = rst[h % 2], rstb[h % 2]
                dma("sp", o0_[:], OBD[0, h].rearrange("(n p) e -> p n e", p=128), r=obdb, w=[ob_])
                dma("sp", o1_[:], OBD[1, h].rearrange("(n p) e -> p n e", p=128), r=obdb, w=[ob_])
                dma("sp", g_[:], GG[:, h * 128:(h + 1) * 128].rearrange("(n p) e -> p n e", p=128), r=ggb, w=[gb_])
                op("dve", lambda e: e.tensor_tensor(out=o0_[:], in0=o0_[:], in1=o1_[:], op=ALU.add), r=[ob_], w=[ob_])
                for n in range(NB):
                    op("act", lambda e: e.activation(out=sqt[:], in_=o0_[:, n, :], func=AF.Square, accum_out=r_[:, n:n + 1]), r=[ob_], w=[sqtb, rb_])
                op("act", lambda e: e.activation(out=r_[:], in_=r_[:], func=AF.Sqrt, bias=eps_[:], scale=1.0 / 128), r=[rb_, gnb], w=[rb_])
                op("dve", lambda e: e.reciprocal(r_[:], r_[:]), r=[rb_], w=[rb_])
                for n0 in range(0, NB, 4):
                    nn = min(4, NB - n0)
                    o_, ob2 = yo[ig % 2], yob[ig % 2]
                    ig += 1
                    pt, pb = P.psum()
                    for q in range(nn):
                        n = n0 + q
                        op("dve", lambda e: e.scalar_tensor_tensor(out=o0_[:, n, :], in0=o0_[:, n, :], scalar=r_[:, n:n + 1], in1=g_[:, n, :], op0=ALU.mult, op1=ALU.mult),
                           r=[ob_, rb_, gb_], w=[ob_])
                        op("pe", lambda e: e.transpose(pt[:, q * 128:(q + 1) * 128], o0_[:, n, :], ident[:]), r=[ob_, cb], w=[pb])
                    op("act", lambda e: e.activation(out=o_[:, 0:nn * 128], in_=pt[:, 0:nn * 128], func=AF.Copy, scale=gn[:, 0:1]), r=[pb, gnb], w=[ob2])
                    dma("sp", YS[512 + h * 128:512 + (h + 1) * 128, n0 * 128:(n0 + nn) * 128], o_[:, 0:nn * 128], r=[ob2], w=ysb)

    for l in layers:
        last = (l == DEPTH - 1)
        if cfg.get("do_mixer", True):
            if l % 2 == 1:
                odd_mixer(l, skip_ctx=last)
            else:
                even_mixer(l, skip_ctx=last)
        if cfg.get("do_ffn", True):
            ffn_layer(l, skip_ctx=last)

    if cfg.get("dump_ys"):
        yd = P.outp("ys_dump", [D, NT])
        with contextlib.ExitStack() as es:
            tb16 = [P.sb(es, f"dy{i}", [128, 8, 512], BF16) for i in range(2)]
            tb32 = [P.sb(es, f"dz{i}", [128, 8, 512], F32) for i in range(2)]
            tbb = [Buf() for _ in range(2)]
            for ti, (t0, T) in enumerate(TILES):
                dma("sp", tb16[ti % 2][:, :, :T], YS[:, t0:t0 + T].rearrange("(c p) t -> p c t", p=128), r=ysb, w=[tbb[ti % 2]])
                op("dve", lambda e: e.tensor_copy(out=tb32[ti % 2][:, :, :T], in_=tb16[ti % 2][:, :, :T]), r=[tbb[ti % 2]], w=[tbb[ti % 2]])
                dma("sp", yd[:, t0:t0 + T].rearrange("(c p) t -> p c t", p=128), tb32[ti % 2][:, :, :T], r=[tbb[ti % 2]])
            kb.barrier()
    if cfg.get("dump_xt"):
        xd = P.outp("xt_dump", [D, NT])
        with contextlib.ExitStack() as es:
            tb = [P.sb(es, f"db{i}", [128, 8, 512], F32) for i in range(2)]
            tbb = [Buf() for _ in range(2)]
            for ti, (t0, T) in enumerate(TILES):
                dma("sp", tb[ti % 2][:, :, :T], XT[:, t0:t0 + T].rearrange("(c p) t -> p c t", p=128), r=[xb[ti]], w=[tbb[ti % 2]])
                dma("sp", xd[:, t0:t0 + T].rearrange("(c p) t -> p c t", p=128), tb[ti % 2][:, :, :T], r=[tbb[ti % 2]])
            kb.barrier()

    with contextlib.ExitStack() as es:
        xt = [P.sb(es, f"fx{i}", [128, 8, 512], F32) for i in range(2)]
        xtb = [Buf() for _ in range(2)]
        sq = [P.sb(es, f"fsq{i}", [128, 8, 512], F32) for i in range(2)]
        sqb = [Buf() for _ in range(2)]
        rs = [P.sb(es, f"frs{i}", [128, 512], F32) for i in range(2)]
        rsb = [Buf() for _ in range(2)]
        ot = [P.sb(es, f"fo{i}", [128, D], F32) for i in range(2)]
        otb = [Buf() for _ in range(2)]
        io = 0
        for ti, (t0, T) in enumerate(TILES):
            if ti == 0:
                continue
            x_, xb_ = xt[ti % 2], xtb[ti % 2]
            q_, qb_ = sq[ti % 2], sqb[ti % 2]
            r_, rb_ = rs[ti % 2], rsb[ti % 2]
            dma("sp", x_[:], XT[:, t0:t0 + T].rearrange("(c p) t -> p c t", p=128), r=[xb[ti]], w=[xb_])
            for c in range(8):
                op("act", lambda e: e.activation(out=q_[:, c, :], in_=x_[:, c, :], func=AF.Square), r=[xb_], w=[qb_])
            pt, pb = P.psum()
            for c in range(8):
                op("pe", lambda e: e.matmul(pt[:], ones[:], q_[:, c, :], start=(c == 0), stop=(c == 7)), r=[qb_, cb], w=[pb])
            op("act", lambda e: e.activation(out=r_[:], in_=pt[:], func=AF.Sqrt, bias=epsb[:], scale=1.0 / D), r=[pb, cb], w=[rb_])
            op("dve", lambda e: e.reciprocal(r_[:], r_[:]), r=[rb_], w=[rb_])
            for c in range(8):
                op("dve", lambda e: e.scalar_tensor_tensor(out=q_[:, c, :], in0=x_[:, c, :], scalar=fng[:, c:c + 1], in1=r_[:],
                                                           op0=ALU.mult, op1=ALU.mult), r=[xb_, rb_, cb, qb_], w=[qb_])
            for bi in range(4):
                o_, ob_ = ot[io % 2], otb[io % 2]
                io += 1
                for half in range(2):
                    pt, pb = P.psum()
                    for q in range(4):
                        c = half * 4 + q
                        op("pe", lambda e: e.transpose(pt[:, q * 128:(q + 1) * 128], q_[:, c, bi * 128:(bi + 1) * 128], ident[:]), r=[qb_, cb], w=[pb])
                    if half:
                        op("act", lambda e: e.activation(out=o_[:, 512:1024], in_=pt[:], func=AF.Copy), r=[pb], w=[ob_])
                    else:
                        op("dve", lambda e: e.tensor_copy(out=o_[:, 0:512], in_=pt[:]), r=[pb], w=[ob_])
                tk0 = t0 - CT + bi * 128
                dma("sp", out[tk0:tk0 + 128, :], o_[:], r=[ob_])
        kb.barrier()

    kb.finish()
    root.close()
    return P


def host_inputs(inputs, b):
    f = lambda a: np.ascontiguousarray(a, dtype=np.float32)
    c2 = np.stack([inputs["c"][b], inputs["c_ctx"]], -1).reshape(8, 128, 2).transpose(1, 0, 2)
    m = {
        "x": f(inputs["x"][b]),
        "ctx": f(inputs["ctx"][b]),
        "c2": f(c2),
        "w_ada": f(inputs["w_ada"]),
        "b_ada": f(inputs["b_ada"].reshape(DEPTH, 48, 128).transpose(0, 2, 1)),
        "norm_mix": f(inputs["norm_mix"].reshape(DEPTH, 8, 128).transpose(0, 2, 1)),
        "norm_ffn": f(inputs["norm_ffn"].reshape(DEPTH, 8, 128).transpose(0, 2, 1)),
        "final_norm": f(inputs["final_norm"].reshape(8, 128).T),
        "ffn_w_up": f(inputs["ffn_w_up"]),
        "ffn_conv": f(inputs["ffn_conv"].reshape(DEPTH, 3, 44, 128).transpose(0, 3, 2, 1)),
        "ffn_w_down": f(inputs["ffn_w_down"]),
        "ident": np.eye(128, dtype=np.float32),
    }
    m["od_w_in"] = f(inputs["od_w_in"])
    m["od_w_out"] = f(inputs["od_w_out"])
    m["lru_conv"] = f(inputs["lru_conv"].reshape(2, 4, 4, 128).transpose(0, 3, 2, 1))
    m["lru_conv_b"] = f(inputs["lru_conv_b"].reshape(2, 4, 128).transpose(0, 2, 1))
    def bd(w):
        o = np.zeros((2, 2, 4, 128, 128), np.float32)
        for c in range(4):
            o[:, :, c, 0:64, 0:64] = w[:, :, 2 * c]
            o[:, :, c, 64:128, 64:128] = w[:, :, 2 * c + 1]
        return o
    m["lru_wa"] = bd(inputs["lru_wa"])
    m["lru_wx"] = bd(inputs["lru_wx"])
    for nm, key in (("lru_ba", "lru_ba"), ("lru_bx", "lru_bx"), ("lru_lam", "lru_lambda")):
        m[nm] = f(inputs[key].reshape(2, 2, 4, 128).transpose(0, 3, 1, 2))
    m["swa_sink"] = f(np.broadcast_to(inputs["swa_sink"][:, None, :], (2, 128, 8)))
    m["ev_w_in"] = f(inputs["ev_w_in"])
    m["ev_w_out"] = f(inputs["ev_w_out"])
    m["diff_lambda"] = f(np.broadcast_to(inputs["diff_lambda"].reshape(2, 1, 256), (2, 128, 256)))
    m["diff_subln"] = f(inputs["diff_subln"].reshape(2, 128, 1))
    m["gdn_conv"] = f(inputs["gdn_conv"].reshape(2, 4, 12, 128).transpose(0, 3, 2, 1))
    m["gdn_a_log"] = f(np.broadcast_to(inputs["gdn_a_log"].reshape(2, 1, 8), (2, 128, 8)))
    m["gdn_dt_bias"] = f(np.broadcast_to(inputs["gdn_dt_bias"].reshape(2, 1, 8), (2, 128, 8)))
    m["gdn_norm"] = f(inputs["gdn_norm"].reshape(2, 128, 1))
    m.update(CONSTS)
    return m


def _make_consts():
    t = np.arange(LT)
    row = (t // 64).astype(np.float64)
    col = (t % 64).astype(np.float64)
    inv = (10000.0 ** (-np.arange(0, 32, 2, dtype=np.float32) / 32)).astype(np.float32)
    ar = (row[:, None].astype(np.float32) * inv).astype(np.float32)
    ac = (col[:, None].astype(np.float32) * inv).astype(np.float32)
    cr, sr, cc, sc = np.cos(ar), np.sin(ar), np.cos(ac), np.sin(ac)
    cosT = np.zeros((128, LT), np.float32)
    sinT = np.zeros((128, LT), np.float32)
    perm = np.zeros((128, 128), np.float32)
    for p in range(128):
        dd = p % 64
        i = dd % 16
        q = dd // 16
        if q == 0:
            cosT[p], sinT[p], partner = cr[:, i], -sr[:, i], p + 16
        elif q == 1:
            cosT[p], sinT[p], partner = cr[:, i], sr[:, i], p - 16
        elif q == 2:
            cosT[p], sinT[p], partner = cc[:, i], -sc[:, i], p + 16
        else:
            cosT[p], sinT[p], partner = cc[:, i], sc[:, i], p - 16
        perm[partner, p] = 1.0
    a = np.arange(128)[:, None]
    bq = np.arange(128)[None, :]
    NEG = -30000.0
    mprev = np.where(a >= bq, 0.0, NEG).astype(np.float32)
    mnext = np.where(a <= bq, 0.0, NEG).astype(np.float32)
    tt = np.arange(128)[:, None]
    ii = np.arange(128)[None, :]
    same = (tt // 64) == (ii // 64)
    gmask = np.zeros((128, 2, 3, 128), np.float32)
    gmask[:, 0, 0] = (tt <= ii) & same
    gmask[:, 0, 1] = (tt > ii) & same
    gmask[:, 0, 2] = np.where((tt <= ii) & same, 0.0, NEG)
    gmask[:, 1, 0] = (tt >= ii) & same
    gmask[:, 1, 1] = (tt < ii) & same
    gmask[:, 1, 2] = np.where((tt >= ii) & same, 0.0, NEG)
    gaux = np.zeros((128, 258), np.float32)
    gaux[0:64, 0:128] = 1.0
    gaux[64:128, 128:256] = 1.0
    gaux[0:64, 256] = 1.0
    gaux[64:128, 257] = 1.0
    return {"cosT": cosT, "sinT": sinT, "permT": perm, "mprev": np.tile(mprev, (1, 4)), "mnext": np.tile(mnext, (1, 4)), "gmask": gmask, "gaux": gaux}


CONSTS = _make_consts()


def kernel(**inputs):
    inputs = {k: np.asarray(v) for k, v in inputs.items()}
    P = build()
    n = 8
    in_maps = []
    for b in range(n):
        m = host_inputs(inputs, b)
        in_maps.append({k: m[k] for k in P.din})
    res = run_bass_kernel_spmd(P.nc, in_maps, core_ids=list(range(n)))
    return np.stack([np.asarray(r["out"], dtype=np.float32) for r in res.results], 0)
```

```python
import contextlib
import math
import numpy as np
import concourse.bass as bass
import concourse.mybir as mybir
from concourse.bass_utils import run_bass_kernel_spmd

F32 = mybir.dt.float32
BF16 = mybir.dt.bfloat16
AF = mybir.ActivationFunctionType
ALU = mybir.AluOpType
AX = mybir.AxisListType

D = 1024
NT = 4352
CT = 256
LT = 4096
DEPTH = 4
FFN = 2816
EPS = 1e-6
TILES = [(0, 256)] + [(256 + 512 * i, 512) for i in range(8)]


class Buf:
    __slots__ = ("w", "r", "name", "excl")

    def __init__(self, name="", excl=False):
        self.w = None
        self.r = []
        self.name = name
        self.excl = excl


class KB:
    EPOCH = 24000
    NDMA = 40

    def __init__(self, nc, same_eng_sync=True):
        self.nc = nc
        self.es = contextlib.ExitStack()
        self.eng = {"pe": nc.tensor, "act": nc.scalar, "dve": nc.vector, "pool": nc.gpsimd, "sp": nc.sync}
        self.same = same_eng_sync
        self.cnt = {e: 0 for e in self.eng}
        self.epoch = {e: 0 for e in self.eng}
        self.sems = {}
        self.known = {e: {} for e in self.eng}
        self.last_tok = {e: None for e in self.eng}
        self.dsem = [self.es.enter_context(nc.semaphore(f"dq{i}")) for i in range(self.NDMA)]
        self.dtot = [0] * self.NDMA
        self.dnext = 0
        self.nwait = 0
        self.nins = 0
        self.snaps = {}
        for e in self.eng:
            self._newsem(e)

    def _newsem(self, e):
        key = (e, self.epoch[e])
        self.sems[key] = self.es.enter_context(self.nc.semaphore(f"s_{e}_{self.epoch[e]}"))
        self.cnt[e] = 0

    def _semh(self, key):
        if isinstance(key, int):
            return self.dsem[key]
        return self.sems[key]

    def _wait(self, e, tok):
        key, val = tok
        if self.known[e].get(key, 0) >= val:
            return
        self.eng[e].wait_ge(self._semh(key), val)
        self.known[e][key] = val
        self.nwait += 1
        snap = self.snaps.get(tok)
        if snap:
            ke = self.known[e]
            for k2, v2 in snap.items():
                if ke.get(k2, 0) < v2:
                    ke[k2] = v2

    def _deps(self, e, r, w):
        toks = []
        for b in r:
            if b.w is not None:
                toks.append(b.w)
        for b in w:
            if b.w is not None:
                toks.append(b.w)
            toks.extend(b.r)
        for tok in toks:
            key = tok[0]
            if (not isinstance(key, int)) and key[0] == e and (e == "pe" or not self.same):
                continue
            self._wait(e, tok)

    def _mark(self, tok, r, w):
        for b in w:
            b.w = tok
            b.r = []
        for b in r:
            if b not in w:
                b.r.append(tok)
                if len(b.r) > 24:
                    d = {}
                    for k, v in b.r:
                        if d.get(k, 0) < v:
                            d[k] = v
                    b.r = list(d.items())

    def op(self, e, fn, r=(), w=()):
        w = list(w) + [b for b in r if b.excl and b not in w]
        r = [b for b in r if not b.excl]
        self._deps(e, r, w)
        if self.cnt[e] >= self.EPOCH:
            self.epoch[e] += 1
            self._newsem(e)
        ins = fn(self.eng[e])
        key = (e, self.epoch[e])
        self.cnt[e] += 1
        ins.then_inc(self.sems[key], 1)
        tok = (key, self.cnt[e])
        self.last_tok[e] = tok
        self.snaps[tok] = dict(self.known[e])
        self._mark(tok, r, w)
        self.nins += 1
        return tok

    def dma(self, e, out, in_, r=(), w=(), **kw):
        r = list(r)
        w = list(w)
        self._deps(e, r, w)
        i = self.dnext
        self.dnext = (self.dnext + 1) % self.NDMA
        if self.dtot[i]:
            self._wait(e, (i, self.dtot[i]))
        if e == "pool":
            self.swq = getattr(self, "swq", [])
            if len(self.swq) >= 3:
                self._wait(e, self.swq[-3])
        ins = self.eng[e].dma_start(out=out, in_=in_, **kw)
        self.dtot[i] += 16
        ins.then_inc(self.dsem[i], 16)
        tok = (i, self.dtot[i])
        if e == "pool":
            self.swq.append(tok)
        self._mark(tok, r, w)
        self.nins += 1
        return tok

    def barrier(self):
        toks = [t for t in self.last_tok.values() if t is not None]
        toks += [(i, self.dtot[i]) for i in range(self.NDMA) if self.dtot[i]]
        for e in self.eng:
            for tok in toks:
                key = tok[0]
                if (not isinstance(key, int)) and key[0] == e:
                    continue
                self._wait(e, tok)

    def finish(self):
        self.barrier()
        self.es.close()


class Prog:
    def __init__(self, cfg=None):
        self.cfg = cfg or {}
        self.nc = bass.Bass("TRN2", target_bir_lowering=False)
        self.kb = KB(self.nc)
        self.root = contextlib.ExitStack()
        self.din = {}
        self.dbuf = {}
        self.psn = 0

    def inp(self, name, shape, dt=F32):
        t = self.nc.dram_tensor(name, list(shape), dt, kind="ExternalInput").ap()
        self.din[name] = t
        return t

    def outp(self, name, shape, dt=F32):
        return self.nc.dram_tensor(name, list(shape), dt, kind="ExternalOutput").ap()

    def scratch(self, name, shape, dt):
        return self.nc.dram_tensor(name, list(shape), dt, kind="Internal").ap()

    def sb(self, es, name, shape, dt):
        self.nsb = getattr(self, "nsb", 0) + 1
        return es.enter_context(self.nc.sbuf_tensor(f"sb{self.nsb}_{name}", list(shape), dt))

    def scope(self, name):
        return self.nc.named_scope(name)

    def setup_psum(self):
        self.ps = [self.root.enter_context(self.nc.psum_tensor(f"ps{i}", [128, 512], F32)) for i in range(8)]
        self.psb = [Buf(f"ps{i}", excl=True) for i in range(8)]

    def psum(self):
        lo, hi = getattr(self, "psr", (0, 8))
        if not (lo <= self.psn < hi):
            self.psn = lo
        i = self.psn
        self.psn = lo + (self.psn + 1 - lo) % (hi - lo)
        return self.ps[i], self.psb[i]


def build(cfg=None):
    cfg = cfg or {}
    P = Prog(cfg)
    nc, kb = P.nc, P.kb
    op, dma = kb.op, kb.dma
    P.setup_psum()
    root = P.root
    layers = cfg.get("layers", list(range(DEPTH)))

    x_in = P.inp("x", [LT, D])
    ctx_in = P.inp("ctx", [CT, D])
    c2_in = P.inp("c2", [128, 8, 2])
    w_ada = P.inp("w_ada", [DEPTH, D, 6 * D])
    b_ada = P.inp("b_ada", [DEPTH, 128, 48])
    norm_mix = P.inp("norm_mix", [DEPTH, 128, 8])
    norm_ffn = P.inp("norm_ffn", [DEPTH, 128, 8])
    final_norm = P.inp("final_norm", [128, 8])
    ffn_w_up = P.inp("ffn_w_up", [DEPTH, D, 2 * FFN])
    ffn_conv = P.inp("ffn_conv", [DEPTH, 128, 44, 3])
    ffn_w_down = P.inp("ffn_w_down", [DEPTH, FFN, D])
    ident_in = P.inp("ident", [128, 128])
    od_w_in = P.inp("od_w_in", [2, D, 1792])
    od_w_out = P.inp("od_w_out", [2, D, D])
    lru_conv = P.inp("lru_conv", [2, 128, 4, 4])
    lru_conv_b = P.inp("lru_conv_b", [2, 128, 4])
    lru_wa = P.inp("lru_wa", [2, 2, 4, 128, 128])
    lru_wx = P.inp("lru_wx", [2, 2, 4, 128, 128])
    lru_ba = P.inp("lru_ba", [2, 128, 2, 4])
    lru_bx = P.inp("lru_bx", [2, 128, 2, 4])
    lru_lam = P.inp("lru_lam", [2, 128, 2, 4])
    swa_sink = P.inp("swa_sink", [2, 128, 8])
    cos_in = P.inp("cosT", [128, LT])
    sin_in = P.inp("sinT", [128, LT])
    perm_in = P.inp("permT", [128, 128])
    mprev_in = P.inp("mprev", [128, 512])
    mnext_in = P.inp("mnext", [128, 512])
    ev_w_in = P.inp("ev_w_in", [2, D, 3600])
    ev_w_out = P.inp("ev_w_out", [2, D, D])
    diff_lambda = P.inp("diff_lambda", [2, 128, 256])
    diff_subln = P.inp("diff_subln", [2, 128, 1])
    gdn_conv = P.inp("gdn_conv", [2, 128, 12, 4])
    gdn_a_log = P.inp("gdn_a_log", [2, 128, 8])
    gdn_dt_bias = P.inp("gdn_dt_bias", [2, 128, 8])
    gdn_norm = P.inp("gdn_norm", [2, 128, 1])
    gmask_in = P.inp("gmask", [128, 2, 3, 128])
    gaux_in = P.inp("gaux", [128, 258])
    out = P.outp("out", [LT, D])

    XT = P.scratch("XT", [D, NT], F32)
    xb = [Buf(f"X{t}") for t in range(9)]
    GS = P.scratch("GS", [FFN, NT], BF16)
    gsb = [Buf(f"G{j}") for j in range(22)]
    wup_b = P.scratch("wup_b", [DEPTH, D, 2 * FFN], BF16)
    wdn_b = P.scratch("wdn_b", [DEPTH, FFN, D], BF16)
    wupb = [Buf() for _ in range(DEPTH)]
    wdnb = [Buf() for _ in range(DEPTH)]

    odin_b = P.scratch("odin_b", [2, D, 1792], BF16)
    odout_b = P.scratch("odout_b", [2, D, D], BF16)
    odinb = [Buf() for _ in range(2)]
    odoutb = [Buf() for _ in range(2)]
    YS = P.scratch("YS", [D, NT], BF16)
    ysb = [Buf("Y")]
    QD = P.scratch("QD", [8, 64, NT], BF16)
    KD = P.scratch("KD", [2, 64, NT], BF16)
    qdb = [Buf("QD")]
    kdb = [Buf("KD")]

    evin_b = P.scratch("evin_b", [2, D, 3600], BF16)
    evout_b = P.scratch("evout_b", [2, D, D], BF16)
    evinb = [Buf() for _ in range(2)]
    evoutb = [Buf() for _ in range(2)]
    QA = P.scratch("QA", [8, 64, NT], BF16)
    KA = P.scratch("KA", [8, 64, NT], BF16)
    VA = P.scratch("VA", [NT, 512], BF16)
    GQ = P.scratch("GQ", [3, 4, 128, NT], BF16)
    GQK = P.scratch("GQK", [4, 128, NT], F32)
    GG = P.scratch("GG", [NT, 512], F32)
    BG = P.scratch("BG", [NT, 16], F32)
    OBD = P.scratch("OBD", [2, 4, NT, 128], F32)
    obdb = [Buf("OBD")]
    qab = [Buf("QA")]
    kab = [Buf("KA")]
    vab = [Buf("VA")]
    gqb = [Buf("GQ")]
    ggb = [Buf("GG")]
    bgb = [Buf("BG")]

    ident = P.sb(root, "ident", [128, 128], F32)
    ones = P.sb(root, "ones", [128, 128], F32)
    epsb = P.sb(root, "epsb", [128, 1], F32)
    ones16 = P.sb(root, "ones16", [128, 128], BF16)
    MOD = P.sb(root, "MOD", [128, DEPTH, 48, 2], F32)
    AMX = P.sb(root, "AMX", [128, DEPTH, 8, 2], F32)
    AFF = P.sb(root, "AFF", [128, DEPTH, 8, 2], F32)
    fconv = P.sb(root, "fconv", [128, DEPTH, 44, 3], F32)
    fng = P.sb(root, "fng", [128, 8], F32)
    cb = Buf("consts")
    modb = Buf("mod")
    dma("sp", ident[:], ident_in[:, :], w=[cb])
    op("dve", lambda e: e.memset(ones[:], 1.0), w=[cb])
    op("dve", lambda e: e.memset(epsb[:], EPS), w=[cb])
    op("dve", lambda e: e.memset(ones16[:], 1.0), w=[cb])
    dma("sp", fconv[:], ffn_conv.rearrange("l p j k -> p l j k"), w=[cb])
    dma("sp", fng[:], final_norm[:, :], w=[cb])

    def cast_rows(dst, src, R, C, wbuf, cchunk):
        dv = dst.rearrange("r (a c) -> (r a) c", c=cchunk)
        sv = src.rearrange("r (a c) -> (r a) c", c=cchunk)
        rows = R * (C // cchunk)
        r0 = 0
        while r0 < rows:
            n = min(2048, rows - r0)
            dma("pool", dv[r0:r0 + n, :], sv[r0:r0 + n, :], w=[wbuf])
            r0 += n

    for l in layers:
        if l % 2 == 0:
            cast_rows(evin_b[l // 2], ev_w_in[l // 2], D, 3600, evinb[l // 2], 1800)
            cast_rows(evout_b[l // 2], ev_w_out[l // 2], D, D, evoutb[l // 2], 1024)
        if l % 2 == 1:
            cast_rows(odin_b[l // 2], od_w_in[l // 2], D, 1792, odinb[l // 2], 1792)
            cast_rows(odout_b[l // 2], od_w_out[l // 2], D, D, odoutb[l // 2], 1024)
        cast_rows(wup_b[l], ffn_w_up[l], D, 2 * FFN, wupb[l], 1408)
        cast_rows(wdn_b[l], ffn_w_down[l], FFN, D, wdnb[l], 1024)

    with contextlib.ExitStack() as es, P.scope("mod"):
        c2 = P.sb(es, "c2", [128, 8, 2], F32)
        s2 = P.sb(es, "s2", [128, 8, 2], F32)
        bad = P.sb(es, "bad", [128, DEPTH, 48], F32)
        nmx = P.sb(es, "nmx", [128, DEPTH, 8], F32)
        nff = P.sb(es, "nff", [128, DEPTH, 8], F32)
        wa = [P.sb(es, f"wa{i}", [128, 8, 512], F32) for i in range(2)]
        wab = [Buf() for _ in range(2)]
        b0 = Buf()
        dma("sp", c2[:], c2_in[:, :, :], w=[b0])
        dma("sp", bad[:], b_ada.rearrange("l p j -> p l j"), w=[b0])
        dma("sp", nmx[:], norm_mix.rearrange("l p c -> p l c"), w=[b0])
        dma("sp", nff[:], norm_ffn.rearrange("l p c -> p l c"), w=[b0])
        op("act", lambda e: e.activation(out=s2[:], in_=c2[:], func=AF.Silu), r=[b0], w=[b0])
        it = 0
        for l in layers:
            for blk in range(12):
                wt, wb = wa[it % 2], wab[it % 2]
                it += 1
                dma("sp", wt[:], w_ada[l][:, blk * 512:(blk + 1) * 512].rearrange("(k p) n -> p k n", p=128), w=[wb])
                pt, pb = P.psum()
                for jj in range(4):
                    for k in range(8):
                        op("pe", lambda e: e.matmul(pt[:, jj * 2:jj * 2 + 2], wt[:, k, jj * 128:(jj + 1) * 128], s2[:, k, :],
                                                    start=(k == 0), stop=(k == 7)), r=[wb, b0], w=[pb])
                for jj in range(4):
                    j = blk * 4 + jj
                    op("dve", lambda e: e.tensor_scalar(out=MOD[:, l, j, :], in0=pt[:, jj * 2:jj * 2 + 2], scalar1=bad[:, l, j:j + 1],
                                                        scalar2=None, op0=ALU.add), r=[pb, b0], w=[modb])
            for s in range(2):
                op("dve", lambda e: e.scalar_tensor_tensor(out=AMX[:, l, :, s], in0=MOD[:, l, 8:16, s], scalar=1.0, in1=nmx[:, l, :],
                                                           op0=ALU.add, op1=ALU.mult), r=[modb, b0], w=[modb])
                op("dve", lambda e: e.scalar_tensor_tensor(out=AFF[:, l, :, s], in0=MOD[:, l, 32:40, s], scalar=1.0, in1=nff[:, l, :],
                                                           op0=ALU.add, op1=ALU.mult), r=[modb, b0], w=[modb])
        kb.barrier()

    dbg_x = cfg.get("x_override")
    if dbg_x:
        xo = P.inp("xo", [D, NT])
        with contextlib.ExitStack() as es:
            tb = [P.sb(es, f"tb{i}", [128, 8, 512], F32) for i in range(2)]
            tbb = [Buf() for _ in range(2)]
            for ti, (t0, T) in enumerate(TILES):
                dma("sp", tb[ti % 2][:, :, :T], xo[:, t0:t0 + T].rearrange("(c p) t -> p c t", p=128), w=[tbb[ti % 2]])
                dma("sp", XT[:, t0:t0 + T].rearrange("(c p) t -> p c t", p=128), tb[ti % 2][:, :, :T], r=[tbb[ti % 2]], w=[xb[ti]])
            kb.barrier()
    else:
        with contextlib.ExitStack() as es:
            tin = [P.sb(es, f"tin{i}", [128, D], F32) for i in range(2)]
            tinb = [Buf() for _ in range(2)]
            tout = [P.sb(es, f"tout{i}", [128, 8, 512], F32) for i in range(2)]
            toutb = [Buf() for _ in range(2)]
            it = 0
            for ti, (t0, T) in enumerate(TILES):
                to, tob = tout[ti % 2], toutb[ti % 2]
                for bi in range(T // 128):
                    tk0 = t0 + bi * 128
                    src = ctx_in[tk0:tk0 + 128, :] if tk0 < CT else x_in[tk0 - CT:tk0 - CT + 128, :]
                    tt, ttb = tin[it % 2], tinb[it % 2]
                    it += 1
                    dma("sp", tt[:], src, w=[ttb])
                    for half in range(2):
                        pt, pb = P.psum()
                        for q in range(4):
                            c = half * 4 + q
                            op("pe", lambda e: e.transpose(pt[:, q * 128:(q + 1) * 128], tt[:, c * 128:(c + 1) * 128], ident[:]),
                               r=[ttb, cb], w=[pb])
                        op("act" if half else "dve",
                           (lambda e: e.activation(out=to[:, 4:8, bi * 128:(bi + 1) * 128], in_=pt[:].rearrange("p (q t) -> p q t", q=4), func=AF.Copy))
                           if half else
                           (lambda e: e.tensor_copy(out=to[:, 0:4, bi * 128:(bi + 1) * 128], in_=pt[:].rearrange("p (q t) -> p q t", q=4))),
                           r=[pb], w=[tob])
                dma("sp", XT[:, t0:t0 + T].rearrange("(c p) t -> p c t", p=128), to[:, :, :T], r=[tob], w=[xb[ti]])
            kb.barrier()

    def norm_stage(es, l, A, shift_idx, U, ub):
        xt = [P.sb(es, f"nx{i}", [128, 8, 512], F32) for i in range(2)]
        xtb = [Buf() for _ in range(2)]
        sq = [P.sb(es, f"nsq{i}", [128, 8, 512], BF16) for i in range(2)]
        sqb = [Buf() for _ in range(2)]
        rs = [P.sb(es, f"nrs{i}", [128, 512], F32) for i in range(2)]
        rsb = [Buf() for _ in range(2)]
        tm = [P.sb(es, f"ntm{i}", [128, 512], F32) for i in range(3)]
        tmb = [Buf() for _ in range(3)]
        k3 = 0
        for ti, (t0, T) in enumerate(TILES):
            s = 1 if ti == 0 else 0
            x_, xb_ = xt[ti % 2], xtb[ti % 2]
            q_, qb_ = sq[ti % 2], sqb[ti % 2]
            r_, rb_ = rs[ti % 2], rsb[ti % 2]
            dma("sp", x_[:, :, :T], XT[:, t0:t0 + T].rearrange("(c p) t -> p c t", p=128), r=[xb[ti]], w=[xb_])
            for c in range(8):
                op("act", lambda e: e.activation(out=q_[:, c, :T], in_=x_[:, c, :T], func=AF.Square), r=[xb_], w=[qb_])
            pt, pb = P.psum()
            for c in range(8):
                op("pe", lambda e: e.matmul(pt[:, :T], ones16[:], q_[:, c, :T], start=(c == 0), stop=(c == 7)), r=[qb_, cb], w=[pb])
            op("act", lambda e: e.activation(out=r_[:, :T], in_=pt[:, :T], func=AF.Sqrt, bias=epsb[:], scale=1.0 / D), r=[pb, cb], w=[rb_])
            op("dve", lambda e: e.reciprocal(r_[:, :T], r_[:, :T]), r=[rb_], w=[rb_])
            for c in range(8):
                t_, tb_ = tm[k3 % 3], tmb[k3 % 3]
                k3 += 1
                op("dve", lambda e: e.tensor_tensor(out=t_[:, :T], in0=x_[:, c, :T], in1=r_[:, :T], op=ALU.mult), r=[xb_, rb_], w=[tb_])
                op("act", lambda e: e.activation(out=U[:, c, t0:t0 + T], in_=t_[:, :T], func=AF.Identity, scale=A[:, l, c, s:s + 1],
                                                 bias=MOD[:, l, shift_idx * 8 + c, s:s + 1]),
                   r=[tb_, modb], w=[ub[ti]])

    def proj_residual(es, l, gate_idx, W, wbuf, KC, SRC, srcb, skip_ctx):
        wsb = P.sb(es, "pr_w", [128, KC, D], BF16)
        wsbb = Buf()
        dma("sp", wsb[:], W.rearrange("(k p) n -> p k n", p=128), r=[wbuf], w=[wsbb])
        gt = [P.sb(es, f"pr_g{i}", [128, KC, 512], BF16) for i in range(2)]
        gtb = [Buf() for _ in range(2)]
        xt = [P.sb(es, f"pr_x{i}", [128, 8, 512], F32) for i in range(2)]
        xtb = [Buf() for _ in range(2)]
        for ti, (t0, T) in enumerate(TILES):
            if skip_ctx and ti == 0:
                continue
            s = 1 if ti == 0 else 0
            g_, gb_ = gt[ti % 2], gtb[ti % 2]
            x_, xb_ = xt[ti % 2], xtb[ti % 2]
            dma("sp", g_[:, :, :T], SRC[:, t0:t0 + T].rearrange("(k p) t -> p k t", p=128), r=srcb, w=[gb_])
            dma("sp", x_[:, :, :T], XT[:, t0:t0 + T].rearrange("(c p) t -> p c t", p=128), r=[xb[ti]], w=[xb_])
            for m in range(8):
                pt, pb = P.psum()
                for k in range(KC):
                    op("pe", lambda e: e.matmul(pt[:, :T], wsb[:, k, m * 128:(m + 1) * 128], g_[:, k, :T], start=(k == 0), stop=(k == KC - 1)),
                       r=[wsbb, gb_], w=[pb])
                op("dve", lambda e: e.scalar_tensor_tensor(out=x_[:, m, :T], in0=pt[:, :T], scalar=MOD[:, l, gate_idx * 8 + m, s:s + 1],
                                                           in1=x_[:, m, :T], op0=ALU.mult, op1=ALU.add), r=[pb, modb, xb_], w=[xb_])
            dma("sp", XT[:, t0:t0 + T].rearrange("(c p) t -> p c t", p=128), x_[:, :, :T], r=[xb_], w=[xb[ti]])

    def ffn_layer(l, skip_ctx):
        with contextlib.ExitStack() as es:
            U = P.sb(es, "U", [128, 8, NT], BF16)
            ub = [Buf(f"U{t}") for t in range(9)]
            with contextlib.ExitStack() as es2, P.scope("ffn_norm"):
                norm_stage(es2, l, AFF, 3, U, ub)
                kb.barrier()
            with contextlib.ExitStack() as es2, P.scope("ffn_up"):
                wt = [P.sb(es2, f"fw{i}", [128, 8, 256], BF16) for i in range(2)]
                wtb = [Buf() for _ in range(2)]
                H = P.sb(es2, "fH", [128, 2, NT], F32)
                Cg = P.sb(es2, "fC", [128, 2, NT], F32)
                hb = [[Buf() for _ in range(9)] for _ in range(2)]
                cgb = [[Buf() for _ in range(9)] for _ in range(2)]
                Go = [P.sb(es2, f"fG{i}", [128, NT], BF16) for i in range(2)]
                gob = [[Buf() for _ in range(9)] for _ in range(2)]
                t_start = 1 if skip_ctx else 0
                lo = CT if skip_ctx else 0
                seg_start = {0, CT}
                seg_end = {CT, NT}

                def conv_tile(j, ti):
                    t0, T = TILES[ti]
                    go_, gb_ = Go[j % 2], gob[j % 2][ti]
                    for gv in range(2):
                        ch = gv * 22 + j
                        oa = t0 + 1 if t0 in seg_start else t0
                        rds = [hb[gv][ti]] + ([hb[gv][ti - 1]] if (t0 not in seg_start) else [])
                        op("dve", lambda e: e.scalar_tensor_tensor(out=Cg[:, gv, oa:t0 + T], in0=H[:, gv, oa - 1:t0 + T - 1], scalar=fconv[:, l, ch, 0:1],
                                                                   in1=Cg[:, gv, oa:t0 + T], op0=ALU.mult, op1=ALU.add), r=rds + [cb], w=[cgb[gv][ti]])
                        ob_ = t0 + T - 1 if (t0 + T) in seg_end else t0 + T
                        rds = [hb[gv][ti]] + ([hb[gv][ti + 1]] if ((t0 + T) not in seg_end) else [])
                        op("dve", lambda e: e.scalar_tensor_tensor(out=Cg[:, gv, t0:ob_], in0=H[:, gv, t0 + 1:ob_ + 1], scalar=fconv[:, l, ch, 2:3],
                                                                   in1=Cg[:, gv, t0:ob_], op0=ALU.mult, op1=ALU.add), r=rds + [cb], w=[cgb[gv][ti]])
                    op("act", lambda e: e.activation(out=Cg[:, 0, t0:t0 + T], in_=Cg[:, 0, t0:t0 + T], func=AF.Silu), r=[cgb[0][ti]], w=[cgb[0][ti]])
                    op("dve", lambda e: e.tensor_tensor(out=go_[:, t0:t0 + T], in0=Cg[:, 0, t0:t0 + T], in1=Cg[:, 1, t0:t0 + T], op=ALU.mult),
                       r=[cgb[0][ti], cgb[1][ti]], w=[gb_])

                for j in range(22):
                    w_, wb_ = wt[j % 2], wtb[j % 2]
                    dma("sp", w_[:, :, 0:128], wup_b[l][:, j * 128:(j + 1) * 128].rearrange("(k p) n -> p k n", p=128), r=[wupb[l]], w=[wb_])
                    dma("sp", w_[:, :, 128:256], wup_b[l][:, FFN + j * 128:FFN + (j + 1) * 128].rearrange("(k p) n -> p k n", p=128),
                        r=[wupb[l]], w=[wb_])
                    for ti, (t0, T) in enumerate(TILES):
                        if ti < t_start:
                            continue
                        for gv in range(2):
                            pt, pb = P.psum()
                            for k in range(8):
                                op("pe", lambda e: e.matmul(pt[:, :T], w_[:, k, gv * 128:(gv + 1) * 128], U[:, k, t0:t0 + T],
                                                            start=(k == 0), stop=(k == 7)), r=[wb_, ub[ti]], w=[pb])
                            op("act", lambda e: e.activation(out=H[:, gv, t0:t0 + T], in_=pt[:, :T], func=AF.Copy), r=[pb], w=[hb[gv][ti]])
                            op("act", lambda e: e.activation(out=Cg[:, gv, t0:t0 + T], in_=pt[:, :T], func=AF.Copy, scale=fconv[:, l, gv * 22 + j, 1:2]),
                               r=[pb, cb], w=[cgb[gv][ti]])
                        if ti - 1 >= t_start:
                            conv_tile(j, ti - 1)
                    conv_tile(j, 8)
                    dma("sp", GS[j * 128:(j + 1) * 128, lo:NT], Go[j % 2][:, lo:NT], r=gob[j % 2][t_start:], w=[gsb[j]])
                kb.barrier()
        with contextlib.ExitStack() as es, P.scope("ffn_down"):
            proj_residual(es, l, 5, wdn_b[l], wdnb[l], 22, GS, gsb, skip_ctx)
            kb.barrier()

    def proj_fm(W, wbuf, c0, U, ub, dst, dstb, wt, wtb, ev="act"):
        dma("sp", wt[:], W[:, c0:c0 + 128].rearrange("(k p) n -> p k n", p=128), r=[wbuf], w=[wtb])
        for ti, (t0, T) in enumerate(TILES):
            pt, pb = P.psum()
            for k in range(8):
                op("pe", lambda e: e.matmul(pt[:, :T], wt[:, k, :], U[:, k, t0:t0 + T], start=(k == 0), stop=(k == 7)), r=[wtb, ub[ti]], w=[pb])
            if ev == "act":
                op("act", lambda e: e.activation(out=dst[:, t0:t0 + T], in_=pt[:, :T], func=AF.Copy), r=[pb], w=[dstb])
            else:
                op("dve", lambda e: e.tensor_copy(out=dst[:, t0:t0 + T], in_=pt[:, :T]), r=[pb], w=[dstb])

    def rope_store(es_, raw, rawb, cosT, sinT, permT, tb, dstrows, dstbuf, tmp, tmpb, ob, obb):
        op("dve", lambda e: e.tensor_copy(out=ob[:, 0:CT], in_=raw[:, 0:CT]), r=[rawb], w=[obb])
        for ti, (t0, T) in enumerate(TILES):
            if ti == 0:
                continue
            l0 = t0 - CT
            pt, pb = P.psum()
            op("pe", lambda e: e.matmul(pt[:, :T], permT[:], raw[:, t0:t0 + T], start=True, stop=True), r=[rawb, tb], w=[pb])
            op("dve", lambda e: e.tensor_tensor(out=tmp[:, :T], in0=pt[:, :T], in1=sinT[:, l0:l0 + T], op=ALU.mult), r=[pb, tb], w=[tmpb])
            op("dve", lambda e: e.tensor_tensor(out=raw[:, t0:t0 + T], in0=raw[:, t0:t0 + T], in1=cosT[:, l0:l0 + T], op=ALU.mult), r=[rawb, tb, pb], w=[rawb])
            op("dve", lambda e: e.tensor_tensor(out=ob[:, t0:t0 + T], in0=raw[:, t0:t0 + T], in1=tmp[:, :T], op=ALU.add), r=[rawb, tmpb], w=[obb])
        for hh in range(2):
            dma("sp", dstrows[hh], ob[hh * 64:(hh + 1) * 64, :], r=[obb], w=[dstbuf])

    def odd_mixer(l, skip_ctx):
        j = l // 2
        W = odin_b[j]
        wbuf = odinb[j]
        with contextlib.ExitStack() as es:
            U = P.sb(es, "U", [128, 8, NT], BF16)
            ub = [Buf(f"U{t}") for t in range(9)]
            with contextlib.ExitStack() as es2, P.scope("mix_norm"):
                norm_stage(es2, l, AMX, 0, U, ub)
                kb.barrier()
            with contextlib.ExitStack() as es2, P.scope("odd_lru"):
                wt = [P.sb(es2, f"ow{i}", [128, 8, 128], BF16) for i in range(2)]
                wtb = [Buf() for _ in range(2)]
                B1 = P.sb(es2, "B1", [128, NT], F32); b1 = Buf()
                B2 = P.sb(es2, "B2", [128, NT], F32); b2 = Buf()
                B2h = P.sb(es2, "B2h", [128, NT], BF16); b2h = Buf()
                B4 = P.sb(es2, "B4", [128, NT], F32); b4 = Buf()
                B5 = P.sb(es2, "B5", [128, NT], F32); b5 = Buf()
                B6 = P.sb(es2, "B6", [128, NT], F32); b6 = Buf()
                B7 = P.sb(es2, "B7", [128, NT], F32); b7 = Buf()
                lc = P.sb(es2, "lc", [128, 4, 4], F32)
                lcb_ = P.sb(es2, "lcb", [128, 4], F32)
                lba = P.sb(es2, "lba", [128, 2, 4], F32)
                lbx = P.sb(es2, "lbx", [128, 2, 4], F32)
                llam = P.sb(es2, "llam", [128, 2, 4], F32)
                c8 = P.sb(es2, "c8", [128, 2, 4], F32)
                c16 = P.sb(es2, "c16", [128, 2, 4], F32)
                bdf = P.sb(es2, "bdf", [128, 2, 128], F32)
                bda = [P.sb(es2, f"bda{i}", [128, 2, 128], BF16) for i in range(2)]
                bdab = [Buf() for _ in range(2)]
                bdfb = Buf()
                sb0 = Buf()
                dma("sp", lc[:], lru_conv[j], w=[sb0])
                dma("sp", lcb_[:], lru_conv_b[j], w=[sb0])
                dma("sp", lba[:], lru_ba[j], w=[sb0])
                dma("sp", lbx[:], lru_bx[j], w=[sb0])
                dma("sp", llam[:], lru_lam[j], w=[sb0])
                op("act", lambda e: e.activation(out=c8[:], in_=llam[:], func=AF.Exp, scale=-1.0), r=[sb0], w=[sb0])
                op("act", lambda e: e.activation(out=c8[:], in_=c8[:], func=AF.Ln, bias=ones[:, 0:1], scale=1.0), r=[sb0, cb], w=[sb0])
                op("dve", lambda e: e.tensor_scalar(out=c16[:], in0=c8[:], scalar1=-16.0, scalar2=None, op0=ALU.mult), r=[sb0], w=[sb0])
                op("dve", lambda e: e.tensor_scalar(out=c8[:], in0=c8[:], scalar1=-8.0, scalar2=None, op0=ALU.mult), r=[sb0], w=[sb0])
                segs = [(0, CT), (CT, NT)]
                ib = 0
                for c in range(4):
                    proj_fm(W, wbuf, c * 128, U, ub, B1, b1, wt[c % 2], wtb[c % 2])
                    op("act", lambda e: e.activation(out=B2[:], in_=B1[:], func=AF.Identity, scale=lc[:, c, 2:3], bias=lcb_[:, c:c + 1]),
                       r=[b1, sb0], w=[b2])
                    for (a, b) in segs:
                        for tap, off in ((0, -2), (1, -1), (3, 1)):
                            if off < 0:
                                oa, ob_, ia, ib_ = a - off, b, a, b + off
                            else:
                                oa, ob_, ia, ib_ = a, b - off, a + off, b
                            op("dve", lambda e: e.scalar_tensor_tensor(out=B2[:, oa:ob_], in0=B1[:, ia:ib_], scalar=lc[:, c, tap:tap + 1], in1=B2[:, oa:ob_],
                                                                       op0=ALU.mult, op1=ALU.add), r=[b1, sb0, b2], w=[b2])
                    op("act", lambda e: e.activation(out=B2h[:], in_=B2[:], func=AF.Copy), r=[b2], w=[b2h])
                    for d in range(2):
                        bd_, bdb_ = bda[ib % 2], bdab[ib % 2]
                        ib += 1
                        dma("sp", bdf[:, 0, :], lru_wa[j, d, c], w=[bdfb])
                        dma("sp", bdf[:, 1, :], lru_wx[j, d, c], w=[bdfb])
                        op("dve", lambda e: e.tensor_copy(out=bd_[:], in_=bdf[:]), r=[bdfb], w=[bdb_])
                        for ti, (t0, T) in enumerate(TILES):
                            pr, prb = P.psum()
                            op("pe", lambda e: e.matmul(pr[:, :T], bd_[:, 0, :], B2h[:, t0:t0 + T], start=True, stop=True), r=[bdb_, b2h], w=[prb])
                            op("act", lambda e: e.activation(out=B4[:, t0:t0 + T], in_=pr[:, :T], func=AF.Sigmoid, bias=lba[:, d, c:c + 1], scale=1.0),
                               r=[prb, sb0], w=[b4])
                            pi, pib = P.psum()
                            op("pe", lambda e: e.matmul(pi[:, :T], bd_[:, 1, :], B2h[:, t0:t0 + T], start=True, stop=True), r=[bdb_, b2h], w=[pib])
                            op("act", lambda e: e.activation(out=B5[:, t0:t0 + T], in_=pi[:, :T], func=AF.Sigmoid, bias=lbx[:, d, c:c + 1], scale=1.0),
                               r=[pib, sb0], w=[b5])
                        op("act", lambda e: e.activation(out=B1[:], in_=B4[:], func=AF.Exp, scale=c16[:, d, c:c + 1]), r=[b4, sb0], w=[b1])
                        op("act", lambda e: e.activation(out=B4[:], in_=B4[:], func=AF.Exp, scale=c8[:, d, c:c + 1]), r=[b4, sb0], w=[b4])
                        op("act", lambda e: e.activation(out=B1[:], in_=B1[:], func=AF.Sqrt, scale=-1.0, bias=ones[:, 0:1]), r=[b1, cb], w=[b1])
                        op("dve", lambda e: e.tensor_tensor(out=B5[:], in0=B5[:], in1=B2[:], op=ALU.mult), r=[b5, b2], w=[b5])
                        op("dve", lambda e: e.tensor_tensor(out=B5[:], in0=B5[:], in1=B1[:], op=ALU.mult), r=[b5, b1], w=[b5])
                        if d == 0:
                            op("dve", lambda e: e.tensor_tensor_scan(out=B6[:], data0=B4[:], data1=B5[:], initial=0.0, op0=ALU.mult, op1=ALU.add),
                               r=[b4, b5], w=[b6])
                        else:
                            op("dve", lambda e: e.tensor_tensor_scan(out=B7[:, 0:CT][:, ::-1], data0=B4[:, 0:CT][:, ::-1], data1=B5[:, 0:CT][:, ::-1],
                                                                     initial=0.0, op0=ALU.mult, op1=ALU.add), r=[b4, b5], w=[b7])
                            op("dve", lambda e: e.tensor_tensor_scan(out=B7[:, CT:NT][:, ::-1], data0=B4[:, CT:NT][:, ::-1], data1=B5[:, CT:NT][:, ::-1],
                                                                     initial=B7[:, 0:1], op0=ALU.mult, op1=ALU.add), r=[b4, b5, b7], w=[b7])
                            op("dve", lambda e: e.tensor_tensor(out=B6[:], in0=B6[:], in1=B7[:], op=ALU.add), r=[b6, b7], w=[b6])
                    proj_fm(W, wbuf, 512 + c * 128, U, ub, B1, b1, wt[c % 2], wtb[c % 2])
                    op("act", lambda e: e.activation(out=B1[:], in_=B1[:], func=AF.Gelu_apprx_tanh), r=[b1], w=[b1])
                    op("dve", lambda e: e.tensor_tensor(out=B2h[:], in0=B6[:], in1=B1[:], op=ALU.mult), r=[b6, b1, b2h], w=[b2h])
                    dma("sp", YS[c * 128:(c + 1) * 128, :], B2h[:], r=[b2h], w=ysb)
                kb.barrier()
            with contextlib.ExitStack() as es2, P.scope("odd_qk"):
                wt = [P.sb(es2, f"aw{i}", [128, 8, 128], BF16) for i in range(2)]
                wtb = [Buf() for _ in range(2)]
                cosT = P.sb(es2, "cosT", [128, LT], F32)
                sinT = P.sb(es2, "sinT", [128, LT], F32)
                permT = P.sb(es2, "permT", [128, 128], F32)
                tb = Buf()
                dma("sp", cosT[:], cos_in[:, :], w=[tb])
                dma("sp", sinT[:], sin_in[:, :], w=[tb])
                dma("sp", permT[:], perm_in[:, :], w=[tb])
                raw = [P.sb(es2, f"raw{i}", [128, NT], F32) for i in range(2)]
                rawb = [Buf() for _ in range(2)]
                tmp = P.sb(es2, "rtmp", [128, 512], F32); tmpb = Buf()
                ob = [P.sb(es2, f"rob{i}", [128, NT], BF16) for i in range(2)]
                obb = [Buf() for _ in range(2)]
                for c in range(5):
                    proj_fm(W, wbuf, 1024 + c * 128, U, ub, raw[c % 2], rawb[c % 2], wt[c % 2], wtb[c % 2])
                    if c < 4:
                        rows = [QD[2 * c + hh] for hh in range(2)]
                        dbuf_ = qdb[0]
                    else:
                        rows = [KD[hh] for hh in range(2)]
                        dbuf_ = kdb[0]
                    rope_store(es2, raw[c % 2], rawb[c % 2], cosT, sinT, permT, tb, rows, dbuf_, tmp, tmpb, ob[c % 2], obb[c % 2])
                kb.barrier()
            with contextlib.ExitStack() as es2, P.scope("odd_attn"):
                wv = P.sb(es2, "wv", [128, 8, 128], BF16); wvb = Buf()
                dma("sp", wv[:], W[:, 1664:1792].rearrange("(k p) n -> p k n", p=128), r=[wbuf], w=[wvb])
                V = P.sb(es2, "V", [128, 34, 2, 65], BF16); vb = Buf()
                op("pool", lambda e: e.memset(V[:], 1.0), w=[vb])
                for blk in range(34):
                    ti = 0 if blk < 2 else 1 + (blk - 2) // 4
                    pt, pb = P.psum()
                    for k in range(8):
                        op("pe", lambda e: e.matmul(pt[:, 0:128], U[:, k, blk * 128:(blk + 1) * 128], wv[:, k, :], start=(k == 0), stop=(k == 7)),
                           r=[wvb, ub[ti]], w=[pb])
                    op("act", lambda e: e.activation(out=V[:, blk, :, 0:64], in_=pt[:, 0:128].rearrange("p (a b) -> p a b", a=2), func=AF.Copy),
                       r=[pb], w=[vb])
                kT = P.sb(es2, "kT", [128, 2, NT], BF16); ktb = Buf()
                op("dve", lambda e: e.memset(kT[64:128, :, :], 0.0), w=[ktb])
                dma("sp", kT[0:64, :, :], KD.rearrange("h d t -> d h t"), r=kdb, w=[ktb])
                qT = [P.sb(es2, f"qT{i}", [128, 4, NT], BF16) for i in range(2)]
                qtb = [Buf() for _ in range(2)]
                for kv in range(2):
                    op("dve", lambda e: e.memset(qT[kv][64:128, :, :], 0.0), w=[qtb[kv]])
                idb16 = P.sb(es2, "idb16", [128, 128], BF16)
                mk = P.sb(es2, "mk", [128, 2, 512], BF16)
                mkf = P.sb(es2, "mkf", [128, 2, 512], F32)
                esk = P.sb(es2, "esk", [128, 8], F32)
                mb = Buf()
                dma("sp", mkf[:, 0, :], mprev_in[:, :], w=[mb])
                dma("sp", mkf[:, 1, :], mnext_in[:, :], w=[mb])
                dma("sp", esk[:], swa_sink[j], w=[mb])
                op("dve", lambda e: e.tensor_copy(out=mk[:], in_=mkf[:]), r=[mb], w=[mb])
                op("dve", lambda e: e.tensor_copy(out=idb16[:], in_=ident[:]), r=[cb], w=[mb])
                op("act", lambda e: e.activation(out=esk[:], in_=esk[:], func=AF.Exp), r=[mb], w=[mb])
                PT = [P.sb(es2, f"PT{i}", [128, 512], BF16) for i in range(10)]
                ptb = [Buf() for _ in range(10)]
                od = [P.sb(es2, f"od{i}", [128, 512], F32) for i in range(2)]
                odb = [Buf() for _ in range(2)]
                rd = [P.sb(es2, f"rd{i}", [128, 8], F32) for i in range(2)]
                rdb = [Buf() for _ in range(2)]
                yt = [P.sb(es2, f"yt{i}", [128, 4, 128], BF16) for i in range(2)]
                ytb = [Buf() for _ in range(2)]
                for kv in range(2):
                    dma("sp", qT[kv][0:64, :, :], QD[kv * 4:(kv + 1) * 4].rearrange("h d t -> d h t"), r=qdb, w=[qtb[kv]])
                ipt = 0
                qblocks = list(range(2, 34)) if skip_ctx else list(range(34))
                for qi, qb_ in enumerate(qblocks):
                    o_, ob_ = od[qi % 2], odb[qi % 2]
                    r_, rb_ = rd[qi % 2], rdb[qi % 2]
                    if qb_ < 2:
                        keys = [(0, None), (1, None)]
                    else:
                        keys = [(0, None), (1, None)]
                        if qb_ - 1 >= 2:
                            keys.append((qb_ - 1, 0))
                        keys.append((qb_, None))
                        if qb_ + 1 < 34:
                            keys.append((qb_ + 1, 1))
                    for kv in range(2):
                        pts = []
                        for (kblk, mtype) in keys:
                            ps_, psb_ = P.psum()
                            if mtype is not None:
                                op("pe", lambda e: e.matmul(ps_[:], idb16[:], mk[:, mtype, :], start=True, stop=False), r=[mb], w=[psb_])
                            op("pe", lambda e: e.matmul(ps_[:].rearrange("p (g q) -> p g q", g=4), kT[:, kv, kblk * 128:(kblk + 1) * 128],
                                                        qT[kv][:, :, qb_ * 128:(qb_ + 1) * 128], start=(mtype is None), stop=True),
                               r=[ktb, qtb[kv]], w=[psb_])
                            p_, pb_ = PT[ipt % 10], ptb[ipt % 10]
                            ipt += 1
                            op("act", lambda e: e.activation(out=p_[:], in_=ps_[:], func=AF.Exp, scale=0.125), r=[psb_], w=[pb_])
                            pts.append((p_, pb_, kblk))
                        po, pob = P.psum()
                        for g in range(4):
                            for ki, (p_, pb_, kblk) in enumerate(pts):
                                op("pe", lambda e: e.matmul(po[:, g * 65:(g + 1) * 65], p_[:, g * 128:(g + 1) * 128], V[:, kblk, kv, :],
                                                            start=(ki == 0), stop=(ki == len(pts) - 1)), r=[pb_, vb], w=[pob])
                        pov = po[:, 0:260].rearrange("p (g e) -> p g e", g=4)
                        op("dve", lambda e: e.tensor_tensor(out=r_[:, kv * 4:(kv + 1) * 4], in0=pov[:, :, 64], in1=esk[:, kv * 4:(kv + 1) * 4], op=ALU.add),
                           r=[pob, mb], w=[rb_])
                        op("dve", lambda e: e.reciprocal(r_[:, kv * 4:(kv + 1) * 4], r_[:, kv * 4:(kv + 1) * 4]), r=[rb_], w=[rb_])
                        for g in range(4):
                            h = kv * 4 + g
                            op("act" if g % 2 else "dve",
                               (lambda e: e.activation(out=o_[:, h * 64:(h + 1) * 64], in_=po[:, g * 65:g * 65 + 64], func=AF.Copy, scale=r_[:, h:h + 1]))
                               if g % 2 else
                               (lambda e: e.tensor_scalar(out=o_[:, h * 64:(h + 1) * 64], in0=po[:, g * 65:g * 65 + 64], scalar1=r_[:, h:h + 1], scalar2=None,
                                                          op0=ALU.mult)),
                               r=[pob, rb_], w=[ob_])
                    y_, yb_ = yt[qi % 2], ytb[qi % 2]
                    pt, pb = P.psum()
                    for c in range(4):
                        op("pe", lambda e: e.transpose(pt[:, c * 128:(c + 1) * 128], o_[:, c * 128:(c + 1) * 128], ident[:]), r=[ob_, cb], w=[pb])
                    op("act", lambda e: e.activation(out=y_[:], in_=pt[:].rearrange("p (c t) -> p c t", c=4), func=AF.Copy), r=[pb], w=[yb_])
                    dma("sp", YS[512:1024, qb_ * 128:(qb_ + 1) * 128].rearrange("(c p) t -> p c t", p=128), y_[:], r=[yb_], w=ysb)
                kb.barrier()
        with contextlib.ExitStack() as es, P.scope("odd_out"):
            proj_residual(es, l, 2, odout_b[j], odoutb[j], 8, YS, ysb, skip_ctx)
            kb.barrier()

    def proj_tm(W, wbuf, c0, ncol, U, ub, wt, wtb, consume):
        dma("sp", wt[:, :, 0:ncol], W[:, c0:c0 + ncol].rearrange("(k p) n -> p k n", p=128), r=[wbuf], w=[wtb])
        for blk in range(34):
            ti = 0 if blk < 2 else 1 + (blk - 2) // 4
            pt, pb = P.psum()
            for k in range(8):
                op("pe", lambda e: e.matmul(pt[:, 0:ncol], U[:, k, blk * 128:(blk + 1) * 128], wt[:, k, 0:ncol], start=(k == 0), stop=(k == 7)),
                   r=[wtb, ub[ti]], w=[pb])
            consume(blk, pt, pb)

    def even_mixer(l, skip_ctx):
        j = l // 2
        lam_init = 0.8 - 0.6 * math.exp(-0.3 * l)
        W = evin_b[j]
        wbuf = evinb[j]
        with contextlib.ExitStack() as es:
            U = P.sb(es, "U", [128, 8, NT], BF16)
            ub = [Buf(f"U{t}") for t in range(9)]
            with contextlib.ExitStack() as es2, P.scope("mix_norm"):
                norm_stage(es2, l, AMX, 0, U, ub)
                kb.barrier()
            with contextlib.ExitStack() as es2, P.scope("ev_qk"):
                wt = [P.sb(es2, f"aw{i}", [128, 8, 128], BF16) for i in range(2)]
                wtb = [Buf() for _ in range(2)]
                cosT = P.sb(es2, "cosT", [128, LT], F32)
                sinT = P.sb(es2, "sinT", [128, LT], F32)
                permT = P.sb(es2, "permT", [128, 128], F32)
                tb = Buf()
                dma("sp", cosT[:], cos_in[:, :], w=[tb])
                dma("sp", sinT[:], sin_in[:, :], w=[tb])
                dma("sp", permT[:], perm_in[:, :], w=[tb])
                raw = [P.sb(es2, f"raw{i}", [128, NT], F32) for i in range(2)]
                rawb = [Buf() for _ in range(2)]
                tmp = P.sb(es2, "rtmp", [128, 512], F32); tmpb = Buf()
                ob = [P.sb(es2, f"rob{i}", [128, NT], BF16) for i in range(2)]
                obb = [Buf() for _ in range(2)]
                for c in range(8):
                    proj_fm(W, wbuf, c * 128, U, ub, raw[c % 2], rawb[c % 2], wt[c % 2], wtb[c % 2])
                    if c < 4:
                        rows = [QA[2 * c + hh] for hh in range(2)]
                        dbuf_ = qab[0]
                    else:
                        rows = [KA[2 * (c - 4) + hh] for hh in range(2)]
                        dbuf_ = kab[0]
                    rope_store(es2, raw[c % 2], rawb[c % 2], cosT, sinT, permT, tb, rows, dbuf_, tmp, tmpb, ob[c % 2], obb[c % 2])
                kb.barrier()
            with contextlib.ExitStack() as es2, P.scope("ev_tm"):
                wt = [P.sb(es2, f"tw{i}", [128, 8, 512], BF16) for i in range(2)]
                wtb = [Buf() for _ in range(2)]
                vo = [P.sb(es2, f"vo{i}", [128, 512], BF16) for i in range(2)]
                vob = [Buf() for _ in range(2)]
                go = [P.sb(es2, f"go{i}", [128, 512], F32) for i in range(2)]
                gob = [Buf() for _ in range(2)]
                bgo = [P.sb(es2, f"bgo{i}", [128, 16], F32) for i in range(2)]
                bgob = [Buf() for _ in range(2)]
                tq = [P.sb(es2, f"tq{i}", [128, 8], F32) for i in range(4)]
                tqb = Buf()
                alog = P.sb(es2, "alog", [128, 8], F32)
                dtb = P.sb(es2, "dtb", [128, 8], F32)
                cb2 = Buf()
                dma("sp", alog[:], gdn_a_log[j], w=[cb2])
                dma("sp", dtb[:], gdn_dt_bias[j], w=[cb2])
                op("act", lambda e: e.activation(out=alog[:], in_=alog[:], func=AF.Exp), r=[cb2], w=[cb2])
                op("dve", lambda e: e.tensor_scalar(out=alog[:], in0=alog[:], scalar1=-1.0, scalar2=None, op0=ALU.mult), r=[cb2], w=[cb2])

                def c_va(blk, pt, pb):
                    o_, ob_ = vo[blk % 2], vob[blk % 2]
                    op("act", lambda e: e.activation(out=o_[:], in_=pt[:], func=AF.Copy), r=[pb], w=[ob_])
                    dma("sp", VA[blk * 128:(blk + 1) * 128, :], o_[:], r=[ob_], w=vab)

                def c_gate(blk, pt, pb):
                    o_, ob_ = go[blk % 2], gob[blk % 2]
                    op("act", lambda e: e.activation(out=o_[:], in_=pt[:], func=AF.Silu), r=[pb], w=[ob_])
                    dma("sp", GG[blk * 128:(blk + 1) * 128, :], o_[:], r=[ob_], w=ggb)

                def c_bg(blk, pt, pb):
                    o_, ob_ = bgo[blk % 2], bgob[blk % 2]
                    x_, ax_, l_, mx_ = tq
                    op("act", lambda e: e.activation(out=o_[:, 0:8], in_=pt[:, 0:8], func=AF.Sigmoid), r=[pb], w=[ob_])
                    op("dve", lambda e: e.tensor_tensor(out=x_[:], in0=pt[:, 8:16], in1=dtb[:], op=ALU.add), r=[pb, cb2, tqb], w=[tqb])
                    op("act", lambda e: e.activation(out=ax_[:], in_=x_[:], func=AF.Abs), r=[tqb], w=[tqb])
                    op("act", lambda e: e.activation(out=l_[:], in_=ax_[:], func=AF.Exp, scale=-1.0), r=[tqb], w=[tqb])
                    op("act", lambda e: e.activation(out=l_[:], in_=l_[:], func=AF.Ln, bias=ones[:, 0:1], scale=1.0), r=[tqb, cb], w=[tqb])
                    op("dve", lambda e: e.tensor_scalar(out=mx_[:], in0=x_[:], scalar1=0.0, scalar2=None, op0=ALU.max), r=[tqb], w=[tqb])
                    op("dve", lambda e: e.tensor_tensor(out=mx_[:], in0=mx_[:], in1=l_[:], op=ALU.add), r=[tqb], w=[tqb])
                    op("dve", lambda e: e.tensor_tensor(out=o_[:, 8:16], in0=mx_[:], in1=alog[:], op=ALU.mult), r=[tqb, cb2, ob_], w=[ob_])
                    dma("sp", BG[blk * 128:(blk + 1) * 128, :], o_[:], r=[ob_], w=bgb)

                proj_tm(W, wbuf, 1024, 512, U, ub, wt[0], wtb[0], c_va)
                proj_tm(W, wbuf, 3072, 512, U, ub, wt[1], wtb[1], c_gate)
                proj_tm(W, wbuf, 3584, 16, U, ub, wt[0], wtb[0], c_bg)
                kb.barrier()
            with contextlib.ExitStack() as es2, P.scope("ev_gqkv"):
                wt = [P.sb(es2, f"gw{i}", [128, 8, 128], BF16) for i in range(2)]
                wtb = [Buf() for _ in range(2)]
                R1 = [P.sb(es2, f"gR{i}", [128, NT], F32) for i in range(2)]
                r1b = [Buf() for _ in range(2)]
                C1 = [P.sb(es2, f"gC{i}", [128, NT], F32) for i in range(2)]
                c1b = [Buf() for _ in range(2)]
                S1 = P.sb(es2, "gS", [128, NT], F32); s1b = Buf()
                CB = [P.sb(es2, f"gCB{i}", [128, NT], BF16) for i in range(2)]
                cbb = [Buf() for _ in range(2)]
                rsq = [P.sb(es2, f"grs{i}", [128, 512], F32) for i in range(2)]
                rsqb = [Buf() for _ in range(2)]
                gc = P.sb(es2, "gc", [128, 12, 4], F32); gcb = Buf()
                dma("sp", gc[:], gdn_conv[j], w=[gcb])
                segs = [(0, CT), (CT, NT)]
                it = 0
                for kind in range(3):
                    for h in range(4):
                        cidx = kind * 4 + h
                        r_, rb_ = R1[it % 2], r1b[it % 2]
                        c_, cb_ = C1[it % 2], c1b[it % 2]
                        o16, o16b = CB[it % 2], cbb[it % 2]
                        proj_fm(W, wbuf, 1536 + cidx * 128, U, ub, r_, rb_, wt[it % 2], wtb[it % 2])
                        it += 1
                        op("act", lambda e: e.activation(out=c_[:], in_=r_[:], func=AF.Copy, scale=gc[:, cidx, 2:3]), r=[rb_, gcb], w=[cb_])
                        for (a, b) in segs:
                            for tap, off in ((0, -2), (1, -1), (3, 1)):
                                if off < 0:
                                    oa, ob_, ia, ib_ = a - off, b, a, b + off
                                else:
                                    oa, ob_, ia, ib_ = a, b - off, a + off, b
                                op("dve", lambda e: e.scalar_tensor_tensor(out=c_[:, oa:ob_], in0=r_[:, ia:ib_], scalar=gc[:, cidx, tap:tap + 1], in1=c_[:, oa:ob_],
                                                                           op0=ALU.mult, op1=ALU.add), r=[rb_, gcb, cb_], w=[cb_])
                        op("act", lambda e: e.activation(out=c_[:], in_=c_[:], func=AF.Silu), r=[cb_], w=[cb_])
                        if kind < 2:
                            op("act", lambda e: e.activation(out=S1[:], in_=c_[:], func=AF.Square), r=[cb_], w=[s1b])
                            for ti, (t0, T) in enumerate(TILES):
                                pt, pb = P.psum()
                                op("pe", lambda e: e.matmul(pt[:, :T], ones[:], S1[:, t0:t0 + T], start=True, stop=True), r=[s1b, cb], w=[pb])
                                q_, qb_ = rsq[ti % 2], rsqb[ti % 2]
                                op("act", lambda e: e.activation(out=q_[:, :T], in_=pt[:, :T], func=AF.Sqrt, bias=epsb[:], scale=1.0), r=[pb, cb], w=[qb_])
                                op("dve", lambda e: e.reciprocal(q_[:, :T], q_[:, :T]), r=[qb_], w=[qb_])
                                if kind == 0:
                                    op("dve", lambda e: e.scalar_tensor_tensor(out=o16[:, t0:t0 + T], in0=c_[:, t0:t0 + T], scalar=128.0 ** -0.5, in1=q_[:, :T],
                                                                               op0=ALU.mult, op1=ALU.mult), r=[cb_, qb_], w=[o16b])
                                else:
                                    op("dve", lambda e: e.tensor_tensor(out=c_[:, t0:t0 + T], in0=c_[:, t0:t0 + T], in1=q_[:, :T], op=ALU.mult), r=[cb_, qb_], w=[cb_])
                                    op("act", lambda e: e.activation(out=o16[:, t0:t0 + T], in_=c_[:, t0:t0 + T], func=AF.Copy), r=[cb_], w=[o16b])
                        else:
                            op("dve", lambda e: e.tensor_copy(out=o16[:], in_=c_[:]), r=[cb_], w=[o16b])
                        dma("sp", GQ[kind, h], o16[:], r=[o16b], w=gqb)
                        if kind == 1:
                            dma("sp", GQK[h], c_[:], r=[cb_], w=gqb)
                kb.barrier()
        with contextlib.ExitStack() as es, P.scope("ev_attn"):
            P.psr = (4, 8)
            acc = [(P.ps[i], P.psb[i]) for i in range(4)]
            kT = P.sb(es, "kT", [128, 2, NT], BF16); ktb = Buf()
            qT = P.sb(es, "qT", [128, NT], BF16); qtb = Buf()
            op("dve", lambda e: e.memset(kT[:], 0.0), w=[ktb])
            V = P.sb(es, "V", [128, 34, 128], BF16); vb = Buf()
            PT = [P.sb(es, f"PT{i}", [128, 512], BF16) for i in range(8)]
            ptb = [Buf() for _ in range(8)]
            onesb = P.sb(es, "onesb", [128, 128], BF16)
            rden = P.sb(es, "rden", [128, 512], F32); rdb = Buf()
            t0b_ = P.sb(es, "t0b", [128, 512], F32); t0bb = Buf()
            oa = P.sb(es, "oa", [128, 512], F32); oab = Buf()
            sqb_ = P.sb(es, "sqb", [128, 512], F32); sqbb = Buf()
            rsd = P.sb(es, "rsd", [128, 512], F32); rsdb = Buf()
            yT = [P.sb(es, f"yT{i}", [128, 512], BF16) for i in range(2)]
            ytb = [Buf() for _ in range(2)]
            lv = P.sb(es, "lv", [128, 256], F32)
            lp = P.sb(es, "lp", [128, 2, 64], F32)
            lsum = P.sb(es, "lsum", [128, 2], F32)
            neglam = P.sb(es, "neglam", [128, 1], F32)
            sg = P.sb(es, "sg", [128, 1], F32)
            eps128 = P.sb(es, "eps128", [128, 1], F32)
            lb = Buf()
            dma("sp", lv[:], diff_lambda[j], w=[lb])
            dma("sp", sg[:], diff_subln[j], w=[lb])
            op("dve", lambda e: e.tensor_copy(out=onesb[:], in_=ones[:]), r=[cb], w=[lb])
            op("dve", lambda e: e.tensor_tensor(out=lp[:, 0, :], in0=lv[:, 0:64], in1=lv[:, 64:128], op=ALU.mult), r=[lb], w=[lb])
            op("dve", lambda e: e.tensor_tensor(out=lp[:, 1, :], in0=lv[:, 128:192], in1=lv[:, 192:256], op=ALU.mult), r=[lb], w=[lb])
            op("dve", lambda e: e.tensor_reduce(out=lsum[:], in_=lp[:], axis=AX.X, op=ALU.add), r=[lb], w=[lb])
            op("act", lambda e: e.activation(out=lsum[:], in_=lsum[:], func=AF.Exp), r=[lb], w=[lb])
            op("dve", lambda e: e.tensor_tensor(out=neglam[:], in0=lsum[:, 1:2], in1=lsum[:, 0:1], op=ALU.subtract), r=[lb], w=[lb])
            op("dve", lambda e: e.tensor_scalar(out=neglam[:], in0=neglam[:], scalar1=-lam_init, scalar2=None, op0=ALU.add), r=[lb], w=[lb])
            op("dve", lambda e: e.tensor_scalar(out=sg[:], in0=sg[:], scalar1=(1.0 - lam_init), scalar2=None, op0=ALU.mult), r=[lb], w=[lb])
            op("dve", lambda e: e.memset(eps128[:], EPS), w=[lb])
            ipt = 0
            iy = 0
            for h in range(4):
                dma("sp", kT[0:64, 0, :], KA[2 * h], r=kab, w=[ktb])
                dma("sp", kT[64:128, 1, :], KA[2 * h + 1], r=kab, w=[ktb])
                dma("sp", qT[:], QA[2 * h:2 * h + 2].rearrange("m d t -> (m d) t"), r=qab, w=[qtb])
                dma("sp", V[:], VA[:, h * 128:(h + 1) * 128].rearrange("(b p) e -> p b e", p=128), r=vab, w=[vb])
                for ti, (t0, T) in enumerate(TILES):
                    if ti == 0 and skip_ctx:
                        continue
                    keys = [0, 1] if ti == 0 else list(range(34))
                    for m in range(2):
                        A_, Ab_ = acc[2 * m]
                        B_, Bb_ = acc[2 * m + 1]
                        LA = 3
                        nk = len(keys)
                        pend = []
                        for ki in range(nk + LA):
                            if ki < nk:
                                kblk = keys[ki]
                                s_, sb_ = P.psum()
                                op("pe", lambda e: e.matmul(s_[:, :T], kT[:, m, kblk * 128:(kblk + 1) * 128], qT[:, t0:t0 + T], start=True, stop=True),
                                   r=[ktb, qtb], w=[sb_])
                                p_, pb_ = PT[ipt % 8], ptb[ipt % 8]
                                ipt += 1
                                op("act", lambda e: e.activation(out=p_[:, :T], in_=s_[:, :T], func=AF.Exp, scale=0.125), r=[sb_], w=[pb_])
                                pend.append((p_, pb_, kblk))
                            if ki >= LA:
                                kj = ki - LA
                                p2, pb2, kb2 = pend[kj]
                                op("pe", lambda e: e.matmul(A_[:, :T], V[:, kb2, :], p2[:, :T], start=(kj == 0), stop=(kj == nk - 1)), r=[pb2, vb], w=[Ab_])
                                op("pe", lambda e: e.matmul(B_[:, :T], onesb[:], p2[:, :T], start=(kj == 0), stop=(kj == nk - 1)), r=[pb2, lb], w=[Bb_])
                        op("dve", lambda e: e.reciprocal(rden[:, :T], B_[:, :T]), r=[Bb_], w=[rdb])
                        if m == 0:
                            op("dve", lambda e: e.tensor_tensor(out=t0b_[:, :T], in0=A_[:, :T], in1=rden[:, :T], op=ALU.mult), r=[Ab_, rdb], w=[t0bb])
                        else:
                            op("dve", lambda e: e.tensor_tensor(out=oa[:, :T], in0=A_[:, :T], in1=rden[:, :T], op=ALU.mult), r=[Ab_, rdb], w=[oab])
                            op("dve", lambda e: e.scalar_tensor_tensor(out=oa[:, :T], in0=oa[:, :T], scalar=neglam[:, 0:1], in1=t0b_[:, :T], op0=ALU.mult, op1=ALU.add),
                               r=[oab, lb, t0bb], w=[oab])
                    y_, yb_ = yT[iy % 2], ytb[iy % 2]
                    iy += 1
                    op("act", lambda e: e.activation(out=sqb_[:, :T], in_=oa[:, :T], func=AF.Square), r=[oab], w=[sqbb])
                    pt, pb = P.psum()
                    op("pe", lambda e: e.matmul(pt[:, :T], ones[:], sqb_[:, :T], start=True, stop=True), r=[sqbb, cb], w=[pb])
                    op("act", lambda e: e.activation(out=rsd[:, :T], in_=pt[:, :T], func=AF.Sqrt, bias=eps128[:], scale=1.0 / 128), r=[pb, lb], w=[rsdb])
                    op("dve", lambda e: e.reciprocal(rsd[:, :T], rsd[:, :T]), r=[rsdb], w=[rsdb])
                    op("dve", lambda e: e.scalar_tensor_tensor(out=y_[:, :T], in0=oa[:, :T], scalar=sg[:, 0:1], in1=rsd[:, :T], op0=ALU.mult, op1=ALU.mult),
                       r=[oab, lb, rsdb], w=[yb_])
                    dma("sp", YS[h * 128:(h + 1) * 128, t0:t0 + T], y_[:, :T], r=[yb_], w=ysb)
            P.psr = (0, 8)
            kb.barrier()
        with contextlib.ExitStack() as es, P.scope("ev_gdn"):
            gdn_core(es, l, j, skip_ctx)
            kb.barrier()
        with contextlib.ExitStack() as es, P.scope("ev_out"):
            proj_residual(es, l, 2, evout_b[j], evoutb[j], 8, YS, ysb, skip_ctx)
            kb.barrier()

    def gdn_core(es0, l, j, skip_ctx):
        NB = NT // 128
        for hg in range(2):
          chains = [(h, d, sub) for h in (2 * hg, 2 * hg + 1) for d in range(2) for sub in range(2)]
          NCH = len(chains)
          with contextlib.ExitStack() as es:
              gm = P.sb(es, "gmask", [128, 2, 3, 128], F32); gmb = Buf()
              gaux = P.sb(es, "gaux", [128, 258], F32)
              dma("sp", gm[:], gmask_in[:, :, :, :], w=[gmb])
              dma("sp", gaux[:], gaux_in[:, :], w=[gmb])
              idb = P.sb(es, "idb", [128, 128], BF16)
              op("dve", lambda e: e.tensor_copy(out=idb[:], in_=ident[:]), r=[cb], w=[gmb])
              qkv = P.sb(es, "qkv", [128, 3, 2, NT], BF16); qkvb = Buf()
              kF = P.sb(es, "kF", [128, 2, NT], F32)
              for kind in range(3):
                  dma("sp", qkv[:, kind, :, :], GQ[kind, 2 * hg:2 * hg + 2].rearrange("h p t -> p h t"), r=gqb, w=[qkvb])
              dma("sp", kF[:], GQK[2 * hg:2 * hg + 2].rearrange("h p t -> p h t"), r=gqb, w=[qkvb])
              BGt = P.sb(es, "BGt", [128, NB, 16], F32); bgtb = Buf()
              dma("sp", BGt[:], BG.rearrange("(n p) f -> p n f", p=128), r=bgb, w=[bgtb])
              NBt = P.sb(es, "NBt", [128, NB, 8], F32)
              op("dve", lambda e: e.tensor_scalar(out=NBt[:], in0=BGt[:, :, 0:8], scalar1=-1.0, scalar2=None, op0=ALU.mult), r=[bgtb], w=[bgtb])

              def ch(name, shape, dt):
                  return [P.sb(es, f"{name}{c}", shape, dt) for c in range(NCH)]
              g2 = ch("g2", [128, 2, 128], F32)
              dT = ch("dT", [128, 128], F32)
              Nb = ch("Nb", [128, 2, 128], F32)
              Wk = ch("Wk", [128, 2, 2, 128], F32)
              MTk = ch("MTk", [128, 2, 128], F32)
              Xb = ch("Xb", [128, 128], BF16)
              AT = ch("AT", [128, 128], BF16)
              ubf = ch("ubf", [128, 128], F32)
              kg = ch("kg", [128, 3, 128], BF16)
              vtok = ch("vtok", [128, 128], BF16)
              wT = ch("wT", [128, 128], BF16)
              vn = ch("vn", [128, 128], BF16)
              t1 = ch("t1", [128, 128], F32)
              ot = ch("ot", [128, 128], F32)
              S = ch("S", [128, 128], F32)
              Sb = ch("Sb", [128, 128], BF16)
              cbuf = [Buf(f"chain{c}") for c in range(NCH)]
              sbuf_ = [Buf(f"S{c}") for c in range(NCH)]
              vnb = [Buf(f"vn{c}") for c in range(NCH)]
              otb = [Buf(f"ot{c}") for c in range(NCH)]
              for c in range(NCH):
                  op("dve", lambda e: e.memset(S[c - c % 2][:], 0.0), w=[sbuf_[c - c % 2]])
                  op("dve", lambda e: e.memset(Sb[c - c % 2][:], 0.0), w=[sbuf_[c - c % 2]])
                  op("dve", lambda e: e.memset(vn[c - c % 2][:], 0.0), w=[vnb[c - c % 2]])
              order = [list(range(NB)), [1, 0] + list(range(NB - 1, 1, -1))]
              SCA = P.sb(es, "SCA", [128, 2, 7, NB, 2], F32); scab = Buf()
              for d in range(2):
                  pa, pab = P.psum()
                  grhs = BGt[:, :, 8 + d * 4 + 2 * hg:8 + d * 4 + 2 * hg + 2]
                  for kind, lh in enumerate((gm[:, d, 0, :], gm[:, d, 1, :], gaux[:, 0:128], gaux[:, 128:256])):
                      op("pe", lambda e: e.matmul(pa[:, kind * 2 * NB:(kind + 1) * 2 * NB].rearrange("p (n f) -> p n f", f=2), lh, grhs, start=True, stop=True),
                         r=[gmb, bgtb], w=[pab])
                  op("act", lambda e: e.activation(out=SCA[:, d, 0:4, :, :], in_=pa[:, 0:8 * NB].rearrange("p (k n f) -> p k n f", k=4, f=2), func=AF.Exp), r=[pab], w=[scab])
                  op("act", lambda e: e.activation(out=SCA[:, d, 6, :, :], in_=pa[:, 0:2 * NB].rearrange("p (n f) -> p n f", f=2), func=AF.Copy), r=[pab, scab], w=[scab])
                  for a_ in range(2):
                      op("dve", lambda e: e.tensor_scalar(out=SCA[:, d, 4 + a_, :, :], in0=SCA[:, d, 1, :, :], scalar1=gaux[:, 256 + a_:257 + a_], scalar2=None, op0=ALU.mult),
                         r=[gmb, scab], w=[scab])

              class Reg:
                  def __init__(self):
                      self.k = 4
                  def get(self):
                      if self.k == 4:
                          self.bank = P.psum()
                          self.k = 0
                      r = (self.bank[0][:, self.k * 128:(self.k + 1) * 128], self.bank[1], self.bank[0], self.k * 128)
                      self.k += 1
                      return r
              rg = Reg()

              for i in range(0, NB, 2):
                  blk = [order[d][i + sub] for (h, d, sub) in chains]
                  for c, (h, d, sub) in enumerate(chains):
                      n = blk[c]
                      gcol = BGt[:, n, 8 + d * 4 + h:8 + d * 4 + h + 1]
                      op("dve", lambda e: e.tensor_scalar(out=g2[c][:, 0, :], in0=ones[:], scalar1=gcol, scalar2=None, op0=ALU.mult), r=[bgtb, cb, cbuf[c]], w=[cbuf[c]])
                  regD = []
                  for c, (h, d, sub) in enumerate(chains):
                      rv, rb_, _, _ = rg.get()
                      regD.append((rv, rb_))
                      op("pe", lambda e: e.matmul(rv, g2[c][:, 0, :], gm[:, d, 0, :], start=True, stop=True), r=[cbuf[c], gmb], w=[rb_])
                  for c, (h, d, sub) in enumerate(chains):
                      rv, rb_ = regD[c]
                      n = blk[c]
                      op("dve", lambda e: e.scalar_tensor_tensor(out=dT[c][:], in0=rv, scalar=SCA[:, d, 6, n, h % 2:h % 2 + 1], in1=gm[:, d, 2, :],
                                                                 op0=ALU.subtract, op1=ALU.add), r=[rb_, scab, gmb, cbuf[c]], w=[cbuf[c]])
                      op("act", lambda e: e.activation(out=dT[c][:], in_=dT[c][:], func=AF.Exp), r=[cbuf[c]], w=[cbuf[c]])
                  regF = []
                  for c, (h, d, sub) in enumerate(chains):
                      n = blk[c]
                      kT_ = qkv[:, 1, h % 2, n * 128:(n + 1) * 128]
                      kF_ = kF[:, h % 2, n * 128:(n + 1) * 128]
                      qT_ = qkv[:, 0, h % 2, n * 128:(n + 1) * 128]
                      vT_ = qkv[:, 2, h % 2, n * 128:(n + 1) * 128]
                      bank = P.psum()
                      pt, pb = bank
                      regF.append(bank)
                      op("pe", lambda e: e.matmul(pt[:, 0:128], kF_, kF_, start=True, stop=True), r=[qkvb], w=[pb])
                      op("pe", lambda e: e.matmul(pt[:, 128:256], kT_, qT_, start=True, stop=True), r=[qkvb], w=[pb])
                      op("pe", lambda e: e.matmul(pt[:, 256:384], kT_, idb[:], start=True, stop=True), r=[qkvb, gmb], w=[pb])
                      op("pe", lambda e: e.matmul(pt[:, 384:512], vT_, idb[:], start=True, stop=True), r=[qkvb, gmb], w=[pb])
                      bcol = BGt[:, n, d * 4 + h:d * 4 + h + 1]
                      nbcol = NBt[:, n, d * 4 + h:d * 4 + h + 1]
                      op("dve", lambda e: e.tensor_tensor(out=AT[c][:], in0=pt[:, 128:256], in1=dT[c][:], op=ALU.mult), r=[pb, cbuf[c]], w=[cbuf[c]])
                      op("dve", lambda e: e.tensor_tensor(out=dT[c][:], in0=dT[c][:], in1=ident[:], op=ALU.subtract), r=[cbuf[c], cb], w=[cbuf[c]])
                      op("dve", lambda e: e.scalar_tensor_tensor(out=g2[c][:, 0, :], in0=pt[:, 0:128], scalar=nbcol, in1=dT[c][:], op0=ALU.mult, op1=ALU.mult),
                         r=[pb, bgtb, cbuf[c]], w=[cbuf[c]])
                      op("act", lambda e: e.activation(out=Nb[c][:, 0, :], in_=g2[c][:, 0, :], func=AF.Copy, scale=-1.0), r=[cbuf[c]], w=[cbuf[c]])
                      op("dve", lambda e: e.tensor_tensor(out=Wk[c][:, 1, 1, :], in0=g2[c][:, 0, :], in1=ident[:], op=ALU.add), r=[cbuf[c], cb], w=[cbuf[c]])
                      op("act", lambda e: e.activation(out=kg[c][:, 0, :], in_=pt[:, 256:384], func=AF.Copy, scale=SCA[:, d, 0, n, h % 2:h % 2 + 1]), r=[pb, cbuf[c], scab], w=[cbuf[c]])
                      op("act", lambda e: e.activation(out=kg[c][:, 1, :], in_=pt[:, 256:384], func=AF.Copy, scale=SCA[:, d, 4, n, h % 2:h % 2 + 1]), r=[pb, cbuf[c], scab], w=[cbuf[c]])
                      op("act", lambda e: e.activation(out=kg[c][:, 2, :], in_=pt[:, 256:384], func=AF.Copy, scale=SCA[:, d, 5, n, h % 2:h % 2 + 1]), r=[pb, cbuf[c], scab], w=[cbuf[c]])
                      op("dve", lambda e: e.tensor_copy(out=vtok[c][:], in_=pt[:, 384:512]), r=[pb, cbuf[c]], w=[cbuf[c]])
                  regH = []
                  for c in range(NCH):
                      rv, rb_, _, _ = rg.get()
                      regH.append((rv, rb_))
                      op("pe", lambda e: e.transpose(rv, Nb[c][:, 0, :], ident[:]), r=[cbuf[c], cb], w=[rb_])
                  for c in range(NCH):
                      rv, rb_ = regH[c]
                      op("act" if c % 2 else "dve",
                         (lambda e: e.activation(out=Nb[c][:, 1, :], in_=rv, func=AF.Copy)) if c % 2 else (lambda e: e.tensor_copy(out=Nb[c][:, 1, :], in_=rv)),
                         r=[rb_, cbuf[c]], w=[cbuf[c]])
                  for k in range(0, 6):
                      regS = []
                      for c in range(NCH):
                          pt, pb = P.psum()
                          regS.append((pt, pb))
                          if k == 0:
                              M_, MT_ = Nb[c][:, 0, :], Nb[c][:, 1, :]
                              op("pe", lambda e: e.matmul(pt[:, 0:128], MT_, M_, start=True, stop=True), r=[cbuf[c]], w=[pb])
                              op("pe", lambda e: e.matmul(pt[:, 256:384], M_, MT_, start=True, stop=True), r=[cbuf[c]], w=[pb])
                          elif k < 5:
                              pp = k % 2
                              op("pe", lambda e: e.matmul(pt[:, 0:256], MTk[c][:, pp, :], Wk[c][:, pp, :, :], start=True, stop=True), r=[cbuf[c]], w=[pb])
                              op("pe", lambda e: e.matmul(pt[:, 256:384], Wk[c][:, pp, 0, :], MTk[c][:, pp, :], start=True, stop=True), r=[cbuf[c]], w=[pb])
                          else:
                              op("pe", lambda e: e.matmul(pt[:, 128:256], MTk[c][:, 1, :], Wk[c][:, 1, 1, :], start=True, stop=True), r=[cbuf[c]], w=[pb])
                      for c in range(NCH):
                          pt, pb = regS[c]
                          np_ = (k + 1) % 2
                          if k < 5:
                              op("act", lambda e: e.activation(out=Wk[c][:, np_, 0, :], in_=pt[:, 0:128], func=AF.Copy), r=[pb, cbuf[c]], w=[cbuf[c]])
                              op("act" if c % 2 else "dve",
                                 (lambda e: e.activation(out=MTk[c][:, np_, :], in_=pt[:, 256:384], func=AF.Copy)) if c % 2 else
                                 (lambda e: e.tensor_copy(out=MTk[c][:, np_, :], in_=pt[:, 256:384])), r=[pb, cbuf[c]], w=[cbuf[c]])
                          if 1 <= k < 5:
                              op("dve", lambda e: e.tensor_tensor(out=Wk[c][:, np_, 1, :], in0=Wk[c][:, k % 2, 1, :], in1=pt[:, 128:256], op=ALU.add), r=[pb, cbuf[c]], w=[cbuf[c]])
                          if k == 5:
                              op("dve", lambda e: e.tensor_tensor(out=Xb[c][:], in0=Wk[c][:, 1, 1, :], in1=pt[:, 128:256], op=ALU.add), r=[pb, cbuf[c]], w=[cbuf[c]])
                  regJ = []
                  for c in range(NCH):
                      bank = P.psum() if c % 2 == 0 else bank
                      pt, pb = bank
                      o0 = (c % 2) * 256
                      regJ.append((pt, pb, o0))
                      op("pe", lambda e: e.matmul(pt[:, o0:o0 + 128], Xb[c][:], vtok[c][:], start=True, stop=True), r=[cbuf[c]], w=[pb])
                      op("pe", lambda e: e.matmul(pt[:, o0 + 128:o0 + 256], kg[c][:, 0, :], Xb[c][:], start=True, stop=True), r=[cbuf[c]], w=[pb])
                  for c, (h, d, sub) in enumerate(chains):
                      pt, pb, o0 = regJ[c]
                      n = blk[c]
                      bcol = BGt[:, n, d * 4 + h:d * 4 + h + 1]
                      op("dve", lambda e: e.tensor_scalar(out=ubf[c][:], in0=pt[:, o0:o0 + 128], scalar1=bcol, scalar2=None, op0=ALU.mult), r=[pb, bgtb, cbuf[c]], w=[cbuf[c]])
                      op("act", lambda e: e.activation(out=wT[c][:], in_=pt[:, o0 + 128:o0 + 256], func=AF.Copy), r=[pb, cbuf[c]], w=[cbuf[c]])
                  for sub_s, step_i in ((0, 0), (0, 1), (1, 0), (1, 1)):
                      regW = {}
                      for c, (h, d, sub) in enumerate(chains):
                          if sub != sub_s:
                              continue
                          a = step_i if d == 0 else 1 - step_i
                          lo, hi = a * 64, (a + 1) * 64
                          rv, rb_, bank_t, off = rg.get()
                          regW[c] = (bank_t, rb_, off)
                          op("pe", lambda e: e.matmul(bank_t[:, off:off + 128], wT[c][:], Sb[c - c % 2][:], start=True, stop=True), r=[cbuf[c], sbuf_[c - c % 2]], w=[rb_])
                      for c, (h, d, sub) in enumerate(chains):
                          if sub != sub_s:
                              continue
                          a = step_i if d == 0 else 1 - step_i
                          lo, hi = a * 64, (a + 1) * 64
                          n = blk[c]
                          bank_t, rb_, off = regW[c]
                          nbcol = NBt[lo:hi, n, d * 4 + h:d * 4 + h + 1]
                          op("dve", lambda e: e.scalar_tensor_tensor(out=vn[c - c % 2][lo:hi, :], in0=bank_t[lo:hi, off:off + 128], scalar=nbcol, in1=ubf[c][lo:hi, :],
                                                                     op0=ALU.mult, op1=ALU.add), r=[rb_, bgtb, cbuf[c]], w=[vnb[c - c % 2]])
                      regO = {}
                      for c, (h, d, sub) in enumerate(chains):
                          if sub != sub_s:
                              continue
                          a = step_i if d == 0 else 1 - step_i
                          lo, hi = a * 64, (a + 1) * 64
                          n = blk[c]
                          bank = P.psum()
                          pt, pb = bank
                          regO[c] = bank
                          qT_ = qkv[:, 0, h % 2, n * 128:(n + 1) * 128]
                          op("pe", lambda e: e.matmul(pt[:, 0:128], qT_, Sb[c - c % 2][:], start=True, stop=True), r=[qkvb, sbuf_[c - c % 2]], w=[pb])
                          op("pe", lambda e: e.matmul(pt[:, 128:256], AT[c][:], vn[c - c % 2][:], start=True, stop=True), r=[cbuf[c], vnb[c - c % 2]], w=[pb])
                          op("pe", lambda e: e.matmul(pt[:, 256:384], kg[c][:, 1 + a, :], vn[c - c % 2][:], start=True, stop=True), r=[cbuf[c], vnb[c - c % 2]], w=[pb])
                      for c, (h, d, sub) in enumerate(chains):
                          if sub != sub_s:
                              continue
                          a = step_i if d == 0 else 1 - step_i
                          lo, hi = a * 64, (a + 1) * 64
                          pt, pb = regO[c]
                          op("dve", lambda e: e.scalar_tensor_tensor(out=S[c - c % 2][:], in0=S[c - c % 2][:], scalar=SCA[:, d, 2 + a, blk[c], h % 2:h % 2 + 1], in1=pt[:, 256:384], op0=ALU.mult, op1=ALU.add),
                             r=[pb, cbuf[c], sbuf_[c - c % 2]], w=[sbuf_[c - c % 2]])
                          op("act", lambda e: e.activation(out=Sb[c - c % 2][:], in_=S[c - c % 2][:], func=AF.Copy), r=[sbuf_[c - c % 2]], w=[sbuf_[c - c % 2]])
                          op("act", lambda e: e.activation(out=t1[c][lo:hi, :], in_=pt[lo:hi, 0:128], func=AF.Copy, scale=SCA[lo:hi, d, 0, blk[c], h % 2:h % 2 + 1]), r=[pb, cbuf[c], scab], w=[cbuf[c]])
                          op("dve", lambda e: e.tensor_tensor(out=ot[c][lo:hi, :], in0=t1[c][lo:hi, :], in1=pt[lo:hi, 128:256], op=ALU.add), r=[pb, cbuf[c]], w=[otb[c]])
                  for c, (h, d, sub) in enumerate(chains):
                      n = blk[c]
                      dma("sp", OBD[d, h, n * 128:(n + 1) * 128, :], ot[c][:], r=[otb[c]], w=obdb)
              kb.barrier()
        with contextlib.ExitStack() as es:
            gn = P.sb(es, "gnorm", [128, 1], F32); gnb = Buf()
            dma("sp", gn[:], gdn_norm[j], w=[gnb])
            eps_ = P.sb(es, "geps", [128, 1], F32)
            op("dve", lambda e: e.memset(eps_[:], EPS), w=[gnb])
            OB = [[P.sb(es, f"OB{i}_{d}", [128, NB, 128], F32) for d in range(2)] for i in range(2)]
            obb = [Buf() for _ in range(2)]
            GT = [P.sb(es, f"GT{i}", [128, NB, 128], F32) for i in range(2)]
            gtb = [Buf() for _ in range(2)]
            rst = [P.sb(es, f"rst{i}", [128, NB], F32) for i in range(2)]
            rstb = [Buf() for _ in range(2)]
            sqt = P.sb(es, "gsq", [128, 128], F32); sqtb = Buf()
            yo = [P.sb(es, f"gyo{i}", [128, 512], BF16) for i in range(2)]
            yob = [Buf() for _ in range(2)]
            ig = 0
            for h in range(4):
                o0_, o1_ = OB[h % 2]
                ob_ = obb[h % 2]
                g_, gb_ = GT[h % 2], gtb[h % 2]
                r_, rb_ = rst[h % 2], rstb[h % 2]
                dma("sp", o0_[:], OBD[0, h].rearrange("(n p) e -> p n e", p=128), r=obdb, w=[ob_])
                dma("sp", o1_[:], OBD[1, h].rearrange("(n p) e -> p n e", p=128), r=obdb, w=[ob_])
                dma("sp", g_[:], GG[:, h * 128:(h + 1) * 128].rearrange("(n p) e -> p n e", p=128), r=ggb, w=[gb_])
                op("dve", lambda e: e.tensor_tensor(out=o0_[:], in0=o0_[:], in1=o1_[:], op=ALU.add), r=[ob_], w=[ob_])
                for n in range(NB):
                    op("act", lambda e: e.activation(out=sqt[:], in_=o0_[:, n, :], func=AF.Square, accum_out=r_[:, n:n + 1]), r=[ob_], w=[sqtb, rb_])
                op("act", lambda e: e.activation(out=r_[:], in_=r_[:], func=AF.Sqrt, bias=eps_[:], scale=1.0 / 128), r=[rb_, gnb], w=[rb_])
                op("dve", lambda e: e.reciprocal(r_[:], r_[:]), r=[rb_], w=[rb_])
                for n0 in range(0, NB, 4):
                    nn = min(4, NB - n0)
                    o_, ob2 = yo[ig % 2], yob[ig % 2]
                    ig += 1
                    pt, pb = P.psum()
                    for q in range(nn):
                        n = n0 + q
                        op("dve", lambda e: e.scalar_tensor_tensor(out=o0_[:, n, :], in0=o0_[:, n, :], scalar=r_[:, n:n + 1], in1=g_[:, n, :], op0=ALU.mult, op1=ALU.mult),
                           r=[ob_, rb_, gb_], w=[ob_])
                        op("pe", lambda e: e.transpose(pt[:, q * 128:(q + 1) * 128], o0_[:, n, :], ident[:]), r=[ob_, cb], w=[pb])
                    op("act", lambda e: e.activation(out=o_[:, 0:nn * 128], in_=pt[:, 0:nn * 128], func=AF.Copy, scale=gn[:, 0:1]), r=[pb, gnb], w=[ob2])
                    dma("sp", YS[512 + h * 128:512 + (h + 1) * 128, n0 * 128:(n0 + nn) * 128], o_[:, 0:nn * 128], r=[ob2], w=ysb)

    for l in layers:
        last = (l == DEPTH - 1)
        if cfg.get("do_mixer", True):
            if l % 2 == 1:
                odd_mixer(l, skip_ctx=last)
            else:
                even_mixer(l, skip_ctx=last)
        if cfg.get("do_ffn", True):
            ffn_layer(l, skip_ctx=last)

    if cfg.get("dump_ys"):
        yd = P.outp("ys_dump", [D, NT])
        with contextlib.ExitStack() as es:
            tb16 = [P.sb(es, f"dy{i}", [128, 8, 512], BF16) for i in range(2)]
            tb32 = [P.sb(es, f"dz{i}", [128, 8, 512], F32) for i in range(2)]
            tbb = [Buf() for _ in range(2)]
            for ti, (t0, T) in enumerate(TILES):
                dma("sp", tb16[ti % 2][:, :, :T], YS[:, t0:t0 + T].rearrange("(c p) t -> p c t", p=128), r=ysb, w=[tbb[ti % 2]])
                op("dve", lambda e: e.tensor_copy(out=tb32[ti % 2][:, :, :T], in_=tb16[ti % 2][:, :, :T]), r=[tbb[ti % 2]], w=[tbb[ti % 2]])
                dma("sp", yd[:, t0:t0 + T].rearrange("(c p) t -> p c t", p=128), tb32[ti % 2][:, :, :T], r=[tbb[ti % 2]])
            kb.barrier()
    if cfg.get("dump_xt"):
        xd = P.outp("xt_dump", [D, NT])
        with contextlib.ExitStack() as es:
            tb = [P.sb(es, f"db{i}", [128, 8, 512], F32) for i in range(2)]
            tbb = [Buf() for _ in range(2)]
            for ti, (t0, T) in enumerate(TILES):
                dma("sp", tb[ti % 2][:, :, :T], XT[:, t0:t0 + T].rearrange("(c p) t -> p c t", p=128), r=[xb[ti]], w=[tbb[ti % 2]])
                dma("sp", xd[:, t0:t0 + T].rearrange("(c p) t -> p c t", p=128), tb[ti % 2][:, :, :T], r=[tbb[ti % 2]])
            kb.barrier()

    with contextlib.ExitStack() as es:
        xt = [P.sb(es, f"fx{i}", [128, 8, 512], F32) for i in range(2)]
        xtb = [Buf() for _ in range(2)]
        sq = [P.sb(es, f"fsq{i}", [128, 8, 512], F32) for i in range(2)]
        sqb = [Buf() for _ in range(2)]
        rs = [P.sb(es, f"frs{i}", [128, 512], F32) for i in range(2)]
        rsb = [Buf() for _ in range(2)]
        ot = [P.sb(es, f"fo{i}", [128, D], F32) for i in range(2)]
        otb = [Buf() for _ in range(2)]
        io = 0
        for ti, (t0, T) in enumerate(TILES):
            if ti == 0:
                continue
            x_, xb_ = xt[ti % 2], xtb[ti % 2]
            q_, qb_ = sq[ti % 2], sqb[ti % 2]
            r_, rb_ = rs[ti % 2], rsb[ti % 2]
            dma("sp", x_[:], XT[:, t0:t0 + T].rearrange("(c p) t -> p c t", p=128), r=[xb[ti]], w=[xb_])
            for c in range(8):
                op("act", lambda e: e.activation(out=q_[:, c, :], in_=x_[:, c, :], func=AF.Square), r=[xb_], w=[qb_])
            pt, pb = P.psum()
            for c in range(8):
                op("pe", lambda e: e.matmul(pt[:], ones[:], q_[:, c, :], start=(c == 0), stop=(c == 7)), r=[qb_, cb], w=[pb])
            op("act", lambda e: e.activation(out=r_[:], in_=pt[:], func=AF.Sqrt, bias=epsb[:], scale=1.0 / D), r=[pb, cb], w=[rb_])
            op("dve", lambda e: e.reciprocal(r_[:], r_[:]), r=[rb_], w=[rb_])
            for c in range(8):
                op("dve", lambda e: e.scalar_tensor_tensor(out=q_[:, c, :], in0=x_[:, c, :], scalar=fng[:, c:c + 1], in1=r_[:],
                                                           op0=ALU.mult, op1=ALU.mult), r=[xb_, rb_, cb, qb_], w=[qb_])
            for bi in range(4):
                o_, ob_ = ot[io % 2], otb[io % 2]
                io += 1
                for half in range(2):
                    pt, pb = P.psum()
                    for q in range(4):
                        c = half * 4 + q
                        op("pe", lambda e: e.transpose(pt[:, q * 128:(q + 1) * 128], q_[:, c, bi * 128:(bi + 1) * 128], ident[:]), r=[qb_, cb], w=[pb])
                    if half:
                        op("act", lambda e: e.activation(out=o_[:, 512:1024], in_=pt[:], func=AF.Copy), r=[pb], w=[ob_])
                    else:
                        op("dve", lambda e: e.tensor_copy(out=o_[:, 0:512], in_=pt[:]), r=[pb], w=[ob_])
                tk0 = t0 - CT + bi * 128
                dma("sp", out[tk0:tk0 + 128, :], o_[:], r=[ob_])
        kb.barrier()

    kb.finish()
    root.close()
    return P


def host_inputs(inputs, b):
    f = lambda a: np.ascontiguousarray(a, dtype=np.float32)
    c2 = np.stack([inputs["c"][b], inputs["c_ctx"]], -1).reshape(8, 128, 2).transpose(1, 0, 2)
    m = {
        "x": f(inputs["x"][b]),
        "ctx": f(inputs["ctx"][b]),
        "c2": f(c2),
        "w_ada": f(inputs["w_ada"]),
        "b_ada": f(inputs["b_ada"].reshape(DEPTH, 48, 128).transpose(0, 2, 1)),
        "norm_mix": f(inputs["norm_mix"].reshape(DEPTH, 8, 128).transpose(0, 2, 1)),
        "norm_ffn": f(inputs["norm_ffn"].reshape(DEPTH, 8, 128).transpose(0, 2, 1)),
        "final_norm": f(inputs["final_norm"].reshape(8, 128).T),
        "ffn_w_up": f(inputs["ffn_w_up"]),
        "ffn_conv": f(inputs["ffn_conv"].reshape(DEPTH, 3, 44, 128).transpose(0, 3, 2, 1)),
        "ffn_w_down": f(inputs["ffn_w_down"]),
        "ident": np.eye(128, dtype=np.float32),
    }
    m["od_w_in"] = f(inputs["od_w_in"])
    m["od_w_out"] = f(inputs["od_w_out"])
    m["lru_conv"] = f(inputs["lru_conv"].reshape(2, 4, 4, 128).transpose(0, 3, 2, 1))
    m["lru_conv_b"] = f(inputs["lru_conv_b"].reshape(2, 4, 128).transpose(0, 2, 1))
    def bd(w):
        o = np.zeros((2, 2, 4, 128, 128), np.float32)
        for c in range(4):
            o[:, :, c, 0:64, 0:64] = w[:, :, 2 * c]
            o[:, :, c, 64:128, 64:128] = w[:, :, 2 * c + 1]
        return o
    m["lru_wa"] = bd(inputs["lru_wa"])
    m["lru_wx"] = bd(inputs["lru_wx"])
    for nm, key in (("lru_ba", "lru_ba"), ("lru_bx", "lru_bx"), ("lru_lam", "lru_lambda")):
        m[nm] = f(inputs[key].reshape(2, 2, 4, 128).transpose(0, 3, 1, 2))
    m["swa_sink"] = f(np.broadcast_to(inputs["swa_sink"][:, None, :], (2, 128, 8)))
    m["ev_w_in"] = f(inputs["ev_w_in"])
    m["ev_w_out"] = f(inputs["ev_w_out"])
    m["diff_lambda"] = f(np.broadcast_to(inputs["diff_lambda"].reshape(2, 1, 256), (2, 128, 256)))
    m["diff_subln"] = f(inputs["diff_subln"].reshape(2, 128, 1))
    m["gdn_conv"] = f(inputs["gdn_conv"].reshape(2, 4, 12, 128).transpose(0, 3, 2, 1))
    m["gdn_a_log"] = f(np.broadcast_to(inputs["gdn_a_log"].reshape(2, 1, 8), (2, 128, 8)))
    m["gdn_dt_bias"] = f(np.broadcast_to(inputs["gdn_dt_bias"].reshape(2, 1, 8), (2, 128, 8)))
    m["gdn_norm"] = f(inputs["gdn_norm"].reshape(2, 128, 1))
    m.update(CONSTS)
    return m


def _make_consts():
    t = np.arange(LT)
    row = (t // 64).astype(np.float64)
    col = (t % 64).astype(np.float64)
    inv = (10000.0 ** (-np.arange(0, 32, 2, dtype=np.float32) / 32)).astype(np.float32)
    ar = (row[:, None].astype(np.float32) * inv).astype(np.float32)
    ac = (col[:, None].astype(np.float32) * inv).astype(np.float32)
    cr, sr, cc, sc = np.cos(ar), np.sin(ar), np.cos(ac), np.sin(ac)
    cosT = np.zeros((128, LT), np.float32)
    sinT = np.zeros((128, LT), np.float32)
    perm = np.zeros((128, 128), np.float32)
    for p in range(128):
        dd = p % 64
        i = dd % 16
        q = dd // 16
        if q == 0:
            cosT[p], sinT[p], partner = cr[:, i], -sr[:, i], p + 16
        elif q == 1:
            cosT[p], sinT[p], partner = cr[:, i], sr[:, i], p - 16
        elif q == 2:
            cosT[p], sinT[p], partner = cc[:, i], -sc[:, i], p + 16
        else:
            cosT[p], sinT[p], partner = cc[:, i], sc[:, i], p - 16
        perm[partner, p] = 1.0
    a = np.arange(128)[:, None]
    bq = np.arange(128)[None, :]
    NEG = -30000.0
    mprev = np.where(a >= bq, 0.0, NEG).astype(np.float32)
    mnext = np.where(a <= bq, 0.0, NEG).astype(np.float32)
    tt = np.arange(128)[:, None]
    ii = np.arange(128)[None, :]
    same = (tt // 64) == (ii // 64)
    gmask = np.zeros((128, 2, 3, 128), np.float32)
    gmask[:, 0, 0] = (tt <= ii) & same
    gmask[:, 0, 1] = (tt > ii) & same
    gmask[:, 0, 2] = np.where((tt <= ii) & same, 0.0, NEG)
    gmask[:, 1, 0] = (tt >= ii) & same
    gmask[:, 1, 1] = (tt < ii) & same
    gmask[:, 1, 2] = np.where((tt >= ii) & same, 0.0, NEG)
    gaux = np.zeros((128, 258), np.float32)
    gaux[0:64, 0:128] = 1.0
    gaux[64:128, 128:256] = 1.0
    gaux[0:64, 256] = 1.0
    gaux[64:128, 257] = 1.0
    return {"cosT": cosT, "sinT": sinT, "permT": perm, "mprev": np.tile(mprev, (1, 4)), "mnext": np.tile(mnext, (1, 4)), "gmask": gmask, "gaux": gaux}


CONSTS = _make_consts()


def kernel(**inputs):
    inputs = {k: np.asarray(v) for k, v in inputs.items()}
    P = build()
    n = 8
    in_maps = []
    for b in range(n):
        m = host_inputs(inputs, b)
        in_maps.append({k: m[k] for k in P.din})
    res = run_bass_kernel_spmd(P.nc, in_maps, core_ids=list(range(n)))
    return np.stack([np.asarray(r["out"], dtype=np.float32) for r in res.results], 0)
```

```python
import contextlib
import math
import numpy as np
import concourse.bass as bass
import concourse.mybir as mybir
from concourse.bass_utils import run_bass_kernel_spmd

F32 = mybir.dt.float32
BF16 = mybir.dt.bfloat16
AF = mybir.ActivationFunctionType
ALU = mybir.AluOpType
AX = mybir.AxisListType

D = 1024
NT = 4352
CT = 256
LT = 4096
DEPTH = 4
FFN = 2816
EPS = 1e-6
TILES = [(0, 256)] + [(256 + 512 * i, 512) for i in range(8)]


class Buf:
    __slots__ = ("w", "r", "name", "excl")

    def __init__(self, name="", excl=False):
        self.w = None
        self.r = []
        self.name = name
        self.excl = excl


class KB:
    EPOCH = 24000
    NDMA = 40

    def __init__(self, nc, same_eng_sync=True):
        self.nc = nc
        self.es = contextlib.ExitStack()
        self.eng = {"pe": nc.tensor, "act": nc.scalar, "dve": nc.vector, "pool": nc.gpsimd, "sp": nc.sync}
        self.same = same_eng_sync
        self.cnt = {e: 0 for e in self.eng}
        self.epoch = {e: 0 for e in self.eng}
        self.sems = {}
        self.known = {e: {} for e in self.eng}
        self.last_tok = {e: None for e in self.eng}
        self.dsem = [self.es.enter_context(nc.semaphore(f"dq{i}")) for i in range(self.NDMA)]
        self.dtot = [0] * self.NDMA
        self.dnext = 0
        self.nwait = 0
        self.nins = 0
        self.snaps = {}
        for e in self.eng:
            self._newsem(e)

    def _newsem(self, e):
        key = (e, self.epoch[e])
        self.sems[key] = self.es.enter_context(self.nc.semaphore(f"s_{e}_{self.epoch[e]}"))
        self.cnt[e] = 0

    def _semh(self, key):
        if isinstance(key, int):
            return self.dsem[key]
        return self.sems[key]

    def _wait(self, e, tok):
        key, val = tok
        if self.known[e].get(key, 0) >= val:
            return
        self.eng[e].wait_ge(self._semh(key), val)
        self.known[e][key] = val
        self.nwait += 1
        snap = self.snaps.get(tok)
        if snap:
            ke = self.known[e]
            for k2, v2 in snap.items():
                if ke.get(k2, 0) < v2:
                    ke[k2] = v2

    def _deps(self, e, r, w):
        toks = []
        for b in r:
            if b.w is not None:
                toks.append(b.w)
        for b in w:
            if b.w is not None:
                toks.append(b.w)
            toks.extend(b.r)
        for tok in toks:
            key = tok[0]
            if (not isinstance(key, int)) and key[0] == e and (e == "pe" or not self.same):
                continue
            self._wait(e, tok)

    def _mark(self, tok, r, w):
        for b in w:
            b.w = tok
            b.r = []
        for b in r:
            if b not in w:
                b.r.append(tok)
                if len(b.r) > 24:
                    d = {}
                    for k, v in b.r:
                        if d.get(k, 0) < v:
                            d[k] = v
                    b.r = list(d.items())

    def op(self, e, fn, r=(), w=()):
        w = list(w) + [b for b in r if b.excl and b not in w]
        r = [b for b in r if not b.excl]
        self._deps(e, r, w)
        if self.cnt[e] >= self.EPOCH:
            self.epoch[e] += 1
            self._newsem(e)
        ins = fn(self.eng[e])
        key = (e, self.epoch[e])
        self.cnt[e] += 1
        ins.then_inc(self.sems[key], 1)
        tok = (key, self.cnt[e])
        self.last_tok[e] = tok
        self.snaps[tok] = dict(self.known[e])
        self._mark(tok, r, w)
        self.nins += 1
        return tok

    def dma(self, e, out, in_, r=(), w=(), **kw):
        r = list(r)
        w = list(w)
        self._deps(e, r, w)
        if e == "pool":
            self.dnext_sw = getattr(self, "dnext_sw", 0)
            i = self.NDMA - 4 + self.dnext_sw
            self.dnext_sw = (self.dnext_sw + 1) % 4
        else:
            i = self.dnext
            self.dnext = (self.dnext + 1) % (self.NDMA - 4)
        if self.dtot[i]:
            self._wait(e, (i, self.dtot[i]))
        if e == "pool":
            self.swq = getattr(self, "swq", [])
            if len(self.swq) >= 3:
                self._wait(e, self.swq[-3])
        ins = self.eng[e].dma_start(out=out, in_=in_, **kw)
        self.dtot[i] += 16
        ins.then_inc(self.dsem[i], 16)
        tok = (i, self.dtot[i])
        if e == "pool":
            self.swq.append(tok)
        self._mark(tok, r, w)
        self.nins += 1
        return tok

    def barrier(self):
        toks = [t for t in self.last_tok.values() if t is not None]
        toks += [(i, self.dtot[i]) for i in range(self.NDMA) if self.dtot[i]]
        for e in self.eng:
            for tok in toks:
                key = tok[0]
                if (not isinstance(key, int)) and key[0] == e:
                    continue
                self._wait(e, tok)

    def finish(self):
        self.barrier()
        self.es.close()


class Prog:
    def __init__(self, cfg=None):
        self.cfg = cfg or {}
        self.nc = bass.Bass("TRN2", target_bir_lowering=False)
        self.kb = KB(self.nc)
        self.root = contextlib.ExitStack()
        self.din = {}
        self.dbuf = {}
        self.psn = 0

    def inp(self, name, shape, dt=F32):
        t = self.nc.dram_tensor(name, list(shape), dt, kind="ExternalInput").ap()
        self.din[name] = t
        return t

    def outp(self, name, shape, dt=F32):
        return self.nc.dram_tensor(name, list(shape), dt, kind="ExternalOutput").ap()

    def scratch(self, name, shape, dt):
        return self.nc.dram_tensor(name, list(shape), dt, kind="Internal").ap()

    def sb(self, es, name, shape, dt):
        self.nsb = getattr(self, "nsb", 0) + 1
        return es.enter_context(self.nc.sbuf_tensor(f"sb{self.nsb}_{name}", list(shape), dt))

    def scope(self, name):
        return self.nc.named_scope(name)

    def setup_psum(self):
        self.ps = [self.root.enter_context(self.nc.psum_tensor(f"ps{i}", [128, 512], F32)) for i in range(8)]
        self.psb = [Buf(f"ps{i}", excl=True) for i in range(8)]

    def psum(self):
        lo, hi = getattr(self, "psr", (0, 8))
        if not (lo <= self.psn < hi):
            self.psn = lo
        i = self.psn
        self.psn = lo + (self.psn + 1 - lo) % (hi - lo)
        return self.ps[i], self.psb[i]


def build(cfg=None):
    cfg = cfg or {}
    P = Prog(cfg)
    nc, kb = P.nc, P.kb
    op, dma = kb.op, kb.dma
    P.setup_psum()
    root = P.root
    layers = cfg.get("layers", list(range(DEPTH)))

    x_in = P.inp("x", [LT, D])
    ctx_in = P.inp("ctx", [CT, D])
    c2_in = P.inp("c2", [128, 8, 2])
    w_ada = P.inp("w_ada", [DEPTH, D, 6 * D])
    b_ada = P.inp("b_ada", [DEPTH, 128, 48])
    norm_mix = P.inp("norm_mix", [DEPTH, 128, 8])
    norm_ffn = P.inp("norm_ffn", [DEPTH, 128, 8])
    final_norm = P.inp("final_norm", [128, 8])
    ffn_w_up = P.inp("ffn_w_up", [DEPTH, D, 2 * FFN])
    ffn_conv = P.inp("ffn_conv", [DEPTH, 128, 44, 3])
    ffn_w_down = P.inp("ffn_w_down", [DEPTH, FFN, D])
    ident_in = P.inp("ident", [128, 128])
    od_w_in = P.inp("od_w_in", [2, D, 1792])
    od_w_out = P.inp("od_w_out", [2, D, D])
    lru_conv = P.inp("lru_conv", [2, 128, 4, 4])
    lru_conv_b = P.inp("lru_conv_b", [2, 128, 4])
    lru_wa = P.inp("lru_wa", [2, 2, 4, 128, 128])
    lru_wx = P.inp("lru_wx", [2, 2, 4, 128, 128])
    lru_ba = P.inp("lru_ba", [2, 128, 2, 4])
    lru_bx = P.inp("lru_bx", [2, 128, 2, 4])
    lru_lam = P.inp("lru_lam", [2, 128, 2, 4])
    swa_sink = P.inp("swa_sink", [2, 128, 8])
    cos_in = P.inp("cosT", [128, LT])
    sin_in = P.inp("sinT", [128, LT])
    perm_in = P.inp("permT", [128, 128])
    mprev_in = P.inp("mprev", [128, 512])
    mnext_in = P.inp("mnext", [128, 512])
    ev_w_in = P.inp("ev_w_in", [2, D, 3600])
    ev_w_out = P.inp("ev_w_out", [2, D, D])
    diff_lambda = P.inp("diff_lambda", [2, 128, 256])
    diff_subln = P.inp("diff_subln", [2, 128, 1])
    gdn_conv = P.inp("gdn_conv", [2, 128, 12, 4])
    gdn_a_log = P.inp("gdn_a_log", [2, 128, 8])
    gdn_dt_bias = P.inp("gdn_dt_bias", [2, 128, 8])
    gdn_norm = P.inp("gdn_norm", [2, 128, 1])
    gmask_in = P.inp("gmask", [128, 2, 3, 128])
    gaux_in = P.inp("gaux", [128, 258])
    out = P.outp("out", [LT, D])

    XT = P.scratch("XT", [D, NT], F32)
    xb = [Buf(f"X{t}") for t in range(9)]
    GS = P.scratch("GS", [FFN, NT], BF16)
    gsb = [Buf(f"G{j}") for j in range(22)]
    wup_b = P.scratch("wup_b", [DEPTH, D, 2 * FFN], BF16)
    wdn_b = P.scratch("wdn_b", [DEPTH, FFN, D], BF16)
    wupb = [Buf() for _ in range(DEPTH)]
    wdnb = [Buf() for _ in range(DEPTH)]

    odin_b = P.scratch("odin_b", [2, D, 1792], BF16)
    odout_b = P.scratch("odout_b", [2, D, D], BF16)
    odinb = [Buf() for _ in range(2)]
    odoutb = [Buf() for _ in range(2)]
    YS = P.scratch("YS", [D, NT], BF16)
    ysb = [Buf("Y")]
    QD = P.scratch("QD", [8, 64, NT], BF16)
    KD = P.scratch("KD", [2, 64, NT], BF16)
    qdb = [Buf("QD")]
    kdb = [Buf("KD")]

    evin_b = P.scratch("evin_b", [2, D, 3600], BF16)
    evout_b = P.scratch("evout_b", [2, D, D], BF16)
    evinb = [Buf() for _ in range(2)]
    evoutb = [Buf() for _ in range(2)]
    QA = P.scratch("QA", [8, 64, NT], BF16)
    KA = P.scratch("KA", [8, 64, NT], BF16)
    VA = P.scratch("VA", [NT, 512], BF16)
    GQ = P.scratch("GQ", [3, 4, 128, NT], BF16)
    GQK = P.scratch("GQK", [4, 128, NT], F32)
    GG = P.scratch("GG", [NT, 512], F32)
    BG = P.scratch("BG", [NT, 16], F32)
    OBD = P.scratch("OBD", [2, 4, NT, 128], F32)
    obdb = [Buf("OBD")]
    qab = [Buf("QA")]
    kab = [Buf("KA")]
    vab = [Buf("VA")]
    gqb = [Buf("GQ")]
    ggb = [Buf("GG")]
    bgb = [Buf("BG")]

    ident = P.sb(root, "ident", [128, 128], F32)
    ones = P.sb(root, "ones", [128, 128], F32)
    epsb = P.sb(root, "epsb", [128, 1], F32)
    ones16 = P.sb(root, "ones16", [128, 128], BF16)
    MOD = P.sb(root, "MOD", [128, DEPTH, 48, 2], F32)
    AMX = P.sb(root, "AMX", [128, DEPTH, 8, 2], F32)
    AFF = P.sb(root, "AFF", [128, DEPTH, 8, 2], F32)
    fconv = P.sb(root, "fconv", [128, DEPTH, 44, 3], F32)
    fng = P.sb(root, "fng", [128, 8], F32)
    cb = Buf("consts")
    modb = Buf("mod")
    dma("sp", ident[:], ident_in[:, :], w=[cb])
    op("dve", lambda e: e.memset(ones[:], 1.0), w=[cb])
    op("dve", lambda e: e.memset(epsb[:], EPS), w=[cb])
    op("dve", lambda e: e.memset(ones16[:], 1.0), w=[cb])
    dma("sp", fconv[:], ffn_conv.rearrange("l p j k -> p l j k"), w=[cb])
    dma("sp", fng[:], final_norm[:, :], w=[cb])

    def cast_rows(dst, src, R, C, wbuf, cchunk):
        dv = dst.rearrange("r (a c) -> (r a) c", c=cchunk)
        sv = src.rearrange("r (a c) -> (r a) c", c=cchunk)
        rows = R * (C // cchunk)
        r0 = 0
        while r0 < rows:
            n = min(2048, rows - r0)
            dma("pool", dv[r0:r0 + n, :], sv[r0:r0 + n, :], w=[wbuf])
            r0 += n

    for l in layers:
        if l % 2 == 0:
            cast_rows(evin_b[l // 2], ev_w_in[l // 2], D, 3600, evinb[l // 2], 1800)
            cast_rows(evout_b[l // 2], ev_w_out[l // 2], D, D, evoutb[l // 2], 1024)
        if l % 2 == 1:
            cast_rows(odin_b[l // 2], od_w_in[l // 2], D, 1792, odinb[l // 2], 1792)
            cast_rows(odout_b[l // 2], od_w_out[l // 2], D, D, odoutb[l // 2], 1024)
        cast_rows(wup_b[l], ffn_w_up[l], D, 2 * FFN, wupb[l], 1408)
        cast_rows(wdn_b[l], ffn_w_down[l], FFN, D, wdnb[l], 1024)

    with contextlib.ExitStack() as es, P.scope("mod"):
        c2 = P.sb(es, "c2", [128, 8, 2], F32)
        s2 = P.sb(es, "s2", [128, 8, 2], F32)
        bad = P.sb(es, "bad", [128, DEPTH, 48], F32)
        nmx = P.sb(es, "nmx", [128, DEPTH, 8], F32)
        nff = P.sb(es, "nff", [128, DEPTH, 8], F32)
        wa = [P.sb(es, f"wa{i}", [128, 8, 512], F32) for i in range(2)]
        wab = [Buf() for _ in range(2)]
        b0 = Buf()
        dma("sp", c2[:], c2_in[:, :, :], w=[b0])
        dma("sp", bad[:], b_ada.rearrange("l p j -> p l j"), w=[b0])
        dma("sp", nmx[:], norm_mix.rearrange("l p c -> p l c"), w=[b0])
        dma("sp", nff[:], norm_ffn.rearrange("l p c -> p l c"), w=[b0])
        op("act", lambda e: e.activation(out=s2[:], in_=c2[:], func=AF.Silu), r=[b0], w=[b0])
        it = 0
        for l in layers:
            for blk in range(12):
                wt, wb = wa[it % 2], wab[it % 2]
                it += 1
                dma("sp", wt[:], w_ada[l][:, blk * 512:(blk + 1) * 512].rearrange("(k p) n -> p k n", p=128), w=[wb])
                pt, pb = P.psum()
                for jj in range(4):
                    for k in range(8):
                        op("pe", lambda e: e.matmul(pt[:, jj * 2:jj * 2 + 2], wt[:, k, jj * 128:(jj + 1) * 128], s2[:, k, :],
                                                    start=(k == 0), stop=(k == 7)), r=[wb, b0], w=[pb])
                for jj in range(4):
                    j = blk * 4 + jj
                    op("dve", lambda e: e.tensor_scalar(out=MOD[:, l, j, :], in0=pt[:, jj * 2:jj * 2 + 2], scalar1=bad[:, l, j:j + 1],
                                                        scalar2=None, op0=ALU.add), r=[pb, b0], w=[modb])
            for s in range(2):
                op("dve", lambda e: e.scalar_tensor_tensor(out=AMX[:, l, :, s], in0=MOD[:, l, 8:16, s], scalar=1.0, in1=nmx[:, l, :],
                                                           op0=ALU.add, op1=ALU.mult), r=[modb, b0], w=[modb])
                op("dve", lambda e: e.scalar_tensor_tensor(out=AFF[:, l, :, s], in0=MOD[:, l, 32:40, s], scalar=1.0, in1=nff[:, l, :],
                                                           op0=ALU.add, op1=ALU.mult), r=[modb, b0], w=[modb])
        kb.barrier()

    dbg_x = cfg.get("x_override")
    if dbg_x:
        xo = P.inp("xo", [D, NT])
        with contextlib.ExitStack() as es:
            tb = [P.sb(es, f"tb{i}", [128, 8, 512], F32) for i in range(2)]
            tbb = [Buf() for _ in range(2)]
            for ti, (t0, T) in enumerate(TILES):
                dma("sp", tb[ti % 2][:, :, :T], xo[:, t0:t0 + T].rearrange("(c p) t -> p c t", p=128), w=[tbb[ti % 2]])
                dma("sp", XT[:, t0:t0 + T].rearrange("(c p) t -> p c t", p=128), tb[ti % 2][:, :, :T], r=[tbb[ti % 2]], w=[xb[ti]])
            kb.barrier()
    else:
        with contextlib.ExitStack() as es:
            tin = [P.sb(es, f"tin{i}", [128, D], F32) for i in range(2)]
            tinb = [Buf() for _ in range(2)]
            tout = [P.sb(es, f"tout{i}", [128, 8, 512], F32) for i in range(2)]
            toutb = [Buf() for _ in range(2)]
            it = 0
            for ti, (t0, T) in enumerate(TILES):
                to, tob = tout[ti % 2], toutb[ti % 2]
                for bi in range(T // 128):
                    tk0 = t0 + bi * 128
                    src = ctx_in[tk0:tk0 + 128, :] if tk0 < CT else x_in[tk0 - CT:tk0 - CT + 128, :]
                    tt, ttb = tin[it % 2], tinb[it % 2]
                    it += 1
                    dma("sp", tt[:], src, w=[ttb])
                    for half in range(2):
                        pt, pb = P.psum()
                        for q in range(4):
                            c = half * 4 + q
                            op("pe", lambda e: e.transpose(pt[:, q * 128:(q + 1) * 128], tt[:, c * 128:(c + 1) * 128], ident[:]),
                               r=[ttb, cb], w=[pb])
                        op("act" if half else "dve",
                           (lambda e: e.activation(out=to[:, 4:8, bi * 128:(bi + 1) * 128], in_=pt[:].rearrange("p (q t) -> p q t", q=4), func=AF.Copy))
                           if half else
                           (lambda e: e.tensor_copy(out=to[:, 0:4, bi * 128:(bi + 1) * 128], in_=pt[:].rearrange("p (q t) -> p q t", q=4))),
                           r=[pb], w=[tob])
                dma("sp", XT[:, t0:t0 + T].rearrange("(c p) t -> p c t", p=128), to[:, :, :T], r=[tob], w=[xb[ti]])
            kb.barrier()

    def norm_stage(es, l, A, shift_idx, U, ub):
        xt = [P.sb(es, f"nx{i}", [128, 8, 512], F32) for i in range(2)]
        xtb = [Buf() for _ in range(2)]
        sq = [P.sb(es, f"nsq{i}", [128, 8, 512], BF16) for i in range(2)]
        sqb = [Buf() for _ in range(2)]
        rs = [P.sb(es, f"nrs{i}", [128, 512], F32) for i in range(2)]
        rsb = [Buf() for _ in range(2)]
        tm = [P.sb(es, f"ntm{i}", [128, 512], F32) for i in range(3)]
        tmb = [Buf() for _ in range(3)]
        k3 = 0
        for ti, (t0, T) in enumerate(TILES):
            s = 1 if ti == 0 else 0
            x_, xb_ = xt[ti % 2], xtb[ti % 2]
            q_, qb_ = sq[ti % 2], sqb[ti % 2]
            r_, rb_ = rs[ti % 2], rsb[ti % 2]
            dma("sp", x_[:, :, :T], XT[:, t0:t0 + T].rearrange("(c p) t -> p c t", p=128), r=[xb[ti]], w=[xb_])
            for c in range(8):
                op("act", lambda e: e.activation(out=q_[:, c, :T], in_=x_[:, c, :T], func=AF.Square), r=[xb_], w=[qb_])
            pt, pb = P.psum()
            for c in range(8):
                op("pe", lambda e: e.matmul(pt[:, :T], ones16[:], q_[:, c, :T], start=(c == 0), stop=(c == 7)), r=[qb_, cb], w=[pb])
            op("act", lambda e: e.activation(out=r_[:, :T], in_=pt[:, :T], func=AF.Sqrt, bias=epsb[:], scale=1.0 / D), r=[pb, cb], w=[rb_])
            op("dve", lambda e: e.reciprocal(r_[:, :T], r_[:, :T]), r=[rb_], w=[rb_])
            for c in range(8):
                t_, tb_ = tm[k3 % 3], tmb[k3 % 3]
                k3 += 1
                op("dve", lambda e: e.tensor_tensor(out=t_[:, :T], in0=x_[:, c, :T], in1=r_[:, :T], op=ALU.mult), r=[xb_, rb_], w=[tb_])
                op("act", lambda e: e.activation(out=U[:, c, t0:t0 + T], in_=t_[:, :T], func=AF.Identity, scale=A[:, l, c, s:s + 1],
                                                 bias=MOD[:, l, shift_idx * 8 + c, s:s + 1]),
                   r=[tb_, modb], w=[ub[ti]])

    def proj_residual(es, l, gate_idx, W, wbuf, KC, SRC, srcb, skip_ctx):
        wsb = P.sb(es, "pr_w", [128, KC, D], BF16)
        wsbb = Buf()
        dma("sp", wsb[:], W.rearrange("(k p) n -> p k n", p=128), r=[wbuf], w=[wsbb])
        gt = [P.sb(es, f"pr_g{i}", [128, KC, 512], BF16) for i in range(2)]
        gtb = [Buf() for _ in range(2)]
        xt = [P.sb(es, f"pr_x{i}", [128, 8, 512], F32) for i in range(2)]
        xtb = [Buf() for _ in range(2)]
        for ti, (t0, T) in enumerate(TILES):
            if skip_ctx and ti == 0:
                continue
            s = 1 if ti == 0 else 0
            g_, gb_ = gt[ti % 2], gtb[ti % 2]
            x_, xb_ = xt[ti % 2], xtb[ti % 2]
            dma("sp", g_[:, :, :T], SRC[:, t0:t0 + T].rearrange("(k p) t -> p k t", p=128), r=srcb, w=[gb_])
            dma("sp", x_[:, :, :T], XT[:, t0:t0 + T].rearrange("(c p) t -> p c t", p=128), r=[xb[ti]], w=[xb_])
            for m in range(8):
                pt, pb = P.psum()
                for k in range(KC):
                    op("pe", lambda e: e.matmul(pt[:, :T], wsb[:, k, m * 128:(m + 1) * 128], g_[:, k, :T], start=(k == 0), stop=(k == KC - 1)),
                       r=[wsbb, gb_], w=[pb])
                op("dve", lambda e: e.scalar_tensor_tensor(out=x_[:, m, :T], in0=pt[:, :T], scalar=MOD[:, l, gate_idx * 8 + m, s:s + 1],
                                                           in1=x_[:, m, :T], op0=ALU.mult, op1=ALU.add), r=[pb, modb, xb_], w=[xb_])
            dma("sp", XT[:, t0:t0 + T].rearrange("(c p) t -> p c t", p=128), x_[:, :, :T], r=[xb_], w=[xb[ti]])

    def ffn_layer(l, skip_ctx):
        with contextlib.ExitStack() as es:
            U = P.sb(es, "U", [128, 8, NT], BF16)
            ub = [Buf(f"U{t}") for t in range(9)]
            with contextlib.ExitStack() as es2, P.scope("ffn_norm"):
                norm_stage(es2, l, AFF, 3, U, ub)
                kb.barrier()
            with contextlib.ExitStack() as es2, P.scope("ffn_up"):
                wt = [P.sb(es2, f"fw{i}", [128, 8, 256], BF16) for i in range(2)]
                wtb = [Buf() for _ in range(2)]
                H = P.sb(es2, "fH", [128, 2, NT], F32)
                Cg = P.sb(es2, "fC", [128, 2, NT], F32)
                hb = [[Buf() for _ in range(9)] for _ in range(2)]
                cgb = [[Buf() for _ in range(9)] for _ in range(2)]
                Go = [P.sb(es2, f"fG{i}", [128, NT], BF16) for i in range(2)]
                gob = [[Buf() for _ in range(9)] for _ in range(2)]
                t_start = 1 if skip_ctx else 0
                lo = CT if skip_ctx else 0
                seg_start = {0, CT}
                seg_end = {CT, NT}

                def conv_tile(j, ti):
                    t0, T = TILES[ti]
                    go_, gb_ = Go[j % 2], gob[j % 2][ti]
                    for gv in range(2):
                        ch = gv * 22 + j
                        oa = t0 + 1 if t0 in seg_start else t0
                        rds = [hb[gv][ti]] + ([hb[gv][ti - 1]] if (t0 not in seg_start) else [])
                        op("dve", lambda e: e.scalar_tensor_tensor(out=Cg[:, gv, oa:t0 + T], in0=H[:, gv, oa - 1:t0 + T - 1], scalar=fconv[:, l, ch, 0:1],
                                                                   in1=Cg[:, gv, oa:t0 + T], op0=ALU.mult, op1=ALU.add), r=rds + [cb], w=[cgb[gv][ti]])
                        ob_ = t0 + T - 1 if (t0 + T) in seg_end else t0 + T
                        rds = [hb[gv][ti]] + ([hb[gv][ti + 1]] if ((t0 + T) not in seg_end) else [])
                        op("dve", lambda e: e.scalar_tensor_tensor(out=Cg[:, gv, t0:ob_], in0=H[:, gv, t0 + 1:ob_ + 1], scalar=fconv[:, l, ch, 2:3],
                                                                   in1=Cg[:, gv, t0:ob_], op0=ALU.mult, op1=ALU.add), r=rds + [cb], w=[cgb[gv][ti]])
                    op("act", lambda e: e.activation(out=Cg[:, 0, t0:t0 + T], in_=Cg[:, 0, t0:t0 + T], func=AF.Silu), r=[cgb[0][ti]], w=[cgb[0][ti]])
                    op("dve", lambda e: e.tensor_tensor(out=go_[:, t0:t0 + T], in0=Cg[:, 0, t0:t0 + T], in1=Cg[:, 1, t0:t0 + T], op=ALU.mult),
                       r=[cgb[0][ti], cgb[1][ti]], w=[gb_])

                for j in range(22):
                    w_, wb_ = wt[j % 2], wtb[j % 2]
                    dma("sp", w_[:, :, 0:128], wup_b[l][:, j * 128:(j + 1) * 128].rearrange("(k p) n -> p k n", p=128), r=[wupb[l]], w=[wb_])
                    dma("sp", w_[:, :, 128:256], wup_b[l][:, FFN + j * 128:FFN + (j + 1) * 128].rearrange("(k p) n -> p k n", p=128),
                        r=[wupb[l]], w=[wb_])
                    for ti, (t0, T) in enumerate(TILES):
                        if ti < t_start:
                            continue
                        for gv in range(2):
                            pt, pb = P.psum()
                            for k in range(8):
                                op("pe", lambda e: e.matmul(pt[:, :T], w_[:, k, gv * 128:(gv + 1) * 128], U[:, k, t0:t0 + T],
                                                            start=(k == 0), stop=(k == 7)), r=[wb_, ub[ti]], w=[pb])
                            op("act", lambda e: e.activation(out=H[:, gv, t0:t0 + T], in_=pt[:, :T], func=AF.Copy), r=[pb], w=[hb[gv][ti]])
                            op("act", lambda e: e.activation(out=Cg[:, gv, t0:t0 + T], in_=pt[:, :T], func=AF.Copy, scale=fconv[:, l, gv * 22 + j, 1:2]),
                               r=[pb, cb], w=[cgb[gv][ti]])
                        if ti - 1 >= t_start:
                            conv_tile(j, ti - 1)
                    conv_tile(j, 8)
                    dma("sp", GS[j * 128:(j + 1) * 128, lo:NT], Go[j % 2][:, lo:NT], r=gob[j % 2][t_start:], w=[gsb[j]])
                kb.barrier()
        with contextlib.ExitStack() as es, P.scope("ffn_down"):
            proj_residual(es, l, 5, wdn_b[l], wdnb[l], 22, GS, gsb, skip_ctx)
            kb.barrier()

    def proj_fm(W, wbuf, c0, U, ub, dst, dstb, wt, wtb, ev="act"):
        dma("sp", wt[:], W[:, c0:c0 + 128].rearrange("(k p) n -> p k n", p=128), r=[wbuf], w=[wtb])
        for ti, (t0, T) in enumerate(TILES):
            pt, pb = P.psum()
            for k in range(8):
                op("pe", lambda e: e.matmul(pt[:, :T], wt[:, k, :], U[:, k, t0:t0 + T], start=(k == 0), stop=(k == 7)), r=[wtb, ub[ti]], w=[pb])
            if ev == "act":
                op("act", lambda e: e.activation(out=dst[:, t0:t0 + T], in_=pt[:, :T], func=AF.Copy), r=[pb], w=[dstb])
            else:
                op("dve", lambda e: e.tensor_copy(out=dst[:, t0:t0 + T], in_=pt[:, :T]), r=[pb], w=[dstb])

    def rope_store(es_, raw, rawb, cosT, sinT, permT, tb, dstrows, dstbuf, tmp, tmpb, ob, obb):
        op("dve", lambda e: e.tensor_copy(out=ob[:, 0:CT], in_=raw[:, 0:CT]), r=[rawb], w=[obb])
        for ti, (t0, T) in enumerate(TILES):
            if ti == 0:
                continue
            l0 = t0 - CT
            pt, pb = P.psum()
            op("pe", lambda e: e.matmul(pt[:, :T], permT[:], raw[:, t0:t0 + T], start=True, stop=True), r=[rawb, tb], w=[pb])
            op("dve", lambda e: e.tensor_tensor(out=tmp[:, :T], in0=pt[:, :T], in1=sinT[:, l0:l0 + T], op=ALU.mult), r=[pb, tb], w=[tmpb])
            op("dve", lambda e: e.tensor_tensor(out=raw[:, t0:t0 + T], in0=raw[:, t0:t0 + T], in1=cosT[:, l0:l0 + T], op=ALU.mult), r=[rawb, tb, pb], w=[rawb])
            op("dve", lambda e: e.tensor_tensor(out=ob[:, t0:t0 + T], in0=raw[:, t0:t0 + T], in1=tmp[:, :T], op=ALU.add), r=[rawb, tmpb], w=[obb])
        for hh in range(2):
            dma("sp", dstrows[hh], ob[hh * 64:(hh + 1) * 64, :], r=[obb], w=[dstbuf])

    def odd_mixer(l, skip_ctx):
        j = l // 2
        W = odin_b[j]
        wbuf = odinb[j]
        with contextlib.ExitStack() as es:
            U = P.sb(es, "U", [128, 8, NT], BF16)
            ub = [Buf(f"U{t}") for t in range(9)]
            with contextlib.ExitStack() as es2, P.scope("mix_norm"):
                norm_stage(es2, l, AMX, 0, U, ub)
                kb.barrier()
            with contextlib.ExitStack() as es2, P.scope("odd_lru"):
                wt = [P.sb(es2, f"ow{i}", [128, 8, 128], BF16) for i in range(2)]
                wtb = [Buf() for _ in range(2)]
                B1 = P.sb(es2, "B1", [128, NT], F32); b1 = Buf()
                B2 = P.sb(es2, "B2", [128, NT], F32); b2 = Buf()
                B2h = P.sb(es2, "B2h", [128, NT], BF16); b2h = Buf()
                B4 = P.sb(es2, "B4", [128, NT], F32); b4 = Buf()
                B5 = P.sb(es2, "B5", [128, NT], F32); b5 = Buf()
                B6 = P.sb(es2, "B6", [128, NT], F32); b6 = Buf()
                B7 = P.sb(es2, "B7", [128, NT], F32); b7 = Buf()
                lc = P.sb(es2, "lc", [128, 4, 4], F32)
                lcb_ = P.sb(es2, "lcb", [128, 4], F32)
                lba = P.sb(es2, "lba", [128, 2, 4], F32)
                lbx = P.sb(es2, "lbx", [128, 2, 4], F32)
                llam = P.sb(es2, "llam", [128, 2, 4], F32)
                c8 = P.sb(es2, "c8", [128, 2, 4], F32)
                c16 = P.sb(es2, "c16", [128, 2, 4], F32)
                bdf = P.sb(es2, "bdf", [128, 2, 128], F32)
                bda = [P.sb(es2, f"bda{i}", [128, 2, 128], BF16) for i in range(2)]
                bdab = [Buf() for _ in range(2)]
                bdfb = Buf()
                sb0 = Buf()
                dma("sp", lc[:], lru_conv[j], w=[sb0])
                dma("sp", lcb_[:], lru_conv_b[j], w=[sb0])
                dma("sp", lba[:], lru_ba[j], w=[sb0])
                dma("sp", lbx[:], lru_bx[j], w=[sb0])
                dma("sp", llam[:], lru_lam[j], w=[sb0])
                op("act", lambda e: e.activation(out=c8[:], in_=llam[:], func=AF.Exp, scale=-1.0), r=[sb0], w=[sb0])
                op("act", lambda e: e.activation(out=c8[:], in_=c8[:], func=AF.Ln, bias=ones[:, 0:1], scale=1.0), r=[sb0, cb], w=[sb0])
                op("dve", lambda e: e.tensor_scalar(out=c16[:], in0=c8[:], scalar1=-16.0, scalar2=None, op0=ALU.mult), r=[sb0], w=[sb0])
                op("dve", lambda e: e.tensor_scalar(out=c8[:], in0=c8[:], scalar1=-8.0, scalar2=None, op0=ALU.mult), r=[sb0], w=[sb0])
                segs = [(0, CT), (CT, NT)]
                ib = 0
                for c in range(4):
                    proj_fm(W, wbuf, c * 128, U, ub, B1, b1, wt[c % 2], wtb[c % 2])
                    op("act", lambda e: e.activation(out=B2[:], in_=B1[:], func=AF.Identity, scale=lc[:, c, 2:3], bias=lcb_[:, c:c + 1]),
                       r=[b1, sb0], w=[b2])
                    for (a, b) in segs:
                        for tap, off in ((0, -2), (1, -1), (3, 1)):
                            if off < 0:
                                oa, ob_, ia, ib_ = a - off, b, a, b + off
                            else:
                                oa, ob_, ia, ib_ = a, b - off, a + off, b
                            op("dve", lambda e: e.scalar_tensor_tensor(out=B2[:, oa:ob_], in0=B1[:, ia:ib_], scalar=lc[:, c, tap:tap + 1], in1=B2[:, oa:ob_],
                                                                       op0=ALU.mult, op1=ALU.add), r=[b1, sb0, b2], w=[b2])
                    op("act", lambda e: e.activation(out=B2h[:], in_=B2[:], func=AF.Copy), r=[b2], w=[b2h])
                    for d in range(2):
                        bd_, bdb_ = bda[ib % 2], bdab[ib % 2]
                        ib += 1
                        dma("sp", bdf[:, 0, :], lru_wa[j, d, c], w=[bdfb])
                        dma("sp", bdf[:, 1, :], lru_wx[j, d, c], w=[bdfb])
                        op("dve", lambda e: e.tensor_copy(out=bd_[:], in_=bdf[:]), r=[bdfb], w=[bdb_])
                        for ti, (t0, T) in enumerate(TILES):
                            pr, prb = P.psum()
                            op("pe", lambda e: e.matmul(pr[:, :T], bd_[:, 0, :], B2h[:, t0:t0 + T], start=True, stop=True), r=[bdb_, b2h], w=[prb])
                            op("act", lambda e: e.activation(out=B4[:, t0:t0 + T], in_=pr[:, :T], func=AF.Sigmoid, bias=lba[:, d, c:c + 1], scale=1.0),
                               r=[prb, sb0], w=[b4])
                            pi, pib = P.psum()
                            op("pe", lambda e: e.matmul(pi[:, :T], bd_[:, 1, :], B2h[:, t0:t0 + T], start=True, stop=True), r=[bdb_, b2h], w=[pib])
                            op("act", lambda e: e.activation(out=B5[:, t0:t0 + T], in_=pi[:, :T], func=AF.Sigmoid, bias=lbx[:, d, c:c + 1], scale=1.0),
                               r=[pib, sb0], w=[b5])
                        op("act", lambda e: e.activation(out=B1[:], in_=B4[:], func=AF.Exp, scale=c16[:, d, c:c + 1]), r=[b4, sb0], w=[b1])
                        op("act", lambda e: e.activation(out=B4[:], in_=B4[:], func=AF.Exp, scale=c8[:, d, c:c + 1]), r=[b4, sb0], w=[b4])
                        op("act", lambda e: e.activation(out=B1[:], in_=B1[:], func=AF.Sqrt, scale=-1.0, bias=ones[:, 0:1]), r=[b1, cb], w=[b1])
                        op("dve", lambda e: e.tensor_tensor(out=B5[:], in0=B5[:], in1=B2[:], op=ALU.mult), r=[b5, b2], w=[b5])
                        op("dve", lambda e: e.tensor_tensor(out=B5[:], in0=B5[:], in1=B1[:], op=ALU.mult), r=[b5, b1], w=[b5])
                        if d == 0:
                            op("dve", lambda e: e.tensor_tensor_scan(out=B6[:], data0=B4[:], data1=B5[:], initial=0.0, op0=ALU.mult, op1=ALU.add),
                               r=[b4, b5], w=[b6])
                        else:
                            op("dve", lambda e: e.tensor_tensor_scan(out=B7[:, 0:CT][:, ::-1], data0=B4[:, 0:CT][:, ::-1], data1=B5[:, 0:CT][:, ::-1],
                                                                     initial=0.0, op0=ALU.mult, op1=ALU.add), r=[b4, b5], w=[b7])
                            op("dve", lambda e: e.tensor_tensor_scan(out=B7[:, CT:NT][:, ::-1], data0=B4[:, CT:NT][:, ::-1], data1=B5[:, CT:NT][:, ::-1],
                                                                     initial=B7[:, 0:1], op0=ALU.mult, op1=ALU.add), r=[b4, b5, b7], w=[b7])
                            op("dve", lambda e: e.tensor_tensor(out=B6[:], in0=B6[:], in1=B7[:], op=ALU.add), r=[b6, b7], w=[b6])
                    proj_fm(W, wbuf, 512 + c * 128, U, ub, B1, b1, wt[c % 2], wtb[c % 2])
                    op("act", lambda e: e.activation(out=B1[:], in_=B1[:], func=AF.Gelu_apprx_tanh), r=[b1], w=[b1])
                    op("dve", lambda e: e.tensor_tensor(out=B2h[:], in0=B6[:], in1=B1[:], op=ALU.mult), r=[b6, b1, b2h], w=[b2h])
                    dma("sp", YS[c * 128:(c + 1) * 128, :], B2h[:], r=[b2h], w=ysb)
                kb.barrier()
            with contextlib.ExitStack() as es2, P.scope("odd_qk"):
                wt = [P.sb(es2, f"aw{i}", [128, 8, 128], BF16) for i in range(2)]
                wtb = [Buf() for _ in range(2)]
                cosT = P.sb(es2, "cosT", [128, LT], F32)
                sinT = P.sb(es2, "sinT", [128, LT], F32)
                permT = P.sb(es2, "permT", [128, 128], F32)
                tb = Buf()
                dma("sp", cosT[:], cos_in[:, :], w=[tb])
                dma("sp", sinT[:], sin_in[:, :], w=[tb])
                dma("sp", permT[:], perm_in[:, :], w=[tb])
                raw = [P.sb(es2, f"raw{i}", [128, NT], F32) for i in range(2)]
                rawb = [Buf() for _ in range(2)]
                tmp = P.sb(es2, "rtmp", [128, 512], F32); tmpb = Buf()
                ob = [P.sb(es2, f"rob{i}", [128, NT], BF16) for i in range(2)]
                obb = [Buf() for _ in range(2)]
                for c in range(5):
                    proj_fm(W, wbuf, 1024 + c * 128, U, ub, raw[c % 2], rawb[c % 2], wt[c % 2], wtb[c % 2])
                    if c < 4:
                        rows = [QD[2 * c + hh] for hh in range(2)]
                        dbuf_ = qdb[0]
                    else:
                        rows = [KD[hh] for hh in range(2)]
                        dbuf_ = kdb[0]
                    rope_store(es2, raw[c % 2], rawb[c % 2], cosT, sinT, permT, tb, rows, dbuf_, tmp, tmpb, ob[c % 2], obb[c % 2])
                kb.barrier()
            with contextlib.ExitStack() as es2, P.scope("odd_attn"):
                wv = P.sb(es2, "wv", [128, 8, 128], BF16); wvb = Buf()
                dma("sp", wv[:], W[:, 1664:1792].rearrange("(k p) n -> p k n", p=128), r=[wbuf], w=[wvb])
                V = P.sb(es2, "V", [128, 34, 2, 65], BF16); vb = Buf()
                op("pool", lambda e: e.memset(V[:], 1.0), w=[vb])
                for blk in range(34):
                    ti = 0 if blk < 2 else 1 + (blk - 2) // 4
                    pt, pb = P.psum()
                    for k in range(8):
                        op("pe", lambda e: e.matmul(pt[:, 0:128], U[:, k, blk * 128:(blk + 1) * 128], wv[:, k, :], start=(k == 0), stop=(k == 7)),
                           r=[wvb, ub[ti]], w=[pb])
                    op("act", lambda e: e.activation(out=V[:, blk, :, 0:64], in_=pt[:, 0:128].rearrange("p (a b) -> p a b", a=2), func=AF.Copy),
                       r=[pb], w=[vb])
                kT = P.sb(es2, "kT", [128, 2, NT], BF16); ktb = Buf()
                op("dve", lambda e: e.memset(kT[64:128, :, :], 0.0), w=[ktb])
                dma("sp", kT[0:64, :, :], KD.rearrange("h d t -> d h t"), r=kdb, w=[ktb])
                qT = [P.sb(es2, f"qT{i}", [128, 4, NT], BF16) for i in range(2)]
                qtb = [Buf() for _ in range(2)]
                for kv in range(2):
                    op("dve", lambda e: e.memset(qT[kv][64:128, :, :], 0.0), w=[qtb[kv]])
                idb16 = P.sb(es2, "idb16", [128, 128], BF16)
                mk = P.sb(es2, "mk", [128, 2, 512], BF16)
                mkf = P.sb(es2, "mkf", [128, 2, 512], F32)
                esk = P.sb(es2, "esk", [128, 8], F32)
                mb = Buf()
                dma("sp", mkf[:, 0, :], mprev_in[:, :], w=[mb])
                dma("sp", mkf[:, 1, :], mnext_in[:, :], w=[mb])
                dma("sp", esk[:], swa_sink[j], w=[mb])
                op("dve", lambda e: e.tensor_copy(out=mk[:], in_=mkf[:]), r=[mb], w=[mb])
                op("dve", lambda e: e.tensor_copy(out=idb16[:], in_=ident[:]), r=[cb], w=[mb])
                op("act", lambda e: e.activation(out=esk[:], in_=esk[:], func=AF.Exp), r=[mb], w=[mb])
                PT = [P.sb(es2, f"PT{i}", [128, 512], BF16) for i in range(10)]
                ptb = [Buf() for _ in range(10)]
                od = [P.sb(es2, f"od{i}", [128, 512], F32) for i in range(2)]
                odb = [Buf() for _ in range(2)]
                rd = [P.sb(es2, f"rd{i}", [128, 8], F32) for i in range(2)]
                rdb = [Buf() for _ in range(2)]
                yt = [P.sb(es2, f"yt{i}", [128, 4, 128], BF16) for i in range(2)]
                ytb = [Buf() for _ in range(2)]
                for kv in range(2):
                    dma("sp", qT[kv][0:64, :, :], QD[kv * 4:(kv + 1) * 4].rearrange("h d t -> d h t"), r=qdb, w=[qtb[kv]])
                ipt = 0
                qblocks = list(range(2, 34)) if skip_ctx else list(range(34))
                for qi, qb_ in enumerate(qblocks):
                    o_, ob_ = od[qi % 2], odb[qi % 2]
                    r_, rb_ = rd[qi % 2], rdb[qi % 2]
                    if qb_ < 2:
                        keys = [(0, None), (1, None)]
                    else:
                        keys = [(0, None), (1, None)]
                        if qb_ - 1 >= 2:
                            keys.append((qb_ - 1, 0))
                        keys.append((qb_, None))
                        if qb_ + 1 < 34:
                            keys.append((qb_ + 1, 1))
                    for kv in range(2):
                        pts = []
                        for (kblk, mtype) in keys:
                            ps_, psb_ = P.psum()
                            if mtype is not None:
                                op("pe", lambda e: e.matmul(ps_[:], idb16[:], mk[:, mtype, :], start=True, stop=False), r=[mb], w=[psb_])
                            op("pe", lambda e: e.matmul(ps_[:].rearrange("p (g q) -> p g q", g=4), kT[:, kv, kblk * 128:(kblk + 1) * 128],
                                                        qT[kv][:, :, qb_ * 128:(qb_ + 1) * 128], start=(mtype is None), stop=True),
                               r=[ktb, qtb[kv]], w=[psb_])
                            p_, pb_ = PT[ipt % 10], ptb[ipt % 10]
                            ipt += 1
                            op("act", lambda e: e.activation(out=p_[:], in_=ps_[:], func=AF.Exp, scale=0.125), r=[psb_], w=[pb_])
                            pts.append((p_, pb_, kblk))
                        po, pob = P.psum()
                        for g in range(4):
                            for ki, (p_, pb_, kblk) in enumerate(pts):
                                op("pe", lambda e: e.matmul(po[:, g * 65:(g + 1) * 65], p_[:, g * 128:(g + 1) * 128], V[:, kblk, kv, :],
                                                            start=(ki == 0), stop=(ki == len(pts) - 1)), r=[pb_, vb], w=[pob])
                        pov = po[:, 0:260].rearrange("p (g e) -> p g e", g=4)
                        op("dve", lambda e: e.tensor_tensor(out=r_[:, kv * 4:(kv + 1) * 4], in0=pov[:, :, 64], in1=esk[:, kv * 4:(kv + 1) * 4], op=ALU.add),
                           r=[pob, mb], w=[rb_])
                        op("dve", lambda e: e.reciprocal(r_[:, kv * 4:(kv + 1) * 4], r_[:, kv * 4:(kv + 1) * 4]), r=[rb_], w=[rb_])
                        for g in range(4):
                            h = kv * 4 + g
                            op("act" if g % 2 else "dve",
                               (lambda e: e.activation(out=o_[:, h * 64:(h + 1) * 64], in_=po[:, g * 65:g * 65 + 64], func=AF.Copy, scale=r_[:, h:h + 1]))
                               if g % 2 else
                               (lambda e: e.tensor_scalar(out=o_[:, h * 64:(h + 1) * 64], in0=po[:, g * 65:g * 65 + 64], scalar1=r_[:, h:h + 1], scalar2=None,
                                                          op0=ALU.mult)),
                               r=[pob, rb_], w=[ob_])
                    y_, yb_ = yt[qi % 2], ytb[qi % 2]
                    pt, pb = P.psum()
                    for c in range(4):
                        op("pe", lambda e: e.transpose(pt[:, c * 128:(c + 1) * 128], o_[:, c * 128:(c + 1) * 128], ident[:]), r=[ob_, cb], w=[pb])
                    op("act", lambda e: e.activation(out=y_[:], in_=pt[:].rearrange("p (c t) -> p c t", c=4), func=AF.Copy), r=[pb], w=[yb_])
                    dma("sp", YS[512:1024, qb_ * 128:(qb_ + 1) * 128].rearrange("(c p) t -> p c t", p=128), y_[:], r=[yb_], w=ysb)
                kb.barrier()
        with contextlib.ExitStack() as es, P.scope("odd_out"):
            proj_residual(es, l, 2, odout_b[j], odoutb[j], 8, YS, ysb, skip_ctx)
            kb.barrier()

    def proj_tm(W, wbuf, c0, ncol, U, ub, wt, wtb, consume):
        dma("sp", wt[:, :, 0:ncol], W[:, c0:c0 + ncol].rearrange("(k p) n -> p k n", p=128), r=[wbuf], w=[wtb])
        for blk in range(34):
            ti = 0 if blk < 2 else 1 + (blk - 2) // 4
            pt, pb = P.psum()
            for k in range(8):
                op("pe", lambda e: e.matmul(pt[:, 0:ncol], U[:, k, blk * 128:(blk + 1) * 128], wt[:, k, 0:ncol], start=(k == 0), stop=(k == 7)),
                   r=[wtb, ub[ti]], w=[pb])
            consume(blk, pt, pb)

    def even_mixer(l, skip_ctx):
        j = l // 2
        lam_init = 0.8 - 0.6 * math.exp(-0.3 * l)
        W = evin_b[j]
        wbuf = evinb[j]
        with contextlib.ExitStack() as es:
            U = P.sb(es, "U", [128, 8, NT], BF16)
            ub = [Buf(f"U{t}") for t in range(9)]
            with contextlib.ExitStack() as es2, P.scope("mix_norm"):
                norm_stage(es2, l, AMX, 0, U, ub)
                kb.barrier()
            with contextlib.ExitStack() as es2, P.scope("ev_qk"):
                wt = [P.sb(es2, f"aw{i}", [128, 8, 128], BF16) for i in range(2)]
                wtb = [Buf() for _ in range(2)]
                cosT = P.sb(es2, "cosT", [128, LT], F32)
                sinT = P.sb(es2, "sinT", [128, LT], F32)
                permT = P.sb(es2, "permT", [128, 128], F32)
                tb = Buf()
                dma("sp", cosT[:], cos_in[:, :], w=[tb])
                dma("sp", sinT[:], sin_in[:, :], w=[tb])
                dma("sp", permT[:], perm_in[:, :], w=[tb])
                raw = [P.sb(es2, f"raw{i}", [128, NT], F32) for i in range(2)]
                rawb = [Buf() for _ in range(2)]
                tmp = P.sb(es2, "rtmp", [128, 512], F32); tmpb = Buf()
                ob = [P.sb(es2, f"rob{i}", [128, NT], BF16) for i in range(2)]
                obb = [Buf() for _ in range(2)]
                for c in range(8):
                    proj_fm(W, wbuf, c * 128, U, ub, raw[c % 2], rawb[c % 2], wt[c % 2], wtb[c % 2])
                    if c < 4:
                        rows = [QA[2 * c + hh] for hh in range(2)]
                        dbuf_ = qab[0]
                    else:
                        rows = [KA[2 * (c - 4) + hh] for hh in range(2)]
                        dbuf_ = kab[0]
                    rope_store(es2, raw[c % 2], rawb[c % 2], cosT, sinT, permT, tb, rows, dbuf_, tmp, tmpb, ob[c % 2], obb[c % 2])
                kb.barrier()
            with contextlib.ExitStack() as es2, P.scope("ev_tm"):
                wt = [P.sb(es2, f"tw{i}", [128, 8, 512], BF16) for i in range(2)]
                wtb = [Buf() for _ in range(2)]
                vo = [P.sb(es2, f"vo{i}", [128, 512], BF16) for i in range(2)]
                vob = [Buf() for _ in range(2)]
                go = [P.sb(es2, f"go{i}", [128, 512], F32) for i in range(2)]
                gob = [Buf() for _ in range(2)]
                bgo = [P.sb(es2, f"bgo{i}", [128, 16], F32) for i in range(2)]
                bgob = [Buf() for _ in range(2)]
                tq = [P.sb(es2, f"tq{i}", [128, 8], F32) for i in range(4)]
                tqb = Buf()
                alog = P.sb(es2, "alog", [128, 8], F32)
                dtb = P.sb(es2, "dtb", [128, 8], F32)
                cb2 = Buf()
                dma("sp", alog[:], gdn_a_log[j], w=[cb2])
                dma("sp", dtb[:], gdn_dt_bias[j], w=[cb2])
                op("act", lambda e: e.activation(out=alog[:], in_=alog[:], func=AF.Exp), r=[cb2], w=[cb2])
                op("dve", lambda e: e.tensor_scalar(out=alog[:], in0=alog[:], scalar1=-1.0, scalar2=None, op0=ALU.mult), r=[cb2], w=[cb2])

                def c_va(blk, pt, pb):
                    o_, ob_ = vo[blk % 2], vob[blk % 2]
                    op("act", lambda e: e.activation(out=o_[:], in_=pt[:], func=AF.Copy), r=[pb], w=[ob_])
                    dma("sp", VA[blk * 128:(blk + 1) * 128, :], o_[:], r=[ob_], w=vab)

                def c_gate(blk, pt, pb):
                    o_, ob_ = go[blk % 2], gob[blk % 2]
                    op("act", lambda e: e.activation(out=o_[:], in_=pt[:], func=AF.Silu), r=[pb], w=[ob_])
                    dma("sp", GG[blk * 128:(blk + 1) * 128, :], o_[:], r=[ob_], w=ggb)

                def c_bg(blk, pt, pb):
                    o_, ob_ = bgo[blk % 2], bgob[blk % 2]
                    x_, ax_, l_, mx_ = tq
                    op("act", lambda e: e.activation(out=o_[:, 0:8], in_=pt[:, 0:8], func=AF.Sigmoid), r=[pb], w=[ob_])
                    op("dve", lambda e: e.tensor_tensor(out=x_[:], in0=pt[:, 8:16], in1=dtb[:], op=ALU.add), r=[pb, cb2, tqb], w=[tqb])
                    op("act", lambda e: e.activation(out=ax_[:], in_=x_[:], func=AF.Abs), r=[tqb], w=[tqb])
                    op("act", lambda e: e.activation(out=l_[:], in_=ax_[:], func=AF.Exp, scale=-1.0), r=[tqb], w=[tqb])
                    op("act", lambda e: e.activation(out=l_[:], in_=l_[:], func=AF.Ln, bias=ones[:, 0:1], scale=1.0), r=[tqb, cb], w=[tqb])
                    op("dve", lambda e: e.tensor_scalar(out=mx_[:], in0=x_[:], scalar1=0.0, scalar2=None, op0=ALU.max), r=[tqb], w=[tqb])
                    op("dve", lambda e: e.tensor_tensor(out=mx_[:], in0=mx_[:], in1=l_[:], op=ALU.add), r=[tqb], w=[tqb])
                    op("dve", lambda e: e.tensor_tensor(out=o_[:, 8:16], in0=mx_[:], in1=alog[:], op=ALU.mult), r=[tqb, cb2, ob_], w=[ob_])
                    dma("sp", BG[blk * 128:(blk + 1) * 128, :], o_[:], r=[ob_], w=bgb)

                proj_tm(W, wbuf, 1024, 512, U, ub, wt[0], wtb[0], c_va)
                proj_tm(W, wbuf, 3072, 512, U, ub, wt[1], wtb[1], c_gate)
                proj_tm(W, wbuf, 3584, 16, U, ub, wt[0], wtb[0], c_bg)
                kb.barrier()
            with contextlib.ExitStack() as es2, P.scope("ev_gqkv"):
                wt = [P.sb(es2, f"gw{i}", [128, 8, 128], BF16) for i in range(2)]
                wtb = [Buf() for _ in range(2)]
                R1 = [P.sb(es2, f"gR{i}", [128, NT], F32) for i in range(2)]
                r1b = [Buf() for _ in range(2)]
                C1 = [P.sb(es2, f"gC{i}", [128, NT], F32) for i in range(2)]
                c1b = [Buf() for _ in range(2)]
                S1 = P.sb(es2, "gS", [128, NT], F32); s1b = Buf()
                CB = [P.sb(es2, f"gCB{i}", [128, NT], BF16) for i in range(2)]
                cbb = [Buf() for _ in range(2)]
                rsq = [P.sb(es2, f"grs{i}", [128, 512], F32) for i in range(2)]
                rsqb = [Buf() for _ in range(2)]
                gc = P.sb(es2, "gc", [128, 12, 4], F32); gcb = Buf()
                dma("sp", gc[:], gdn_conv[j], w=[gcb])
                segs = [(0, CT), (CT, NT)]
                it = 0
                for kind in range(3):
                    for h in range(4):
                        cidx = kind * 4 + h
                        r_, rb_ = R1[it % 2], r1b[it % 2]
                        c_, cb_ = C1[it % 2], c1b[it % 2]
                        o16, o16b = CB[it % 2], cbb[it % 2]
                        proj_fm(W, wbuf, 1536 + cidx * 128, U, ub, r_, rb_, wt[it % 2], wtb[it % 2])
                        it += 1
                        op("act", lambda e: e.activation(out=c_[:], in_=r_[:], func=AF.Copy, scale=gc[:, cidx, 2:3]), r=[rb_, gcb], w=[cb_])
                        for (a, b) in segs:
                            for tap, off in ((0, -2), (1, -1), (3, 1)):
                                if off < 0:
                                    oa, ob_, ia, ib_ = a - off, b, a, b + off
                                else:
                                    oa, ob_, ia, ib_ = a, b - off, a + off, b
                                op("dve", lambda e: e.scalar_tensor_tensor(out=c_[:, oa:ob_], in0=r_[:, ia:ib_], scalar=gc[:, cidx, tap:tap + 1], in1=c_[:, oa:ob_],
                                                                           op0=ALU.mult, op1=ALU.add), r=[rb_, gcb, cb_], w=[cb_])
                        op("act", lambda e: e.activation(out=c_[:], in_=c_[:], func=AF.Silu), r=[cb_], w=[cb_])
                        if kind < 2:
                            op("act", lambda e: e.activation(out=S1[:], in_=c_[:], func=AF.Square), r=[cb_], w=[s1b])
                            for ti, (t0, T) in enumerate(TILES):
                                pt, pb = P.psum()
                                op("pe", lambda e: e.matmul(pt[:, :T], ones[:], S1[:, t0:t0 + T], start=True, stop=True), r=[s1b, cb], w=[pb])
                                q_, qb_ = rsq[ti % 2], rsqb[ti % 2]
                                op("act", lambda e: e.activation(out=q_[:, :T], in_=pt[:, :T], func=AF.Sqrt, bias=epsb[:], scale=1.0), r=[pb, cb], w=[qb_])
                                op("dve", lambda e: e.reciprocal(q_[:, :T], q_[:, :T]), r=[qb_], w=[qb_])
                                if kind == 0:
                                    op("dve", lambda e: e.scalar_tensor_tensor(out=o16[:, t0:t0 + T], in0=c_[:, t0:t0 + T], scalar=128.0 ** -0.5, in1=q_[:, :T],
                                                                               op0=ALU.mult, op1=ALU.mult), r=[cb_, qb_], w=[o16b])
                                else:
                                    op("dve", lambda e: e.tensor_tensor(out=c_[:, t0:t0 + T], in0=c_[:, t0:t0 + T], in1=q_[:, :T], op=ALU.mult), r=[cb_, qb_], w=[cb_])
                                    op("act", lambda e: e.activation(out=o16[:, t0:t0 + T], in_=c_[:, t0:t0 + T], func=AF.Copy), r=[cb_], w=[o16b])
                        else:
                            op("dve", lambda e: e.tensor_copy(out=o16[:], in_=c_[:]), r=[cb_], w=[o16b])
                        dma("sp", GQ[kind, h], o16[:], r=[o16b], w=gqb)
                        if kind == 1:
                            dma("sp", GQK[h], c_[:], r=[cb_], w=gqb)
                kb.barrier()
        with contextlib.ExitStack() as es, P.scope("ev_attn"):
            P.psr = (4, 8)
            acc = [(P.ps[i], P.psb[i]) for i in range(4)]
            kT = P.sb(es, "kT", [128, 2, NT], BF16); ktb = Buf()
            qT = P.sb(es, "qT", [128, NT], BF16); qtb = Buf()
            op("dve", lambda e: e.memset(kT[:], 0.0), w=[ktb])
            V = P.sb(es, "V", [128, 34, 128], BF16); vb = Buf()
            PT = [P.sb(es, f"PT{i}", [128, 512], BF16) for i in range(8)]
            ptb = [Buf() for _ in range(8)]
            onesb = P.sb(es, "onesb", [128, 128], BF16)
            rden = P.sb(es, "rden", [128, 512], F32); rdb = Buf()
            t0b_ = P.sb(es, "t0b", [128, 512], F32); t0bb = Buf()
            oa = P.sb(es, "oa", [128, 512], F32); oab = Buf()
            sqb_ = P.sb(es, "sqb", [128, 512], F32); sqbb = Buf()
            rsd = P.sb(es, "rsd", [128, 512], F32); rsdb = Buf()
            yT = [P.sb(es, f"yT{i}", [128, 512], BF16) for i in range(2)]
            ytb = [Buf() for _ in range(2)]
            lv = P.sb(es, "lv", [128, 256], F32)
            lp = P.sb(es, "lp", [128, 2, 64], F32)
            lsum = P.sb(es, "lsum", [128, 2], F32)
            neglam = P.sb(es, "neglam", [128, 1], F32)
            sg = P.sb(es, "sg", [128, 1], F32)
            eps128 = P.sb(es, "eps128", [128, 1], F32)
            lb = Buf()
            dma("sp", lv[:], diff_lambda[j], w=[lb])
            dma("sp", sg[:], diff_subln[j], w=[lb])
            op("dve", lambda e: e.tensor_copy(out=onesb[:], in_=ones[:]), r=[cb], w=[lb])
            op("dve", lambda e: e.tensor_tensor(out=lp[:, 0, :], in0=lv[:, 0:64], in1=lv[:, 64:128], op=ALU.mult), r=[lb], w=[lb])
            op("dve", lambda e: e.tensor_tensor(out=lp[:, 1, :], in0=lv[:, 128:192], in1=lv[:, 192:256], op=ALU.mult), r=[lb], w=[lb])
            op("dve", lambda e: e.tensor_reduce(out=lsum[:], in_=lp[:], axis=AX.X, op=ALU.add), r=[lb], w=[lb])
            op("act", lambda e: e.activation(out=lsum[:], in_=lsum[:], func=AF.Exp), r=[lb], w=[lb])
            op("dve", lambda e: e.tensor_tensor(out=neglam[:], in0=lsum[:, 1:2], in1=lsum[:, 0:1], op=ALU.subtract), r=[lb], w=[lb])
            op("dve", lambda e: e.tensor_scalar(out=neglam[:], in0=neglam[:], scalar1=-lam_init, scalar2=None, op0=ALU.add), r=[lb], w=[lb])
            op("dve", lambda e: e.tensor_scalar(out=sg[:], in0=sg[:], scalar1=(1.0 - lam_init), scalar2=None, op0=ALU.mult), r=[lb], w=[lb])
            op("dve", lambda e: e.memset(eps128[:], EPS), w=[lb])
            ipt = 0
            iy = 0
            for h in range(4):
                dma("sp", kT[0:64, 0, :], KA[2 * h], r=kab, w=[ktb])
                dma("sp", kT[64:128, 1, :], KA[2 * h + 1], r=kab, w=[ktb])
                dma("sp", qT[:], QA[2 * h:2 * h + 2].rearrange("m d t -> (m d) t"), r=qab, w=[qtb])
                dma("sp", V[:], VA[:, h * 128:(h + 1) * 128].rearrange("(b p) e -> p b e", p=128), r=vab, w=[vb])
                for ti, (t0, T) in enumerate(TILES):
                    if ti == 0 and skip_ctx:
                        continue
                    keys = [0, 1] if ti == 0 else list(range(34))
                    for m in range(2):
                        A_, Ab_ = acc[2 * m]
                        B_, Bb_ = acc[2 * m + 1]
                        LA = 3
                        nk = len(keys)
                        pend = []
                        for ki in range(nk + LA):
                            if ki < nk:
                                kblk = keys[ki]
                                s_, sb_ = P.psum()
                                op("pe", lambda e: e.matmul(s_[:, :T], kT[:, m, kblk * 128:(kblk + 1) * 128], qT[:, t0:t0 + T], start=True, stop=True),
                                   r=[ktb, qtb], w=[sb_])
                                p_, pb_ = PT[ipt % 8], ptb[ipt % 8]
                                ipt += 1
                                op("act", lambda e: e.activation(out=p_[:, :T], in_=s_[:, :T], func=AF.Exp, scale=0.125), r=[sb_], w=[pb_])
                                pend.append((p_, pb_, kblk))
                            if ki >= LA:
                                kj = ki - LA
                                p2, pb2, kb2 = pend[kj]
                                op("pe", lambda e: e.matmul(A_[:, :T], V[:, kb2, :], p2[:, :T], start=(kj == 0), stop=(kj == nk - 1)), r=[pb2, vb], w=[Ab_])
                                op("pe", lambda e: e.matmul(B_[:, :T], onesb[:], p2[:, :T], start=(kj == 0), stop=(kj == nk - 1)), r=[pb2, lb], w=[Bb_])
                        op("dve", lambda e: e.reciprocal(rden[:, :T], B_[:, :T]), r=[Bb_], w=[rdb])
                        if m == 0:
                            op("dve", lambda e: e.tensor_tensor(out=t0b_[:, :T], in0=A_[:, :T], in1=rden[:, :T], op=ALU.mult), r=[Ab_, rdb], w=[t0bb])
                        else:
                            op("dve", lambda e: e.tensor_tensor(out=oa[:, :T], in0=A_[:, :T], in1=rden[:, :T], op=ALU.mult), r=[Ab_, rdb], w=[oab])
                            op("dve", lambda e: e.scalar_tensor_tensor(out=oa[:, :T], in0=oa[:, :T], scalar=neglam[:, 0:1], in1=t0b_[:, :T], op0=ALU.mult, op1=ALU.add),
                               r=[oab, lb, t0bb], w=[oab])
                    y_, yb_ = yT[iy % 2], ytb[iy % 2]
                    iy += 1
                    op("act", lambda e: e.activation(out=sqb_[:, :T], in_=oa[:, :T], func=AF.Square), r=[oab], w=[sqbb])
                    pt, pb = P.psum()
                    op("pe", lambda e: e.matmul(pt[:, :T], ones[:], sqb_[:, :T], start=True, stop=True), r=[sqbb, cb], w=[pb])
                    op("act", lambda e: e.activation(out=rsd[:, :T], in_=pt[:, :T], func=AF.Sqrt, bias=eps128[:], scale=1.0 / 128), r=[pb, lb], w=[rsdb])
                    op("dve", lambda e: e.reciprocal(rsd[:, :T], rsd[:, :T]), r=[rsdb], w=[rsdb])
                    op("dve", lambda e: e.scalar_tensor_tensor(out=y_[:, :T], in0=oa[:, :T], scalar=sg[:, 0:1], in1=rsd[:, :T], op0=ALU.mult, op1=ALU.mult),
                       r=[oab, lb, rsdb], w=[yb_])
                    dma("sp", YS[h * 128:(h + 1) * 128, t0:t0 + T], y_[:, :T], r=[yb_], w=ysb)
            P.psr = (0, 8)
            kb.barrier()
        with contextlib.ExitStack() as es, P.scope("ev_gdn"):
            gdn_core(es, l, j, skip_ctx)
            kb.barrier()
        with contextlib.ExitStack() as es, P.scope("ev_out"):
            proj_residual(es, l, 2, evout_b[j], evoutb[j], 8, YS, ysb, skip_ctx)
            kb.barrier()

    def gdn_core(es0, l, j, skip_ctx):
        NB = NT // 128
        for hg in range(2):
          chains = [(h, d, sub) for h in (2 * hg, 2 * hg + 1) for d in range(2) for sub in range(2)]
          NCH = len(chains)
          with contextlib.ExitStack() as es:
              gm = P.sb(es, "gmask", [128, 2, 3, 128], F32); gmb = Buf()
              gaux = P.sb(es, "gaux", [128, 258], F32)
              dma("sp", gm[:], gmask_in[:, :, :, :], w=[gmb])
              dma("sp", gaux[:], gaux_in[:, :], w=[gmb])
              idb = P.sb(es, "idb", [128, 128], BF16)
              op("dve", lambda e: e.tensor_copy(out=idb[:], in_=ident[:]), r=[cb], w=[gmb])
              qkv = P.sb(es, "qkv", [128, 3, 2, NT], BF16); qkvb = Buf()
              kF = P.sb(es, "kF", [128, 2, NT], F32)
              for kind in range(3):
                  dma("sp", qkv[:, kind, :, :], GQ[kind, 2 * hg:2 * hg + 2].rearrange("h p t -> p h t"), r=gqb, w=[qkvb])
              dma("sp", kF[:], GQK[2 * hg:2 * hg + 2].rearrange("h p t -> p h t"), r=gqb, w=[qkvb])
              BGt = P.sb(es, "BGt", [128, NB, 16], F32); bgtb = Buf()
              dma("sp", BGt[:], BG.rearrange("(n p) f -> p n f", p=128), r=bgb, w=[bgtb])
              NBt = P.sb(es, "NBt", [128, NB, 8], F32)
              op("dve", lambda e: e.tensor_scalar(out=NBt[:], in0=BGt[:, :, 0:8], scalar1=-1.0, scalar2=None, op0=ALU.mult), r=[bgtb], w=[bgtb])

              def ch(name, shape, dt):
                  return [P.sb(es, f"{name}{c}", shape, dt) for c in range(NCH)]
              g2 = ch("g2", [128, 2, 128], F32)
              dT = ch("dT", [128, 128], F32)
              Nb = ch("Nb", [128, 2, 128], F32)
              Wk = ch("Wk", [128, 2, 2, 128], F32)
              MTk = ch("MTk", [128, 2, 128], F32)
              Xb = ch("Xb", [128, 128], BF16)
              AT = ch("AT", [128, 128], BF16)
              ubf = ch("ubf", [128, 128], F32)
              kg = ch("kg", [128, 3, 128], BF16)
              vtok = ch("vtok", [128, 128], BF16)
              wT = ch("wT", [128, 128], BF16)
              vn = ch("vn", [128, 128], BF16)
              t1 = ch("t1", [128, 128], F32)
              ot = ch("ot", [128, 128], F32)
              S = ch("S", [128, 128], F32)
              Sb = ch("Sb", [128, 128], BF16)
              cbuf = [Buf(f"chain{c}") for c in range(NCH)]
              sbuf_ = [Buf(f"S{c}") for c in range(NCH)]
              vnb = [Buf(f"vn{c}") for c in range(NCH)]
              otb = [Buf(f"ot{c}") for c in range(NCH)]
              for c in range(NCH):
                  op("dve", lambda e: e.memset(S[c - c % 2][:], 0.0), w=[sbuf_[c - c % 2]])
                  op("dve", lambda e: e.memset(Sb[c - c % 2][:], 0.0), w=[sbuf_[c - c % 2]])
                  op("dve", lambda e: e.memset(vn[c - c % 2][:], 0.0), w=[vnb[c - c % 2]])
              order = [list(range(NB)), [1, 0] + list(range(NB - 1, 1, -1))]
              SCA = P.sb(es, "SCA", [128, 2, 7, NB, 2], F32); scab = Buf()
              for d in range(2):
                  pa, pab = P.psum()
                  grhs = BGt[:, :, 8 + d * 4 + 2 * hg:8 + d * 4 + 2 * hg + 2]
                  for kind, lh in enumerate((gm[:, d, 0, :], gm[:, d, 1, :], gaux[:, 0:128], gaux[:, 128:256])):
                      op("pe", lambda e: e.matmul(pa[:, kind * 2 * NB:(kind + 1) * 2 * NB].rearrange("p (n f) -> p n f", f=2), lh, grhs, start=True, stop=True),
                         r=[gmb, bgtb], w=[pab])
                  op("act", lambda e: e.activation(out=SCA[:, d, 0:4, :, :], in_=pa[:, 0:8 * NB].rearrange("p (k n f) -> p k n f", k=4, f=2), func=AF.Exp), r=[pab], w=[scab])
                  op("act", lambda e: e.activation(out=SCA[:, d, 6, :, :], in_=pa[:, 0:2 * NB].rearrange("p (n f) -> p n f", f=2), func=AF.Copy), r=[pab, scab], w=[scab])
                  for a_ in range(2):
                      op("dve", lambda e: e.tensor_scalar(out=SCA[:, d, 4 + a_, :, :], in0=SCA[:, d, 1, :, :], scalar1=gaux[:, 256 + a_:257 + a_], scalar2=None, op0=ALU.mult),
                         r=[gmb, scab], w=[scab])

              class Reg:
                  def __init__(self):
                      self.k = 4
                  def get(self):
                      if self.k == 4:
                          self.bank = P.psum()
                          self.k = 0
                      r = (self.bank[0][:, self.k * 128:(self.k + 1) * 128], self.bank[1], self.bank[0], self.k * 128)
                      self.k += 1
                      return r
              rg = Reg()

              for i in range(0, NB, 2):
                  blk = [order[d][i + sub] for (h, d, sub) in chains]
                  for c, (h, d, sub) in enumerate(chains):
                      n = blk[c]
                      gcol = BGt[:, n, 8 + d * 4 + h:8 + d * 4 + h + 1]
                      op("dve", lambda e: e.tensor_scalar(out=g2[c][:, 0, :], in0=ones[:], scalar1=gcol, scalar2=None, op0=ALU.mult), r=[bgtb, cb, cbuf[c]], w=[cbuf[c]])
                  regD = []
                  for c, (h, d, sub) in enumerate(chains):
                      rv, rb_, _, _ = rg.get()
                      regD.append((rv, rb_))
                      op("pe", lambda e: e.matmul(rv, g2[c][:, 0, :], gm[:, d, 0, :], start=True, stop=True), r=[cbuf[c], gmb], w=[rb_])
                  for c, (h, d, sub) in enumerate(chains):
                      rv, rb_ = regD[c]
                      n = blk[c]
                      op("dve", lambda e: e.scalar_tensor_tensor(out=dT[c][:], in0=rv, scalar=SCA[:, d, 6, n, h % 2:h % 2 + 1], in1=gm[:, d, 2, :],
                                                                 op0=ALU.subtract, op1=ALU.add), r=[rb_, scab, gmb, cbuf[c]], w=[cbuf[c]])
                      op("act", lambda e: e.activation(out=dT[c][:], in_=dT[c][:], func=AF.Exp), r=[cbuf[c]], w=[cbuf[c]])
                  regF = []
                  for c, (h, d, sub) in enumerate(chains):
                      n = blk[c]
                      kT_ = qkv[:, 1, h % 2, n * 128:(n + 1) * 128]
                      kF_ = kF[:, h % 2, n * 128:(n + 1) * 128]
                      qT_ = qkv[:, 0, h % 2, n * 128:(n + 1) * 128]
                      vT_ = qkv[:, 2, h % 2, n * 128:(n + 1) * 128]
                      bank = P.psum()
                      pt, pb = bank
                      regF.append(bank)
                      op("pe", lambda e: e.matmul(pt[:, 0:128], kF_, kF_, start=True, stop=True), r=[qkvb], w=[pb])
                      op("pe", lambda e: e.matmul(pt[:, 128:256], kT_, qT_, start=True, stop=True), r=[qkvb], w=[pb])
                      op("pe", lambda e: e.matmul(pt[:, 256:384], kT_, idb[:], start=True, stop=True), r=[qkvb, gmb], w=[pb])
                      op("pe", lambda e: e.matmul(pt[:, 384:512], vT_, idb[:], start=True, stop=True), r=[qkvb, gmb], w=[pb])
                      bcol = BGt[:, n, d * 4 + h:d * 4 + h + 1]
                      nbcol = NBt[:, n, d * 4 + h:d * 4 + h + 1]
                      op("dve", lambda e: e.tensor_tensor(out=AT[c][:], in0=pt[:, 128:256], in1=dT[c][:], op=ALU.mult), r=[pb, cbuf[c]], w=[cbuf[c]])
                      op("dve", lambda e: e.tensor_tensor(out=dT[c][:], in0=dT[c][:], in1=ident[:], op=ALU.subtract), r=[cbuf[c], cb], w=[cbuf[c]])
                      op("dve", lambda e: e.scalar_tensor_tensor(out=g2[c][:, 0, :], in0=pt[:, 0:128], scalar=nbcol, in1=dT[c][:], op0=ALU.mult, op1=ALU.mult),
                         r=[pb, bgtb, cbuf[c]], w=[cbuf[c]])
                      op("act", lambda e: e.activation(out=Nb[c][:, 0, :], in_=g2[c][:, 0, :], func=AF.Copy, scale=-1.0), r=[cbuf[c]], w=[cbuf[c]])
                      op("dve", lambda e: e.tensor_tensor(out=Wk[c][:, 1, 1, :], in0=g2[c][:, 0, :], in1=ident[:], op=ALU.add), r=[cbuf[c], cb], w=[cbuf[c]])
                      op("act", lambda e: e.activation(out=kg[c][:, 0, :], in_=pt[:, 256:384], func=AF.Copy, scale=SCA[:, d, 0, n, h % 2:h % 2 + 1]), r=[pb, cbuf[c], scab], w=[cbuf[c]])
                      op("act", lambda e: e.activation(out=kg[c][:, 1, :], in_=pt[:, 256:384], func=AF.Copy, scale=SCA[:, d, 4, n, h % 2:h % 2 + 1]), r=[pb, cbuf[c], scab], w=[cbuf[c]])
                      op("act", lambda e: e.activation(out=kg[c][:, 2, :], in_=pt[:, 256:384], func=AF.Copy, scale=SCA[:, d, 5, n, h % 2:h % 2 + 1]), r=[pb, cbuf[c], scab], w=[cbuf[c]])
                      op("dve", lambda e: e.tensor_copy(out=vtok[c][:], in_=pt[:, 384:512]), r=[pb, cbuf[c]], w=[cbuf[c]])
                  regH = []
                  for c in range(NCH):
                      rv, rb_, _, _ = rg.get()
                      regH.append((rv, rb_))
                      op("pe", lambda e: e.transpose(rv, Nb[c][:, 0, :], ident[:]), r=[cbuf[c], cb], w=[rb_])
                  for c in range(NCH):
                      rv, rb_ = regH[c]
                      op("act" if c % 2 else "dve",
                         (lambda e: e.activation(out=Nb[c][:, 1, :], in_=rv, func=AF.Copy)) if c % 2 else (lambda e: e.tensor_copy(out=Nb[c][:, 1, :], in_=rv)),
                         r=[rb_, cbuf[c]], w=[cbuf[c]])
                  for k in range(0, 6):
                      regS = []
                      for c in range(NCH):
                          pt, pb = P.psum()
                          regS.append((pt, pb))
                          if k == 0:
                              M_, MT_ = Nb[c][:, 0, :], Nb[c][:, 1, :]
                              op("pe", lambda e: e.matmul(pt[:, 0:128], MT_, M_, start=True, stop=True), r=[cbuf[c]], w=[pb])
                              op("pe", lambda e: e.matmul(pt[:, 256:384], M_, MT_, start=True, stop=True), r=[cbuf[c]], w=[pb])
                          elif k < 5:
                              pp = k % 2
                              op("pe", lambda e: e.matmul(pt[:, 0:256], MTk[c][:, pp, :], Wk[c][:, pp, :, :], start=True, stop=True), r=[cbuf[c]], w=[pb])
                              op("pe", lambda e: e.matmul(pt[:, 256:384], Wk[c][:, pp, 0, :], MTk[c][:, pp, :], start=True, stop=True), r=[cbuf[c]], w=[pb])
                          else:
                              op("pe", lambda e: e.matmul(pt[:, 128:256], MTk[c][:, 1, :], Wk[c][:, 1, 1, :], start=True, stop=True), r=[cbuf[c]], w=[pb])
                      for c in range(NCH):
                          pt, pb = regS[c]
                          np_ = (k + 1) % 2
                          if k < 5:
                              op("act", lambda e: e.activation(out=Wk[c][:, np_, 0, :], in_=pt[:, 0:128], func=AF.Copy), r=[pb, cbuf[c]], w=[cbuf[c]])
                              op("act" if c % 2 else "dve",
                                 (lambda e: e.activation(out=MTk[c][:, np_, :], in_=pt[:, 256:384], func=AF.Copy)) if c % 2 else
                                 (lambda e: e.tensor_copy(out=MTk[c][:, np_, :], in_=pt[:, 256:384])), r=[pb, cbuf[c]], w=[cbuf[c]])
                          if 1 <= k < 5:
                              op("dve", lambda e: e.tensor_tensor(out=Wk[c][:, np_, 1, :], in0=Wk[c][:, k % 2, 1, :], in1=pt[:, 128:256], op=ALU.add), r=[pb, cbuf[c]], w=[cbuf[c]])
                          if k == 5:
                              op("dve", lambda e: e.tensor_tensor(out=Xb[c][:], in0=Wk[c][:, 1, 1, :], in1=pt[:, 128:256], op=ALU.add), r=[pb, cbuf[c]], w=[cbuf[c]])
                  regJ = []
                  for c in range(NCH):
                      bank = P.psum() if c % 2 == 0 else bank
                      pt, pb = bank
                      o0 = (c % 2) * 256
                      regJ.append((pt, pb, o0))
                      op("pe", lambda e: e.matmul(pt[:, o0:o0 + 128], Xb[c][:], vtok[c][:], start=True, stop=True), r=[cbuf[c]], w=[pb])
                      op("pe", lambda e: e.matmul(pt[:, o0 + 128:o0 + 256], kg[c][:, 0, :], Xb[c][:], start=True, stop=True), r=[cbuf[c]], w=[pb])
                  for c, (h, d, sub) in enumerate(chains):
                      pt, pb, o0 = regJ[c]
                      n = blk[c]
                      bcol = BGt[:, n, d * 4 + h:d * 4 + h + 1]
                      op("dve", lambda e: e.tensor_scalar(out=ubf[c][:], in0=pt[:, o0:o0 + 128], scalar1=bcol, scalar2=None, op0=ALU.mult), r=[pb, bgtb, cbuf[c]], w=[cbuf[c]])
                      op("act", lambda e: e.activation(out=wT[c][:], in_=pt[:, o0 + 128:o0 + 256], func=AF.Copy), r=[pb, cbuf[c]], w=[cbuf[c]])
                  for sub_s, step_i in ((0, 0), (0, 1), (1, 0), (1, 1)):
                      regW = {}
                      for c, (h, d, sub) in enumerate(chains):
                          if sub != sub_s:
                              continue
                          a = step_i if d == 0 else 1 - step_i
                          lo, hi = a * 64, (a + 1) * 64
                          rv, rb_, bank_t, off = rg.get()
                          regW[c] = (bank_t, rb_, off)
                          op("pe", lambda e: e.matmul(bank_t[:, off:off + 128], wT[c][:], Sb[c - c % 2][:], start=True, stop=True), r=[cbuf[c], sbuf_[c - c % 2]], w=[rb_])
                      for c, (h, d, sub) in enumerate(chains):
                          if sub != sub_s:
                              continue
                          a = step_i if d == 0 else 1 - step_i
                          lo, hi = a * 64, (a + 1) * 64
                          n = blk[c]
                          bank_t, rb_, off = regW[c]
                          nbcol = NBt[lo:hi, n, d * 4 + h:d * 4 + h + 1]
                          op("dve", lambda e: e.scalar_tensor_tensor(out=vn[c - c % 2][lo:hi, :], in0=bank_t[lo:hi, off:off + 128], scalar=nbcol, in1=ubf[c][lo:hi, :],
                                                                     op0=ALU.mult, op1=ALU.add), r=[rb_, bgtb, cbuf[c]], w=[vnb[c - c % 2]])
                      regO = {}
                      for c, (h, d, sub) in enumerate(chains):
                          if sub != sub_s:
                              continue
                          a = step_i if d == 0 else 1 - step_i
                          lo, hi = a * 64, (a + 1) * 64
                          n = blk[c]
                          bank = P.psum()
                          pt, pb = bank
                          regO[c] = bank
                          qT_ = qkv[:, 0, h % 2, n * 128:(n + 1) * 128]
                          op("pe", lambda e: e.matmul(pt[:, 0:128], qT_, Sb[c - c % 2][:], start=True, stop=True), r=[qkvb, sbuf_[c - c % 2]], w=[pb])
                          op("pe", lambda e: e.matmul(pt[:, 128:256], AT[c][:], vn[c - c % 2][:], start=True, stop=True), r=[cbuf[c], vnb[c - c % 2]], w=[pb])
                          op("pe", lambda e: e.matmul(pt[:, 256:384], kg[c][:, 1 + a, :], vn[c - c % 2][:], start=True, stop=True), r=[cbuf[c], vnb[c - c % 2]], w=[pb])
                      for c, (h, d, sub) in enumerate(chains):
                          if sub != sub_s:
                              continue
                          a = step_i if d == 0 else 1 - step_i
                          lo, hi = a * 64, (a + 1) * 64
                          pt, pb = regO[c]
                          op("dve", lambda e: e.scalar_tensor_tensor(out=S[c - c % 2][:], in0=S[c - c % 2][:], scalar=SCA[:, d, 2 + a, blk[c], h % 2:h % 2 + 1], in1=pt[:, 256:384], op0=ALU.mult, op1=ALU.add),
                             r=[pb, cbuf[c], sbuf_[c - c % 2]], w=[sbuf_[c - c % 2]])
                          op("act", lambda e: e.activation(out=Sb[c - c % 2][:], in_=S[c - c % 2][:], func=AF.Copy), r=[sbuf_[c - c % 2]], w=[sbuf_[c - c % 2]])
                          op("act", lambda e: e.activation(out=t1[c][lo:hi, :], in_=pt[lo:hi, 0:128], func=AF.Copy, scale=SCA[lo:hi, d, 0, blk[c], h % 2:h % 2 + 1]), r=[pb, cbuf[c], scab], w=[cbuf[c]])
                          op("dve", lambda e: e.tensor_tensor(out=ot[c][lo:hi, :], in0=t1[c][lo:hi, :], in1=pt[lo:hi, 128:256], op=ALU.add), r=[pb, cbuf[c]], w=[otb[c]])
                  for c, (h, d, sub) in enumerate(chains):
                      n = blk[c]
                      dma("sp", OBD[d, h, n * 128:(n + 1) * 128, :], ot[c][:], r=[otb[c]], w=obdb)
              kb.barrier()
        with contextlib.ExitStack() as es:
            gn = P.sb(es, "gnorm", [128, 1], F32); gnb = Buf()
            dma("sp", gn[:], gdn_norm[j], w=[gnb])
            eps_ = P.sb(es, "geps", [128, 1], F32)
            op("dve", lambda e: e.memset(eps_[:], EPS), w=[gnb])
            OB = [[P.sb(es, f"OB{i}_{d}", [128, NB, 128], F32) for d in range(2)] for i in range(2)]
            obb = [Buf() for _ in range(2)]
            GT = [P.sb(es, f"GT{i}", [128, NB, 128], F32) for i in range(2)]
            gtb = [Buf() for _ in range(2)]
            rst = [P.sb(es, f"rst{i}", [128, NB], F32) for i in range(2)]
            rstb = [Buf() for _ in range(2)]
            sqt = P.sb(es, "gsq", [128, 128], F32); sqtb = Buf()
            yo = [P.sb(es, f"gyo{i}", [128, 512], BF16) for i in range(2)]
            yob = [Buf() for _ in range(2)]
            ig = 0
            for h in range(4):
                o0_, o1_ = OB[h % 2]
                ob_ = obb[h % 2]
                g_, gb_ = GT[h % 2], gtb[h % 2]
                r_, rb_ = rst[h % 2], rstb[h % 2]
                dma("sp", o0_[:], OBD[0, h].rearrange("(n p) e -> p n e", p=128), r=obdb, w=[ob_])
                dma("sp", o1_[:], OBD[1, h].rearrange("(n p) e -> p n e", p=128), r=obdb, w=[ob_])
                dma("sp", g_[:], GG[:, h * 128:(h + 1) * 128].rearrange("(n p) e -> p n e", p=128), r=ggb, w=[gb_])
                op("dve", lambda e: e.tensor_tensor(out=o0_[:], in0=o0_[:], in1=o1_[:], op=ALU.add), r=[ob_], w=[ob_])
                for n in range(NB):
                    op("act", lambda e: e.activation(out=sqt[:], in_=o0_[:, n, :], func=AF.Square, accum_out=r_[:, n:n + 1]), r=[ob_], w=[sqtb, rb_])
                op("act", lambda e: e.activation(out=r_[:], in_=r_[:], func=AF.Sqrt, bias=eps_[:], scale=1.0 / 128), r=[rb_, gnb], w=[rb_])
                op("dve", lambda e: e.reciprocal(r_[:], r_[:]), r=[rb_], w=[rb_])
                for n0 in range(0, NB, 4):
                    nn = min(4, NB - n0)
                    o_, ob2 = yo[ig % 2], yob[ig % 2]
                    ig += 1
                    pt, pb = P.psum()
                    for q in range(nn):
                        n = n0 + q
                        op("dve", lambda e: e.scalar_tensor_tensor(out=o0_[:, n, :], in0=o0_[:, n, :], scalar=r_[:, n:n + 1], in1=g_[:, n, :], op0=ALU.mult, op1=ALU.mult),
                           r=[ob_, rb_, gb_], w=[ob_])
                        op("pe", lambda e: e.transpose(pt[:, q * 128:(q + 1) * 128], o0_[:, n, :], ident[:]), r=[ob_, cb], w=[pb])
                    op("act", lambda e: e.activation(out=o_[:, 0:nn * 128], in_=pt[:, 0:nn * 128], func=AF.Copy, scale=gn[:, 0:1]), r=[pb, gnb], w=[ob2])
                    dma("sp", YS[512 + h * 128:512 + (h + 1) * 128, n0 * 128:(n0 + nn) * 128], o_[:, 0:nn * 128], r=[ob2], w=ysb)

    for l in layers:
        last = (l == DEPTH - 1)
        if cfg.get("do_mixer", True):
            if l % 2 == 1:
                odd_mixer(l, skip_ctx=last)
            else:
                even_mixer(l, skip_ctx=last)
        if cfg.get("do_ffn", True):
            ffn_layer(l, skip_ctx=last)

    if cfg.get("dump_ys"):
        yd = P.outp("ys_dump", [D, NT])
        with contextlib.ExitStack() as es:
            tb16 = [P.sb(es, f"dy{i}", [128, 8, 512], BF16) for i in range(2)]
            tb32 = [P.sb(es, f"dz{i}", [128, 8, 512], F32) for i in range(2)]
            tbb = [Buf() for _ in range(2)]
            for ti, (t0, T) in enumerate(TILES):
                dma("sp", tb16[ti % 2][:, :, :T], YS[:, t0:t0 + T].rearrange("(c p) t -> p c t", p=128), r=ysb, w=[tbb[ti % 2]])
                op("dve", lambda e: e.tensor_copy(out=tb32[ti % 2][:, :, :T], in_=tb16[ti % 2][:, :, :T]), r=[tbb[ti % 2]], w=[tbb[ti % 2]])
                dma("sp", yd[:, t0:t0 + T].rearrange("(c p) t -> p c t", p=128), tb32[ti % 2][:, :, :T], r=[tbb[ti % 2]])
            kb.barrier()
    if cfg.get("dump_xt"):
        xd = P.outp("xt_dump", [D, NT])
        with contextlib.ExitStack() as es:
            tb = [P.sb(es, f"db{i}", [128, 8, 512], F32) for i in range(2)]
            tbb = [Buf() for _ in range(2)]
            for ti, (t0, T) in enumerate(TILES):
                dma("sp", tb[ti % 2][:, :, :T], XT[:, t0:t0 + T].rearrange("(c p) t -> p c t", p=128), r=[xb[ti]], w=[tbb[ti % 2]])
                dma("sp", xd[:, t0:t0 + T].rearrange("(c p) t -> p c t", p=128), tb[ti % 2][:, :, :T], r=[tbb[ti % 2]])
            kb.barrier()

    with contextlib.ExitStack() as es:
        xt = [P.sb(es, f"fx{i}", [128, 8, 512], F32) for i in range(2)]
        xtb = [Buf() for _ in range(2)]
        sq = [P.sb(es, f"fsq{i}", [128, 8, 512], F32) for i in range(2)]
        sqb = [Buf() for _ in range(2)]
        rs = [P.sb(es, f"frs{i}", [128, 512], F32) for i in range(2)]
        rsb = [Buf() for _ in range(2)]
        ot = [P.sb(es, f"fo{i}", [128, D], F32) for i in range(2)]
        otb = [Buf() for _ in range(2)]
        io = 0
        for ti, (t0, T) in enumerate(TILES):
            if ti == 0:
                continue
            x_, xb_ = xt[ti % 2], xtb[ti % 2]
            q_, qb_ = sq[ti % 2], sqb[ti % 2]
            r_, rb_ = rs[ti % 2], rsb[ti % 2]
            dma("sp", x_[:], XT[:, t0:t0 + T].rearrange("(c p) t -> p c t", p=128), r=[xb[ti]], w=[xb_])
            for c in range(8):
                op("act", lambda e: e.activation(out=q_[:, c, :], in_=x_[:, c, :], func=AF.Square), r=[xb_], w=[qb_])
            pt, pb = P.psum()
            for c in range(8):
                op("pe", lambda e: e.matmul(pt[:], ones[:], q_[:, c, :], start=(c == 0), stop=(c == 7)), r=[qb_, cb], w=[pb])
            op("act", lambda e: e.activation(out=r_[:], in_=pt[:], func=AF.Sqrt, bias=epsb[:], scale=1.0 / D), r=[pb, cb], w=[rb_])
            op("dve", lambda e: e.reciprocal(r_[:], r_[:]), r=[rb_], w=[rb_])
            for c in range(8):
                op("dve", lambda e: e.scalar_tensor_tensor(out=q_[:, c, :], in0=x_[:, c, :], scalar=fng[:, c:c + 1], in1=r_[:],
                                                           op0=ALU.mult, op1=ALU.mult), r=[xb_, rb_, cb, qb_], w=[qb_])
            for bi in range(4):
                o_, ob_ = ot[io % 2], otb[io % 2]
                io += 1
                for half in range(2):
                    pt, pb = P.psum()
                    for q in range(4):
                        c = half * 4 + q
                        op("pe", lambda e: e.transpose(pt[:, q * 128:(q + 1) * 128], q_[:, c, bi * 128:(bi + 1) * 128], ident[:]), r=[qb_, cb], w=[pb])
                    if half:
                        op("act", lambda e: e.activation(out=o_[:, 512:1024], in_=pt[:], func=AF.Copy), r=[pb], w=[ob_])
                    else:
                        op("dve", lambda e: e.tensor_copy(out=o_[:, 0:512], in_=pt[:]), r=[pb], w=[ob_])
                tk0 = t0 - CT + bi * 128
                dma("sp", out[tk0:tk0 + 128, :], o_[:], r=[ob_])
        kb.barrier()

    kb.finish()
    root.close()
    return P


def host_inputs(inputs, b):
    f = lambda a: np.ascontiguousarray(a, dtype=np.float32)
    c2 = np.stack([inputs["c"][b], inputs["c_ctx"]], -1).reshape(8, 128, 2).transpose(1, 0, 2)
    m = {
        "x": f(inputs["x"][b]),
        "ctx": f(inputs["ctx"][b]),
        "c2": f(c2),
        "w_ada": f(inputs["w_ada"]),
        "b_ada": f(inputs["b_ada"].reshape(DEPTH, 48, 128).transpose(0, 2, 1)),
        "norm_mix": f(inputs["norm_mix"].reshape(DEPTH, 8, 128).transpose(0, 2, 1)),
        "norm_ffn": f(inputs["norm_ffn"].reshape(DEPTH, 8, 128).transpose(0, 2, 1)),
        "final_norm": f(inputs["final_norm"].reshape(8, 128).T),
        "ffn_w_up": f(inputs["ffn_w_up"]),
        "ffn_conv": f(inputs["ffn_conv"].reshape(DEPTH, 3, 44, 128).transpose(0, 3, 2, 1)),
        "ffn_w_down": f(inputs["ffn_w_down"]),
        "ident": np.eye(128, dtype=np.float32),
    }
    m["od_w_in"] = f(inputs["od_w_in"])
    m["od_w_out"] = f(inputs["od_w_out"])
    m["lru_conv"] = f(inputs["lru_conv"].reshape(2, 4, 4, 128).transpose(0, 3, 2, 1))
    m["lru_conv_b"] = f(inputs["lru_conv_b"].reshape(2, 4, 128).transpose(0, 2, 1))
    def bd(w):
        o = np.zeros((2, 2, 4, 128, 128), np.float32)
        for c in range(4):
            o[:, :, c, 0:64, 0:64] = w[:, :, 2 * c]
            o[:, :, c, 64:128, 64:128] = w[:, :, 2 * c + 1]
        return o
    m["lru_wa"] = bd(inputs["lru_wa"])
    m["lru_wx"] = bd(inputs["lru_wx"])
    for nm, key in (("lru_ba", "lru_ba"), ("lru_bx", "lru_bx"), ("lru_lam", "lru_lambda")):
        m[nm] = f(inputs[key].reshape(2, 2, 4, 128).transpose(0, 3, 1, 2))
    m["swa_sink"] = f(np.broadcast_to(inputs["swa_sink"][:, None, :], (2, 128, 8)))
    m["ev_w_in"] = f(inputs["ev_w_in"])
    m["ev_w_out"] = f(inputs["ev_w_out"])
    m["diff_lambda"] = f(np.broadcast_to(inputs["diff_lambda"].reshape(2, 1, 256), (2, 128, 256)))
    m["diff_subln"] = f(inputs["diff_subln"].reshape(2, 128, 1))
    m["gdn_conv"] = f(inputs["gdn_conv"].reshape(2, 4, 12, 128).transpose(0, 3, 2, 1))
    m["gdn_a_log"] = f(np.broadcast_to(inputs["gdn_a_log"].reshape(2, 1, 8), (2, 128, 8)))
    m["gdn_dt_bias"] = f(np.broadcast_to(inputs["gdn_dt_bias"].reshape(2, 1, 8), (2, 128, 8)))
    m["gdn_norm"] = f(inputs["gdn_norm"].reshape(2, 128, 1))
    m.update(CONSTS)
    return m


def _make_consts():
    t = np.arange(LT)
    row = (t // 64).astype(np.float64)
    col = (t % 64).astype(np.float64)
    inv = (10000.0 ** (-np.arange(0, 32, 2, dtype=np.float32) / 32)).astype(np.float32)
    ar = (row[:, None].astype(np.float32) * inv).astype(np.float32)
    ac = (col[:, None].astype(np.float32) * inv).astype(np.float32)
    cr, sr, cc, sc = np.cos(ar), np.sin(ar), np.cos(ac), np.sin(ac)
    cosT = np.zeros((128, LT), np.float32)
    sinT = np.zeros((128, LT), np.float32)
    perm = np.zeros((128, 128), np.float32)
    for p in range(128):
        dd = p % 64
        i = dd % 16
        q = dd // 16
        if q == 0:
            cosT[p], sinT[p], partner = cr[:, i], -sr[:, i], p + 16
        elif q == 1:
            cosT[p], sinT[p], partner = cr[:, i], sr[:, i], p - 16
        elif q == 2:
            cosT[p], sinT[p], partner = cc[:, i], -sc[:, i], p + 16
        else:
            cosT[p], sinT[p], partner = cc[:, i], sc[:, i], p - 16
        perm[partner, p] = 1.0
    a = np.arange(128)[:, None]
    bq = np.arange(128)[None, :]
    NEG = -30000.0
    mprev = np.where(a >= bq, 0.0, NEG).astype(np.float32)
    mnext = np.where(a <= bq, 0.0, NEG).astype(np.float32)
    tt = np.arange(128)[:, None]
    ii = np.arange(128)[None, :]
    same = (tt // 64) == (ii // 64)
    gmask = np.zeros((128, 2, 3, 128), np.float32)
    gmask[:, 0, 0] = (tt <= ii) & same
    gmask[:, 0, 1] = (tt > ii) & same
    gmask[:, 0, 2] = np.where((tt <= ii) & same, 0.0, NEG)
    gmask[:, 1, 0] = (tt >= ii) & same
    gmask[:, 1, 1] = (tt < ii) & same
    gmask[:, 1, 2] = np.where((tt >= ii) & same, 0.0, NEG)
    gaux = np.zeros((128, 258), np.float32)
    gaux[0:64, 0:128] = 1.0
    gaux[64:128, 128:256] = 1.0
    gaux[0:64, 256] = 1.0
    gaux[64:128, 257] = 1.0
    return {"cosT": cosT, "sinT": sinT, "permT": perm, "mprev": np.tile(mprev, (1, 4)), "mnext": np.tile(mnext, (1, 4)), "gmask": gmask, "gaux": gaux}


CONSTS = _make_consts()


def kernel(**inputs):
    inputs = {k: np.asarray(v) for k, v in inputs.items()}
    P = build()
    n = 8
    in_maps = []
    for b in range(n):
        m = host_inputs(inputs, b)
        in_maps.append({k: m[k] for k in P.din})
    res = run_bass_kernel_spmd(P.nc, in_maps, core_ids=list(range(n)))
    return np.stack([np.asarray(r["out"], dtype=np.float32) for r in res.results], 0)
```
